# Optimizing a Trainium2 kernel written in Bass

```python
import jax, jax.numpy as jnp
from jax import lax
import numpy as np

D_MODEL = 1024
BATCH = 4
SEQ = 4096
DEPTH = 4
DEC_BATCH = 8
DEC_SEQ = 32
PAST_LEN = 4096

CHUNK = 64
DN_HEADS = 4
DN_HEAD_DIM = 128
DN_DIM = DN_HEADS * DN_HEAD_DIM
CONV_WIDTH = 4
CONV_DIM = 3 * DN_DIM
SWA_HEADS = 8
SWA_KV_HEADS = 2
SWA_GROUP = SWA_HEADS // SWA_KV_HEADS
SWA_HEAD_DIM = 64
SWA_DIM = SWA_HEADS * SWA_HEAD_DIM
SWA_KV_DIM = SWA_KV_HEADS * SWA_HEAD_DIM
WINDOW = 128
WINDOW_CHUNKS = WINDOW // CHUNK
D_FF = 2816
RMS_EPS = 1e-6
L2_EPS = 1e-6
NEG_INF = -1e30
IN_SPLITS = (CONV_DIM, DN_DIM, DN_HEADS, DN_HEADS, SWA_DIM, SWA_KV_DIM, SWA_KV_DIM, D_MODEL, D_MODEL)
IN_DIM = sum(IN_SPLITS)
IN_OFFSETS = tuple(int(o) for o in np.cumsum(IN_SPLITS)[:-1])

kernel_name = 'hybrid_gdn_swa_macaron_stream_step'


def rms_norm(x, w):
    xf = x.astype(jnp.float32)
    y = xf * lax.rsqrt(jnp.mean(xf * xf, axis=-1, keepdims=True) + RMS_EPS)
    return (y * w.astype(jnp.float32)).astype(x.dtype)


def l2_norm(x):
    return x * lax.rsqrt(jnp.sum(x * x, axis=-1, keepdims=True) + L2_EPS)


def swiglu(h, wg, wu, wd):
    return (jax.nn.silu(h @ wg) * (h @ wu)) @ wd


def alibi_slopes():
    return 2.0 ** (-8.0 * jnp.arange(1, SWA_HEADS + 1, dtype=jnp.float32) / SWA_HEADS)


def causal_conv(x, buf, w):
    T = x.shape[1]
    xp = jnp.concatenate([buf.astype(x.dtype), x], axis=1)
    y = xp[:, 0:T] * w[0]
    for i in range(1, CONV_WIDTH):
        y = y + xp[:, i:i + T] * w[i]
    return jax.nn.silu(y), xp[:, T:]


def gated_delta_rule(q, k, v, beta, g, S0, chunk):
    B, T, H, DK = q.shape
    DV = v.shape[-1]
    n = T // chunk

    def blocks(a):
        a = a.reshape((B, n, chunk, H) + a.shape[3:])
        return jnp.swapaxes(jnp.moveaxis(a, 1, 0), 2, 3)

    q, k, v, beta, g = (blocks(a) for a in (q, k, v, beta, g))
    g_cum = jnp.cumsum(g, axis=-1)
    idx = jnp.arange(chunk)
    incl = idx[:, None] >= idx[None, :]
    strict = idx[:, None] > idx[None, :]
    decay = jnp.exp(jnp.where(incl, g_cum[..., :, None] - g_cum[..., None, :], -jnp.inf))
    kb = k * beta[..., None]
    lower = jnp.where(strict, jnp.einsum('nbhid,nbhjd->nbhij', kb, k) * decay, 0.0)
    eye = jnp.eye(chunk, dtype=jnp.float32)
    t_mat = lax.linalg.triangular_solve(lower + eye, jnp.broadcast_to(eye, lower.shape),
                                        left_side=True, lower=True)
    w = t_mat @ (kb * jnp.exp(g_cum)[..., None])
    u = t_mat @ (v * beta[..., None])
    q_dec = q * jnp.exp(g_cum)[..., None]
    intra = jnp.einsum('nbhid,nbhjd->nbhij', q, k) * decay
    g_last = g_cum[..., -1]
    k_rem = k * jnp.exp(g_last[..., None] - g_cum)[..., None]

    def step(S, xs):
        w_c, u_c, qd_c, in_c, kr_c, gl_c = xs
        v_new = u_c - w_c @ S
        o = qd_c @ S + in_c @ v_new
        S = S * jnp.exp(gl_c)[..., None, None] + jnp.swapaxes(kr_c, -1, -2) @ v_new
        return S, o

    S, o = lax.scan(step, S0, (w, u, q_dec, intra, k_rem, g_last))
    o = jnp.moveaxis(jnp.swapaxes(o, 2, 3), 0, 1).reshape(B, T, H, DV)
    return o, S


def swa_attention(q, k, v, q_pos, k_pos, sinks):
    s = jnp.einsum('bnqkgd,bnskd->bnkgqs', q.astype(jnp.float32), k.astype(jnp.float32))
    s = s * (SWA_HEAD_DIM ** -0.5)
    slopes = alibi_slopes().reshape(SWA_KV_HEADS, SWA_GROUP)
    dist = jnp.abs(q_pos[:, :, None] - k_pos[:, None, :]).astype(jnp.float32)
    s = s - slopes[None, None, :, :, None, None] * dist[None, :, None, None]
    qc = q_pos[:, :, None] // CHUNK
    kc = k_pos[:, None, :] // CHUNK
    vis = (kc <= qc) & (kc >= qc - WINDOW_CHUNKS) & (k_pos[:, None, :] >= 0)
    s = jnp.where(vis[None, :, None, None], s, NEG_INF)
    sink = sinks.astype(jnp.float32).reshape(SWA_KV_HEADS, SWA_GROUP)[None, None, :, :, None, None]
    m = jnp.maximum(jnp.max(s, axis=-1, keepdims=True), sink)
    p = jnp.exp(s - m)
    p = p / (jnp.sum(p, axis=-1, keepdims=True) + jnp.exp(sink - m))
    return jnp.einsum('bnkgqs,bnskd->bnqkgd', p, v.astype(jnp.float32))


def token_mixer(h, start, conv_buf, S0, k_cache, v_cache, w_in, conv_w, a_log, dt_bias, dn_norm,
                q_norm, k_norm, sinks, w_o_dn, w_o_swa, w_out):
    B, T, _ = h.shape
    z = h @ w_in
    qkv_dn, gate_dn, b_dn, a_dn, q_s, k_s, v_s, g_a, g_b = jnp.split(z, IN_OFFSETS, axis=-1)

    qkv, new_buf = causal_conv(qkv_dn, conv_buf, conv_w)
    qd, kd, vd = jnp.split(qkv.astype(jnp.float32), 3, axis=-1)
    qd = l2_norm(qd.reshape(B, T, DN_HEADS, DN_HEAD_DIM)) * (DN_HEAD_DIM ** -0.5)
    kd = l2_norm(kd.reshape(B, T, DN_HEADS, DN_HEAD_DIM))
    vd = vd.reshape(B, T, DN_HEADS, DN_HEAD_DIM)
    beta = jax.nn.sigmoid(b_dn.astype(jnp.float32))
    g = -jnp.exp(a_log.astype(jnp.float32)) * jax.nn.softplus(a_dn.astype(jnp.float32) + dt_bias.astype(jnp.float32))
    chunk = CHUNK if T >= CHUNK else T
    o_dn, S = gated_delta_rule(qd, kd, vd, beta, g, S0.astype(jnp.float32), chunk)
    o_dn = rms_norm(o_dn, dn_norm) * jax.nn.silu(gate_dn.astype(jnp.float32).reshape(B, T, DN_HEADS, DN_HEAD_DIM))
    y_dn = o_dn.reshape(B, T, DN_DIM).astype(h.dtype) @ w_o_dn

    qs = rms_norm(q_s.reshape(B, T, SWA_KV_HEADS, SWA_GROUP, SWA_HEAD_DIM), q_norm)
    ks = rms_norm(k_s.reshape(B, T, SWA_KV_HEADS, SWA_HEAD_DIM), k_norm)
    vs = v_s.reshape(B, T, SWA_KV_HEADS, SWA_HEAD_DIM)
    if k_cache is None:
        n = T // CHUNK
        band = WINDOW_CHUNKS + 1
        pad = ((0, 0), (WINDOW_CHUNKS * CHUNK, 0), (0, 0), (0, 0))
        kp = jnp.pad(ks, pad).reshape(B, n + WINDOW_CHUNKS, CHUNK, SWA_KV_HEADS, SWA_HEAD_DIM)
        vp = jnp.pad(vs, pad).reshape(B, n + WINDOW_CHUNKS, CHUNK, SWA_KV_HEADS, SWA_HEAD_DIM)
        kb = jnp.stack([kp[:, i:i + n] for i in range(band)], axis=2).reshape(B, n, band * CHUNK, SWA_KV_HEADS, SWA_HEAD_DIM)
        vb = jnp.stack([vp[:, i:i + n] for i in range(band)], axis=2).reshape(B, n, band * CHUNK, SWA_KV_HEADS, SWA_HEAD_DIM)
        q_pos = jnp.arange(T).reshape(n, CHUNK)
        k_pos = (jnp.arange(n)[:, None] - WINDOW_CHUNKS) * CHUNK + jnp.arange(band * CHUNK)[None, :]
        o_s = swa_attention(qs.reshape(B, n, CHUNK, SWA_KV_HEADS, SWA_GROUP, SWA_HEAD_DIM), kb, vb, q_pos, k_pos, sinks)
        keep = min(WINDOW, T)
        k_rows, v_rows = ks[:, T - keep:], vs[:, T - keep:]
    else:
        w_c = k_cache.shape[1]
        q_pos = start + jnp.arange(T)
        k_pos = jnp.concatenate([start - w_c + jnp.arange(w_c), q_pos])[None]
        kc = jnp.concatenate([k_cache.astype(ks.dtype), ks], axis=1)[:, None]
        vc = jnp.concatenate([v_cache.astype(vs.dtype), vs], axis=1)[:, None]
        o_s = swa_attention(qs[:, None], kc, vc, q_pos[None], k_pos, sinks)
        k_rows, v_rows = ks, vs
    y_swa = o_s.reshape(B, T, SWA_DIM).astype(h.dtype) @ w_o_swa

    merged = jax.nn.sigmoid(g_a) * y_dn + jax.nn.sigmoid(g_b) * y_swa
    return merged @ w_out, new_buf, S, k_rows, v_rows


def trunk(x, start, conv_state, dn_state, k_cache, v_cache, params):
    (ffn1_norm, ffn1_wg, ffn1_wu, ffn1_wd, mix_norm, w_in, conv_w, a_log, dt_bias, dn_norm,
     q_norm, k_norm, sinks, w_o_dn, w_o_swa, w_out, ffn2_norm, ffn2_wg, ffn2_wu, ffn2_wd) = params
    B = x.shape[0]
    dn_out, conv_out, k_out, v_out = [], [], [], []
    for l in range(DEPTH):
        x = x + 0.5 * swiglu(rms_norm(x, ffn1_norm[l]), ffn1_wg[l], ffn1_wu[l], ffn1_wd[l])
        if k_cache is None:
            buf = jnp.zeros((B, CONV_WIDTH - 1, CONV_DIM), x.dtype)
            S0 = jnp.zeros((B, DN_HEADS, DN_HEAD_DIM, DN_HEAD_DIM), jnp.float32)
            kc, vc = None, None
        else:
            buf, S0, kc, vc = conv_state[l], dn_state[l], k_cache[l], v_cache[l]
        mix, nb, S, kr, vr = token_mixer(rms_norm(x, mix_norm[l]), start, buf, S0, kc, vc, w_in[l], conv_w[l],
                                         a_log[l], dt_bias[l], dn_norm[l], q_norm[l], k_norm[l], sinks[l],
                                         w_o_dn[l], w_o_swa[l], w_out[l])
        x = x + mix.astype(x.dtype)
        x = x + 0.5 * swiglu(rms_norm(x, ffn2_norm[l]), ffn2_wg[l], ffn2_wu[l], ffn2_wd[l])
        dn_out.append(S)
        conv_out.append(nb)
        k_out.append(kr)
        v_out.append(vr)
    return x, jnp.stack(dn_out), jnp.stack(conv_out), jnp.stack(k_out), jnp.stack(v_out)


def setup_inputs(seed: int = 0) -> dict:
    key = jax.random.key(seed)
    ks = jax.random.split(key, 32)
    f32 = jnp.float32
    nrm = lambda k, shape, scale: jax.random.normal(k, shape, f32) * scale
    gain = lambda k, shape: 1.0 + 0.01 * jax.random.normal(k, shape, f32)
    w_cache = min(WINDOW, PAST_LEN)
    dt = jnp.exp(jax.random.uniform(ks[20], (DEPTH, DN_HEADS), f32, np.log(0.001), np.log(0.1)))
    return {
        'x_prompt': nrm(ks[0], (BATCH, SEQ, D_MODEL), 1.0),
        'x_sample': nrm(ks[1], (DEC_BATCH, DEC_SEQ, D_MODEL), 1.0),
        'state_dn': nrm(ks[2], (DEPTH, DEC_BATCH, DN_HEADS, DN_HEAD_DIM, DN_HEAD_DIM), 0.1),
        'state_conv': nrm(ks[3], (DEPTH, DEC_BATCH, CONV_WIDTH - 1, CONV_DIM), 1.0),
        'cache_swa_k': nrm(ks[4], (DEPTH, DEC_BATCH, w_cache, SWA_KV_HEADS, SWA_HEAD_DIM), 1.0),
        'cache_swa_v': nrm(ks[5], (DEPTH, DEC_BATCH, w_cache, SWA_KV_HEADS, SWA_HEAD_DIM), 1.0),
        'ffn1_norm': gain(ks[6], (DEPTH, D_MODEL)),
        'ffn1_wg': nrm(ks[7], (DEPTH, D_MODEL, D_FF), D_MODEL ** -0.5),
        'ffn1_wu': nrm(ks[8], (DEPTH, D_MODEL, D_FF), D_MODEL ** -0.5),
        'ffn1_wd': nrm(ks[9], (DEPTH, D_FF, D_MODEL), D_FF ** -0.5),
        'mix_norm': gain(ks[10], (DEPTH, D_MODEL)),
        'w_in': nrm(ks[11], (DEPTH, D_MODEL, IN_DIM), D_MODEL ** -0.5),
        'conv_w': nrm(ks[12], (DEPTH, CONV_WIDTH, CONV_DIM), CONV_WIDTH ** -0.5),
        'a_log': jnp.log(jax.random.uniform(ks[13], (DEPTH, DN_HEADS), f32, 1.0, 16.0)),
        'dt_bias': dt + jnp.log(-jnp.expm1(-dt)),
        'dn_norm': gain(ks[14], (DEPTH, DN_HEAD_DIM)),
        'q_norm': gain(ks[15], (DEPTH, SWA_HEAD_DIM)),
        'k_norm': gain(ks[16], (DEPTH, SWA_HEAD_DIM)),
        'sinks': nrm(ks[17], (DEPTH, SWA_HEADS), 0.5),
        'w_o_dn': nrm(ks[18], (DEPTH, DN_DIM, D_MODEL), DN_DIM ** -0.5),
        'w_o_swa': nrm(ks[19], (DEPTH, SWA_DIM, D_MODEL), SWA_DIM ** -0.5),
        'w_out': nrm(ks[21], (DEPTH, D_MODEL, D_MODEL), D_MODEL ** -0.5),
        'ffn2_norm': gain(ks[22], (DEPTH, D_MODEL)),
        'ffn2_wg': nrm(ks[23], (DEPTH, D_MODEL, D_FF), D_MODEL ** -0.5),
        'ffn2_wu': nrm(ks[24], (DEPTH, D_MODEL, D_FF), D_MODEL ** -0.5),
        'ffn2_wd': nrm(ks[25], (DEPTH, D_FF, D_MODEL), D_FF ** -0.5),
    }


def reference(x_prompt, x_sample, state_dn, state_conv, cache_swa_k, cache_swa_v, ffn1_norm, ffn1_wg,
              ffn1_wu, ffn1_wd, mix_norm, w_in, conv_w, a_log, dt_bias, dn_norm, q_norm, k_norm, sinks,
              w_o_dn, w_o_swa, w_out, ffn2_norm, ffn2_wg, ffn2_wu, ffn2_wd):
    params = (ffn1_norm, ffn1_wg, ffn1_wu, ffn1_wd, mix_norm, w_in, conv_w, a_log, dt_bias, dn_norm,
              q_norm, k_norm, sinks, w_o_dn, w_o_swa, w_out, ffn2_norm, ffn2_wg, ffn2_wu, ffn2_wd)
    y_prompt, dn_prompt, conv_prompt, swa_k_prompt, swa_v_prompt = trunk(
        x_prompt, 0, None, None, None, None, params)
    y_sample, dn_sample, conv_sample, swa_k_sample, swa_v_sample = trunk(
        x_sample, PAST_LEN, state_conv, state_dn, cache_swa_k, cache_swa_v, params)
    return (y_prompt, y_sample, dn_prompt, dn_sample, conv_prompt, conv_sample,
            swa_k_prompt, swa_v_prompt, swa_k_sample, swa_v_sample)
```

```python
import contextlib
import numpy as np
import concourse.bass as bass
import concourse.mybir as mybir
from concourse.bass_utils import run_bass_kernel_spmd

F32 = mybir.dt.float32
BF16 = mybir.dt.bfloat16
ALU = mybir.AluOpType
AF = mybir.ActivationFunctionType
AX = mybir.AxisListType

D = 1024
DEPTH = 4
NSLOT = DEPTH + 1
NP_ = 2048
NS_ = 32
NT = NP_ + NS_
GT = 1056
DFF = 2816
WIN_COLS = 5000
C_QKV, C_GATE, C_QS, C_KD, C_VBA, C_GA, C_GB = 0, 1536, 2048, 2560, 2816, 2952, 3976
NPRM = 218
P_NW, P_CW, P_DNW, P_QNW, P_KNW, P_ALOG, P_DTB, P_SINK = 0, 24, 72, 200, 201, 202, 206, 210
RMS_EPS = 1e-6
L2_EPS = 1e-6
WSLOT = 5632


class Sched:
    ENGS = ("pe", "act", "dve", "pool", "sp")

    def __init__(self, nc):
        self.nc = nc
        self.ops = []
        self.lastw = {}
        self.readers = {}
        self.chan_tot = {}
        self.bank_rd = {}
        self.force_small = False

    def op(self, eng, fn, reads=(), writes=(), dma=None, amt=16, small=False):
        i = len(self.ops)
        deps = set()
        raw = set()
        for r in reads:
            w = self.lastw.get(r)
            if w is not None:
                deps.add(w)
                raw.add(w)
        for r in writes:
            w = self.lastw.get(r)
            if w is not None:
                deps.add(w)
            q = self.readers.get(r)
            if q:
                deps.update(q)
        for r in reads:
            self.readers.setdefault(r, []).append(i)
            if isinstance(r, tuple) and r[0] == "ps":
                br = self.bank_rd.setdefault(r[1], {})
                for e2, j in br.items():
                    if e2 != eng:
                        deps.add(j)
                br[eng] = i
        for r in writes:
            self.lastw[r] = i
            self.readers[r] = []
        o = dict(eng=eng, fn=fn, deps=deps, dma=dma, inc=False, cnt=None, amt=amt, small=(small or self.force_small), raw=raw)
        if dma is not None:
            self.chan_tot[dma] = self.chan_tot.get(dma, 0) + amt
            o["cnt"] = self.chan_tot[dma]
            o["inc"] = True
        self.ops.append(o)
        return i

    def emit(self, final_waits=()):
        nc = self.nc
        ops = self.ops
        waited = {e: {} for e in self.ENGS}
        for i, o in enumerate(ops):
            e = o["eng"]
            ws = []
            for d in sorted(o["deps"]):
                p = ops[d]
                if p["dma"] is not None:
                    key = ("ch", p["dma"])
                    if waited[e].get(key, -1) >= p["cnt"]:
                        continue
                    waited[e][key] = p["cnt"]
                    ws.append(d)
                else:
                    if p["eng"] == e and o["dma"] is None and not (p["small"] and d in o["raw"]):
                        continue
                    key = ("en", p["eng"])
                    if waited[e].get(key, -1) >= d:
                        continue
                    waited[e][key] = d
                    p["inc"] = True
                    ws.append(d)
            o["waits"] = ws
        cnt = {e: 0 for e in self.ENGS}
        for o in ops:
            if o["dma"] is None and o["inc"]:
                cnt[o["eng"]] += 1
                o["cnt"] = cnt[o["eng"]]
        chans = sorted(self.chan_tot)
        with contextlib.ExitStack() as st:
            esem = {e: st.enter_context(nc.semaphore("se_" + e)) for e in self.ENGS}
            csem = {c: st.enter_context(nc.semaphore("sc_" + c)) for c in chans}
            block = st.enter_context(nc.Block())
            handles = {"pe": "tensor", "act": "scalar", "dve": "vector", "pool": "gpsimd", "sp": "sync"}

            def make(e):
                def body(eng):
                    for o in ops:
                        if o["eng"] != e:
                            continue
                        best = {}
                        for d in o["waits"]:
                            p = ops[d]
                            key = ("ch", p["dma"]) if p["dma"] is not None else ("en", p["eng"])
                            if best.get(key, -1) < p["cnt"]:
                                best[key] = p["cnt"]
                        for key, v in best.items():
                            sem = csem[key[1]] if key[0] == "ch" else esem[key[1]]
                            eng.wait_ge(sem, v)
                        ins = o["fn"](eng)
                        if o["dma"] is not None:
                            ins.then_inc(csem[o["dma"]], o["amt"])
                        elif o["inc"]:
                            ins.then_inc(esem[e], 1)
                    if e == "sp":
                        for c in final_waits:
                            eng.wait_ge(csem[c], self.chan_tot[c])
                return body

            for e in self.ENGS:
                getattr(block, handles[e])(make(e))
        return len(ops)


def build_nc(nslot=NSLOT, ngroup=2, phases=None):
    PH = phases or {"ffn1", "proj", "dn", "swa", "merge", "ffn2", "handoff"}
    if "proj" in PH and not (PH & {"p_qkv", "p_gate", "p_qk", "p_vba"}):
        PH = PH | {"p_qkv", "p_gate", "p_qk", "p_vba"}
    NSLOT = nslot
    nc = bass.Bass("TRN2", target_bir_lowering=False)

    def din(name, shape, dt=F32):
        return nc.dram_tensor(name, list(shape), dt, kind="ExternalInput").ap()

    def dout(name, shape, dt=F32):
        return nc.dram_tensor(name, list(shape), dt, kind="ExternalOutput").ap()

    xT0 = din("xT0", [8, 128, NT])
    wg = [din("wg1", [NSLOT, D, DFF]), din("wg2", [NSLOT, D, DFF])]
    wu = [din("wu1", [NSLOT, D, DFF]), din("wu2", [NSLOT, D, DFF])]
    wd = [din("wd1", [NSLOT, DFF, D]), din("wd2", [NSLOT, DFF, D])]
    win = din("win", [NSLOT, D, WIN_COLS])
    wodn = din("wodn", [NSLOT, 512, D])
    woswa = din("woswa", [NSLOT, 512, D])
    wout = din("wout", [NSLOT, D, D])
    prm_d = din("prm", [NSLOT, 128, NPRM])
    sdn_d = din("sdn", [NSLOT, 128, 512])
    sconv_d = din("sconv", [NSLOT, 128, 36])
    skc_d = din("skc", [NSLOT, 128, 256])
    svc_d = din("svc", [NSLOT, 64, 2, 2, 64])
    mcore_d = din("mcore", [128, 1])
    cf_d = din("cf", [128, 128 + 64 + 64 + 128 + 1536])
    yT = dout("yT", [8, 128, NT])
    o_dn = dout("o_dn", [NSLOT, 2, 128, 512])
    o_conv = dout("o_conv", [NSLOT, 2, 128, 36])
    o_kp = dout("o_kp", [NSLOT, 128, 256])
    o_ks = dout("o_ks", [NSLOT, 128, 64])
    o_vp = dout("o_vp", [NSLOT, 64, 256])
    o_vs = dout("o_vs", [NSLOT, 32, 128])
    import os
    DUMPU = os.environ.get("DUMPU", "") == "1"
    if DUMPU:
        dU = dout("dU", [2, 128, 8, GT], BF16)
    send_f = nc.dram_tensor("send_f", [128, 548], F32).ap()
    recv_f = nc.dram_tensor("recv_f", [256, 548], F32).ap()
    send_b = nc.dram_tensor("send_b", [128, 520], BF16).ap()
    recv_b = nc.dram_tensor("recv_b", [256, 520], BF16).ap()

    S = Sched(nc)
    with contextlib.ExitStack() as st:
        def sb(name, shape, dt=F32):
            return st.enter_context(nc.sbuf_tensor("s_" + name, list(shape), dt))

        def psum(name, shape, dt=F32):
            return st.enter_context(nc.psum_tensor(name, list(shape), dt))

        xT = sb("xT", [128, 8, NT])
        hT = sb("hT", [128, 8, GT], BF16)
        U = sb("U", [128, 22, GT], BF16)
        wring = sb("wring", [128, 2, WSLOT], BF16)
        kwin = sb("kwin", [128, 2, 128], BF16)
        ksc = sb("ksc", [128, 2, 128], BF16)
        vbuf = sb("vbuf", [64, 18, 132], BF16)
        vsb = sb("vsb", [64, 3, 132], BF16)
        S_f = sb("S_f", [128, 2, 4, 128])
        S_b = sb("S_b", [128, 2, 4, 128], BF16)
        halo = sb("halo", [128, 2, 12, 3])
        stage = sb("stage", [128, 520])
        cacc = sb("cacc", [128, 548])
        cact = sb("cact", [128, 512])
        rstd = sb("rstd", [128, 512])
        sqb = sb("sqb", [128, 520], BF16)
        sigb = [cacc, cact]
        SG = ["cacc", "cact"]
        kscf = cact[:, 0:256]
        vscf = cact[0:64, 256:512].rearrange("p (a b c) -> p a b c", a=2, b=2)
        cf = sb("cf", [128, 1920])
        cfb = sb("cfb", [128, 384], BF16)
        prm = sb("prm", [128, NPRM])
        prm2 = sb("prm2", [128, 16])
        mcore = sb("mcore", [128, 1])
        kout = sb("kout", [128, 2, 160])
        vout = sb("vout", [64, 3, 128])
        rtmp = cacc
        rtmpb = sqb
        zba = sb("zba", [64, 17, 8])
        tp_beta = sb("tp_beta", [64, 17, 4])
        tp_negb = sb("tp_negb", [64, 17, 4])
        tp_g = sb("tp_g", [64, 17, 4])
        tp_gc = sb("tp_gc", [64, 17, 4])
        tp_egc = sb("tp_egc", [64, 17, 4])
        tp_ekr = sb("tp_ekr", [64, 17, 4])
        tp_egl = sb("tp_egl", [128, 17, 4])
        tp_t = sb("tp_t", [64, 17, 4])
        Gm = sb("Gm", [64, 4, 64])
        dnA = sb("dnA", [64, 4, 128])
        DTs = dnA[:, :, 0:64]
        DTu = dnA[:, :, 64:128]
        tA = Gm
        Pk = [sb("Pk0", [64, 4, 64]), sb("Pk1", [64, 4, 64])]
        PTk = [sb("PTk0", [64, 4, 64]), sb("PTk1", [64, 4, 64])]
        Rk0 = sb("Rk0", [64, 4, 64])
        Rk = [Rk0, Rk0]
        Rbf = sb("Rbf", [64, 4, 64], BF16)
        inT = sb("inT", [64, 4, 64], BF16)
        kg = sb("kg", [64, 4, 128], BF16)
        kr = sb("kr", [64, 4, 128], BF16)
        vtm = sb("vtm", [64, 4, 128], BF16)
        u_sb = sb("u_sb", [64, 4, 128])
        w0T = sb("w0T", [128, 4, 64], BF16)
        vnew = sb("vnew", [64, 4, 128], BF16)
        o_sb = dnA
        on2 = vtm
        ss4 = sb("ss4", [64, 8])
        tS = sb("tS", [128, 4, 128])
        sct = sb("sct", [64, 8, 64])
        t1 = sct[:].rearrange("p (h a) d -> p h (a d)", a=2)
        PTs = [kg[:].rearrange("p h (a d) -> p (h a) d", a=2), vtm[:].rearrange("p h (a d) -> p (h a) d", a=2),
               vnew[:].rearrange("p h (a d) -> p (h a) d", a=2)]
        PTn = ["kg", "vtm", "vnew"]
        den = ss4
        osw = kr[:].rearrange("p h (a d) -> p (h a) d", a=2)

        pf = [psum("pf%d" % i, [128, 512]) for i in range(7)]
        pb = psum("pb", [128, 2, 512], BF16)

        ident_f = cf[:, 0:128]
        SL = cf[0:64, 128:192]
        UT = cf[0:64, 192:256]
        ones_f = cf[0:64, 256:384]
        BT = cf[0:64, 384:1920].rearrange("p (a h i) -> p a h i", a=3, h=8)
        ident_b = cfb[:, 0:128]
        ones_b = cfb[:, 128:256]
        blk_b = cfb[:, 256:384]

        op = S.op

        def T_(tiles):
            for (c0, n) in tiles:
                S.force_small = (n < 128)
                yield (c0, n)
            S.force_small = False

        def RU(j, lc0, n):
            return [("U", j, b) for b in range(lc0 // 64, (lc0 + n - 1) // 64 + 1)]

        def RH(lc0, n):
            return [("hT", b) for b in range(lc0 // 512, (lc0 + n - 1) // 512 + 1)]

        def RX(c0, n):
            return [("xT", b) for b in range(c0 // 512, (c0 + n - 1) // 512 + 1)]

        bank_rr = {"i": 0}

        op("sp", lambda e: e.dma_start(out=xT[:], in_=xT0.rearrange("j p t -> p j t")),
           writes=[("xT", b) for b in range(5)], dma="ld_x")
        op("sp", lambda e: e.dma_start(out=cf[:], in_=cf_d), writes=["cf"], dma="ld_cf")
        op("sp", lambda e: e.dma_start(out=mcore[:], in_=mcore_d), writes=["mcore"], dma="ld_mc")
        op("dve", lambda e: e.tensor_copy(cfb[:, 0:128], cf[:, 0:128]), reads=["cf"], writes=["cfb"])
        op("dve", lambda e: e.memset(cfb[:, 128:256], 1.0), writes=["cfb"])
        op("dve", lambda e: e.memset(cfb[:, 256:384], 0.0), writes=["cfb"])
        op("dve", lambda e: e.memset(cfb[0:64, 256:320], 1.0), writes=["cfb"])
        op("dve", lambda e: e.memset(cfb[64:128, 320:384], 1.0), writes=["cfb"])
        op("dve", lambda e: e.memset(vbuf[:], 1.0), writes=["vbuf_all"])
        op("dve", lambda e: e.memset(vsb[:], 1.0), writes=["vsb_all"])
        op("dve", lambda e: e.memset(zba[:], 0.0), writes=["zba"])
        op("dve", lambda e: e.memset(S_f[:, 0], 0.0), writes=["S_f0"])
        op("dve", lambda e: e.memset(halo[:, 0], 0.0), writes=[("halo", 0, j) for j in range(12)])
        op("dve", lambda e: e.memset(kwin[:], 0.0), writes=["kwin"])
        op("dve", lambda e: e.memset(vbuf[:, 0:2, 0:64], 0.0), reads=["vbuf_all"], writes=[("vbuf", 0), ("vbuf", 1)])
        op("dve", lambda e: e.memset(vbuf[:, 0:2, 66:130], 0.0), writes=[("vbuf", 0), ("vbuf", 1)])

        wstate = {"n": 0}

        def load_panel(src2d, K, ncols):
            KC = K // 128
            s = wstate["n"] % 2
            wstate["n"] += 1
            view = wring[:, s, 0:KC * ncols].rearrange("p (k n) -> p k n", k=KC)
            op("pool", lambda e: e.dma_start(out=view, in_=src2d.rearrange("(k p) n -> p k n", p=128)),
               writes=[("wr", s)], dma="w%d" % s)
            return view, ("wr", s)

        def next_bank(cands):
            b = cands[bank_rr["i"] % len(cands)]
            bank_rr["i"] += 1
            return b

        def rms_norm_to_hT(tiles, g, nwi):
            for (c0, n) in T_(tiles):
                lc0 = c0 - g * 1024
                hview = hT[:, :, lc0:lc0 + n]
                op("act", lambda e, hview=hview, c0=c0, n=n: e.activation(hview, xT[:, :, c0:c0 + n], AF.Square),
                   reads=RX(c0, n), writes=RH(lc0, n))
                ps = pf[4]
                for k in range(8):
                    op("pe", lambda e, k=k, lc0=lc0, n=n: e.matmul(ps[:, 0:n], ones_b, hT[:, k, lc0:lc0 + n],
                                                                    start=(k == 0), stop=(k == 7)),
                       reads=RH(lc0, n) + ["cfb"], writes=[("ps", 4)])
                op("act", lambda e, n=n: e.activation(rstd[:, 0:n], ps[:, 0:n], AF.Ln, bias=RMS_EPS, scale=1.0 / D),
                   reads=[("ps", 4)], writes=["rstd"])
                op("act", lambda e, n=n: e.activation(rstd[:, 0:n], rstd[:, 0:n], AF.Exp, scale=-0.5),
                   reads=["rstd"], writes=["rstd"])
                for k in range(8):
                    op("dve", lambda e, k=k, c0=c0, lc0=lc0, n=n: e.scalar_tensor_tensor(
                        hT[:, k, lc0:lc0 + n], xT[:, k, c0:c0 + n], prm[:, P_NW + nwi * 8 + k:P_NW + nwi * 8 + k + 1],
                        rstd[:, 0:n], ALU.mult, ALU.mult),
                       reads=RX(c0, n) + ["rstd", "prm"], writes=RH(lc0, n))

        def ffn(s, tiles, g, fi):
            for (f0, ncols) in [(0, 512), (512, 512), (1024, 512), (1536, 512), (2048, 512), (2560, 256)]:
                vg, rg = load_panel(wg[fi][s, :, f0:f0 + ncols], D, ncols)
                vu, ru = load_panel(wu[fi][s, :, f0:f0 + ncols], D, ncols)
                for jj in range(ncols // 128):
                    f = f0 // 128 + jj
                    for (c0, n) in T_(tiles):
                        lc0 = c0 - g * 1024
                        bg = next_bank([0, 1])
                        bu = bg + 2
                        for k in range(8):
                            op("pe", lambda e, k=k, jj=jj, lc0=lc0, n=n, bg=bg, vg=vg: e.matmul(
                                pf[bg][:, 0:n], vg[:, k, jj * 128:(jj + 1) * 128], hT[:, k, lc0:lc0 + n],
                                start=(k == 0), stop=(k == 7)),
                               reads=RH(lc0, n) + [rg], writes=[("ps", bg)])
                        for k in range(8):
                            op("pe", lambda e, k=k, jj=jj, lc0=lc0, n=n, bu=bu, vu=vu: e.matmul(
                                pf[bu][:, 0:n], vu[:, k, jj * 128:(jj + 1) * 128], hT[:, k, lc0:lc0 + n],
                                start=(k == 0), stop=(k == 7)),
                               reads=RH(lc0, n) + [ru], writes=[("ps", bu)])
                        sl = bg
                        op("act", lambda e, n=n, bg=bg, sl=sl: e.activation(sigb[sl][:, 0:n], pf[bg][:, 0:n], AF.Silu),
                           reads=[("ps", bg)], writes=[SG[sl]])
                        op("dve", lambda e, n=n, bu=bu, sl=sl, f=f, lc0=lc0: e.tensor_tensor(
                            U[:, f, lc0:lc0 + n], sigb[sl][:, 0:n], pf[bu][:, 0:n], ALU.mult),
                           reads=[SG[sl], ("ps", bu)], writes=RU(f, lc0, n))
            for ob in range(4):
                vd, rd = load_panel(wd[fi][s, :, ob * 256:(ob + 1) * 256], DFF, 256)
                for jj in range(2):
                    o = ob * 2 + jj
                    for (c0, n) in T_(tiles):
                        lc0 = c0 - g * 1024
                        b = next_bank([0, 1, 2, 3])
                        for k in range(22):
                            op("pe", lambda e, k=k, jj=jj, lc0=lc0, n=n, b=b, vd=vd: e.matmul(
                                pf[b][:, 0:n], vd[:, k, jj * 128:(jj + 1) * 128], U[:, k, lc0:lc0 + n],
                                start=(k == 0), stop=(k == 21)),
                               reads=RU(k, lc0, n) + [rd], writes=[("ps", b)])
                        op("dve", lambda e, o=o, c0=c0, n=n, b=b: e.scalar_tensor_tensor(
                            xT[:, o, c0:c0 + n], pf[b][:, 0:n], 0.5, xT[:, o, c0:c0 + n], ALU.mult, ALU.add),
                           reads=[("ps", b)] + RX(c0, n), writes=RX(c0, n))

        def rsqrt_chain(ps_ap, n, scale, eps):
            op("act", lambda e: e.activation(rstd[:, 0:n], ps_ap, AF.Ln, bias=eps, scale=scale),
               reads=[("ps", 4)], writes=["rstd"])
            op("act", lambda e: e.activation(rstd[:, 0:n], rstd[:, 0:n], AF.Exp, scale=-0.5),
               reads=["rstd"], writes=["rstd"])

        def proj_in(s, tiles, g):
            W = win
            for pb_ in (range(3) if "p_qkv" in PH else []):
                v, r = load_panel(W[s, :, C_QKV + pb_ * 512:C_QKV + (pb_ + 1) * 512], D, 512)
                for jj in range(4):
                    j = pb_ * 4 + jj
                    for (c0, n) in T_(tiles):
                        lc0 = c0 - g * 1024
                        stq = 1 if c0 >= NP_ else 0
                        b = next_bank([0, 1, 2, 3])
                        for k in range(8):
                            op("pe", lambda e, k=k, jj=jj, lc0=lc0, n=n, b=b, v=v: e.matmul(
                                pf[b][:, 0:n], v[:, k, jj * 128:(jj + 1) * 128], hT[:, k, lc0:lc0 + n],
                                start=(k == 0), stop=(k == 7)),
                               reads=RH(lc0, n) + [r], writes=[("ps", b)])
                        op("dve", lambda e, stq=stq, j=j: e.tensor_copy(stage[:, 0:3], halo[:, stq, j, :]),
                           reads=[("halo", stq, j)], writes=["stage"], small=True)
                        op("act", lambda e, n=n, b=b: e.copy(stage[:, 3:3 + n], pf[b][:, 0:n]),
                           reads=[("ps", b)], writes=["stage"])
                        op("dve", lambda e, stq=stq, j=j, n=n: e.tensor_copy(halo[:, stq, j, :], stage[:, n:n + 3]),
                           reads=["stage"], writes=[("halo", stq, j)], small=True)
                        cw0 = P_CW + j * 4
                        op("dve", lambda e, n=n, cw0=cw0: e.tensor_scalar(
                            cacc[:, 0:n], stage[:, 0:n], prm[:, cw0:cw0 + 1], None, ALU.mult),
                           reads=["stage", "prm"], writes=["cacc"])
                        for t in range(1, 4):
                            op("dve", lambda e, n=n, cw0=cw0, t=t: e.scalar_tensor_tensor(
                                cacc[:, 0:n], stage[:, t:t + n], prm[:, cw0 + t:cw0 + t + 1], cacc[:, 0:n],
                                ALU.mult, ALU.add),
                               reads=["stage", "prm", "cacc"], writes=["cacc"])
                        if j >= 8:
                            op("act", lambda e, n=n, j=j, lc0=lc0: e.activation(U[:, j, lc0:lc0 + n], cacc[:, 0:n], AF.Silu),
                               reads=["cacc"], writes=RU(j, lc0, n))
                        else:
                            op("act", lambda e, n=n: e.activation(cact[:, 0:n], cacc[:, 0:n], AF.Silu),
                               reads=["cacc"], writes=["cact"])
                            op("act", lambda e, n=n: e.activation(sqb[:, 0:n], cact[:, 0:n], AF.Square),
                               reads=["cact"], writes=["sqb"])
                            op("pe", lambda e, n=n: e.matmul(pf[4][:, 0:n], ones_b, sqb[:, 0:n], start=True, stop=True),
                               reads=["sqb", "cfb"], writes=[("ps", 4)])
                            rsqrt_chain(pf[4][:, 0:n], n, 1.0, L2_EPS)
                            qs = (128.0 ** -0.5) if j < 4 else 1.0
                            op("dve", lambda e, n=n, j=j, lc0=lc0, qs=qs: e.scalar_tensor_tensor(
                                U[:, j, lc0:lc0 + n], cact[:, 0:n], qs, rstd[:, 0:n], ALU.mult, ALU.mult),
                               reads=["cact", "rstd"], writes=RU(j, lc0, n))
            v, r = load_panel(W[s, :, C_GATE:C_GATE + 512], D, 512)
            for jj in (range(4) if "p_gate" in PH else []):
                for (c0, n) in T_(tiles):
                    lc0 = c0 - g * 1024
                    b = next_bank([0, 1, 2, 3])
                    for k in range(8):
                        op("pe", lambda e, k=k, jj=jj, lc0=lc0, n=n, b=b, v=v: e.matmul(
                            pf[b][:, 0:n], v[:, k, jj * 128:(jj + 1) * 128], hT[:, k, lc0:lc0 + n],
                            start=(k == 0), stop=(k == 7)),
                           reads=RH(lc0, n) + [r], writes=[("ps", b)])
                    op("act", lambda e, n=n, b=b, jj=jj, lc0=lc0: e.activation(U[:, 12 + jj, lc0:lc0 + n], pf[b][:, 0:n], AF.Silu),
                       reads=[("ps", b)], writes=RU(12 + jj, lc0, n))
            for (cbase, nch, ubase, pcol) in ([(C_QS, 4, 16, "q"), (C_KD, 2, 20, "k")] if "p_qk" in PH else []):
                v, r = load_panel(W[s, :, cbase:cbase + nch * 128], D, nch * 128)
                for jj in range(nch):
                    for (c0, n) in T_(tiles):
                        lc0 = c0 - g * 1024
                        b = next_bank([0, 1, 2, 3])
                        for k in range(8):
                            op("pe", lambda e, k=k, jj=jj, lc0=lc0, n=n, b=b, v=v: e.matmul(
                                pf[b][:, 0:n], v[:, k, jj * 128:(jj + 1) * 128], hT[:, k, lc0:lc0 + n],
                                start=(k == 0), stop=(k == 7)),
                               reads=RH(lc0, n) + [r], writes=[("ps", b)])
                        op("act", lambda e, n=n, b=b: e.activation(sqb[:, 0:n], pf[b][:, 0:n], AF.Square),
                           reads=[("ps", b)], writes=["sqb"])
                        op("pe", lambda e, n=n: e.matmul(pf[4][:, 0:n], blk_b, sqb[:, 0:n], start=True, stop=True),
                           reads=["sqb", "cfb"], writes=[("ps", 4)])
                        rsqrt_chain(pf[4][:, 0:n], n, 1.0 / 64, RMS_EPS)
                        sc = prm2[:, 0:1] if pcol == "q" else prm[:, P_KNW:P_KNW + 1]
                        op("dve", lambda e, n=n, b=b, jj=jj, lc0=lc0, sc=sc, ubase=ubase: e.scalar_tensor_tensor(
                            U[:, ubase + jj, lc0:lc0 + n], pf[b][:, 0:n], sc, rstd[:, 0:n], ALU.mult, ALU.mult),
                           reads=[("ps", b), "rstd", "prm", "prm2"], writes=RU(ubase + jj, lc0, n))
                        if pcol == "k":
                            if c0 >= NP_:
                                op("dve", lambda e, n=n, b=b, jj=jj, sc=sc: e.scalar_tensor_tensor(
                                    kout[:, jj, 128:160], pf[b][:, 0:n], sc, rstd[:, 0:n], ALU.mult, ALU.mult),
                                   reads=[("ps", b), "rstd", "prm"], writes=["kout"])
                            elif c0 + n == NP_:
                                op("dve", lambda e, n=n, b=b, jj=jj, sc=sc: e.scalar_tensor_tensor(
                                    kout[:, jj, 0:128], pf[b][:, n - 128:n], sc, rstd[:, n - 128:n], ALU.mult, ALU.mult),
                                   reads=[("ps", b), "rstd", "prm"], writes=["kout"])
            v, r = load_panel(W[s, :, C_VBA:C_VBA + 136], D, 136)
            chunks = chunk_list(g) if "p_vba" in PH else []
            for ci, (stq, lc0, C) in enumerate(chunks):
                b = next_bank([0, 1, 2, 3])
                for k in range(8):
                    op("pe", lambda e, k=k, lc0=lc0, C=C, b=b, v=v: e.matmul(
                        pf[b][0:C, 0:136], hT[:, k, lc0:lc0 + C], v[:, k, :], start=(k == 0), stop=(k == 7)),
                       reads=RH(lc0, C) + [r], writes=[("ps", b)])
                if stq == 0:
                    vdst = vbuf[0:C, 2 + ci, :]
                    vres = ("vbuf", 2 + ci)
                else:
                    vdst = vsb[0:C, 2, :]
                    vres = ("vsb", 2)
                import os
                DV = os.environ.get("DBGV", "")
                if "noact" not in DV: op("dve", lambda e, C=C, b=b, vdst=vdst: e.tensor_copy(
                    vdst.rearrange("p (k d) -> p k d", k=2)[:, :, 0:64],
                    pf[b][0:C, 0:128].rearrange("p (k d) -> p k d", k=2)),
                   reads=[("ps", b), "vbuf_all", "vsb_all"], writes=[vres])
                if "nozba" not in DV: op("dve", lambda e, C=C, b=b, ci=ci: e.tensor_copy(zba[0:C, ci, :], pf[b][0:C, 128:136]),
                   reads=[("ps", b)], writes=["zba"], small=True)
                if "novout" in DV:
                    pass
                elif stq == 1:
                    op("dve", lambda e, C=C, b=b: e.tensor_copy(vout[0:C, 2, :], pf[b][0:C, 0:128]),
                       reads=[("ps", b)], writes=["vout"])
                elif g == ngroup - 1 and ci >= 14:
                    op("dve", lambda e, C=C, b=b, ci=ci: e.tensor_copy(vout[0:C, ci - 14, :], pf[b][0:C, 0:128]),
                       reads=[("ps", b)], writes=["vout"])

        def chunk_list(g):
            ch = [(0, 64 * i, 64) for i in range(16)]
            if g == ngroup - 1:
                ch.append((1, 1024, 32))
            return ch

        def dn_params(g):
            chunks = chunk_list(g)
            nch = len(chunks)
            npc = 16
            has_s = nch > 16
            A3 = lambda t, a=0, b=4: t[:, 0:nch, a:b]
            op("act", lambda e: e.activation(tp_beta[:, 0:nch, :], zba[:, 0:nch, 0:4], AF.Sigmoid),
               reads=["zba"], writes=["tp_beta"], small=True)
            op("dve", lambda e: e.tensor_scalar(tp_negb[:, 0:nch, :], tp_beta[:, 0:nch, :], -1.0, None, ALU.mult),
               reads=["tp_beta"], writes=["tp_negb"], small=True)
            op("dve", lambda e: e.tensor_tensor(tp_t[:, 0:nch, :], zba[:, 0:nch, 4:8],
                                                prm[0:64, P_DTB:P_DTB + 4].unsqueeze(1).broadcast_to([64, nch, 4]), ALU.add),
               reads=["zba", "prm"], writes=["tp_t"], small=True)
            op("act", lambda e: e.activation(tp_t[:, 0:nch, :], tp_t[:, 0:nch, :], AF.Exp), reads=["tp_t"], writes=["tp_t"], small=True)
            op("act", lambda e: e.activation(tp_t[:, 0:nch, :], tp_t[:, 0:nch, :], AF.Ln, bias=1.0), reads=["tp_t"], writes=["tp_t"], small=True)
            op("dve", lambda e: e.scalar_tensor_tensor(tp_g[:, 0:nch, :], tp_t[:, 0:nch, :], -1.0,
                                                       prm2[0:64, 1:5].unsqueeze(1).broadcast_to([64, nch, 4]), ALU.mult, ALU.mult),
               reads=["tp_t", "prm2"], writes=["tp_g"], small=True)
            g2 = tp_g[:].rearrange("p c h -> p (c h)")
            op("pe", lambda e: e.matmul(pf[5][0:64, 0:npc * 4], UT, g2[:, 0:npc * 4], start=True, stop=True),
               reads=["tp_g", "cf"], writes=[("ps", 5)])
            op("pe", lambda e: e.matmul(pf[6][:, 0:npc * 4], ones_f, g2[:, 0:npc * 4], start=True, stop=True),
               reads=["tp_g", "cf"], writes=[("ps", 6)])
            if has_s:
                op("pe", lambda e: e.matmul(pf[5][0:32, 64:68], cf[0:32, 192:224], g2[0:32, 64:68], start=True, stop=True),
                   reads=["tp_g", "cf"], writes=[("ps", 5)])
                op("pe", lambda e: e.matmul(pf[6][:, 64:68], cf[0:32, 256:384], g2[0:32, 64:68], start=True, stop=True),
                   reads=["tp_g", "cf"], writes=[("ps", 6)])
            n4 = nch * 4
            f2 = lambda t: t[:].rearrange("p c h -> p (c h)")[:, 0:n4]
            op("act", lambda e: e.copy(f2(tp_gc), pf[5][0:64, 0:n4]), reads=[("ps", 5)], writes=["tp_gc"], small=True)
            op("act", lambda e: e.activation(f2(tp_egc), pf[5][0:64, 0:n4], AF.Exp), reads=[("ps", 5)], writes=["tp_egc"], small=True)
            op("dve", lambda e: e.tensor_tensor(f2(tp_ekr), pf[6][0:64, 0:n4], f2(tp_gc), ALU.subtract),
               reads=[("ps", 6), "tp_gc"], writes=["tp_ekr"], small=True)
            op("act", lambda e: e.activation(f2(tp_ekr), f2(tp_ekr), AF.Exp), reads=["tp_ekr"], writes=["tp_ekr"], small=True)
            op("act", lambda e: e.activation(tp_egl[:].rearrange("p c h -> p (c h)")[:, 0:n4], pf[6][:, 0:n4], AF.Exp),
               reads=[("ps", 6)], writes=["tp_egl"], small=True)

        def bc_h(ap2, C, n):
            return ap2.unsqueeze(2).broadcast_to([C, 4, n])

        def bc_m(ap2, C):
            return ap2.unsqueeze(1).broadcast_to([C, 4, C])

        def dn_chunk(ci, stq, lc0, C):
            S.force_small = (C < 64)
            cs = slice(lc0, lc0 + C)
            qT = lambda h: U[:, h, cs]
            kT = lambda h: U[:, 4 + h, cs]
            vT = lambda h: U[:, 8 + h, cs]
            rq = [r for h in range(4) for r in RU(h, lc0, C)]
            rk = [r for h in range(4) for r in RU(4 + h, lc0, C)]
            rv = [r for h in range(4) for r in RU(8 + h, lc0, C)]
            v3 = lambda t, n=None: (t[0:C, :, 0:(n or C)])
            op("dve", lambda e: e.tensor_tensor(v3(Gm), bc_h(tp_g[0:C, ci, :], C, C), bc_m(cf[0:C, 192:192 + C], C), ALU.mult),
               reads=["tp_g", "cf"], writes=["Gm"])
            op("pe", lambda e: e.matmul(pf[5][0:C, 0:4 * C].rearrange("p (h i) -> p h i", h=4), cf[0:C, 128:128 + C], v3(Gm),
                                        start=True, stop=True), reads=["Gm", "cf"], writes=[("ps", 5)])
            op("act", lambda e: e.activation(v3(DTu), pf[5][0:C, 0:4 * C].rearrange("p (h i) -> p h i", h=4), AF.Exp),
               reads=[("ps", 5)], writes=["DTu"])
            op("dve", lambda e: e.tensor_tensor(v3(DTs), v3(DTu), bc_m(cf[0:C, 192:192 + C], C), ALU.mult),
               reads=["DTu", "cf"], writes=["DTs"])
            for h in range(4):
                op("pe", lambda e, h=h: e.matmul(pf[6][0:C, h * C:(h + 1) * C], kT(h), kT(h), start=True, stop=True),
                   reads=RU(4 + h, lc0, C), writes=[("ps", 6)])
            for h in range(4):
                op("pe", lambda e, h=h: e.matmul(pf[5][0:C, h * C:(h + 1) * C], kT(h), qT(h), start=True, stop=True),
                   reads=RU(4 + h, lc0, C) + RU(h, lc0, C) + ["DTu"], writes=[("ps", 5)])
            ps3 = lambda b: pf[b][0:C, 0:4 * C].rearrange("p (h i) -> p h i", h=4)
            op("dve", lambda e: e.tensor_tensor(v3(inT), ps3(5), v3(DTs), ALU.mult), reads=[("ps", 5), "DTs"], writes=["inT"])
            op("dve", lambda e: e.tensor_tensor(v3(DTu), v3(DTs), bc_m(cf[0:C, 0:C], C), ALU.subtract),
               reads=["DTs", "cf"], writes=["DTu"])
            op("dve", lambda e: e.tensor_tensor(v3(tA), ps3(6), v3(DTu), ALU.mult), reads=[("ps", 6), "DTu"], writes=["Gm"])
            op("dve", lambda e: e.tensor_tensor(v3(Pk[0]), v3(tA), bc_h(tp_negb[0:C, ci, :], C, C), ALU.mult),
               reads=["Gm", "tp_negb"], writes=["Pk0"])
            for h in range(4):
                op("pe", lambda e, h=h: e.transpose(pf[6][0:C, h * C:(h + 1) * C], Pk[0][0:C, h, 0:C], cf[0:C, 0:C]),
                   reads=["Pk0", "cf"], writes=[("ps", 6)])
            op("act", lambda e: e.copy(v3(PTk[0]), ps3(6)), reads=[("ps", 6)], writes=["PTk0"])
            op("dve", lambda e: e.tensor_tensor(v3(Rk[0]), v3(Pk[0]), bc_m(cf[0:C, 0:C], C), ALU.add),
               reads=["Pk0", "cf"], writes=["Rk"])
            nlev = 5 if C == 64 else 4
            cur = 0
            for lv in range(nlev):
                nx = 1 - cur
                last = (lv == nlev - 1)
                if not last:
                    for h in range(4):
                        op("pe", lambda e, h=h, cur=cur: e.matmul(pf[5][0:C, h * C:(h + 1) * C], PTk[cur][0:C, h, 0:C], Pk[cur][0:C, h, 0:C],
                                                                  start=True, stop=True),
                           reads=["Pk%d" % cur, "PTk%d" % cur], writes=[("ps", 5)])
                for h in range(4):
                    op("pe", lambda e, h=h, cur=cur: e.matmul(pf[6][0:C, h * C:(h + 1) * C], Pk[cur][0:C, h, 0:C], PTk[cur][0:C, h, 0:C],
                                                              start=True, stop=True),
                       reads=["Pk%d" % cur, "PTk%d" % cur], writes=[("ps", 6)])
                if not last:
                    op("act", lambda e, nx=nx: e.copy(v3(Pk[nx]), ps3(5)), reads=[("ps", 5)], writes=["Pk%d" % nx])
                op("dve", lambda e, nx=nx: e.tensor_copy(v3(PTk[nx]), ps3(6)), reads=[("ps", 6)], writes=["PTk%d" % nx])
                for h in range(4):
                    op("pe", lambda e, h=h, cur=cur, nx=nx: e.matmul(pf[5][0:C, h * C:(h + 1) * C], PTk[nx][0:C, h, 0:C], Rk[cur][0:C, h, 0:C],
                                                                     start=True, stop=True),
                       reads=["PTk%d" % nx, "Rk"], writes=[("ps", 5)])
                if not last:
                    op("dve", lambda e, cur=cur, nx=nx: e.tensor_tensor(v3(Rk[nx]), ps3(5), v3(Rk[cur]), ALU.add),
                       reads=[("ps", 5), "Rk"], writes=["Rk"])
                else:
                    op("dve", lambda e, cur=cur: e.tensor_tensor(v3(Rbf), ps3(5), v3(Rk[cur]), ALU.add),
                       reads=[("ps", 5), "Rk"], writes=["Rbf"])
                cur = nx
            for h in range(4):
                op("pe", lambda e, h=h: e.transpose(pb[0:C, 0, h * 128:(h + 1) * 128], kT(h), ident_b),
                   reads=RU(4 + h, lc0, C) + ["cfb"], writes=[("ps", 7)])
            for h in range(4):
                op("pe", lambda e, h=h: e.transpose(pb[0:C, 1, h * 128:(h + 1) * 128], vT(h), ident_b),
                   reads=RU(8 + h, lc0, C) + ["cfb"], writes=[("ps", 7)])
            pb3 = lambda i: pb[0:C, i, :].rearrange("p (h d) -> p h d", h=4)
            op("dve", lambda e: e.tensor_tensor(kg[0:C], pb3(0), bc_h(tp_egc[0:C, ci, :], C, 128), ALU.mult),
               reads=[("ps", 7), "tp_egc"], writes=["kg"])
            op("dve", lambda e: e.tensor_tensor(kr[0:C], pb3(0), bc_h(tp_ekr[0:C, ci, :], C, 128), ALU.mult),
               reads=[("ps", 7), "tp_ekr"], writes=["kr"])
            op("act", lambda e: e.copy(vtm[0:C], pb3(1)), reads=[("ps", 7)], writes=["vtm"])
            for h in range(4):
                op("pe", lambda e, h=h: e.matmul(pf[6][0:C, h * 128:(h + 1) * 128], Rbf[0:C, h, 0:C], vtm[0:C, h, :], start=True, stop=True),
                   reads=["Rbf", "vtm"], writes=[("ps", 6)])
            for h in range(4):
                op("pe", lambda e, h=h: e.matmul(pf[5][:, h * C:(h + 1) * C], kg[0:C, h, :], Rbf[0:C, h, 0:C], start=True, stop=True),
                   reads=["Rbf", "kg"], writes=[("ps", 5)])
            op("dve", lambda e: e.tensor_tensor(u_sb[0:C], pf[6][0:C, :].rearrange("p (h d) -> p h d", h=4),
                                                bc_h(tp_beta[0:C, ci, :], C, 128), ALU.mult),
               reads=[("ps", 6), "tp_beta"], writes=["u_sb"])
            op("act", lambda e: e.copy(w0T[:, :, 0:C], pf[5][:, 0:4 * C].rearrange("p (h i) -> p h i", h=4)),
               reads=[("ps", 5)], writes=["w0T"])
            Sres = "S_b%d" % stq
            Sfres = "S_f%d" % stq
            for h in range(4):
                op("pe", lambda e, h=h: e.matmul(pf[6][0:C, h * 128:(h + 1) * 128], w0T[:, h, 0:C], S_b[:, stq, h, :], start=True, stop=True),
                   reads=["w0T", Sres], writes=[("ps", 6)])
            p64 = lambda b: pf[b][0:C, :].rearrange("p (h d) -> p h d", h=4)
            op("dve", lambda e: e.tensor_tensor(t1[0:C], p64(6), bc_h(tp_negb[0:C, ci, :], C, 128), ALU.mult),
               reads=[("ps", 6), "tp_negb"], writes=["sct"])
            op("dve", lambda e: e.tensor_tensor(vnew[0:C], t1[0:C], u_sb[0:C], ALU.add), reads=["sct", "u_sb"], writes=["vnew"])
            for h in range(4):
                op("pe", lambda e, h=h: e.matmul(pf[5][0:C, h * 128:(h + 1) * 128], qT(h), S_b[:, stq, h, :], start=True, stop=True),
                   reads=RU(h, lc0, C) + [Sres], writes=[("ps", 5)])
            for h in range(4):
                op("pe", lambda e, h=h: e.matmul(pf[6][0:C, h * 128:(h + 1) * 128], inT[0:C, h, 0:C], vnew[0:C, h, :], start=True, stop=True),
                   reads=["inT", "vnew"], writes=[("ps", 6)])
            op("dve", lambda e: e.tensor_tensor(t1[0:C], p64(5), bc_h(tp_egc[0:C, ci, :], C, 128), ALU.mult),
               reads=[("ps", 5), "tp_egc"], writes=["sct"])
            op("dve", lambda e: e.tensor_tensor(o_sb[0:C], t1[0:C], p64(6), ALU.add), reads=["sct", ("ps", 6)], writes=["DTs", "DTu"])
            for h in range(4):
                op("pe", lambda e, h=h: e.matmul(pf[5][:, h * 128:(h + 1) * 128], kr[0:C, h, :], vnew[0:C, h, :], start=True, stop=True),
                   reads=["kr", "vnew"], writes=[("ps", 5)])
            op("dve", lambda e: e.tensor_tensor(tS[:], S_f[:, stq], tp_egl[:, ci, :].unsqueeze(2).broadcast_to([128, 4, 128]), ALU.mult),
               reads=[Sfres, "tp_egl"], writes=["tS"])
            op("dve", lambda e: e.tensor_tensor(S_f[:, stq], tS[:], pf[5][:, :].rearrange("p (h d) -> p h d", h=4), ALU.add),
               reads=["tS", ("ps", 5)], writes=[Sfres])
            op("act", lambda e: e.copy(S_b[:, stq], S_f[:, stq]), reads=[Sfres], writes=[Sres])
            op("dve", lambda e: e.tensor_tensor(t1[0:C], o_sb[0:C], o_sb[0:C], ALU.mult), reads=["DTs", "DTu"], writes=["sct"])
            op("dve", lambda e: e.tensor_reduce(ss4[0:C, 0:4], t1[0:C], AX.X, ALU.add), reads=["sct"], writes=["ss4"], small=True)
            op("act", lambda e: e.activation(ss4[0:C, 0:4], ss4[0:C, 0:4], AF.Ln, bias=RMS_EPS, scale=1.0 / 128), reads=["ss4"], writes=["ss4"], small=True)
            op("act", lambda e: e.activation(ss4[0:C, 0:4], ss4[0:C, 0:4], AF.Exp, scale=-0.5), reads=["ss4"], writes=["ss4"], small=True)
            op("dve", lambda e: e.tensor_tensor(t1[0:C], o_sb[0:C], bc_h(ss4[0:C, 0:4], C, 128), ALU.mult),
               reads=["DTs", "DTu", "ss4"], writes=["sct"])
            op("dve", lambda e: e.tensor_tensor(on2[0:C], t1[0:C], prm[0:C, P_DNW:P_DNW + 128].unsqueeze(1).broadcast_to([C, 4, 128]), ALU.mult),
               reads=["sct", "prm"], writes=["vtm"])
            for h in range(4):
                op("pe", lambda e, h=h: e.transpose(pb[:, 0, h * C:(h + 1) * C], on2[0:C, h, :], cfb[0:C, 0:C]),
                   reads=["vtm", "cfb"], writes=[("ps", 7)])
            gview = U[:, 12:16, cs]
            rg = [r for h in range(4) for r in RU(12 + h, lc0, C)]
            op("dve", lambda e: e.tensor_tensor(gview, pb[:, 0, 0:4 * C].rearrange("p (h i) -> p h i", h=4), gview, ALU.mult),
               reads=[("ps", 7)] + rg, writes=rg)

        def swa_chunk(ci, stq, lc0, C, g):
            S.force_small = (C < 64)
            cs = slice(lc0, lc0 + C)
            if stq == 0:
                keys = []
                for dl in (2, 1, 0):
                    kc = ci - dl
                    if kc < 0:
                        w0 = (kc + 2) * 64
                        keys.append((dl, lambda kv, hf, w0=w0: kwin[hf * 64:(hf + 1) * 64, kv, w0:w0 + 64], ["kwin"],
                                     vbuf[:, kc + 2, :], [("vbuf", kc + 2)], 64, g == 0))
                    else:
                        keys.append((dl, lambda kv, hf, kc=kc: U[hf * 64:(hf + 1) * 64, 20 + kv, kc * 64:kc * 64 + 64],
                                     RU(20, kc * 64, 64) + RU(21, kc * 64, 64),
                                     vbuf[:, kc + 2, :], [("vbuf", kc + 2)], 64, False))
            else:
                keys = [(2, lambda kv, hf: ksc[hf * 64:(hf + 1) * 64, kv, 0:64], ["ksc"], vsb[:, 0, :], [("vsb", 0)], 64, False),
                        (1, lambda kv, hf: ksc[hf * 64:(hf + 1) * 64, kv, 64:128], ["ksc"], vsb[:, 1, :], [("vsb", 1)], 64, False),
                        (0, lambda kv, hf: U[hf * 64:(hf + 1) * 64, 20 + kv, cs], RU(20, lc0, C) + RU(21, lc0, C),
                         vsb[:, 2, :], [("vsb", 2)], 32, False)]
            rq = [r for j in range(4) for r in RU(16 + j, lc0, C)]
            for idx, (dl, kfn, kres, vap, vres, SK, masked) in enumerate(keys):
                for hh in range(8):
                    kv, j, hf = hh // 4, hh // 2, hh % 2
                    sbk = 5 if hf == 0 else 3
                    op("pe", lambda e, hh=hh, kv=kv, j=j, hf=hf, kfn=kfn, SK=SK, sbk=sbk: e.matmul(
                        pf[sbk][0:SK, j * 64:j * 64 + C], kfn(kv, hf), U[hf * 64:(hf + 1) * 64, 16 + j, cs], start=True, stop=True),
                       reads=kres + rq, writes=[("ps", sbk)])
                import os
                SWL = int(os.environ.get("SWL", "9"))
                if SWL < 2:
                    continue
                for hf, sbk in ((0, 5), (1, 3)):
                    sc3 = pf[sbk][0:SK, 0:256].rearrange("p (h i) -> p h i", h=4)[:, :, 0:C]
                    op("dve", lambda e, sc3=sc3, dl=dl, SK=SK, hf=hf: e.tensor_tensor(
                        sct[0:SK, hf * 4:hf * 4 + 4, 0:C], sc3, BT[0:SK, dl, hf * 4:hf * 4 + 4, 0:C], ALU.add),
                       reads=[("ps", sbk), "cf"], writes=["sct"])
                PT = PTs[idx]
                pn = PTn[idx]
                op("act", lambda e, SK=SK, PT=PT: e.activation(PT[0:SK, :, 0:C], sct[0:SK, :, 0:C], AF.Exp), reads=["sct"], writes=[pn])
                if masked:
                    op("dve", lambda e, SK=SK, PT=PT: e.tensor_scalar(PT[0:SK, :, 0:C], PT[0:SK, :, 0:C], mcore[0:SK, 0:1], None, ALU.mult),
                       reads=[pn, "mcore"], writes=[pn])
            for hh in (range(8) if SWL >= 3 else []):
                kv = hh // 4
                bk = 6 if hh < 4 else 4
                hp = (hh % 2) * 4 + hh // 2
                for idx, (dl, kfn, kres, vap, vres, SK, masked) in enumerate(keys):
                    op("pe", lambda e, hh=hh, kv=kv, bk=bk, vap=vap, SK=SK, idx=idx, hp=hp: e.matmul(
                        pf[bk][0:C, (hh % 4) * 66:(hh % 4) * 66 + 66], PTs[idx][0:SK, hp, 0:C], vap[0:SK, kv * 66:kv * 66 + 66],
                        start=(idx == 0), stop=(idx == 2)),
                       reads=[PTn[idx]] + vres, writes=[("ps", bk)])
            for half, bk in (((0, 6), (1, 4)) if SWL >= 4 else []):
                o3 = pf[bk][0:C, 0:264].rearrange("p (h d) -> p h d", h=4)
                op("dve", lambda e, o3=o3, half=half: e.tensor_tensor(den[0:C, half * 4:half * 4 + 4], o3[:, :, 64],
                                                                      prm2[0:C, 5 + half * 4:9 + half * 4], ALU.add),
                   reads=[("ps", bk), "prm2"], writes=["ss4"], small=True)
                op("dve", lambda e, half=half: e.reciprocal(den[0:C, half * 4:half * 4 + 4], den[0:C, half * 4:half * 4 + 4]),
                   reads=["ss4"], writes=["ss4"], small=True)
                op("dve", lambda e, o3=o3, half=half: e.tensor_tensor(osw[0:C, half * 4:half * 4 + 4, :], o3[:, :, 0:64],
                                                                      den[0:C, half * 4:half * 4 + 4].unsqueeze(2).broadcast_to([C, 4, 64]), ALU.mult),
                   reads=[("ps", bk), "ss4"], writes=["kr"])
            if SWL < 5:
                return
            for j in range(4):
                op("pe", lambda e, j=j: e.transpose(pb[:, 1, j * C:(j + 1) * C], osw[0:C, 2 * j:2 * j + 2, :].rearrange("p a d -> p (a d)"), cfb[0:C, 0:C]),
                   reads=["kr", "cfb"], writes=[("ps", 7)])
            op("act", lambda e: e.copy(U[:, 16:20, cs], pb[:, 1, 0:4 * C].rearrange("p (j i) -> p j i", j=4)),
               reads=[("ps", 7)], writes=rq)

        def merge(s, tiles, g):
            import os
            MRG = os.environ.get("MRG", "both")
            for passi, (wo, ucb, gc0) in enumerate([(wodn, 12, C_GA), (woswa, 16, C_GB)]):
                if (MRG == "dn" and passi == 1):
                    continue
                for cb in range(2):
                    vo, ro = load_panel(wo[s, :, cb * 512:(cb + 1) * 512], 512, 512)
                    vgp, rgp = load_panel(win[s, :, gc0 + cb * 512:gc0 + (cb + 1) * 512], D, 512)
                    for jj in range(4):
                        c = cb * 4 + jj
                        for (c0, n) in T_(tiles):
                            lc0 = c0 - g * 1024
                            by = next_bank([0, 1])
                            bg = by + 2
                            for k in range(4):
                                op("pe", lambda e, k=k, jj=jj, lc0=lc0, n=n, by=by, vo=vo, ucb=ucb: e.matmul(
                                    pf[by][:, 0:n], vo[:, k, jj * 128:(jj + 1) * 128], U[:, ucb + k, lc0:lc0 + n],
                                    start=(k == 0), stop=(k == 3)),
                                   reads=RU(ucb + k, lc0, n) + [ro], writes=[("ps", by)])
                            for k in range(8):
                                op("pe", lambda e, k=k, jj=jj, lc0=lc0, n=n, bg=bg, vgp=vgp: e.matmul(
                                    pf[bg][:, 0:n], vgp[:, k, jj * 128:(jj + 1) * 128], hT[:, k, lc0:lc0 + n],
                                    start=(k == 0), stop=(k == 7)),
                                   reads=RH(lc0, n) + [rgp], writes=[("ps", bg)])
                            sl = by
                            op("act", lambda e, n=n, bg=bg, sl=sl: e.activation(sigb[sl][:, 0:n], pf[bg][:, 0:n], AF.Sigmoid),
                               reads=[("ps", bg)], writes=[SG[sl]])
                            if passi == 0:
                                op("dve", lambda e, n=n, by=by, sl=sl, c=c, lc0=lc0: e.tensor_tensor(
                                    U[:, c, lc0:lc0 + n], sigb[sl][:, 0:n], pf[by][:, 0:n], ALU.mult),
                                   reads=[SG[sl], ("ps", by)], writes=RU(c, lc0, n))
                            else:
                                op("dve", lambda e, n=n, by=by, sl=sl: e.tensor_tensor(
                                    sigb[sl][:, 0:n], sigb[sl][:, 0:n], pf[by][:, 0:n], ALU.mult),
                                   reads=[SG[sl], ("ps", by)], writes=[SG[sl]])
                                op("dve", lambda e, n=n, sl=sl, c=c, lc0=lc0: e.tensor_tensor(
                                    U[:, c, lc0:lc0 + n], sigb[sl][:, 0:n], U[:, c, lc0:lc0 + n], ALU.add),
                                   reads=[SG[sl]] + RU(c, lc0, n), writes=RU(c, lc0, n))
            for cb in range(2):
                vw, rw = load_panel(wout[s, :, cb * 512:(cb + 1) * 512], D, 512)
                for jj in range(4):
                    c = cb * 4 + jj
                    for (c0, n) in T_(tiles):
                        lc0 = c0 - g * 1024
                        b = next_bank([0, 1, 2, 3])
                        for k in range(8):
                            op("pe", lambda e, k=k, jj=jj, lc0=lc0, n=n, b=b, vw=vw: e.matmul(
                                pf[b][:, 0:n], vw[:, k, jj * 128:(jj + 1) * 128], U[:, k, lc0:lc0 + n],
                                start=(k == 0), stop=(k == 7)),
                               reads=RU(k, lc0, n) + [rw], writes=[("ps", b)])
                        op("dve", lambda e, c=c, c0=c0, n=n, b=b: e.tensor_tensor(
                            xT[:, c, c0:c0 + n], pf[b][:, 0:n], xT[:, c, c0:c0 + n], ALU.add),
                           reads=[("ps", b)] + RX(c0, n), writes=RX(c0, n))

        halo_all = lambda stq: [("halo", stq, j) for j in range(12)]
        for s in range(nslot):
            op("sp", lambda e, s=s: e.dma_start(out=prm[:], in_=prm_d[s]), writes=["prm"], dma="ld_p")
            op("act", lambda e: e.mul(prm2[:, 0:1], prm[:, P_QNW:P_QNW + 1], 0.125), reads=["prm"], writes=["prm2"], small=True)
            op("act", lambda e: e.activation(prm2[:, 1:5], prm[:, P_ALOG:P_ALOG + 4], AF.Exp), reads=["prm"], writes=["prm2"], small=True)
            op("act", lambda e: e.activation(prm2[:, 5:13], prm[:, P_SINK:P_SINK + 8], AF.Exp), reads=["prm"], writes=["prm2"], small=True)
            op("sp", lambda e, s=s: e.dma_start(out=S_f[:, 1].rearrange("p h d -> p (h d)"), in_=sdn_d[s]), writes=["S_f1"], dma="ld_s1")
            op("sp", lambda e, s=s: e.dma_start(out=halo[:, 1].rearrange("p j r -> p (j r)"), in_=sconv_d[s]), writes=halo_all(1), dma="ld_s2")
            op("sp", lambda e, s=s: e.dma_start(out=kscf[:], in_=skc_d[s]), writes=["cact"], dma="ld_s3")
            op("sp", lambda e, s=s: e.dma_start(out=vscf[:], in_=svc_d[s]), writes=["cact"], dma="ld_s4")
            op("dve", lambda e: e.tensor_copy(ksc[:].rearrange("p a b -> p (a b)"), kscf[:]), reads=["cact"], writes=["ksc"])
            op("dve", lambda e: e.tensor_copy(vsb[:, 0:2, :].rearrange("p c (k d) -> p c k d", k=2)[:, :, :, 0:64], vscf[:]),
               reads=["cact", "vsb_all"], writes=[("vsb", 0), ("vsb", 1)])
            op("act", lambda e: e.copy(S_b[:, 1], S_f[:, 1]), reads=["S_f1"], writes=["S_b1"])
            if s >= 1 and "handoff" in PH:
                op("sp", lambda e: e.dma_start(out=rtmp[:], in_=recv_f[0:128, :]), reads=["recv_f"], writes=["cacc"], dma="ld_r1")
                op("sp", lambda e: e.dma_start(out=rtmpb[:], in_=recv_b[0:128, :]), reads=["recv_b"], writes=["sqb"], dma="ld_r2")
                op("dve", lambda e: e.tensor_scalar(S_f[:, 0].rearrange("p h d -> p (h d)"), rtmp[:, 0:512], mcore[:, 0:1], None, ALU.mult),
                   reads=["cacc", "mcore"], writes=["S_f0"])
                op("dve", lambda e: e.tensor_scalar(halo[:, 0].rearrange("p j r -> p (j r)"), rtmp[:, 512:548], mcore[:, 0:1], None, ALU.mult),
                   reads=["cacc", "mcore"], writes=halo_all(0))
                op("dve", lambda e: e.tensor_scalar(kwin[:].rearrange("p a b -> p (a b)"), rtmpb[:, 0:256], mcore[:, 0:1], None, ALU.mult),
                   reads=["sqb", "mcore"], writes=["kwin"])
                op("dve", lambda e: e.tensor_scalar(vbuf[:, 0:2, :].rearrange("p a b -> p (a b)"), rtmpb[0:64, 256:520], mcore[0:64, 0:1], None, ALU.mult),
                   reads=["sqb", "mcore"], writes=[("vbuf", 0), ("vbuf", 1)])
            op("act", lambda e: e.copy(S_b[:, 0], S_f[:, 0]), reads=["S_f0"], writes=["S_b0"])

            for g in range(ngroup):
                tiles = [(g * 1024, 512), (g * 1024 + 512, 512)]
                if g == ngroup - 1:
                    tiles.append((NP_, NS_))
                if "ffn1" in PH:
                    rms_norm_to_hT(tiles, g, 0)
                    ffn(s, tiles, g, 0)
                rms_norm_to_hT(tiles, g, 1)
                if "proj" in PH:
                    proj_in(s, tiles, g)
                chunks = chunk_list(g)
                if "dn" in PH:
                    dn_params(g)
                    for ci, (stq, lc0, C) in enumerate(chunks):
                        dn_chunk(ci, stq, lc0, C)
                S.force_small = False
                if "swa" in PH:
                    for ci, (stq, lc0, C) in enumerate(chunks):
                        swa_chunk(ci, stq, lc0, C, g)
                S.force_small = False
                op("dve", lambda e: e.tensor_copy(kwin[:], U[:, 20:22, 896:1024]),
                   reads=RU(20, 896, 128) + RU(21, 896, 128), writes=["kwin"])
                op("dve", lambda e: e.tensor_copy(vbuf[:, 0:2, :], vbuf[:, 16:18, :]),
                   reads=[("vbuf", 16), ("vbuf", 17)], writes=[("vbuf", 0), ("vbuf", 1)])
                if DUMPU and s == 0:
                    op("sp", lambda e, g=g: e.dma_start(out=dU[g], in_=U[:, 12:20, :]),
                       reads=[("U", j, b) for j in range(12, 20) for b in range(17)], dma="dbgU")
                if "merge" in PH:
                    merge(s, tiles, g)
                if "ffn2" in PH:
                    rms_norm_to_hT(tiles, g, 2)
                    ffn(s, tiles, g, 1)

            op("sp", lambda e, s=s: e.dma_start(out=o_dn[s, 0], in_=S_f[:, 0].rearrange("p h d -> p (h d)")), reads=["S_f0"], dma="o_S0")
            op("sp", lambda e, s=s: e.dma_start(out=o_dn[s, 1], in_=S_f[:, 1].rearrange("p h d -> p (h d)")), reads=["S_f1"], dma="o_S1")
            op("sp", lambda e, s=s: e.dma_start(out=o_conv[s, 0], in_=halo[:, 0].rearrange("p j r -> p (j r)")), reads=halo_all(0), dma="o_h0")
            op("sp", lambda e, s=s: e.dma_start(out=o_conv[s, 1], in_=halo[:, 1].rearrange("p j r -> p (j r)")), reads=halo_all(1), dma="o_h1")
            op("sp", lambda e, s=s: e.dma_start(out=o_kp[s].rearrange("p (a b) -> p a b", a=2), in_=kout[:, :, 0:128]), reads=["kout"], dma="o_k")
            op("sp", lambda e, s=s: e.dma_start(out=o_ks[s].rearrange("p (a b) -> p a b", a=2), in_=kout[:, :, 128:160]), reads=["kout"], dma="o_k")
            op("sp", lambda e, s=s: e.dma_start(out=o_vp[s].rearrange("p (a b) -> p a b", a=2), in_=vout[:, 0:2, :]), reads=["vout"], dma="o_v")
            op("sp", lambda e, s=s: e.dma_start(out=o_vs[s], in_=vout[0:32, 2, :]), reads=["vout"], dma="o_v")
            if s < nslot - 1 and "handoff" in PH:
                op("sp", lambda e: e.dma_start(out=send_f[:, 0:512], in_=S_f[:, 0].rearrange("p h d -> p (h d)")), reads=["S_f0", "recv_f"], writes=["send_f"], dma="snd_f")
                op("sp", lambda e: e.dma_start(out=send_f[:, 512:548], in_=halo[:, 0].rearrange("p j r -> p (j r)")), reads=halo_all(0), writes=["send_f"], dma="snd_f")
                op("sp", lambda e: e.dma_start(out=send_b[:, 0:256], in_=kwin[:].rearrange("p a b -> p (a b)")), reads=["kwin", "recv_b"], writes=["send_b"], dma="snd_b")
                op("sp", lambda e: e.dma_start(out=send_b[0:64, 256:520], in_=vbuf[:, 0:2, :].rearrange("p a b -> p (a b)")), reads=[("vbuf", 0), ("vbuf", 1)], writes=["send_b"], dma="snd_b")
                groups = [[0, 1], [2, 3], [4, 5], [6, 7]]
                op("pool", lambda e: e.collective_compute("AllGather", ALU.bypass, replica_groups=groups, ins=[send_f], outs=[recv_f]),
                   reads=["send_f"], writes=["recv_f"], dma="cc_f", amt=1)
                op("pool", lambda e: e.collective_compute("AllGather", ALU.bypass, replica_groups=groups, ins=[send_b], outs=[recv_b]),
                   reads=["send_b"], writes=["recv_b"], dma="cc_b", amt=1)
        op("sp", lambda e: e.dma_start(out=yT.rearrange("j p t -> p j t"), in_=xT[:]), reads=[("xT", b) for b in range(5)], dma="out")
        nops = S.emit(final_waits=(["dbgU"] if DUMPU else []) + ["out", "o_S0", "o_S1", "o_h0", "o_h1", "o_k", "o_v"])
    return nc, nops


def _consts():
    cf = np.zeros((128, 1920), np.float32)
    cf[:, 0:128] = np.eye(128, dtype=np.float32)
    p = np.arange(64)[:, None]
    j = np.arange(64)[None, :]
    cf[0:64, 128:192] = (p > j)
    cf[0:64, 192:256] = (p <= j)
    cf[0:64, 256:384] = 1.0
    slopes = 2.0 ** (-8.0 * np.arange(1, 9, dtype=np.float32) / 8)
    s_ = np.arange(64)[:, None].astype(np.float32)
    i_ = np.arange(64)[None, :].astype(np.float32)
    BT = np.zeros((64, 3, 8, 64), np.float32)
    for dl in range(3):
        dist = np.abs(i_ + 64 * dl - s_)
        for h in range(8):
            BT[:, dl, (h % 2) * 4 + h // 2, :] = -slopes[h] * dist
    cf[0:64, 384:1920] = BT.reshape(64, -1)
    return cf


def _slot_stack(arr, odd):
    z = np.zeros((1,) + arr.shape[1:], arr.dtype)
    return np.ascontiguousarray(np.concatenate([z, arr] if odd else [arr, z], axis=0))


def kernel(**inp):
    f = lambda k: np.asarray(inp[k], dtype=np.float32)
    x_prompt, x_sample = f("x_prompt"), f("x_sample")
    state_dn, state_conv = f("state_dn"), f("state_conv")
    cache_k, cache_v = f("cache_swa_k"), f("cache_swa_v")
    w_in = f("w_in")
    idx = np.concatenate([np.arange(0, 1536), np.arange(1536, 2048), np.arange(2056, 2568),
                          np.arange(2568, 2632), np.arange(2568, 2632), np.arange(2632, 2696), np.arange(2632, 2696),
                          np.arange(2696, 2824), np.arange(2048, 2056), np.arange(2824, 3848), np.arange(3848, 4872)])
    assert idx.size == WIN_COLS
    win_r = w_in[:, :, idx]
    prm = np.zeros((DEPTH, 128, NPRM), np.float32)
    for i, k in enumerate(["ffn1_norm", "mix_norm", "ffn2_norm"]):
        prm[:, :, P_NW + i * 8:P_NW + i * 8 + 8] = f(k).reshape(DEPTH, 8, 128).transpose(0, 2, 1)
    prm[:, :, P_CW:P_CW + 48] = f("conv_w").reshape(DEPTH, 4, 12, 128).transpose(0, 3, 2, 1).reshape(DEPTH, 128, 48)
    prm[:, :, P_DNW:P_DNW + 128] = f("dn_norm")[:, None, :]
    prm[:, :, P_QNW] = np.tile(f("q_norm"), (1, 2))
    prm[:, :, P_KNW] = np.tile(f("k_norm"), (1, 2))
    prm[:, :, P_ALOG:P_ALOG + 4] = f("a_log")[:, None, :]
    prm[:, :, P_DTB:P_DTB + 4] = f("dt_bias")[:, None, :]
    prm[:, :, P_SINK:P_SINK + 8] = f("sinks")[:, None, :]
    big = {"wg1": f("ffn1_wg"), "wu1": f("ffn1_wu"), "wd1": f("ffn1_wd"), "wg2": f("ffn2_wg"), "wu2": f("ffn2_wu"),
           "wd2": f("ffn2_wd"), "win": win_r, "wodn": f("w_o_dn"), "woswa": f("w_o_swa"), "wout": f("w_out"), "prm": prm}
    stacks = [{k: _slot_stack(v, odd) for k, v in big.items()} for odd in (False, True)]
    cf = _consts()
    in_maps = []
    for c in range(8):
        b, odd = c // 2, c % 2
        m = dict(stacks[odd])
        xt = np.concatenate([x_prompt[b, odd * NP_:(odd + 1) * NP_], x_sample[c]], axis=0)
        m["xT0"] = np.ascontiguousarray(xt.T.reshape(8, 128, NT))
        sdn = state_dn[:, c].transpose(0, 2, 1, 3).reshape(DEPTH, 128, 512)
        sconv = state_conv[:, c].reshape(DEPTH, 3, 12, 128).transpose(0, 3, 2, 1).reshape(DEPTH, 128, 36)
        ck = cache_k[:, c].transpose(0, 2, 3, 1)
        skc = np.concatenate([ck, ck], axis=2).transpose(0, 2, 1, 3).reshape(DEPTH, 128, 256)
        svc = cache_v[:, c].reshape(DEPTH, 2, 64, 2, 64).transpose(0, 2, 1, 3, 4)
        m["sdn"] = _slot_stack(sdn, odd)
        m["sconv"] = _slot_stack(sconv, odd)
        m["skc"] = _slot_stack(skc, odd)
        m["svc"] = _slot_stack(svc, odd)
        m["mcore"] = np.full((128, 1), float(odd), np.float32)
        m["cf"] = cf
        in_maps.append(m)
    nc, _ = build_nc()
    res = run_bass_kernel_spmd(nc, in_maps, core_ids=list(range(8)))
    R = res.results
    y_prompt = np.zeros((4, 4096, D), np.float32)
    y_sample = np.zeros((8, NS_, D), np.float32)
    dn_prompt = np.zeros((DEPTH, 4, 4, 128, 128), np.float32)
    dn_sample = np.zeros((DEPTH, 8, 4, 128, 128), np.float32)
    conv_prompt = np.zeros((DEPTH, 4, 3, 1536), np.float32)
    conv_sample = np.zeros((DEPTH, 8, 3, 1536), np.float32)
    kp = np.zeros((DEPTH, 4, 128, 2, 64), np.float32)
    vp = np.zeros((DEPTH, 4, 128, 2, 64), np.float32)
    ks = np.zeros((DEPTH, 8, NS_, 2, 64), np.float32)
    vs = np.zeros((DEPTH, 8, NS_, 2, 64), np.float32)
    for c in range(8):
        b, odd = c // 2, c % 2
        r = R[c]
        yt = r["yT"].reshape(D, NT).T
        y_prompt[b, odd * NP_:(odd + 1) * NP_] = yt[0:NP_]
        y_sample[c] = yt[NP_:]
        for l in range(DEPTH):
            s = l + odd
            dn_sample[l, c] = r["o_dn"][s, 1].reshape(128, 4, 128).transpose(1, 0, 2)
            conv_sample[l, c] = r["o_conv"][s, 1].reshape(128, 12, 3).transpose(2, 1, 0).reshape(3, 1536)
            ks[l, c] = r["o_ks"][s].reshape(128, 2, 32)[0:64].transpose(2, 1, 0)
            vs[l, c] = r["o_vs"][s].reshape(32, 2, 64)
            if odd:
                dn_prompt[l, b] = r["o_dn"][s, 0].reshape(128, 4, 128).transpose(1, 0, 2)
                conv_prompt[l, b] = r["o_conv"][s, 0].reshape(128, 12, 3).transpose(2, 1, 0).reshape(3, 1536)
                kp[l, b] = r["o_kp"][s].reshape(128, 2, 128)[0:64].transpose(2, 1, 0)
                vp[l, b] = r["o_vp"][s].reshape(64, 2, 2, 64).transpose(1, 0, 2, 3).reshape(128, 2, 64)
    return (y_prompt, y_sample, dn_prompt, dn_sample, conv_prompt, conv_sample, kp, vp, ks, vs)
```

```python
import contextlib
import numpy as np
import concourse.bass as bass
import concourse.mybir as mybir
from concourse.bass_utils import run_bass_kernel_spmd

F32 = mybir.dt.float32
BF16 = mybir.dt.bfloat16
ALU = mybir.AluOpType
AF = mybir.ActivationFunctionType
AX = mybir.AxisListType

D = 1024
DEPTH = 4
NSLOT = DEPTH + 1
NP_ = 2048
NS_ = 32
NT = NP_ + NS_
GT = 1056
DFF = 2816
WIN_COLS = 5000
C_QKV, C_GATE, C_QS, C_KD, C_VBA, C_GA, C_GB = 0, 1536, 2048, 2560, 2816, 2952, 3976
NPRM = 218
P_NW, P_CW, P_DNW, P_QNW, P_KNW, P_ALOG, P_DTB, P_SINK = 0, 24, 72, 200, 201, 202, 206, 210
RMS_EPS = 1e-6
L2_EPS = 1e-6
WSLOT = 5632


class Sched:
    ENGS = ("pe", "act", "dve", "pool", "sp")

    def __init__(self, nc):
        self.nc = nc
        self.ops = []
        self.lastw = {}
        self.readers = {}
        self.chan_tot = {}
        self.bank_rd = {}
        self.force_small = False

    def op(self, eng, fn, reads=(), writes=(), dma=None, amt=16, small=False):
        i = len(self.ops)
        deps = set()
        raw = set()
        for r in reads:
            w = self.lastw.get(r)
            if w is not None:
                deps.add(w)
                raw.add(w)
        for r in writes:
            w = self.lastw.get(r)
            if w is not None:
                deps.add(w)
            q = self.readers.get(r)
            if q:
                deps.update(q)
        for r in reads:
            self.readers.setdefault(r, []).append(i)
            if isinstance(r, tuple) and r[0] == "ps":
                br = self.bank_rd.setdefault(r[1], {})
                for e2, j in br.items():
                    if e2 != eng:
                        deps.add(j)
                br[eng] = i
        for r in writes:
            self.lastw[r] = i
            self.readers[r] = []
        o = dict(eng=eng, fn=fn, deps=deps, dma=dma, inc=False, cnt=None, amt=amt, small=(small or self.force_small), raw=raw)
        if dma is not None:
            self.chan_tot[dma] = self.chan_tot.get(dma, 0) + amt
            o["cnt"] = self.chan_tot[dma]
            o["inc"] = True
        self.ops.append(o)
        return i

    def emit(self, final_waits=()):
        nc = self.nc
        ops = self.ops
        waited = {e: {} for e in self.ENGS}
        for i, o in enumerate(ops):
            e = o["eng"]
            ws = []
            for d in sorted(o["deps"]):
                p = ops[d]
                if p["dma"] is not None:
                    key = ("ch", p["dma"])
                    if waited[e].get(key, -1) >= p["cnt"]:
                        continue
                    waited[e][key] = p["cnt"]
                    ws.append(d)
                else:
                    if p["eng"] == e and o["dma"] is None and not (p["small"] and d in o["raw"]):
                        continue
                    key = ("en", p["eng"])
                    if waited[e].get(key, -1) >= d:
                        continue
                    waited[e][key] = d
                    p["inc"] = True
                    ws.append(d)
            o["waits"] = ws
        cnt = {e: 0 for e in self.ENGS}
        for o in ops:
            if o["dma"] is None and o["inc"]:
                cnt[o["eng"]] += 1
                o["cnt"] = cnt[o["eng"]]
        chans = sorted(self.chan_tot)
        with contextlib.ExitStack() as st:
            esem = {e: st.enter_context(nc.semaphore("se_" + e)) for e in self.ENGS}
            csem = {c: st.enter_context(nc.semaphore("sc_" + c)) for c in chans}
            block = st.enter_context(nc.Block())
            handles = {"pe": "tensor", "act": "scalar", "dve": "vector", "pool": "gpsimd", "sp": "sync"}

            def make(e):
                def body(eng):
                    for o in ops:
                        if o["eng"] != e:
                            continue
                        best = {}
                        for d in o["waits"]:
                            p = ops[d]
                            key = ("ch", p["dma"]) if p["dma"] is not None else ("en", p["eng"])
                            if best.get(key, -1) < p["cnt"]:
                                best[key] = p["cnt"]
                        for key, v in best.items():
                            sem = csem[key[1]] if key[0] == "ch" else esem[key[1]]
                            eng.wait_ge(sem, v)
                        ins = o["fn"](eng)
                        if o["dma"] is not None:
                            ins.then_inc(csem[o["dma"]], o["amt"])
                        elif o["inc"]:
                            ins.then_inc(esem[e], 1)
                    if e == "sp":
                        for c in final_waits:
                            eng.wait_ge(csem[c], self.chan_tot[c])
                return body

            for e in self.ENGS:
                getattr(block, handles[e])(make(e))
        return len(ops)


def build_nc(nslot=NSLOT, ngroup=2, phases=None):
    PH = phases or {"ffn1", "proj", "dn", "swa", "merge", "ffn2", "handoff"}
    if "proj" in PH and not (PH & {"p_qkv", "p_gate", "p_qk", "p_vba"}):
        PH = PH | {"p_qkv", "p_gate", "p_qk", "p_vba"}
    NSLOT = nslot
    nc = bass.Bass("TRN2", target_bir_lowering=False)

    def din(name, shape, dt=F32):
        return nc.dram_tensor(name, list(shape), dt, kind="ExternalInput").ap()

    def dout(name, shape, dt=F32):
        return nc.dram_tensor(name, list(shape), dt, kind="ExternalOutput").ap()

    xT0 = din("xT0", [8, 128, NT])
    wg = [din("wg1", [NSLOT, D, DFF]), din("wg2", [NSLOT, D, DFF])]
    wu = [din("wu1", [NSLOT, D, DFF]), din("wu2", [NSLOT, D, DFF])]
    wd = [din("wd1", [NSLOT, DFF, D]), din("wd2", [NSLOT, DFF, D])]
    win = din("win", [NSLOT, D, WIN_COLS])
    wodn = din("wodn", [NSLOT, 512, D])
    woswa = din("woswa", [NSLOT, 512, D])
    wout = din("wout", [NSLOT, D, D])
    prm_d = din("prm", [NSLOT, 128, NPRM])
    sdn_d = din("sdn", [NSLOT, 128, 512])
    sconv_d = din("sconv", [NSLOT, 128, 36])
    skc_d = din("skc", [NSLOT, 128, 256])
    svc_d = din("svc", [NSLOT, 64, 2, 2, 64])
    mcore_d = din("mcore", [128, 1])
    cf_d = din("cf", [128, 128 + 64 + 64 + 128 + 1536])
    yT = dout("yT", [8, 128, NT])
    o_dn = dout("o_dn", [NSLOT, 2, 128, 512])
    o_conv = dout("o_conv", [NSLOT, 2, 128, 36])
    o_kp = dout("o_kp", [NSLOT, 128, 256])
    o_ks = dout("o_ks", [NSLOT, 128, 64])
    o_vp = dout("o_vp", [NSLOT, 64, 256])
    o_vs = dout("o_vs", [NSLOT, 32, 128])
    import os
    DUMPU = os.environ.get("DUMPU", "") == "1"
    if DUMPU:
        dU = dout("dU", [2, 128, 8, GT], BF16)
    send_f = nc.dram_tensor("send_f", [128, 548], F32).ap()
    recv_f = nc.dram_tensor("recv_f", [256, 548], F32).ap()
    send_b = nc.dram_tensor("send_b", [128, 520], BF16).ap()
    recv_b = nc.dram_tensor("recv_b", [256, 520], BF16).ap()

    S = Sched(nc)
    with contextlib.ExitStack() as st:
        def sb(name, shape, dt=F32):
            return st.enter_context(nc.sbuf_tensor("s_" + name, list(shape), dt))

        def psum(name, shape, dt=F32):
            return st.enter_context(nc.psum_tensor(name, list(shape), dt))

        xT = sb("xT", [128, 8, NT])
        hT = sb("hT", [128, 8, GT], BF16)
        U = sb("U", [128, 22, GT], BF16)
        wring = sb("wring", [128, 2, WSLOT], BF16)
        kwin = sb("kwin", [128, 2, 128], BF16)
        ksc = sb("ksc", [128, 2, 128], BF16)
        vbuf = sb("vbuf", [64, 18, 132], BF16)
        vsb = sb("vsb", [64, 3, 132], BF16)
        S_f = sb("S_f", [128, 2, 4, 128])
        S_b = sb("S_b", [128, 2, 4, 128], BF16)
        halo = sb("halo", [128, 2, 12, 3])
        stage = sb("stage", [128, 520])
        cacc = sb("cacc", [128, 548])
        cact = sb("cact", [128, 512])
        rstd = sb("rstd", [128, 512])
        sqb = sb("sqb", [128, 520], BF16)
        sigb = [cacc, cact]
        SG = ["cacc", "cact"]
        kscf = cact[:, 0:256]
        vscf = cact[0:64, 256:512].rearrange("p (a b c) -> p a b c", a=2, b=2)
        cf = sb("cf", [128, 1920])
        cfb = sb("cfb", [128, 384], BF16)
        prm = sb("prm", [128, NPRM])
        prm2 = sb("prm2", [128, 16])
        mcore = sb("mcore", [128, 1])
        kout = sb("kout", [128, 2, 160])
        vout = sb("vout", [64, 3, 128])
        rtmp = cacc
        rtmpb = sqb
        zba = sb("zba", [64, 17, 8])
        tp_beta = sb("tp_beta", [64, 17, 4])
        tp_negb = sb("tp_negb", [64, 17, 4])
        tp_g = sb("tp_g", [64, 17, 4])
        tp_gc = sb("tp_gc", [64, 17, 4])
        tp_egc = sb("tp_egc", [64, 17, 4])
        tp_ekr = sb("tp_ekr", [64, 17, 4])
        tp_egl = sb("tp_egl", [128, 17, 4])
        tp_t = sb("tp_t", [64, 17, 4])
        Gm = sb("Gm", [64, 4, 64])
        dnA = sb("dnA", [64, 4, 128])
        DTs = dnA[:, :, 0:64]
        DTu = dnA[:, :, 64:128]
        tA = Gm
        Pk0 = sb("Pk0", [64, 4, 64])
        PTk0 = sb("PTk0", [64, 4, 64])
        Pk = [Pk0, Pk0]
        PTk = [PTk0, PTk0]
        Rk0 = sb("Rk0", [64, 4, 64])
        Rk = [Rk0, Rk0]
        Rbf = sb("Rbf", [64, 4, 64], BF16)
        inT2 = [sb("inT0", [64, 4, 64], BF16), sb("inT1", [64, 4, 64], BF16)]
        kg = sb("kg", [64, 4, 128], BF16)
        kr2 = [sb("kr0", [64, 4, 128], BF16), sb("kr1", [64, 4, 128], BF16)]
        vtm = sb("vtm", [64, 4, 128], BF16)
        u_sb2 = [sb("u_sb", [64, 4, 128]), cact[0:64, 0:512].rearrange("p (h d) -> p h d", h=4)]
        w0T2 = [sb("w0T0", [128, 4, 64], BF16), sb("w0T1", [128, 4, 64], BF16)]
        vnew = sb("vnew", [64, 4, 128], BF16)
        o_sb = rstd[0:64, 0:512].rearrange("p (h d) -> p h d", h=4)
        on2 = sqb[0:64, 0:512].rearrange("p (h d) -> p h d", h=4)
        ss4 = sb("ss4", [64, 8])
        tS = sb("tS", [128, 4, 128])
        sct = stage[0:64, 0:512].rearrange("p (h d) -> p h d", h=8)
        t1 = cacc[0:64, 0:512].rearrange("p (h d) -> p h d", h=4)
        PTs = [sb("PT0", [64, 8, 64], BF16), sb("PT1", [64, 8, 64], BF16), sb("PT2", [64, 8, 64], BF16)]
        PTn = ["PT0", "PT1", "PT2"]
        den = sb("den", [64, 8])
        osw = sb("osw", [64, 8, 64], BF16)

        pf = [psum("pf%d" % i, [128, 512]) for i in range(7)]
        pb = psum("pb", [128, 2, 512], BF16)

        ident_f = cf[:, 0:128]
        SL = cf[0:64, 128:192]
        UT = cf[0:64, 192:256]
        ones_f = cf[0:64, 256:384]
        BT = cf[0:64, 384:1920].rearrange("p (a h i) -> p a h i", a=3, h=8)
        ident_b = cfb[:, 0:128]
        ones_b = cfb[:, 128:256]
        blk_b = cfb[:, 256:384]

        op = S.op

        def T_(tiles):
            for (c0, n) in tiles:
                S.force_small = (n < 128)
                yield (c0, n)
            S.force_small = False

        def RU(j, lc0, n):
            return [("U", j, b) for b in range(lc0 // 64, (lc0 + n - 1) // 64 + 1)]

        def RH(lc0, n):
            return [("hT", b) for b in range(lc0 // 512, (lc0 + n - 1) // 512 + 1)]

        def RX(c0, n):
            return [("xT", b) for b in range(c0 // 512, (c0 + n - 1) // 512 + 1)]

        bank_rr = {"i": 0}

        op("sp", lambda e: e.dma_start(out=xT[:], in_=xT0.rearrange("j p t -> p j t")),
           writes=[("xT", b) for b in range(5)], dma="ld_x")
        op("sp", lambda e: e.dma_start(out=cf[:], in_=cf_d), writes=["cf"], dma="ld_cf")
        op("sp", lambda e: e.dma_start(out=mcore[:], in_=mcore_d), writes=["mcore"], dma="ld_mc")
        op("dve", lambda e: e.tensor_copy(cfb[:, 0:128], cf[:, 0:128]), reads=["cf"], writes=["cfb"])
        op("dve", lambda e: e.memset(cfb[:, 128:256], 1.0), writes=["cfb"])
        op("dve", lambda e: e.memset(cfb[:, 256:384], 0.0), writes=["cfb"])
        op("dve", lambda e: e.memset(cfb[0:64, 256:320], 1.0), writes=["cfb"])
        op("dve", lambda e: e.memset(cfb[64:128, 320:384], 1.0), writes=["cfb"])
        op("dve", lambda e: e.memset(vbuf[:], 1.0), writes=["vbuf_all"])
        op("dve", lambda e: e.memset(vsb[:], 1.0), writes=["vsb_all"])
        op("dve", lambda e: e.memset(zba[:], 0.0), writes=["zba"])
        op("dve", lambda e: e.memset(S_f[:, 0], 0.0), writes=["S_f0"])
        op("dve", lambda e: e.memset(halo[:, 0], 0.0), writes=[("halo", 0, j) for j in range(12)])
        op("dve", lambda e: e.memset(kwin[:], 0.0), writes=["kwin"])
        op("dve", lambda e: e.memset(vbuf[:, 0:2, 0:64], 0.0), reads=["vbuf_all"], writes=[("vbuf", 0), ("vbuf", 1)])
        op("dve", lambda e: e.memset(vbuf[:, 0:2, 66:130], 0.0), writes=[("vbuf", 0), ("vbuf", 1)])

        wstate = {"n": 0}

        def load_panel(src2d, K, ncols):
            KC = K // 128
            s = wstate["n"] % 2
            wstate["n"] += 1
            view = wring[:, s, 0:KC * ncols].rearrange("p (k n) -> p k n", k=KC)
            op("pool", lambda e: e.dma_start(out=view, in_=src2d.rearrange("(k p) n -> p k n", p=128)),
               writes=[("wr", s)], dma="w%d" % s)
            return view, ("wr", s)

        def next_bank(cands):
            b = cands[bank_rr["i"] % len(cands)]
            bank_rr["i"] += 1
            return b

        def rms_norm_to_hT(tiles, g, nwi):
            for (c0, n) in T_(tiles):
                lc0 = c0 - g * 1024
                hview = hT[:, :, lc0:lc0 + n]
                op("act", lambda e, hview=hview, c0=c0, n=n: e.activation(hview, xT[:, :, c0:c0 + n], AF.Square),
                   reads=RX(c0, n), writes=RH(lc0, n))
                ps = pf[4]
                for k in range(8):
                    op("pe", lambda e, k=k, lc0=lc0, n=n: e.matmul(ps[:, 0:n], ones_b, hT[:, k, lc0:lc0 + n],
                                                                    start=(k == 0), stop=(k == 7)),
                       reads=RH(lc0, n) + ["cfb"], writes=[("ps", 4)])
                op("act", lambda e, n=n: e.activation(rstd[:, 0:n], ps[:, 0:n], AF.Ln, bias=RMS_EPS, scale=1.0 / D),
                   reads=[("ps", 4)], writes=["rstd"])
                op("act", lambda e, n=n: e.activation(rstd[:, 0:n], rstd[:, 0:n], AF.Exp, scale=-0.5),
                   reads=["rstd"], writes=["rstd"])
                for k in range(8):
                    op("dve", lambda e, k=k, c0=c0, lc0=lc0, n=n: e.scalar_tensor_tensor(
                        hT[:, k, lc0:lc0 + n], xT[:, k, c0:c0 + n], prm[:, P_NW + nwi * 8 + k:P_NW + nwi * 8 + k + 1],
                        rstd[:, 0:n], ALU.mult, ALU.mult),
                       reads=RX(c0, n) + ["rstd", "prm"], writes=RH(lc0, n))

        def ffn(s, tiles, g, fi):
            for (f0, ncols) in [(0, 512), (512, 512), (1024, 512), (1536, 512), (2048, 512), (2560, 256)]:
                vg, rg = load_panel(wg[fi][s, :, f0:f0 + ncols], D, ncols)
                vu, ru = load_panel(wu[fi][s, :, f0:f0 + ncols], D, ncols)
                for jj in range(ncols // 128):
                    f = f0 // 128 + jj
                    for (c0, n) in T_(tiles):
                        lc0 = c0 - g * 1024
                        bg = next_bank([0, 1])
                        bu = bg + 2
                        for k in range(8):
                            op("pe", lambda e, k=k, jj=jj, lc0=lc0, n=n, bg=bg, vg=vg: e.matmul(
                                pf[bg][:, 0:n], vg[:, k, jj * 128:(jj + 1) * 128], hT[:, k, lc0:lc0 + n],
                                start=(k == 0), stop=(k == 7)),
                               reads=RH(lc0, n) + [rg], writes=[("ps", bg)])
                        for k in range(8):
                            op("pe", lambda e, k=k, jj=jj, lc0=lc0, n=n, bu=bu, vu=vu: e.matmul(
                                pf[bu][:, 0:n], vu[:, k, jj * 128:(jj + 1) * 128], hT[:, k, lc0:lc0 + n],
                                start=(k == 0), stop=(k == 7)),
                               reads=RH(lc0, n) + [ru], writes=[("ps", bu)])
                        sl = bg
                        op("act", lambda e, n=n, bg=bg, sl=sl: e.activation(sigb[sl][:, 0:n], pf[bg][:, 0:n], AF.Silu),
                           reads=[("ps", bg)], writes=[SG[sl]])
                        op("dve", lambda e, n=n, bu=bu, sl=sl, f=f, lc0=lc0: e.tensor_tensor(
                            U[:, f, lc0:lc0 + n], sigb[sl][:, 0:n], pf[bu][:, 0:n], ALU.mult),
                           reads=[SG[sl], ("ps", bu)], writes=RU(f, lc0, n))
            for ob in range(4):
                vd, rd = load_panel(wd[fi][s, :, ob * 256:(ob + 1) * 256], DFF, 256)
                for jj in range(2):
                    o = ob * 2 + jj
                    for (c0, n) in T_(tiles):
                        lc0 = c0 - g * 1024
                        b = next_bank([0, 1, 2, 3])
                        for k in range(22):
                            op("pe", lambda e, k=k, jj=jj, lc0=lc0, n=n, b=b, vd=vd: e.matmul(
                                pf[b][:, 0:n], vd[:, k, jj * 128:(jj + 1) * 128], U[:, k, lc0:lc0 + n],
                                start=(k == 0), stop=(k == 21)),
                               reads=RU(k, lc0, n) + [rd], writes=[("ps", b)])
                        op("dve", lambda e, o=o, c0=c0, n=n, b=b: e.scalar_tensor_tensor(
                            xT[:, o, c0:c0 + n], pf[b][:, 0:n], 0.5, xT[:, o, c0:c0 + n], ALU.mult, ALU.add),
                           reads=[("ps", b)] + RX(c0, n), writes=RX(c0, n))

        def rsqrt_chain(ps_ap, n, scale, eps):
            op("act", lambda e: e.activation(rstd[:, 0:n], ps_ap, AF.Ln, bias=eps, scale=scale),
               reads=[("ps", 4)], writes=["rstd"])
            op("act", lambda e: e.activation(rstd[:, 0:n], rstd[:, 0:n], AF.Exp, scale=-0.5),
               reads=["rstd"], writes=["rstd"])

        def proj_in(s, tiles, g):
            W = win
            for pb_ in (range(3) if "p_qkv" in PH else []):
                v, r = load_panel(W[s, :, C_QKV + pb_ * 512:C_QKV + (pb_ + 1) * 512], D, 512)
                for jj in range(4):
                    j = pb_ * 4 + jj
                    for (c0, n) in T_(tiles):
                        lc0 = c0 - g * 1024
                        stq = 1 if c0 >= NP_ else 0
                        b = next_bank([0, 1, 2, 3])
                        for k in range(8):
                            op("pe", lambda e, k=k, jj=jj, lc0=lc0, n=n, b=b, v=v: e.matmul(
                                pf[b][:, 0:n], v[:, k, jj * 128:(jj + 1) * 128], hT[:, k, lc0:lc0 + n],
                                start=(k == 0), stop=(k == 7)),
                               reads=RH(lc0, n) + [r], writes=[("ps", b)])
                        op("dve", lambda e, stq=stq, j=j: e.tensor_copy(stage[:, 0:3], halo[:, stq, j, :]),
                           reads=[("halo", stq, j)], writes=["stage"], small=True)
                        op("act", lambda e, n=n, b=b: e.copy(stage[:, 3:3 + n], pf[b][:, 0:n]),
                           reads=[("ps", b)], writes=["stage"])
                        op("dve", lambda e, stq=stq, j=j, n=n: e.tensor_copy(halo[:, stq, j, :], stage[:, n:n + 3]),
                           reads=["stage"], writes=[("halo", stq, j)], small=True)
                        cw0 = P_CW + j * 4
                        op("dve", lambda e, n=n, cw0=cw0: e.tensor_scalar(
                            cacc[:, 0:n], stage[:, 0:n], prm[:, cw0:cw0 + 1], None, ALU.mult),
                           reads=["stage", "prm"], writes=["cacc"])
                        for t in range(1, 4):
                            op("dve", lambda e, n=n, cw0=cw0, t=t: e.scalar_tensor_tensor(
                                cacc[:, 0:n], stage[:, t:t + n], prm[:, cw0 + t:cw0 + t + 1], cacc[:, 0:n],
                                ALU.mult, ALU.add),
                               reads=["stage", "prm", "cacc"], writes=["cacc"])
                        if j >= 8:
                            op("act", lambda e, n=n, j=j, lc0=lc0: e.activation(U[:, j, lc0:lc0 + n], cacc[:, 0:n], AF.Silu),
                               reads=["cacc"], writes=RU(j, lc0, n))
                        else:
                            op("act", lambda e, n=n: e.activation(cact[:, 0:n], cacc[:, 0:n], AF.Silu),
                               reads=["cacc"], writes=["cact"])
                            op("act", lambda e, n=n: e.activation(sqb[:, 0:n], cact[:, 0:n], AF.Square),
                               reads=["cact"], writes=["sqb"])
                            op("pe", lambda e, n=n: e.matmul(pf[4][:, 0:n], ones_b, sqb[:, 0:n], start=True, stop=True),
                               reads=["sqb", "cfb"], writes=[("ps", 4)])
                            rsqrt_chain(pf[4][:, 0:n], n, 1.0, L2_EPS)
                            qs = (128.0 ** -0.5) if j < 4 else 1.0
                            op("dve", lambda e, n=n, j=j, lc0=lc0, qs=qs: e.scalar_tensor_tensor(
                                U[:, j, lc0:lc0 + n], cact[:, 0:n], qs, rstd[:, 0:n], ALU.mult, ALU.mult),
                               reads=["cact", "rstd"], writes=RU(j, lc0, n))
            v, r = load_panel(W[s, :, C_GATE:C_GATE + 512], D, 512)
            for jj in (range(4) if "p_gate" in PH else []):
                for (c0, n) in T_(tiles):
                    lc0 = c0 - g * 1024
                    b = next_bank([0, 1, 2, 3])
                    for k in range(8):
                        op("pe", lambda e, k=k, jj=jj, lc0=lc0, n=n, b=b, v=v: e.matmul(
                            pf[b][:, 0:n], v[:, k, jj * 128:(jj + 1) * 128], hT[:, k, lc0:lc0 + n],
                            start=(k == 0), stop=(k == 7)),
                           reads=RH(lc0, n) + [r], writes=[("ps", b)])
                    op("act", lambda e, n=n, b=b, jj=jj, lc0=lc0: e.activation(U[:, 12 + jj, lc0:lc0 + n], pf[b][:, 0:n], AF.Silu),
                       reads=[("ps", b)], writes=RU(12 + jj, lc0, n))
            for (cbase, nch, ubase, pcol) in ([(C_QS, 4, 16, "q"), (C_KD, 2, 20, "k")] if "p_qk" in PH else []):
                v, r = load_panel(W[s, :, cbase:cbase + nch * 128], D, nch * 128)
                for jj in range(nch):
                    for (c0, n) in T_(tiles):
                        lc0 = c0 - g * 1024
                        b = next_bank([0, 1, 2, 3])
                        for k in range(8):
                            op("pe", lambda e, k=k, jj=jj, lc0=lc0, n=n, b=b, v=v: e.matmul(
                                pf[b][:, 0:n], v[:, k, jj * 128:(jj + 1) * 128], hT[:, k, lc0:lc0 + n],
                                start=(k == 0), stop=(k == 7)),
                               reads=RH(lc0, n) + [r], writes=[("ps", b)])
                        op("act", lambda e, n=n, b=b: e.activation(sqb[:, 0:n], pf[b][:, 0:n], AF.Square),
                           reads=[("ps", b)], writes=["sqb"])
                        op("pe", lambda e, n=n: e.matmul(pf[4][:, 0:n], blk_b, sqb[:, 0:n], start=True, stop=True),
                           reads=["sqb", "cfb"], writes=[("ps", 4)])
                        rsqrt_chain(pf[4][:, 0:n], n, 1.0 / 64, RMS_EPS)
                        sc = prm2[:, 0:1] if pcol == "q" else prm[:, P_KNW:P_KNW + 1]
                        op("dve", lambda e, n=n, b=b, jj=jj, lc0=lc0, sc=sc, ubase=ubase: e.scalar_tensor_tensor(
                            U[:, ubase + jj, lc0:lc0 + n], pf[b][:, 0:n], sc, rstd[:, 0:n], ALU.mult, ALU.mult),
                           reads=[("ps", b), "rstd", "prm", "prm2"], writes=RU(ubase + jj, lc0, n))
                        if pcol == "k":
                            if c0 >= NP_:
                                op("dve", lambda e, n=n, b=b, jj=jj, sc=sc: e.scalar_tensor_tensor(
                                    kout[:, jj, 128:160], pf[b][:, 0:n], sc, rstd[:, 0:n], ALU.mult, ALU.mult),
                                   reads=[("ps", b), "rstd", "prm"], writes=["kout"])
                            elif c0 + n == NP_:
                                op("dve", lambda e, n=n, b=b, jj=jj, sc=sc: e.scalar_tensor_tensor(
                                    kout[:, jj, 0:128], pf[b][:, n - 128:n], sc, rstd[:, n - 128:n], ALU.mult, ALU.mult),
                                   reads=[("ps", b), "rstd", "prm"], writes=["kout"])
            v, r = load_panel(W[s, :, C_VBA:C_VBA + 136], D, 136)
            chunks = chunk_list(g) if "p_vba" in PH else []
            for ci, (stq, lc0, C) in enumerate(chunks):
                b = next_bank([0, 1, 2, 3])
                for k in range(8):
                    op("pe", lambda e, k=k, lc0=lc0, C=C, b=b, v=v: e.matmul(
                        pf[b][0:C, 0:136], hT[:, k, lc0:lc0 + C], v[:, k, :], start=(k == 0), stop=(k == 7)),
                       reads=RH(lc0, C) + [r], writes=[("ps", b)])
                if stq == 0:
                    vdst = vbuf[0:C, 2 + ci, :]
                    vres = ("vbuf", 2 + ci)
                else:
                    vdst = vsb[0:C, 2, :]
                    vres = ("vsb", 2)
                import os
                DV = os.environ.get("DBGV", "")
                if "noact" not in DV: op("dve", lambda e, C=C, b=b, vdst=vdst: e.tensor_copy(
                    vdst.rearrange("p (k d) -> p k d", k=2)[:, :, 0:64],
                    pf[b][0:C, 0:128].rearrange("p (k d) -> p k d", k=2)),
                   reads=[("ps", b), "vbuf_all", "vsb_all"], writes=[vres])
                if "nozba" not in DV: op("dve", lambda e, C=C, b=b, ci=ci: e.tensor_copy(zba[0:C, ci, :], pf[b][0:C, 128:136]),
                   reads=[("ps", b)], writes=["zba"], small=True)
                if "novout" in DV:
                    pass
                elif stq == 1:
                    op("dve", lambda e, C=C, b=b: e.tensor_copy(vout[0:C, 2, :], pf[b][0:C, 0:128]),
                       reads=[("ps", b)], writes=["vout"])
                elif g == ngroup - 1 and ci >= 14:
                    op("dve", lambda e, C=C, b=b, ci=ci: e.tensor_copy(vout[0:C, ci - 14, :], pf[b][0:C, 0:128]),
                       reads=[("ps", b)], writes=["vout"])

        def chunk_list(g):
            ch = [(0, 64 * i, 64) for i in range(16)]
            if g == ngroup - 1:
                ch.append((1, 1024, 32))
            return ch

        def dn_params(g):
            chunks = chunk_list(g)
            nch = len(chunks)
            npc = 16
            has_s = nch > 16
            A3 = lambda t, a=0, b=4: t[:, 0:nch, a:b]
            op("act", lambda e: e.activation(tp_beta[:, 0:nch, :], zba[:, 0:nch, 0:4], AF.Sigmoid),
               reads=["zba"], writes=["tp_beta"], small=True)
            op("dve", lambda e: e.tensor_scalar(tp_negb[:, 0:nch, :], tp_beta[:, 0:nch, :], -1.0, None, ALU.mult),
               reads=["tp_beta"], writes=["tp_negb"], small=True)
            op("dve", lambda e: e.tensor_tensor(tp_t[:, 0:nch, :], zba[:, 0:nch, 4:8],
                                                prm[0:64, P_DTB:P_DTB + 4].unsqueeze(1).broadcast_to([64, nch, 4]), ALU.add),
               reads=["zba", "prm"], writes=["tp_t"], small=True)
            op("act", lambda e: e.activation(tp_t[:, 0:nch, :], tp_t[:, 0:nch, :], AF.Exp), reads=["tp_t"], writes=["tp_t"], small=True)
            op("act", lambda e: e.activation(tp_t[:, 0:nch, :], tp_t[:, 0:nch, :], AF.Ln, bias=1.0), reads=["tp_t"], writes=["tp_t"], small=True)
            op("dve", lambda e: e.scalar_tensor_tensor(tp_g[:, 0:nch, :], tp_t[:, 0:nch, :], -1.0,
                                                       prm2[0:64, 1:5].unsqueeze(1).broadcast_to([64, nch, 4]), ALU.mult, ALU.mult),
               reads=["tp_t", "prm2"], writes=["tp_g"], small=True)
            g2 = tp_g[:].rearrange("p c h -> p (c h)")
            op("pe", lambda e: e.matmul(pf[5][0:64, 0:npc * 4], UT, g2[:, 0:npc * 4], start=True, stop=True),
               reads=["tp_g", "cf"], writes=[("ps", 5)])
            op("pe", lambda e: e.matmul(pf[6][:, 0:npc * 4], ones_f, g2[:, 0:npc * 4], start=True, stop=True),
               reads=["tp_g", "cf"], writes=[("ps", 6)])
            if has_s:
                op("pe", lambda e: e.matmul(pf[5][0:32, 64:68], cf[0:32, 192:224], g2[0:32, 64:68], start=True, stop=True),
                   reads=["tp_g", "cf"], writes=[("ps", 5)])
                op("pe", lambda e: e.matmul(pf[6][:, 64:68], cf[0:32, 256:384], g2[0:32, 64:68], start=True, stop=True),
                   reads=["tp_g", "cf"], writes=[("ps", 6)])
            n4 = nch * 4
            f2 = lambda t: t[:].rearrange("p c h -> p (c h)")[:, 0:n4]
            op("act", lambda e: e.copy(f2(tp_gc), pf[5][0:64, 0:n4]), reads=[("ps", 5)], writes=["tp_gc"], small=True)
            op("act", lambda e: e.activation(f2(tp_egc), pf[5][0:64, 0:n4], AF.Exp), reads=[("ps", 5)], writes=["tp_egc"], small=True)
            op("dve", lambda e: e.tensor_tensor(f2(tp_ekr), pf[6][0:64, 0:n4], f2(tp_gc), ALU.subtract),
               reads=[("ps", 6), "tp_gc"], writes=["tp_ekr"], small=True)
            op("act", lambda e: e.activation(f2(tp_ekr), f2(tp_ekr), AF.Exp), reads=["tp_ekr"], writes=["tp_ekr"], small=True)
            op("act", lambda e: e.activation(tp_egl[:].rearrange("p c h -> p (c h)")[:, 0:n4], pf[6][:, 0:n4], AF.Exp),
               reads=[("ps", 6)], writes=["tp_egl"], small=True)

        def bc_h(ap2, C, n):
            return ap2.unsqueeze(2).broadcast_to([C, 4, n])

        def bc_m(ap2, C):
            return ap2.unsqueeze(1).broadcast_to([C, 4, C])

        def dn_chunk(ci, stq, lc0, C):
            fs = (C < 64)
            pre, stp = [], []
            curl = [pre]

            def op(*a, **k):
                k["small"] = k.get("small", False) or fs
                curl[0].append((a, k))
            par = ci % 2
            inT, kr, u_sb, w0T = inT2[par], kr2[par], u_sb2[par], w0T2[par]
            n_inT, n_kr, n_u, n_w0 = "inT%d" % par, "kr%d" % par, ("u_sb" if par == 0 else "cact"), "w0T%d" % par
            cs = slice(lc0, lc0 + C)
            qT = lambda h: U[:, h, cs]
            kT = lambda h: U[:, 4 + h, cs]
            vT = lambda h: U[:, 8 + h, cs]
            rq = [r for h in range(4) for r in RU(h, lc0, C)]
            rk = [r for h in range(4) for r in RU(4 + h, lc0, C)]
            rv = [r for h in range(4) for r in RU(8 + h, lc0, C)]
            v3 = lambda t, n=None: (t[0:C, :, 0:(n or C)])
            op("dve", lambda e: e.tensor_tensor(v3(Gm), bc_h(tp_g[0:C, ci, :], C, C), bc_m(cf[0:C, 192:192 + C], C), ALU.mult),
               reads=["tp_g", "cf"], writes=["Gm"])
            op("pe", lambda e: e.matmul(pf[5][0:C, 0:4 * C].rearrange("p (h i) -> p h i", h=4), cf[0:C, 128:128 + C], v3(Gm),
                                        start=True, stop=True), reads=["Gm", "cf"], writes=[("ps", 5)])
            op("act", lambda e: e.activation(v3(DTu), pf[5][0:C, 0:4 * C].rearrange("p (h i) -> p h i", h=4), AF.Exp),
               reads=[("ps", 5)], writes=["DTu"])
            op("dve", lambda e: e.tensor_tensor(v3(DTs), v3(DTu), bc_m(cf[0:C, 192:192 + C], C), ALU.mult),
               reads=["DTu", "cf"], writes=["DTs"])
            for h in range(4):
                op("pe", lambda e, h=h: e.matmul(pf[6][0:C, h * C:(h + 1) * C], kT(h), kT(h), start=True, stop=True),
                   reads=RU(4 + h, lc0, C), writes=[("ps", 6)])
            for h in range(4):
                op("pe", lambda e, h=h: e.matmul(pf[5][0:C, h * C:(h + 1) * C], kT(h), qT(h), start=True, stop=True),
                   reads=RU(4 + h, lc0, C) + RU(h, lc0, C) + ["DTu"], writes=[("ps", 5)])
            ps3 = lambda b: pf[b][0:C, 0:4 * C].rearrange("p (h i) -> p h i", h=4)
            op("dve", lambda e: e.tensor_tensor(v3(inT), ps3(5), v3(DTs), ALU.mult), reads=[("ps", 5), "DTs"], writes=[n_inT])
            op("dve", lambda e: e.tensor_tensor(v3(DTu), v3(DTs), bc_m(cf[0:C, 0:C], C), ALU.subtract),
               reads=["DTs", "cf"], writes=["DTu"])
            op("dve", lambda e: e.tensor_tensor(v3(tA), ps3(6), v3(DTu), ALU.mult), reads=[("ps", 6), "DTu"], writes=["Gm"])
            op("dve", lambda e: e.tensor_tensor(v3(Pk[0]), v3(tA), bc_h(tp_negb[0:C, ci, :], C, C), ALU.mult),
               reads=["Gm", "tp_negb"], writes=["Pk"])
            for h in range(4):
                op("pe", lambda e, h=h: e.transpose(pf[6][0:C, h * C:(h + 1) * C], Pk[0][0:C, h, 0:C], cf[0:C, 0:C]),
                   reads=["Pk", "cf"], writes=[("ps", 6)])
            op("act", lambda e: e.copy(v3(PTk[0]), ps3(6)), reads=[("ps", 6)], writes=["PTk"])
            op("dve", lambda e: e.tensor_tensor(v3(Rk[0]), v3(Pk[0]), bc_m(cf[0:C, 0:C], C), ALU.add),
               reads=["Pk", "cf"], writes=["Rk"])
            nlev = 5 if C == 64 else 4
            cur = 0
            for lv in range(nlev):
                nx = 1 - cur
                last = (lv == nlev - 1)
                if not last:
                    for h in range(4):
                        op("pe", lambda e, h=h, cur=cur: e.matmul(pf[5][0:C, h * C:(h + 1) * C], PTk[cur][0:C, h, 0:C], Pk[cur][0:C, h, 0:C],
                                                                  start=True, stop=True),
                           reads=["Pk", "PTk"], writes=[("ps", 5)])
                for h in range(4):
                    op("pe", lambda e, h=h, cur=cur: e.matmul(pf[6][0:C, h * C:(h + 1) * C], Pk[cur][0:C, h, 0:C], PTk[cur][0:C, h, 0:C],
                                                              start=True, stop=True),
                       reads=["Pk", "PTk"], writes=[("ps", 6)])
                if not last:
                    op("act", lambda e, nx=nx: e.copy(v3(Pk[nx]), ps3(5)), reads=[("ps", 5)], writes=["Pk"])
                op("dve", lambda e, nx=nx: e.tensor_copy(v3(PTk[nx]), ps3(6)), reads=[("ps", 6)], writes=["PTk"])
                for h in range(4):
                    op("pe", lambda e, h=h, cur=cur, nx=nx: e.matmul(pf[5][0:C, h * C:(h + 1) * C], PTk[nx][0:C, h, 0:C], Rk[cur][0:C, h, 0:C],
                                                                     start=True, stop=True),
                       reads=["PTk", "Rk"], writes=[("ps", 5)])
                if not last:
                    op("dve", lambda e, cur=cur, nx=nx: e.tensor_tensor(v3(Rk[nx]), ps3(5), v3(Rk[cur]), ALU.add),
                       reads=[("ps", 5), "Rk"], writes=["Rk"])
                else:
                    op("dve", lambda e, cur=cur: e.tensor_tensor(v3(Rbf), ps3(5), v3(Rk[cur]), ALU.add),
                       reads=[("ps", 5), "Rk"], writes=["Rbf"])
                cur = nx
            for h in range(4):
                op("pe", lambda e, h=h: e.transpose(pb[0:C, 0, h * 128:(h + 1) * 128], kT(h), ident_b),
                   reads=RU(4 + h, lc0, C) + ["cfb"], writes=[("ps", 7)])
            pb3 = lambda i: pb[0:C, i, :].rearrange("p (h d) -> p h d", h=4)
            op("dve", lambda e: e.tensor_tensor(kg[0:C], pb3(0), bc_h(tp_egc[0:C, ci, :], C, 128), ALU.mult),
               reads=[("ps", 7), "tp_egc"], writes=["kg"])
            op("dve", lambda e: e.tensor_tensor(kr[0:C], pb3(0), bc_h(tp_ekr[0:C, ci, :], C, 128), ALU.mult),
               reads=[("ps", 7), "tp_ekr"], writes=[n_kr])
            for h in range(4):
                op("pe", lambda e, h=h: e.transpose(pb[0:C, 0, h * 128:(h + 1) * 128], vT(h), ident_b),
                   reads=RU(8 + h, lc0, C) + ["cfb"], writes=[("ps", 7)])
            op("act", lambda e: e.copy(vtm[0:C], pb3(0)), reads=[("ps", 7)], writes=["vtm"])
            for h in range(4):
                op("pe", lambda e, h=h: e.matmul(pf[6][0:C, h * 128:(h + 1) * 128], Rbf[0:C, h, 0:C], vtm[0:C, h, :], start=True, stop=True),
                   reads=["Rbf", "vtm"], writes=[("ps", 6)])
            for h in range(4):
                op("pe", lambda e, h=h: e.matmul(pf[5][:, h * C:(h + 1) * C], kg[0:C, h, :], Rbf[0:C, h, 0:C], start=True, stop=True),
                   reads=["Rbf", "kg"], writes=[("ps", 5)])
            op("dve", lambda e: e.tensor_tensor(u_sb[0:C], pf[6][0:C, :].rearrange("p (h d) -> p h d", h=4),
                                                bc_h(tp_beta[0:C, ci, :], C, 128), ALU.mult),
               reads=[("ps", 6), "tp_beta"], writes=[n_u])
            op("act", lambda e: e.copy(w0T[:, :, 0:C], pf[5][:, 0:4 * C].rearrange("p (h i) -> p h i", h=4)),
               reads=[("ps", 5)], writes=[n_w0])
            curl[0] = stp
            Sres = "S_b%d" % stq
            Sfres = "S_f%d" % stq
            for h in range(4):
                op("pe", lambda e, h=h: e.matmul(pf[1][0:C, h * 128:(h + 1) * 128], w0T[:, h, 0:C], S_b[:, stq, h, :], start=True, stop=True),
                   reads=[n_w0, Sres], writes=[("ps", 1)])
            p64 = lambda b: pf[b][0:C, :].rearrange("p (h d) -> p h d", h=4)
            op("dve", lambda e: e.tensor_tensor(t1[0:C], p64(1), bc_h(tp_negb[0:C, ci, :], C, 128), ALU.mult),
               reads=[("ps", 1), "tp_negb"], writes=["cacc"])
            op("dve", lambda e: e.tensor_tensor(vnew[0:C], t1[0:C], u_sb[0:C], ALU.add), reads=["cacc", n_u], writes=["vnew"])
            for h in range(4):
                op("pe", lambda e, h=h: e.matmul(pf[0][0:C, h * 128:(h + 1) * 128], qT(h), S_b[:, stq, h, :], start=True, stop=True),
                   reads=RU(h, lc0, C) + [Sres], writes=[("ps", 0)])
            for h in range(4):
                op("pe", lambda e, h=h: e.matmul(pf[1][0:C, h * 128:(h + 1) * 128], inT[0:C, h, 0:C], vnew[0:C, h, :], start=True, stop=True),
                   reads=[n_inT, "vnew"], writes=[("ps", 1)])
            op("dve", lambda e: e.tensor_tensor(t1[0:C], p64(0), bc_h(tp_egc[0:C, ci, :], C, 128), ALU.mult),
               reads=[("ps", 0), "tp_egc"], writes=["cacc"])
            op("dve", lambda e: e.tensor_tensor(o_sb[0:C], t1[0:C], p64(1), ALU.add), reads=["cacc", ("ps", 1)], writes=["rstd"])
            for h in range(4):
                op("pe", lambda e, h=h: e.matmul(pf[0][:, h * 128:(h + 1) * 128], kr[0:C, h, :], vnew[0:C, h, :], start=True, stop=True),
                   reads=[n_kr, "vnew"], writes=[("ps", 0)])
            op("dve", lambda e: e.tensor_tensor(tS[:], S_f[:, stq], tp_egl[:, ci, :].unsqueeze(2).broadcast_to([128, 4, 128]), ALU.mult),
               reads=[Sfres, "tp_egl"], writes=["tS"])
            op("dve", lambda e: e.tensor_tensor(S_f[:, stq], tS[:], pf[0][:, :].rearrange("p (h d) -> p h d", h=4), ALU.add),
               reads=["tS", ("ps", 0)], writes=[Sfres])
            op("act", lambda e: e.copy(S_b[:, stq], S_f[:, stq]), reads=[Sfres], writes=[Sres])
            op("dve", lambda e: e.tensor_tensor(t1[0:C], o_sb[0:C], o_sb[0:C], ALU.mult), reads=["rstd"], writes=["cacc"])
            op("dve", lambda e: e.tensor_reduce(ss4[0:C, 0:4], t1[0:C], AX.X, ALU.add), reads=["cacc"], writes=["ss4"], small=True)
            op("act", lambda e: e.activation(ss4[0:C, 0:4], ss4[0:C, 0:4], AF.Ln, bias=RMS_EPS, scale=1.0 / 128), reads=["ss4"], writes=["ss4"], small=True)
            op("act", lambda e: e.activation(ss4[0:C, 0:4], ss4[0:C, 0:4], AF.Exp, scale=-0.5), reads=["ss4"], writes=["ss4"], small=True)
            op("dve", lambda e: e.tensor_tensor(t1[0:C], o_sb[0:C], bc_h(ss4[0:C, 0:4], C, 128), ALU.mult),
               reads=["rstd", "ss4"], writes=["cacc"])
            op("dve", lambda e: e.tensor_tensor(on2[0:C], t1[0:C], prm[0:C, P_DNW:P_DNW + 128].unsqueeze(1).broadcast_to([C, 4, 128]), ALU.mult),
               reads=["cacc", "prm"], writes=["sqb"])
            for h in range(4):
                op("pe", lambda e, h=h: e.transpose(pb[:, 1, h * C:(h + 1) * C], on2[0:C, h, :], cfb[0:C, 0:C]),
                   reads=["sqb", "cfb"], writes=[("ps", 7)])
            gview = U[:, 12:16, cs]
            rg = [r for h in range(4) for r in RU(12 + h, lc0, C)]
            op("dve", lambda e: e.tensor_tensor(gview, pb[:, 1, 0:4 * C].rearrange("p (h i) -> p h i", h=4), gview, ALU.mult),
               reads=[("ps", 7)] + rg, writes=rg)

            return pre, stp

        def swa_chunk(ci, stq, lc0, C, g):
            fs = (C < 64)
            lst = []

            def op(*a, **k):
                k["small"] = k.get("small", False) or fs
                lst.append((a, k))
            cs = slice(lc0, lc0 + C)
            if stq == 0:
                keys = []
                for dl in (2, 1, 0):
                    kc = ci - dl
                    if kc < 0:
                        w0 = (kc + 2) * 64
                        keys.append((dl, lambda kv, hf, w0=w0: kwin[hf * 64:(hf + 1) * 64, kv, w0:w0 + 64], ["kwin"],
                                     vbuf[:, kc + 2, :], [("vbuf", kc + 2)], 64, g == 0))
                    else:
                        keys.append((dl, lambda kv, hf, kc=kc: U[hf * 64:(hf + 1) * 64, 20 + kv, kc * 64:kc * 64 + 64],
                                     RU(20, kc * 64, 64) + RU(21, kc * 64, 64),
                                     vbuf[:, kc + 2, :], [("vbuf", kc + 2)], 64, False))
            else:
                keys = [(2, lambda kv, hf: ksc[hf * 64:(hf + 1) * 64, kv, 0:64], ["ksc"], vsb[:, 0, :], [("vsb", 0)], 64, False),
                        (1, lambda kv, hf: ksc[hf * 64:(hf + 1) * 64, kv, 64:128], ["ksc"], vsb[:, 1, :], [("vsb", 1)], 64, False),
                        (0, lambda kv, hf: U[hf * 64:(hf + 1) * 64, 20 + kv, cs], RU(20, lc0, C) + RU(21, lc0, C),
                         vsb[:, 2, :], [("vsb", 2)], 32, False)]
            rq = [r for j in range(4) for r in RU(16 + j, lc0, C)]
            for idx, (dl, kfn, kres, vap, vres, SK, masked) in enumerate(keys):
                for hh in range(8):
                    kv, j, hf = hh // 4, hh // 2, hh % 2
                    sbk = 2 if hf == 0 else 3
                    op("pe", lambda e, hh=hh, kv=kv, j=j, hf=hf, kfn=kfn, SK=SK, sbk=sbk: e.matmul(
                        pf[sbk][0:SK, j * 64:j * 64 + C], kfn(kv, hf), U[hf * 64:(hf + 1) * 64, 16 + j, cs], start=True, stop=True),
                       reads=kres + rq, writes=[("ps", sbk)])
                import os
                SWL = int(os.environ.get("SWL", "9"))
                if SWL < 2:
                    continue
                for hf, sbk in ((0, 2), (1, 3)):
                    sc3 = pf[sbk][0:SK, 0:256].rearrange("p (h i) -> p h i", h=4)[:, :, 0:C]
                    op("dve", lambda e, sc3=sc3, dl=dl, SK=SK, hf=hf: e.tensor_tensor(
                        sct[0:SK, hf * 4:hf * 4 + 4, 0:C], sc3, BT[0:SK, dl, hf * 4:hf * 4 + 4, 0:C], ALU.add),
                       reads=[("ps", sbk), "cf"], writes=["stage"])
                PT = PTs[idx]
                pn = PTn[idx]
                op("act", lambda e, SK=SK, PT=PT: e.activation(PT[0:SK, :, 0:C], sct[0:SK, :, 0:C], AF.Exp), reads=["stage"], writes=[pn])
                if masked:
                    op("dve", lambda e, SK=SK, PT=PT: e.tensor_scalar(PT[0:SK, :, 0:C], PT[0:SK, :, 0:C], mcore[0:SK, 0:1], None, ALU.mult),
                       reads=[pn, "mcore"], writes=[pn])
            for half in ((0, 1) if SWL >= 3 else []):
                bk = 4
                for hh in range(half * 4, half * 4 + 4):
                    kv = hh // 4
                    hp = (hh % 2) * 4 + hh // 2
                    for idx, (dl, kfn, kres, vap, vres, SK, masked) in enumerate(keys):
                        op("pe", lambda e, hh=hh, kv=kv, bk=bk, vap=vap, SK=SK, idx=idx, hp=hp: e.matmul(
                            pf[bk][0:C, (hh % 4) * 66:(hh % 4) * 66 + 66], PTs[idx][0:SK, hp, 0:C], vap[0:SK, kv * 66:kv * 66 + 66],
                            start=(idx == 0), stop=(idx == 2)),
                           reads=[PTn[idx]] + vres, writes=[("ps", bk)])
                o3 = pf[bk][0:C, 0:264].rearrange("p (h d) -> p h d", h=4)
                op("dve", lambda e, o3=o3, half=half: e.tensor_tensor(den[0:C, half * 4:half * 4 + 4], o3[:, :, 64],
                                                                      prm2[0:C, 5 + half * 4:9 + half * 4], ALU.add),
                   reads=[("ps", bk), "prm2"], writes=["den"], small=True)
                op("dve", lambda e, half=half: e.reciprocal(den[0:C, half * 4:half * 4 + 4], den[0:C, half * 4:half * 4 + 4]),
                   reads=["den"], writes=["den"], small=True)
                op("dve", lambda e, o3=o3, half=half: e.tensor_tensor(osw[0:C, half * 4:half * 4 + 4, :], o3[:, :, 0:64],
                                                                      den[0:C, half * 4:half * 4 + 4].unsqueeze(2).broadcast_to([C, 4, 64]), ALU.mult),
                   reads=[("ps", bk), "den"], writes=["osw"])
            if SWL < 5:
                return lst
            for j in range(4):
                op("pe", lambda e, j=j: e.transpose(pb[:, 1, 256 + j * C:256 + (j + 1) * C], osw[0:C, 2 * j:2 * j + 2, :].rearrange("p a d -> p (a d)"), cfb[0:C, 0:C]),
                   reads=["osw", "cfb"], writes=[("ps", 7)])
            op("act", lambda e: e.copy(U[:, 16:20, cs], pb[:, 1, 256:256 + 4 * C].rearrange("p (j i) -> p j i", j=4)),
               reads=[("ps", 7)], writes=rq)
            return lst

        def merge(s, tiles, g):
            import os
            MRG = os.environ.get("MRG", "both")
            for passi, (wo, ucb, gc0) in enumerate([(wodn, 12, C_GA), (woswa, 16, C_GB)]):
                if (MRG == "dn" and passi == 1):
                    continue
                for cb in range(2):
                    vo, ro = load_panel(wo[s, :, cb * 512:(cb + 1) * 512], 512, 512)
                    vgp, rgp = load_panel(win[s, :, gc0 + cb * 512:gc0 + (cb + 1) * 512], D, 512)
                    for jj in range(4):
                        c = cb * 4 + jj
                        for (c0, n) in T_(tiles):
                            lc0 = c0 - g * 1024
                            by = next_bank([0, 1])
                            bg = by + 2
                            for k in range(4):
                                op("pe", lambda e, k=k, jj=jj, lc0=lc0, n=n, by=by, vo=vo, ucb=ucb: e.matmul(
                                    pf[by][:, 0:n], vo[:, k, jj * 128:(jj + 1) * 128], U[:, ucb + k, lc0:lc0 + n],
                                    start=(k == 0), stop=(k == 3)),
                                   reads=RU(ucb + k, lc0, n) + [ro], writes=[("ps", by)])
                            for k in range(8):
                                op("pe", lambda e, k=k, jj=jj, lc0=lc0, n=n, bg=bg, vgp=vgp: e.matmul(
                                    pf[bg][:, 0:n], vgp[:, k, jj * 128:(jj + 1) * 128], hT[:, k, lc0:lc0 + n],
                                    start=(k == 0), stop=(k == 7)),
                                   reads=RH(lc0, n) + [rgp], writes=[("ps", bg)])
                            sl = by
                            op("act", lambda e, n=n, bg=bg, sl=sl: e.activation(sigb[sl][:, 0:n], pf[bg][:, 0:n], AF.Sigmoid),
                               reads=[("ps", bg)], writes=[SG[sl]])
                            if passi == 0:
                                op("dve", lambda e, n=n, by=by, sl=sl, c=c, lc0=lc0: e.tensor_tensor(
                                    U[:, c, lc0:lc0 + n], sigb[sl][:, 0:n], pf[by][:, 0:n], ALU.mult),
                                   reads=[SG[sl], ("ps", by)], writes=RU(c, lc0, n))
                            else:
                                op("dve", lambda e, n=n, by=by, sl=sl: e.tensor_tensor(
                                    sigb[sl][:, 0:n], sigb[sl][:, 0:n], pf[by][:, 0:n], ALU.mult),
                                   reads=[SG[sl], ("ps", by)], writes=[SG[sl]])
                                op("dve", lambda e, n=n, sl=sl, c=c, lc0=lc0: e.tensor_tensor(
                                    U[:, c, lc0:lc0 + n], sigb[sl][:, 0:n], U[:, c, lc0:lc0 + n], ALU.add),
                                   reads=[SG[sl]] + RU(c, lc0, n), writes=RU(c, lc0, n))
            for cb in range(2):
                vw, rw = load_panel(wout[s, :, cb * 512:(cb + 1) * 512], D, 512)
                for jj in range(4):
                    c = cb * 4 + jj
                    for (c0, n) in T_(tiles):
                        lc0 = c0 - g * 1024
                        b = next_bank([0, 1, 2, 3])
                        for k in range(8):
                            op("pe", lambda e, k=k, jj=jj, lc0=lc0, n=n, b=b, vw=vw: e.matmul(
                                pf[b][:, 0:n], vw[:, k, jj * 128:(jj + 1) * 128], U[:, k, lc0:lc0 + n],
                                start=(k == 0), stop=(k == 7)),
                               reads=RU(k, lc0, n) + [rw], writes=[("ps", b)])
                        op("dve", lambda e, c=c, c0=c0, n=n, b=b: e.tensor_tensor(
                            xT[:, c, c0:c0 + n], pf[b][:, 0:n], xT[:, c, c0:c0 + n], ALU.add),
                           reads=[("ps", b)] + RX(c0, n), writes=RX(c0, n))

        halo_all = lambda stq: [("halo", stq, j) for j in range(12)]
        for s in range(nslot):
            op("sp", lambda e, s=s: e.dma_start(out=prm[:], in_=prm_d[s]), writes=["prm"], dma="ld_p")
            op("act", lambda e: e.mul(prm2[:, 0:1], prm[:, P_QNW:P_QNW + 1], 0.125), reads=["prm"], writes=["prm2"], small=True)
            op("act", lambda e: e.activation(prm2[:, 1:5], prm[:, P_ALOG:P_ALOG + 4], AF.Exp), reads=["prm"], writes=["prm2"], small=True)
            op("act", lambda e: e.activation(prm2[:, 5:13], prm[:, P_SINK:P_SINK + 8], AF.Exp), reads=["prm"], writes=["prm2"], small=True)
            op("sp", lambda e, s=s: e.dma_start(out=S_f[:, 1].rearrange("p h d -> p (h d)"), in_=sdn_d[s]), writes=["S_f1"], dma="ld_s1")
            op("sp", lambda e, s=s: e.dma_start(out=halo[:, 1].rearrange("p j r -> p (j r)"), in_=sconv_d[s]), writes=halo_all(1), dma="ld_s2")
            op("sp", lambda e, s=s: e.dma_start(out=kscf[:], in_=skc_d[s]), writes=["cact"], dma="ld_s3")
            op("sp", lambda e, s=s: e.dma_start(out=vscf[:], in_=svc_d[s]), writes=["cact"], dma="ld_s4")
            op("dve", lambda e: e.tensor_copy(ksc[:].rearrange("p a b -> p (a b)"), kscf[:]), reads=["cact"], writes=["ksc"])
            op("dve", lambda e: e.tensor_copy(vsb[:, 0:2, :].rearrange("p c (k d) -> p c k d", k=2)[:, :, :, 0:64], vscf[:]),
               reads=["cact", "vsb_all"], writes=[("vsb", 0), ("vsb", 1)])
            op("act", lambda e: e.copy(S_b[:, 1], S_f[:, 1]), reads=["S_f1"], writes=["S_b1"])
            if s >= 1 and "handoff" in PH:
                op("sp", lambda e: e.dma_start(out=rtmp[:], in_=recv_f[0:128, :]), reads=["recv_f"], writes=["cacc"], dma="ld_r1")
                op("sp", lambda e: e.dma_start(out=rtmpb[:], in_=recv_b[0:128, :]), reads=["recv_b"], writes=["sqb"], dma="ld_r2")
                op("dve", lambda e: e.tensor_scalar(S_f[:, 0].rearrange("p h d -> p (h d)"), rtmp[:, 0:512], mcore[:, 0:1], None, ALU.mult),
                   reads=["cacc", "mcore"], writes=["S_f0"])
                op("dve", lambda e: e.tensor_scalar(halo[:, 0].rearrange("p j r -> p (j r)"), rtmp[:, 512:548], mcore[:, 0:1], None, ALU.mult),
                   reads=["cacc", "mcore"], writes=halo_all(0))
                op("dve", lambda e: e.tensor_scalar(kwin[:].rearrange("p a b -> p (a b)"), rtmpb[:, 0:256], mcore[:, 0:1], None, ALU.mult),
                   reads=["sqb", "mcore"], writes=["kwin"])
                op("dve", lambda e: e.tensor_scalar(vbuf[:, 0:2, :].rearrange("p a b -> p (a b)"), rtmpb[0:64, 256:520], mcore[0:64, 0:1], None, ALU.mult),
                   reads=["sqb", "mcore"], writes=[("vbuf", 0), ("vbuf", 1)])
            op("act", lambda e: e.copy(S_b[:, 0], S_f[:, 0]), reads=["S_f0"], writes=["S_b0"])

            for g in range(ngroup):
                tiles = [(g * 1024, 512), (g * 1024 + 512, 512)]
                if g == ngroup - 1:
                    tiles.append((NP_, NS_))
                if "ffn1" in PH:
                    rms_norm_to_hT(tiles, g, 0)
                    ffn(s, tiles, g, 0)
                rms_norm_to_hT(tiles, g, 1)
                if "proj" in PH:
                    proj_in(s, tiles, g)
                chunks = chunk_list(g)
                S.force_small = False
                if "dn" in PH:
                    dn_params(g)
                streams = []
                dnl = [dn_chunk(ci, stq, lc0, C) for ci, (stq, lc0, C) in enumerate(chunks)] if "dn" in PH else None
                swl = [swa_chunk(ci, stq, lc0, C, g) for ci, (stq, lc0, C) in enumerate(chunks)] if "swa" in PH else None

                def emit_merged(lists):
                    items = []
                    for li, L in enumerate(lists):
                        for k, it in enumerate(L):
                            items.append(((k + 0.5) / len(L), li, k, it))
                    items.sort(key=lambda t: (t[0], t[1], t[2]))
                    for _, _, _, (a, k) in items:
                        S.op(*a, **k)
                import os
                if os.environ.get("SEQ", "") == "2":
                    emit_merged([dnl[0][0]])
                    for ci in range(len(chunks)):
                        if ci + 1 < len(chunks):
                            emit_merged([dnl[ci + 1][0]])
                        emit_merged([dnl[ci][1]])
                    dnl = None
                if os.environ.get("SEQ", "0") == "3":
                    for ci in range(len(chunks)):
                        lists = []
                        if dnl is not None:
                            lists.append(dnl[ci][0] + dnl[ci][1])
                        if swl is not None:
                            lists.append(swl[ci])
                        emit_merged([L for L in lists if L])
                    dnl = None
                    swl = None
                if os.environ.get("SEQ", "") == "1":
                    for ci in range(len(chunks)):
                        if dnl is not None:
                            emit_merged([dnl[ci][0]])
                            emit_merged([dnl[ci][1]])
                        if swl is not None:
                            emit_merged([swl[ci]])
                    dnl = None
                    swl = None
                if dnl is not None:
                    emit_merged([dnl[0][0]])
                for ci in range(len(chunks)):
                    lists = []
                    if dnl is not None:
                        lists.append(dnl[ci][1])
                        if ci + 1 < len(chunks):
                            lists.append(dnl[ci + 1][0])
                    if swl is not None:
                        lists.append(swl[ci])
                    emit_merged([L for L in lists if L])
                op("dve", lambda e: e.tensor_copy(kwin[:], U[:, 20:22, 896:1024]),
                   reads=RU(20, 896, 128) + RU(21, 896, 128), writes=["kwin"])
                op("dve", lambda e: e.tensor_copy(vbuf[:, 0:2, :], vbuf[:, 16:18, :]),
                   reads=[("vbuf", 16), ("vbuf", 17)], writes=[("vbuf", 0), ("vbuf", 1)])
                if DUMPU and s == 0:
                    op("sp", lambda e, g=g: e.dma_start(out=dU[g], in_=U[:, 12:20, :]),
                       reads=[("U", j, b) for j in range(12, 20) for b in range(17)], dma="dbgU")
                if "merge" in PH:
                    merge(s, tiles, g)
                if "ffn2" in PH:
                    rms_norm_to_hT(tiles, g, 2)
                    ffn(s, tiles, g, 1)

            op("sp", lambda e, s=s: e.dma_start(out=o_dn[s, 0], in_=S_f[:, 0].rearrange("p h d -> p (h d)")), reads=["S_f0"], dma="o_S0")
            op("sp", lambda e, s=s: e.dma_start(out=o_dn[s, 1], in_=S_f[:, 1].rearrange("p h d -> p (h d)")), reads=["S_f1"], dma="o_S1")
            op("sp", lambda e, s=s: e.dma_start(out=o_conv[s, 0], in_=halo[:, 0].rearrange("p j r -> p (j r)")), reads=halo_all(0), dma="o_h0")
            op("sp", lambda e, s=s: e.dma_start(out=o_conv[s, 1], in_=halo[:, 1].rearrange("p j r -> p (j r)")), reads=halo_all(1), dma="o_h1")
            op("sp", lambda e, s=s: e.dma_start(out=o_kp[s].rearrange("p (a b) -> p a b", a=2), in_=kout[:, :, 0:128]), reads=["kout"], dma="o_k")
            op("sp", lambda e, s=s: e.dma_start(out=o_ks[s].rearrange("p (a b) -> p a b", a=2), in_=kout[:, :, 128:160]), reads=["kout"], dma="o_k")
            op("sp", lambda e, s=s: e.dma_start(out=o_vp[s].rearrange("p (a b) -> p a b", a=2), in_=vout[:, 0:2, :]), reads=["vout"], dma="o_v")
            op("sp", lambda e, s=s: e.dma_start(out=o_vs[s], in_=vout[0:32, 2, :]), reads=["vout"], dma="o_v")
            if s < nslot - 1 and "handoff" in PH:
                op("sp", lambda e: e.dma_start(out=send_f[:, 0:512], in_=S_f[:, 0].rearrange("p h d -> p (h d)")), reads=["S_f0", "recv_f"], writes=["send_f"], dma="snd_f")
                op("sp", lambda e: e.dma_start(out=send_f[:, 512:548], in_=halo[:, 0].rearrange("p j r -> p (j r)")), reads=halo_all(0), writes=["send_f"], dma="snd_f")
                op("sp", lambda e: e.dma_start(out=send_b[:, 0:256], in_=kwin[:].rearrange("p a b -> p (a b)")), reads=["kwin", "recv_b"], writes=["send_b"], dma="snd_b")
                op("sp", lambda e: e.dma_start(out=send_b[0:64, 256:520], in_=vbuf[:, 0:2, :].rearrange("p a b -> p (a b)")), reads=[("vbuf", 0), ("vbuf", 1)], writes=["send_b"], dma="snd_b")
                groups = [[0, 1], [2, 3], [4, 5], [6, 7]]
                op("pool", lambda e: e.collective_compute("AllGather", ALU.bypass, replica_groups=groups, ins=[send_f], outs=[recv_f]),
                   reads=["send_f"], writes=["recv_f"], dma="cc_f", amt=1)
                op("pool", lambda e: e.collective_compute("AllGather", ALU.bypass, replica_groups=groups, ins=[send_b], outs=[recv_b]),
                   reads=["send_b"], writes=["recv_b"], dma="cc_b", amt=1)
        op("sp", lambda e: e.dma_start(out=yT.rearrange("j p t -> p j t"), in_=xT[:]), reads=[("xT", b) for b in range(5)], dma="out")
        nops = S.emit(final_waits=(["dbgU"] if DUMPU else []) + ["out", "o_S0", "o_S1", "o_h0", "o_h1", "o_k", "o_v"])
    return nc, nops


def _consts():
    cf = np.zeros((128, 1920), np.float32)
    cf[:, 0:128] = np.eye(128, dtype=np.float32)
    p = np.arange(64)[:, None]
    j = np.arange(64)[None, :]
    cf[0:64, 128:192] = (p > j)
    cf[0:64, 192:256] = (p <= j)
    cf[0:64, 256:384] = 1.0
    slopes = 2.0 ** (-8.0 * np.arange(1, 9, dtype=np.float32) / 8)
    s_ = np.arange(64)[:, None].astype(np.float32)
    i_ = np.arange(64)[None, :].astype(np.float32)
    BT = np.zeros((64, 3, 8, 64), np.float32)
    for dl in range(3):
        dist = np.abs(i_ + 64 * dl - s_)
        for h in range(8):
            BT[:, dl, (h % 2) * 4 + h // 2, :] = -slopes[h] * dist
    cf[0:64, 384:1920] = BT.reshape(64, -1)
    return cf


def _slot_stack(arr, odd):
    z = np.zeros((1,) + arr.shape[1:], arr.dtype)
    return np.ascontiguousarray(np.concatenate([z, arr] if odd else [arr, z], axis=0))


def kernel(**inp):
    f = lambda k: np.asarray(inp[k], dtype=np.float32)
    x_prompt, x_sample = f("x_prompt"), f("x_sample")
    state_dn, state_conv = f("state_dn"), f("state_conv")
    cache_k, cache_v = f("cache_swa_k"), f("cache_swa_v")
    w_in = f("w_in")
    idx = np.concatenate([np.arange(0, 1536), np.arange(1536, 2048), np.arange(2056, 2568),
                          np.arange(2568, 2632), np.arange(2568, 2632), np.arange(2632, 2696), np.arange(2632, 2696),
                          np.arange(2696, 2824), np.arange(2048, 2056), np.arange(2824, 3848), np.arange(3848, 4872)])
    assert idx.size == WIN_COLS
    win_r = w_in[:, :, idx]
    prm = np.zeros((DEPTH, 128, NPRM), np.float32)
    for i, k in enumerate(["ffn1_norm", "mix_norm", "ffn2_norm"]):
        prm[:, :, P_NW + i * 8:P_NW + i * 8 + 8] = f(k).reshape(DEPTH, 8, 128).transpose(0, 2, 1)
    prm[:, :, P_CW:P_CW + 48] = f("conv_w").reshape(DEPTH, 4, 12, 128).transpose(0, 3, 2, 1).reshape(DEPTH, 128, 48)
    prm[:, :, P_DNW:P_DNW + 128] = f("dn_norm")[:, None, :]
    prm[:, :, P_QNW] = np.tile(f("q_norm"), (1, 2))
    prm[:, :, P_KNW] = np.tile(f("k_norm"), (1, 2))
    prm[:, :, P_ALOG:P_ALOG + 4] = f("a_log")[:, None, :]
    prm[:, :, P_DTB:P_DTB + 4] = f("dt_bias")[:, None, :]
    prm[:, :, P_SINK:P_SINK + 8] = f("sinks")[:, None, :]
    big = {"wg1": f("ffn1_wg"), "wu1": f("ffn1_wu"), "wd1": f("ffn1_wd"), "wg2": f("ffn2_wg"), "wu2": f("ffn2_wu"),
           "wd2": f("ffn2_wd"), "win": win_r, "wodn": f("w_o_dn"), "woswa": f("w_o_swa"), "wout": f("w_out"), "prm": prm}
    stacks = [{k: _slot_stack(v, odd) for k, v in big.items()} for odd in (False, True)]
    cf = _consts()
    in_maps = []
    for c in range(8):
        b, odd = c // 2, c % 2
        m = dict(stacks[odd])
        xt = np.concatenate([x_prompt[b, odd * NP_:(odd + 1) * NP_], x_sample[c]], axis=0)
        m["xT0"] = np.ascontiguousarray(xt.T.reshape(8, 128, NT))
        sdn = state_dn[:, c].transpose(0, 2, 1, 3).reshape(DEPTH, 128, 512)
        sconv = state_conv[:, c].reshape(DEPTH, 3, 12, 128).transpose(0, 3, 2, 1).reshape(DEPTH, 128, 36)
        ck = cache_k[:, c].transpose(0, 2, 3, 1)
        skc = np.concatenate([ck, ck], axis=2).transpose(0, 2, 1, 3).reshape(DEPTH, 128, 256)
        svc = cache_v[:, c].reshape(DEPTH, 2, 64, 2, 64).transpose(0, 2, 1, 3, 4)
        m["sdn"] = _slot_stack(sdn, odd)
        m["sconv"] = _slot_stack(sconv, odd)
        m["skc"] = _slot_stack(skc, odd)
        m["svc"] = _slot_stack(svc, odd)
        m["mcore"] = np.full((128, 1), float(odd), np.float32)
        m["cf"] = cf
        in_maps.append(m)
    nc, _ = build_nc()
    res = run_bass_kernel_spmd(nc, in_maps, core_ids=list(range(8)))
    R = res.results
    y_prompt = np.zeros((4, 4096, D), np.float32)
    y_sample = np.zeros((8, NS_, D), np.float32)
    dn_prompt = np.zeros((DEPTH, 4, 4, 128, 128), np.float32)
    dn_sample = np.zeros((DEPTH, 8, 4, 128, 128), np.float32)
    conv_prompt = np.zeros((DEPTH, 4, 3, 1536), np.float32)
    conv_sample = np.zeros((DEPTH, 8, 3, 1536), np.float32)
    kp = np.zeros((DEPTH, 4, 128, 2, 64), np.float32)
    vp = np.zeros((DEPTH, 4, 128, 2, 64), np.float32)
    ks = np.zeros((DEPTH, 8, NS_, 2, 64), np.float32)
    vs = np.zeros((DEPTH, 8, NS_, 2, 64), np.float32)
    for c in range(8):
        b, odd = c // 2, c % 2
        r = R[c]
        yt = r["yT"].reshape(D, NT).T
        y_prompt[b, odd * NP_:(odd + 1) * NP_] = yt[0:NP_]
        y_sample[c] = yt[NP_:]
        for l in range(DEPTH):
            s = l + odd
            dn_sample[l, c] = r["o_dn"][s, 1].reshape(128, 4, 128).transpose(1, 0, 2)
            conv_sample[l, c] = r["o_conv"][s, 1].reshape(128, 12, 3).transpose(2, 1, 0).reshape(3, 1536)
            ks[l, c] = r["o_ks"][s].reshape(128, 2, 32)[0:64].transpose(2, 1, 0)
            vs[l, c] = r["o_vs"][s].reshape(32, 2, 64)
            if odd:
                dn_prompt[l, b] = r["o_dn"][s, 0].reshape(128, 4, 128).transpose(1, 0, 2)
                conv_prompt[l, b] = r["o_conv"][s, 0].reshape(128, 12, 3).transpose(2, 1, 0).reshape(3, 1536)
                kp[l, b] = r["o_kp"][s].reshape(128, 2, 128)[0:64].transpose(2, 1, 0)
                vp[l, b] = r["o_vp"][s].reshape(64, 2, 2, 64).transpose(1, 0, 2, 3).reshape(128, 2, 64)
    return (y_prompt, y_sample, dn_prompt, dn_sample, conv_prompt, conv_sample, kp, vp, ks, vs)
```

```python
import contextlib
import numpy as np
import concourse.bass as bass
import concourse.mybir as mybir
from concourse.bass_utils import run_bass_kernel_spmd

F32 = mybir.dt.float32
BF16 = mybir.dt.bfloat16
ALU = mybir.AluOpType
AF = mybir.ActivationFunctionType
AX = mybir.AxisListType

D = 1024
DEPTH = 4
NSLOT = DEPTH + 1
NP_ = 2048
NS_ = 32
NT = NP_ + NS_
GT = 1056
DFF = 2816
WIN_COLS = 5000
C_QKV, C_GATE, C_QS, C_KD, C_VBA, C_GA, C_GB = 0, 1536, 2048, 2560, 2816, 2952, 3976
NPRM = 218
P_NW, P_CW, P_DNW, P_QNW, P_KNW, P_ALOG, P_DTB, P_SINK = 0, 24, 72, 200, 201, 202, 206, 210
RMS_EPS = 1e-6
L2_EPS = 1e-6
WSLOT = 5632


class Sched:
    ENGS = ("pe", "act", "dve", "pool", "sp")

    def __init__(self, nc):
        self.nc = nc
        self.ops = []
        self.lastw = {}
        self.readers = {}
        self.chan_tot = {}
        self.bank_rd = {}
        self.force_small = False

    def op(self, eng, fn, reads=(), writes=(), dma=None, amt=16, small=False):
        i = len(self.ops)
        deps = set()
        raw = set()
        for r in reads:
            w = self.lastw.get(r)
            if w is not None:
                deps.add(w)
                raw.add(w)
        for r in writes:
            w = self.lastw.get(r)
            if w is not None:
                deps.add(w)
            q = self.readers.get(r)
            if q:
                deps.update(q)
        for r in reads:
            self.readers.setdefault(r, []).append(i)
            if isinstance(r, tuple) and r[0] == "ps":
                br = self.bank_rd.setdefault(r[1], {})
                for e2, j in br.items():
                    if e2 != eng:
                        deps.add(j)
                br[eng] = i
        for r in writes:
            self.lastw[r] = i
            self.readers[r] = []
        o = dict(eng=eng, fn=fn, deps=deps, dma=dma, inc=False, cnt=None, amt=amt, small=(small or self.force_small), raw=raw)
        if dma is not None:
            self.chan_tot[dma] = self.chan_tot.get(dma, 0) + amt
            o["cnt"] = self.chan_tot[dma]
            o["inc"] = True
        self.ops.append(o)
        return i

    def emit(self, final_waits=()):
        nc = self.nc
        ops = self.ops
        waited = {e: {} for e in self.ENGS}
        for i, o in enumerate(ops):
            e = o["eng"]
            ws = []
            for d in sorted(o["deps"]):
                p = ops[d]
                if p["dma"] is not None:
                    key = ("ch", p["dma"])
                    if waited[e].get(key, -1) >= p["cnt"]:
                        continue
                    waited[e][key] = p["cnt"]
                    ws.append(d)
                else:
                    if p["eng"] == e and o["dma"] is None and not (p["small"] and d in o["raw"]):
                        continue
                    key = ("en", p["eng"])
                    if waited[e].get(key, -1) >= d:
                        continue
                    waited[e][key] = d
                    p["inc"] = True
                    ws.append(d)
            o["waits"] = ws
        cnt = {e: 0 for e in self.ENGS}
        for o in ops:
            if o["dma"] is None and o["inc"]:
                cnt[o["eng"]] += 1
                o["cnt"] = cnt[o["eng"]]
        chans = sorted(self.chan_tot)
        with contextlib.ExitStack() as st:
            esem = {e: st.enter_context(nc.semaphore("se_" + e)) for e in self.ENGS}
            csem = {c: st.enter_context(nc.semaphore("sc_" + c)) for c in chans}
            block = st.enter_context(nc.Block())
            handles = {"pe": "tensor", "act": "scalar", "dve": "vector", "pool": "gpsimd", "sp": "sync"}

            def make(e):
                def body(eng):
                    for o in ops:
                        if o["eng"] != e:
                            continue
                        best = {}
                        for d in o["waits"]:
                            p = ops[d]
                            key = ("ch", p["dma"]) if p["dma"] is not None else ("en", p["eng"])
                            if best.get(key, -1) < p["cnt"]:
                                best[key] = p["cnt"]
                        for key, v in best.items():
                            sem = csem[key[1]] if key[0] == "ch" else esem[key[1]]
                            eng.wait_ge(sem, v)
                        ins = o["fn"](eng)
                        if o["dma"] is not None:
                            ins.then_inc(csem[o["dma"]], o["amt"])
                        elif o["inc"]:
                            ins.then_inc(esem[e], 1)
                    if e == "sp":
                        for c in final_waits:
                            eng.wait_ge(csem[c], self.chan_tot[c])
                return body

            for e in self.ENGS:
                getattr(block, handles[e])(make(e))
        return len(ops)


def build_nc(nslot=NSLOT, ngroup=2, phases=None):
    PH = phases or {"ffn1", "proj", "dn", "swa", "merge", "ffn2", "handoff"}
    if "proj" in PH and not (PH & {"p_qkv", "p_gate", "p_qk", "p_vba"}):
        PH = PH | {"p_qkv", "p_gate", "p_qk", "p_vba"}
    NSLOT = nslot
    nc = bass.Bass("TRN2", target_bir_lowering=False)

    def din(name, shape, dt=F32):
        return nc.dram_tensor(name, list(shape), dt, kind="ExternalInput").ap()

    def dout(name, shape, dt=F32):
        return nc.dram_tensor(name, list(shape), dt, kind="ExternalOutput").ap()

    xT0 = din("xT0", [8, 128, NT])
    wg = [din("wg1", [NSLOT, D, DFF]), din("wg2", [NSLOT, D, DFF])]
    wu = [din("wu1", [NSLOT, D, DFF]), din("wu2", [NSLOT, D, DFF])]
    wd = [din("wd1", [NSLOT, DFF, D]), din("wd2", [NSLOT, DFF, D])]
    win = din("win", [NSLOT, D, WIN_COLS])
    wodn = din("wodn", [NSLOT, 512, D])
    woswa = din("woswa", [NSLOT, 512, D])
    wout = din("wout", [NSLOT, D, D])
    prm_d = din("prm", [NSLOT, 128, NPRM])
    sdn_d = din("sdn", [NSLOT, 128, 512])
    sconv_d = din("sconv", [NSLOT, 128, 36])
    skc_d = din("skc", [NSLOT, 128, 256])
    svc_d = din("svc", [NSLOT, 64, 2, 2, 64])
    mcore_d = din("mcore", [128, 1])
    cf_d = din("cf", [128, 128 + 64 + 64 + 128 + 1536])
    yT = dout("yT", [8, 128, NT])
    o_dn = dout("o_dn", [NSLOT, 2, 128, 512])
    o_conv = dout("o_conv", [NSLOT, 2, 128, 36])
    o_kp = dout("o_kp", [NSLOT, 128, 256])
    o_ks = dout("o_ks", [NSLOT, 128, 64])
    o_vp = dout("o_vp", [NSLOT, 64, 256])
    o_vs = dout("o_vs", [NSLOT, 32, 128])
    import os
    DUMPU = os.environ.get("DUMPU", "") == "1"
    if DUMPU:
        dU = dout("dU", [2, 128, 8, GT], BF16)
    send_f = nc.dram_tensor("send_f", [128, 548], F32).ap()
    recv_f = nc.dram_tensor("recv_f", [256, 548], F32).ap()
    send_b = nc.dram_tensor("send_b", [128, 520], BF16).ap()
    recv_b = nc.dram_tensor("recv_b", [256, 520], BF16).ap()

    S = Sched(nc)
    with contextlib.ExitStack() as st:
        def sb(name, shape, dt=F32):
            return st.enter_context(nc.sbuf_tensor("s_" + name, list(shape), dt))

        def psum(name, shape, dt=F32):
            return st.enter_context(nc.psum_tensor(name, list(shape), dt))

        xT = sb("xT", [128, 8, NT])
        hT = sb("hT", [128, 8, GT], BF16)
        U = sb("U", [128, 22, GT], BF16)
        wring = sb("wring", [128, 2, WSLOT], BF16)
        kwin = sb("kwin", [128, 2, 128], BF16)
        ksc = sb("ksc", [128, 2, 128], BF16)
        vbuf = sb("vbuf", [64, 18, 132], BF16)
        vsb = sb("vsb", [64, 3, 132], BF16)
        S_f = sb("S_f", [128, 2, 4, 128])
        S_b = sb("S_b", [128, 2, 4, 128], BF16)
        halo = sb("halo", [128, 2, 12, 3])
        stage = sb("stage", [128, 520])
        cacc = sb("cacc", [128, 548])
        cact = sb("cact", [128, 512])
        rstd = sb("rstd", [128, 512])
        sqb = sb("sqb", [128, 520], BF16)
        sigb = [cacc, cact]
        SG = ["cacc", "cact"]
        kscf = cact[:, 0:256]
        vscf = cact[0:64, 256:512].rearrange("p (a b c) -> p a b c", a=2, b=2)
        cf = sb("cf", [128, 1920])
        cfb = sb("cfb", [128, 384], BF16)
        prm = sb("prm", [128, NPRM])
        prm2 = sb("prm2", [128, 16])
        mcore = sb("mcore", [128, 1])
        kout = sb("kout", [128, 2, 160])
        vout = sb("vout", [64, 3, 128])
        rtmp = cacc
        rtmpb = sqb
        zba = sb("zba", [64, 17, 8])
        tp_beta = sb("tp_beta", [64, 17, 4])
        tp_negb = sb("tp_negb", [64, 17, 4])
        tp_g = sb("tp_g", [64, 17, 4])
        tp_gc = sb("tp_gc", [64, 17, 4])
        tp_egc = sb("tp_egc", [64, 17, 4])
        tp_ekr = sb("tp_ekr", [64, 17, 4])
        tp_egl = sb("tp_egl", [128, 17, 4])
        tp_t = sb("tp_t", [64, 17, 4])
        Gm = sb("Gm", [64, 4, 64])
        dnA = sb("dnA", [64, 4, 128])
        DTs = dnA[:, :, 0:64]
        DTu = dnA[:, :, 64:128]
        tA = Gm
        Pk0 = sb("Pk0", [64, 4, 64])
        PTk0 = sb("PTk0", [64, 4, 64])
        Pk = [Pk0, Pk0]
        PTk = [PTk0, PTk0]
        Rk0 = sb("Rk0", [64, 4, 64])
        Rk = [Rk0, Rk0]
        Rbf = sb("Rbf", [64, 4, 64], BF16)
        inT2 = [sb("inT0", [64, 4, 64], BF16), sb("inT1", [64, 4, 64], BF16)]
        kg = sb("kg", [64, 4, 128], BF16)
        kr2 = [sb("kr0", [64, 4, 128], BF16), sb("kr1", [64, 4, 128], BF16)]
        vtm = sb("vtm", [64, 4, 128], BF16)
        u_sb2 = [sb("u_sb", [64, 4, 128]), cact[0:64, 0:512].rearrange("p (h d) -> p h d", h=4)]
        w0T2 = [sb("w0T0", [128, 4, 64], BF16), sb("w0T1", [128, 4, 64], BF16)]
        vnew = sb("vnew", [64, 4, 128], BF16)
        o_sb = rstd[0:64, 0:512].rearrange("p (h d) -> p h d", h=4)
        on2 = sqb[0:64, 0:512].rearrange("p (h d) -> p h d", h=4)
        ss4 = sb("ss4", [64, 8])
        tS = sb("tS", [128, 4, 128])
        sct = stage[0:64, 0:512].rearrange("p (h d) -> p h d", h=8)
        t1 = cacc[0:64, 0:512].rearrange("p (h d) -> p h d", h=4)
        PTs = [sb("PT0", [64, 8, 64], BF16), sb("PT1", [64, 8, 64], BF16), sb("PT2", [64, 8, 64], BF16)]
        PTn = ["PT0", "PT1", "PT2"]
        den = sb("den", [64, 8])
        osw = sb("osw", [64, 8, 64], BF16)

        pf = [psum("pf%d" % i, [128, 512]) for i in range(7)]
        pb = psum("pb", [128, 2, 512], BF16)

        ident_f = cf[:, 0:128]
        SL = cf[0:64, 128:192]
        UT = cf[0:64, 192:256]
        ones_f = cf[0:64, 256:384]
        BT = cf[0:64, 384:1920].rearrange("p (a h i) -> p a h i", a=3, h=8)
        ident_b = cfb[:, 0:128]
        ones_b = cfb[:, 128:256]
        blk_b = cfb[:, 256:384]

        op = S.op

        def T_(tiles):
            for (c0, n) in tiles:
                S.force_small = (n < 128)
                yield (c0, n)
            S.force_small = False

        def RU(j, lc0, n):
            return [("U", j, b) for b in range(lc0 // 64, (lc0 + n - 1) // 64 + 1)]

        def RH(lc0, n):
            return [("hT", b) for b in range(lc0 // 512, (lc0 + n - 1) // 512 + 1)]

        def RX(c0, n):
            return [("xT", b) for b in range(c0 // 512, (c0 + n - 1) // 512 + 1)]

        bank_rr = {"i": 0}

        op("sp", lambda e: e.dma_start(out=xT[:], in_=xT0.rearrange("j p t -> p j t")),
           writes=[("xT", b) for b in range(5)], dma="ld_x")
        op("sp", lambda e: e.dma_start(out=cf[:], in_=cf_d), writes=["cf"], dma="ld_cf")
        op("sp", lambda e: e.dma_start(out=mcore[:], in_=mcore_d), writes=["mcore"], dma="ld_mc")
        op("dve", lambda e: e.tensor_copy(cfb[:, 0:128], cf[:, 0:128]), reads=["cf"], writes=["cfb"])
        op("dve", lambda e: e.memset(cfb[:, 128:256], 1.0), writes=["cfb"])
        op("dve", lambda e: e.memset(cfb[:, 256:384], 0.0), writes=["cfb"])
        op("dve", lambda e: e.memset(cfb[0:64, 256:320], 1.0), writes=["cfb"])
        op("dve", lambda e: e.memset(cfb[64:128, 320:384], 1.0), writes=["cfb"])
        op("dve", lambda e: e.memset(vbuf[:], 1.0), writes=["vbuf_all"])
        op("dve", lambda e: e.memset(vsb[:], 1.0), writes=["vsb_all"])
        op("dve", lambda e: e.memset(zba[:], 0.0), writes=["zba"])
        op("dve", lambda e: e.memset(S_f[:, 0], 0.0), writes=["S_f0"])
        op("dve", lambda e: e.memset(halo[:, 0], 0.0), writes=[("halo", 0, j) for j in range(12)])
        op("dve", lambda e: e.memset(kwin[:], 0.0), writes=["kwin"])
        op("dve", lambda e: e.memset(vbuf[:, 0:2, 0:64], 0.0), reads=["vbuf_all"], writes=[("vbuf", 0), ("vbuf", 1)])
        op("dve", lambda e: e.memset(vbuf[:, 0:2, 66:130], 0.0), writes=[("vbuf", 0), ("vbuf", 1)])

        wstate = {"n": 0}

        def load_panel(src2d, K, ncols):
            KC = K // 128
            s = wstate["n"] % 2
            wstate["n"] += 1
            view = wring[:, s, 0:KC * ncols].rearrange("p (k n) -> p k n", k=KC)
            op("pool", lambda e: e.dma_start(out=view, in_=src2d.rearrange("(k p) n -> p k n", p=128)),
               writes=[("wr", s)], dma="w%d" % s)
            return view, ("wr", s)

        def load_panels(specs):
            s_ = wstate["n"] % 2
            wstate["n"] += 1
            off = 0
            views = []
            for (src2d, K, ncols) in specs:
                KC = K // 128
                view = wring[:, s_, off:off + KC * ncols].rearrange("p (k n) -> p k n", k=KC)
                off += KC * ncols
                assert off <= WSLOT
                op("pool", lambda e, view=view, src2d=src2d: e.dma_start(out=view, in_=src2d.rearrange("(k p) n -> p k n", p=128)),
                   writes=[("wr", s_)], dma="w%d" % s_)
                views.append(view)
            return views, ("wr", s_)

        def next_bank(cands):
            b = cands[bank_rr["i"] % len(cands)]
            bank_rr["i"] += 1
            return b

        def rms_norm_to_hT(tiles, g, nwi):
            for (c0, n) in T_(tiles):
                lc0 = c0 - g * 1024
                hview = hT[:, :, lc0:lc0 + n]
                op("act", lambda e, hview=hview, c0=c0, n=n: e.activation(hview, xT[:, :, c0:c0 + n], AF.Square),
                   reads=RX(c0, n), writes=RH(lc0, n))
                ps = pf[4]
                for k in range(8):
                    op("pe", lambda e, k=k, lc0=lc0, n=n: e.matmul(ps[:, 0:n], ones_b, hT[:, k, lc0:lc0 + n],
                                                                    start=(k == 0), stop=(k == 7)),
                       reads=RH(lc0, n) + ["cfb"], writes=[("ps", 4)])
                op("act", lambda e, n=n: e.activation(rstd[:, 0:n], ps[:, 0:n], AF.Ln, bias=RMS_EPS, scale=1.0 / D),
                   reads=[("ps", 4)], writes=["rstd"])
                op("act", lambda e, n=n: e.activation(rstd[:, 0:n], rstd[:, 0:n], AF.Exp, scale=-0.5),
                   reads=["rstd"], writes=["rstd"])
                for k in range(8):
                    op("dve", lambda e, k=k, c0=c0, lc0=lc0, n=n: e.scalar_tensor_tensor(
                        hT[:, k, lc0:lc0 + n], xT[:, k, c0:c0 + n], prm[:, P_NW + nwi * 8 + k:P_NW + nwi * 8 + k + 1],
                        rstd[:, 0:n], ALU.mult, ALU.mult),
                       reads=RX(c0, n) + ["rstd", "prm"], writes=RH(lc0, n))

        def ffn(s, tiles, g, fi):
            for f0 in range(0, DFF, 256):
                ncols = 256
                (vg, vu), rg = load_panels([(wg[fi][s, :, f0:f0 + ncols], D, ncols), (wu[fi][s, :, f0:f0 + ncols], D, ncols)])
                ru = rg
                for jj in range(ncols // 128):
                    f = f0 // 128 + jj
                    for (c0, n) in T_(tiles):
                        lc0 = c0 - g * 1024
                        bg = next_bank([0, 1])
                        bu = bg + 2
                        for k in range(8):
                            op("pe", lambda e, k=k, jj=jj, lc0=lc0, n=n, bg=bg, vg=vg: e.matmul(
                                pf[bg][:, 0:n], vg[:, k, jj * 128:(jj + 1) * 128], hT[:, k, lc0:lc0 + n],
                                start=(k == 0), stop=(k == 7)),
                               reads=RH(lc0, n) + [rg], writes=[("ps", bg)])
                        for k in range(8):
                            op("pe", lambda e, k=k, jj=jj, lc0=lc0, n=n, bu=bu, vu=vu: e.matmul(
                                pf[bu][:, 0:n], vu[:, k, jj * 128:(jj + 1) * 128], hT[:, k, lc0:lc0 + n],
                                start=(k == 0), stop=(k == 7)),
                               reads=RH(lc0, n) + [ru], writes=[("ps", bu)])
                        sl = bg
                        op("act", lambda e, n=n, bg=bg, sl=sl: e.activation(sigb[sl][:, 0:n], pf[bg][:, 0:n], AF.Silu),
                           reads=[("ps", bg)], writes=[SG[sl]])
                        op("dve", lambda e, n=n, bu=bu, sl=sl, f=f, lc0=lc0: e.tensor_tensor(
                            U[:, f, lc0:lc0 + n], sigb[sl][:, 0:n], pf[bu][:, 0:n], ALU.mult),
                           reads=[SG[sl], ("ps", bu)], writes=RU(f, lc0, n))
            for ob in range(4):
                vd, rd = load_panel(wd[fi][s, :, ob * 256:(ob + 1) * 256], DFF, 256)
                for jj in range(2):
                    o = ob * 2 + jj
                    for (c0, n) in T_(tiles):
                        lc0 = c0 - g * 1024
                        b = next_bank([0, 1, 2, 3])
                        for k in range(22):
                            op("pe", lambda e, k=k, jj=jj, lc0=lc0, n=n, b=b, vd=vd: e.matmul(
                                pf[b][:, 0:n], vd[:, k, jj * 128:(jj + 1) * 128], U[:, k, lc0:lc0 + n],
                                start=(k == 0), stop=(k == 21)),
                               reads=RU(k, lc0, n) + [rd], writes=[("ps", b)])
                        op("dve", lambda e, o=o, c0=c0, n=n, b=b: e.scalar_tensor_tensor(
                            xT[:, o, c0:c0 + n], pf[b][:, 0:n], 0.5, xT[:, o, c0:c0 + n], ALU.mult, ALU.add),
                           reads=[("ps", b)] + RX(c0, n), writes=RX(c0, n))

        def rsqrt_chain(ps_ap, n, scale, eps):
            op("act", lambda e: e.activation(rstd[:, 0:n], ps_ap, AF.Ln, bias=eps, scale=scale),
               reads=[("ps", 4)], writes=["rstd"])
            op("act", lambda e: e.activation(rstd[:, 0:n], rstd[:, 0:n], AF.Exp, scale=-0.5),
               reads=["rstd"], writes=["rstd"])

        def proj_in(s, tiles, g):
            W = win
            for pb_ in (range(3) if "p_qkv" in PH else []):
                v, r = load_panel(W[s, :, C_QKV + pb_ * 512:C_QKV + (pb_ + 1) * 512], D, 512)
                for jj in range(4):
                    j = pb_ * 4 + jj
                    for (c0, n) in T_(tiles):
                        lc0 = c0 - g * 1024
                        stq = 1 if c0 >= NP_ else 0
                        b = next_bank([0, 1, 2, 3])
                        for k in range(8):
                            op("pe", lambda e, k=k, jj=jj, lc0=lc0, n=n, b=b, v=v: e.matmul(
                                pf[b][:, 0:n], v[:, k, jj * 128:(jj + 1) * 128], hT[:, k, lc0:lc0 + n],
                                start=(k == 0), stop=(k == 7)),
                               reads=RH(lc0, n) + [r], writes=[("ps", b)])
                        op("dve", lambda e, stq=stq, j=j: e.tensor_copy(stage[:, 0:3], halo[:, stq, j, :]),
                           reads=[("halo", stq, j)], writes=["stage"], small=True)
                        op("act", lambda e, n=n, b=b: e.copy(stage[:, 3:3 + n], pf[b][:, 0:n]),
                           reads=[("ps", b)], writes=["stage"])
                        op("dve", lambda e, stq=stq, j=j, n=n: e.tensor_copy(halo[:, stq, j, :], stage[:, n:n + 3]),
                           reads=["stage"], writes=[("halo", stq, j)], small=True)
                        cw0 = P_CW + j * 4
                        op("dve", lambda e, n=n, cw0=cw0: e.tensor_scalar(
                            cacc[:, 0:n], stage[:, 0:n], prm[:, cw0:cw0 + 1], None, ALU.mult),
                           reads=["stage", "prm"], writes=["cacc"])
                        for t in range(1, 4):
                            op("dve", lambda e, n=n, cw0=cw0, t=t: e.scalar_tensor_tensor(
                                cacc[:, 0:n], stage[:, t:t + n], prm[:, cw0 + t:cw0 + t + 1], cacc[:, 0:n],
                                ALU.mult, ALU.add),
                               reads=["stage", "prm", "cacc"], writes=["cacc"])
                        if j >= 8:
                            op("act", lambda e, n=n, j=j, lc0=lc0: e.activation(U[:, j, lc0:lc0 + n], cacc[:, 0:n], AF.Silu),
                               reads=["cacc"], writes=RU(j, lc0, n))
                        else:
                            op("act", lambda e, n=n: e.activation(cact[:, 0:n], cacc[:, 0:n], AF.Silu),
                               reads=["cacc"], writes=["cact"])
                            op("act", lambda e, n=n: e.activation(sqb[:, 0:n], cact[:, 0:n], AF.Square),
                               reads=["cact"], writes=["sqb"])
                            op("pe", lambda e, n=n: e.matmul(pf[4][:, 0:n], ones_b, sqb[:, 0:n], start=True, stop=True),
                               reads=["sqb", "cfb"], writes=[("ps", 4)])
                            rsqrt_chain(pf[4][:, 0:n], n, 1.0, L2_EPS)
                            qs = (128.0 ** -0.5) if j < 4 else 1.0
                            op("dve", lambda e, n=n, j=j, lc0=lc0, qs=qs: e.scalar_tensor_tensor(
                                U[:, j, lc0:lc0 + n], cact[:, 0:n], qs, rstd[:, 0:n], ALU.mult, ALU.mult),
                               reads=["cact", "rstd"], writes=RU(j, lc0, n))
            v, r = load_panel(W[s, :, C_GATE:C_GATE + 512], D, 512)
            for jj in (range(4) if "p_gate" in PH else []):
                for (c0, n) in T_(tiles):
                    lc0 = c0 - g * 1024
                    b = next_bank([0, 1, 2, 3])
                    for k in range(8):
                        op("pe", lambda e, k=k, jj=jj, lc0=lc0, n=n, b=b, v=v: e.matmul(
                            pf[b][:, 0:n], v[:, k, jj * 128:(jj + 1) * 128], hT[:, k, lc0:lc0 + n],
                            start=(k == 0), stop=(k == 7)),
                           reads=RH(lc0, n) + [r], writes=[("ps", b)])
                    op("act", lambda e, n=n, b=b, jj=jj, lc0=lc0: e.activation(U[:, 12 + jj, lc0:lc0 + n], pf[b][:, 0:n], AF.Silu),
                       reads=[("ps", b)], writes=RU(12 + jj, lc0, n))
            for (cbase, nch, ubase, pcol) in ([(C_QS, 4, 16, "q"), (C_KD, 2, 20, "k")] if "p_qk" in PH else []):
                v, r = load_panel(W[s, :, cbase:cbase + nch * 128], D, nch * 128)
                for jj in range(nch):
                    for (c0, n) in T_(tiles):
                        lc0 = c0 - g * 1024
                        b = next_bank([0, 1, 2, 3])
                        for k in range(8):
                            op("pe", lambda e, k=k, jj=jj, lc0=lc0, n=n, b=b, v=v: e.matmul(
                                pf[b][:, 0:n], v[:, k, jj * 128:(jj + 1) * 128], hT[:, k, lc0:lc0 + n],
                                start=(k == 0), stop=(k == 7)),
                               reads=RH(lc0, n) + [r], writes=[("ps", b)])
                        op("act", lambda e, n=n, b=b: e.activation(sqb[:, 0:n], pf[b][:, 0:n], AF.Square),
                           reads=[("ps", b)], writes=["sqb"])
                        op("pe", lambda e, n=n: e.matmul(pf[4][:, 0:n], blk_b, sqb[:, 0:n], start=True, stop=True),
                           reads=["sqb", "cfb"], writes=[("ps", 4)])
                        rsqrt_chain(pf[4][:, 0:n], n, 1.0 / 64, RMS_EPS)
                        sc = prm2[:, 0:1] if pcol == "q" else prm[:, P_KNW:P_KNW + 1]
                        op("dve", lambda e, n=n, b=b, jj=jj, lc0=lc0, sc=sc, ubase=ubase: e.scalar_tensor_tensor(
                            U[:, ubase + jj, lc0:lc0 + n], pf[b][:, 0:n], sc, rstd[:, 0:n], ALU.mult, ALU.mult),
                           reads=[("ps", b), "rstd", "prm", "prm2"], writes=RU(ubase + jj, lc0, n))
                        if pcol == "k":
                            if c0 >= NP_:
                                op("dve", lambda e, n=n, b=b, jj=jj, sc=sc: e.scalar_tensor_tensor(
                                    kout[:, jj, 128:160], pf[b][:, 0:n], sc, rstd[:, 0:n], ALU.mult, ALU.mult),
                                   reads=[("ps", b), "rstd", "prm"], writes=["kout"])
                            elif c0 + n == NP_:
                                op("dve", lambda e, n=n, b=b, jj=jj, sc=sc: e.scalar_tensor_tensor(
                                    kout[:, jj, 0:128], pf[b][:, n - 128:n], sc, rstd[:, n - 128:n], ALU.mult, ALU.mult),
                                   reads=[("ps", b), "rstd", "prm"], writes=["kout"])
            v, r = load_panel(W[s, :, C_VBA:C_VBA + 136], D, 136)
            chunks = chunk_list(g) if "p_vba" in PH else []
            for ci, (stq, lc0, C) in enumerate(chunks):
                b = next_bank([0, 1, 2, 3])
                for k in range(8):
                    op("pe", lambda e, k=k, lc0=lc0, C=C, b=b, v=v: e.matmul(
                        pf[b][0:C, 0:136], hT[:, k, lc0:lc0 + C], v[:, k, :], start=(k == 0), stop=(k == 7)),
                       reads=RH(lc0, C) + [r], writes=[("ps", b)])
                if stq == 0:
                    vdst = vbuf[0:C, 2 + ci, :]
                    vres = ("vbuf", 2 + ci)
                else:
                    vdst = vsb[0:C, 2, :]
                    vres = ("vsb", 2)
                import os
                DV = os.environ.get("DBGV", "")
                if "noact" not in DV: op("dve", lambda e, C=C, b=b, vdst=vdst: e.tensor_copy(
                    vdst.rearrange("p (k d) -> p k d", k=2)[:, :, 0:64],
                    pf[b][0:C, 0:128].rearrange("p (k d) -> p k d", k=2)),
                   reads=[("ps", b), "vbuf_all", "vsb_all"], writes=[vres])
                if "nozba" not in DV: op("dve", lambda e, C=C, b=b, ci=ci: e.tensor_copy(zba[0:C, ci, :], pf[b][0:C, 128:136]),
                   reads=[("ps", b)], writes=["zba"], small=True)
                if "novout" in DV:
                    pass
                elif stq == 1:
                    op("dve", lambda e, C=C, b=b: e.tensor_copy(vout[0:C, 2, :], pf[b][0:C, 0:128]),
                       reads=[("ps", b)], writes=["vout"])
                elif g == ngroup - 1 and ci >= 14:
                    op("dve", lambda e, C=C, b=b, ci=ci: e.tensor_copy(vout[0:C, ci - 14, :], pf[b][0:C, 0:128]),
                       reads=[("ps", b)], writes=["vout"])

        def chunk_list(g):
            ch = [(0, 64 * i, 64) for i in range(16)]
            if g == ngroup - 1:
                ch.append((1, 1024, 32))
            return ch

        def dn_params(g):
            chunks = chunk_list(g)
            nch = len(chunks)
            npc = 16
            has_s = nch > 16
            A3 = lambda t, a=0, b=4: t[:, 0:nch, a:b]
            op("act", lambda e: e.activation(tp_beta[:, 0:nch, :], zba[:, 0:nch, 0:4], AF.Sigmoid),
               reads=["zba"], writes=["tp_beta"], small=True)
            op("dve", lambda e: e.tensor_scalar(tp_negb[:, 0:nch, :], tp_beta[:, 0:nch, :], -1.0, None, ALU.mult),
               reads=["tp_beta"], writes=["tp_negb"], small=True)
            op("dve", lambda e: e.tensor_tensor(tp_t[:, 0:nch, :], zba[:, 0:nch, 4:8],
                                                prm[0:64, P_DTB:P_DTB + 4].unsqueeze(1).broadcast_to([64, nch, 4]), ALU.add),
               reads=["zba", "prm"], writes=["tp_t"], small=True)
            op("act", lambda e: e.activation(tp_t[:, 0:nch, :], tp_t[:, 0:nch, :], AF.Exp), reads=["tp_t"], writes=["tp_t"], small=True)
            op("act", lambda e: e.activation(tp_t[:, 0:nch, :], tp_t[:, 0:nch, :], AF.Ln, bias=1.0), reads=["tp_t"], writes=["tp_t"], small=True)
            op("dve", lambda e: e.scalar_tensor_tensor(tp_g[:, 0:nch, :], tp_t[:, 0:nch, :], -1.0,
                                                       prm2[0:64, 1:5].unsqueeze(1).broadcast_to([64, nch, 4]), ALU.mult, ALU.mult),
               reads=["tp_t", "prm2"], writes=["tp_g"], small=True)
            g2 = tp_g[:].rearrange("p c h -> p (c h)")
            op("pe", lambda e: e.matmul(pf[5][0:64, 0:npc * 4], UT, g2[:, 0:npc * 4], start=True, stop=True),
               reads=["tp_g", "cf"], writes=[("ps", 5)])
            op("pe", lambda e: e.matmul(pf[6][:, 0:npc * 4], ones_f, g2[:, 0:npc * 4], start=True, stop=True),
               reads=["tp_g", "cf"], writes=[("ps", 6)])
            if has_s:
                op("pe", lambda e: e.matmul(pf[5][0:32, 64:68], cf[0:32, 192:224], g2[0:32, 64:68], start=True, stop=True),
                   reads=["tp_g", "cf"], writes=[("ps", 5)])
                op("pe", lambda e: e.matmul(pf[6][:, 64:68], cf[0:32, 256:384], g2[0:32, 64:68], start=True, stop=True),
                   reads=["tp_g", "cf"], writes=[("ps", 6)])
            n4 = nch * 4
            f2 = lambda t: t[:].rearrange("p c h -> p (c h)")[:, 0:n4]
            op("act", lambda e: e.copy(f2(tp_gc), pf[5][0:64, 0:n4]), reads=[("ps", 5)], writes=["tp_gc"], small=True)
            op("act", lambda e: e.activation(f2(tp_egc), pf[5][0:64, 0:n4], AF.Exp), reads=[("ps", 5)], writes=["tp_egc"], small=True)
            op("dve", lambda e: e.tensor_tensor(f2(tp_ekr), pf[6][0:64, 0:n4], f2(tp_gc), ALU.subtract),
               reads=[("ps", 6), "tp_gc"], writes=["tp_ekr"], small=True)
            op("act", lambda e: e.activation(f2(tp_ekr), f2(tp_ekr), AF.Exp), reads=["tp_ekr"], writes=["tp_ekr"], small=True)
            op("act", lambda e: e.activation(tp_egl[:].rearrange("p c h -> p (c h)")[:, 0:n4], pf[6][:, 0:n4], AF.Exp),
               reads=[("ps", 6)], writes=["tp_egl"], small=True)

        def bc_h(ap2, C, n):
            return ap2.unsqueeze(2).broadcast_to([C, 4, n])

        def bc_m(ap2, C):
            return ap2.unsqueeze(1).broadcast_to([C, 4, C])

        def dn_chunk(ci, stq, lc0, C):
            fs = (C < 64)
            pre, stp = [], []
            curl = [pre]

            def op(*a, **k):
                k["small"] = k.get("small", False) or fs
                curl[0].append((a, k))
            par = ci % 2
            inT, kr, u_sb, w0T = inT2[par], kr2[par], u_sb2[par], w0T2[par]
            n_inT, n_kr, n_u, n_w0 = "inT%d" % par, "kr%d" % par, ("u_sb" if par == 0 else "cact"), "w0T%d" % par
            cs = slice(lc0, lc0 + C)
            qT = lambda h: U[:, h, cs]
            kT = lambda h: U[:, 4 + h, cs]
            vT = lambda h: U[:, 8 + h, cs]
            rq = [r for h in range(4) for r in RU(h, lc0, C)]
            rk = [r for h in range(4) for r in RU(4 + h, lc0, C)]
            rv = [r for h in range(4) for r in RU(8 + h, lc0, C)]
            v3 = lambda t, n=None: (t[0:C, :, 0:(n or C)])
            op("dve", lambda e: e.tensor_tensor(v3(Gm), bc_h(tp_g[0:C, ci, :], C, C), bc_m(cf[0:C, 192:192 + C], C), ALU.mult),
               reads=["tp_g", "cf"], writes=["Gm"])
            op("pe", lambda e: e.matmul(pf[5][0:C, 0:4 * C].rearrange("p (h i) -> p h i", h=4), cf[0:C, 128:128 + C], v3(Gm),
                                        start=True, stop=True), reads=["Gm", "cf"], writes=[("ps", 5)])
            op("act", lambda e: e.activation(v3(DTu), pf[5][0:C, 0:4 * C].rearrange("p (h i) -> p h i", h=4), AF.Exp),
               reads=[("ps", 5)], writes=["DTu"])
            op("dve", lambda e: e.tensor_tensor(v3(DTs), v3(DTu), bc_m(cf[0:C, 192:192 + C], C), ALU.mult),
               reads=["DTu", "cf"], writes=["DTs"])
            for h in range(4):
                op("pe", lambda e, h=h: e.matmul(pf[6][0:C, h * C:(h + 1) * C], kT(h), kT(h), start=True, stop=True),
                   reads=RU(4 + h, lc0, C), writes=[("ps", 6)])
            for h in range(4):
                op("pe", lambda e, h=h: e.matmul(pf[5][0:C, h * C:(h + 1) * C], kT(h), qT(h), start=True, stop=True),
                   reads=RU(4 + h, lc0, C) + RU(h, lc0, C) + ["DTu"], writes=[("ps", 5)])
            ps3 = lambda b: pf[b][0:C, 0:4 * C].rearrange("p (h i) -> p h i", h=4)
            op("dve", lambda e: e.tensor_tensor(v3(inT), ps3(5), v3(DTs), ALU.mult), reads=[("ps", 5), "DTs"], writes=[n_inT])
            op("dve", lambda e: e.tensor_tensor(v3(DTu), v3(DTs), bc_m(cf[0:C, 0:C], C), ALU.subtract),
               reads=["DTs", "cf"], writes=["DTu"])
            op("dve", lambda e: e.tensor_tensor(v3(tA), ps3(6), v3(DTu), ALU.mult), reads=[("ps", 6), "DTu"], writes=["Gm"])
            op("dve", lambda e: e.tensor_tensor(v3(Pk[0]), v3(tA), bc_h(tp_negb[0:C, ci, :], C, C), ALU.mult),
               reads=["Gm", "tp_negb"], writes=["Pk"])
            for h in range(4):
                op("pe", lambda e, h=h: e.transpose(pf[6][0:C, h * C:(h + 1) * C], Pk[0][0:C, h, 0:C], cf[0:C, 0:C]),
                   reads=["Pk", "cf"], writes=[("ps", 6)])
            op("act", lambda e: e.copy(v3(PTk[0]), ps3(6)), reads=[("ps", 6)], writes=["PTk"])
            op("dve", lambda e: e.tensor_tensor(v3(Rk[0]), v3(Pk[0]), bc_m(cf[0:C, 0:C], C), ALU.add),
               reads=["Pk", "cf"], writes=["Rk"])
            nlev = 5 if C == 64 else 4
            cur = 0
            for lv in range(nlev):
                nx = 1 - cur
                last = (lv == nlev - 1)
                if not last:
                    for h in range(4):
                        op("pe", lambda e, h=h, cur=cur: e.matmul(pf[5][0:C, h * C:(h + 1) * C], PTk[cur][0:C, h, 0:C], Pk[cur][0:C, h, 0:C],
                                                                  start=True, stop=True),
                           reads=["Pk", "PTk"], writes=[("ps", 5)])
                for h in range(4):
                    op("pe", lambda e, h=h, cur=cur: e.matmul(pf[6][0:C, h * C:(h + 1) * C], Pk[cur][0:C, h, 0:C], PTk[cur][0:C, h, 0:C],
                                                              start=True, stop=True),
                       reads=["Pk", "PTk"], writes=[("ps", 6)])
                if not last:
                    op("act", lambda e, nx=nx: e.copy(v3(Pk[nx]), ps3(5)), reads=[("ps", 5)], writes=["Pk"])
                op("dve", lambda e, nx=nx: e.tensor_copy(v3(PTk[nx]), ps3(6)), reads=[("ps", 6)], writes=["PTk"])
                for h in range(4):
                    op("pe", lambda e, h=h, cur=cur, nx=nx: e.matmul(pf[5][0:C, h * C:(h + 1) * C], PTk[nx][0:C, h, 0:C], Rk[cur][0:C, h, 0:C],
                                                                     start=True, stop=True),
                       reads=["PTk", "Rk"], writes=[("ps", 5)])
                if not last:
                    op("dve", lambda e, cur=cur, nx=nx: e.tensor_tensor(v3(Rk[nx]), ps3(5), v3(Rk[cur]), ALU.add),
                       reads=[("ps", 5), "Rk"], writes=["Rk"])
                else:
                    op("dve", lambda e, cur=cur: e.tensor_tensor(v3(Rbf), ps3(5), v3(Rk[cur]), ALU.add),
                       reads=[("ps", 5), "Rk"], writes=["Rbf"])
                cur = nx
            for h in range(4):
                op("pe", lambda e, h=h: e.transpose(pb[0:C, 0, h * 128:(h + 1) * 128], kT(h), ident_b),
                   reads=RU(4 + h, lc0, C) + ["cfb"], writes=[("ps", 7)])
            pb3 = lambda i: pb[0:C, i, :].rearrange("p (h d) -> p h d", h=4)
            op("dve", lambda e: e.tensor_tensor(kg[0:C], pb3(0), bc_h(tp_egc[0:C, ci, :], C, 128), ALU.mult),
               reads=[("ps", 7), "tp_egc"], writes=["kg"])
            op("dve", lambda e: e.tensor_tensor(kr[0:C], pb3(0), bc_h(tp_ekr[0:C, ci, :], C, 128), ALU.mult),
               reads=[("ps", 7), "tp_ekr"], writes=[n_kr])
            for h in range(4):
                op("pe", lambda e, h=h: e.transpose(pb[0:C, 0, h * 128:(h + 1) * 128], vT(h), ident_b),
                   reads=RU(8 + h, lc0, C) + ["cfb"], writes=[("ps", 7)])
            op("act", lambda e: e.copy(vtm[0:C], pb3(0)), reads=[("ps", 7)], writes=["vtm"])
            for h in range(4):
                op("pe", lambda e, h=h: e.matmul(pf[6][0:C, h * 128:(h + 1) * 128], Rbf[0:C, h, 0:C], vtm[0:C, h, :], start=True, stop=True),
                   reads=["Rbf", "vtm"], writes=[("ps", 6)])
            for h in range(4):
                op("pe", lambda e, h=h: e.matmul(pf[5][:, h * C:(h + 1) * C], kg[0:C, h, :], Rbf[0:C, h, 0:C], start=True, stop=True),
                   reads=["Rbf", "kg"], writes=[("ps", 5)])
            op("dve", lambda e: e.tensor_tensor(u_sb[0:C], pf[6][0:C, :].rearrange("p (h d) -> p h d", h=4),
                                                bc_h(tp_beta[0:C, ci, :], C, 128), ALU.mult),
               reads=[("ps", 6), "tp_beta"], writes=[n_u])
            op("act", lambda e: e.copy(w0T[:, :, 0:C], pf[5][:, 0:4 * C].rearrange("p (h i) -> p h i", h=4)),
               reads=[("ps", 5)], writes=[n_w0])
            curl[0] = stp
            Sres = "S_b%d" % stq
            Sfres = "S_f%d" % stq
            for h in range(4):
                op("pe", lambda e, h=h: e.matmul(pf[1][0:C, h * 128:(h + 1) * 128], w0T[:, h, 0:C], S_b[:, stq, h, :], start=True, stop=True),
                   reads=[n_w0, Sres], writes=[("ps", 1)])
            p64 = lambda b: pf[b][0:C, :].rearrange("p (h d) -> p h d", h=4)
            op("dve", lambda e: e.tensor_tensor(t1[0:C], p64(1), bc_h(tp_negb[0:C, ci, :], C, 128), ALU.mult),
               reads=[("ps", 1), "tp_negb"], writes=["cacc"])
            op("dve", lambda e: e.tensor_tensor(vnew[0:C], t1[0:C], u_sb[0:C], ALU.add), reads=["cacc", n_u], writes=["vnew"])
            for h in range(4):
                op("pe", lambda e, h=h: e.matmul(pf[0][0:C, h * 128:(h + 1) * 128], qT(h), S_b[:, stq, h, :], start=True, stop=True),
                   reads=RU(h, lc0, C) + [Sres], writes=[("ps", 0)])
            for h in range(4):
                op("pe", lambda e, h=h: e.matmul(pf[1][0:C, h * 128:(h + 1) * 128], inT[0:C, h, 0:C], vnew[0:C, h, :], start=True, stop=True),
                   reads=[n_inT, "vnew"], writes=[("ps", 1)])
            op("dve", lambda e: e.tensor_tensor(t1[0:C], p64(0), bc_h(tp_egc[0:C, ci, :], C, 128), ALU.mult),
               reads=[("ps", 0), "tp_egc"], writes=["cacc"])
            op("dve", lambda e: e.tensor_tensor(o_sb[0:C], t1[0:C], p64(1), ALU.add), reads=["cacc", ("ps", 1)], writes=["rstd"])
            for h in range(4):
                op("pe", lambda e, h=h: e.matmul(pf[0][:, h * 128:(h + 1) * 128], kr[0:C, h, :], vnew[0:C, h, :], start=True, stop=True),
                   reads=[n_kr, "vnew"], writes=[("ps", 0)])
            op("dve", lambda e: e.tensor_tensor(tS[:], S_f[:, stq], tp_egl[:, ci, :].unsqueeze(2).broadcast_to([128, 4, 128]), ALU.mult),
               reads=[Sfres, "tp_egl"], writes=["tS"])
            op("dve", lambda e: e.tensor_tensor(S_f[:, stq], tS[:], pf[0][:, :].rearrange("p (h d) -> p h d", h=4), ALU.add),
               reads=["tS", ("ps", 0)], writes=[Sfres])
            op("act", lambda e: e.copy(S_b[:, stq], S_f[:, stq]), reads=[Sfres], writes=[Sres])
            op("dve", lambda e: e.tensor_tensor(t1[0:C], o_sb[0:C], o_sb[0:C], ALU.mult), reads=["rstd"], writes=["cacc"])
            op("dve", lambda e: e.tensor_reduce(ss4[0:C, 0:4], t1[0:C], AX.X, ALU.add), reads=["cacc"], writes=["ss4"], small=True)
            op("act", lambda e: e.activation(ss4[0:C, 0:4], ss4[0:C, 0:4], AF.Ln, bias=RMS_EPS, scale=1.0 / 128), reads=["ss4"], writes=["ss4"], small=True)
            op("act", lambda e: e.activation(ss4[0:C, 0:4], ss4[0:C, 0:4], AF.Exp, scale=-0.5), reads=["ss4"], writes=["ss4"], small=True)
            op("dve", lambda e: e.tensor_tensor(t1[0:C], o_sb[0:C], bc_h(ss4[0:C, 0:4], C, 128), ALU.mult),
               reads=["rstd", "ss4"], writes=["cacc"])
            op("dve", lambda e: e.tensor_tensor(on2[0:C], t1[0:C], prm[0:C, P_DNW:P_DNW + 128].unsqueeze(1).broadcast_to([C, 4, 128]), ALU.mult),
               reads=["cacc", "prm"], writes=["sqb"])
            for h in range(4):
                op("pe", lambda e, h=h: e.transpose(pb[:, 1, h * C:(h + 1) * C], on2[0:C, h, :], cfb[0:C, 0:C]),
                   reads=["sqb", "cfb"], writes=[("ps", 7)])
            gview = U[:, 12:16, cs]
            rg = [r for h in range(4) for r in RU(12 + h, lc0, C)]
            op("dve", lambda e: e.tensor_tensor(gview, pb[:, 1, 0:4 * C].rearrange("p (h i) -> p h i", h=4), gview, ALU.mult),
               reads=[("ps", 7)] + rg, writes=rg)

            return pre, stp

        def swa_chunk(ci, stq, lc0, C, g):
            fs = (C < 64)
            lst = []

            def op(*a, **k):
                k["small"] = k.get("small", False) or fs
                lst.append((a, k))
            cs = slice(lc0, lc0 + C)
            if stq == 0:
                keys = []
                for dl in (2, 1, 0):
                    kc = ci - dl
                    if kc < 0:
                        w0 = (kc + 2) * 64
                        keys.append((dl, lambda kv, hf, w0=w0: kwin[hf * 64:(hf + 1) * 64, kv, w0:w0 + 64], ["kwin"],
                                     vbuf[:, kc + 2, :], [("vbuf", kc + 2)], 64, g == 0))
                    else:
                        keys.append((dl, lambda kv, hf, kc=kc: U[hf * 64:(hf + 1) * 64, 20 + kv, kc * 64:kc * 64 + 64],
                                     RU(20, kc * 64, 64) + RU(21, kc * 64, 64),
                                     vbuf[:, kc + 2, :], [("vbuf", kc + 2)], 64, False))
            else:
                keys = [(2, lambda kv, hf: ksc[hf * 64:(hf + 1) * 64, kv, 0:64], ["ksc"], vsb[:, 0, :], [("vsb", 0)], 64, False),
                        (1, lambda kv, hf: ksc[hf * 64:(hf + 1) * 64, kv, 64:128], ["ksc"], vsb[:, 1, :], [("vsb", 1)], 64, False),
                        (0, lambda kv, hf: U[hf * 64:(hf + 1) * 64, 20 + kv, cs], RU(20, lc0, C) + RU(21, lc0, C),
                         vsb[:, 2, :], [("vsb", 2)], 32, False)]
            rq = [r for j in range(4) for r in RU(16 + j, lc0, C)]
            for idx, (dl, kfn, kres, vap, vres, SK, masked) in enumerate(keys):
                for hh in range(8):
                    kv, j, hf = hh // 4, hh // 2, hh % 2
                    sbk = 2 if hf == 0 else 3
                    op("pe", lambda e, hh=hh, kv=kv, j=j, hf=hf, kfn=kfn, SK=SK, sbk=sbk: e.matmul(
                        pf[sbk][0:SK, j * 64:j * 64 + C], kfn(kv, hf), U[hf * 64:(hf + 1) * 64, 16 + j, cs], start=True, stop=True),
                       reads=kres + rq, writes=[("ps", sbk)])
                import os
                SWL = int(os.environ.get("SWL", "9"))
                if SWL < 2:
                    continue
                for hf, sbk in ((0, 2), (1, 3)):
                    sc3 = pf[sbk][0:SK, 0:256].rearrange("p (h i) -> p h i", h=4)[:, :, 0:C]
                    op("dve", lambda e, sc3=sc3, dl=dl, SK=SK, hf=hf: e.tensor_tensor(
                        sct[0:SK, hf * 4:hf * 4 + 4, 0:C], sc3, BT[0:SK, dl, hf * 4:hf * 4 + 4, 0:C], ALU.add),
                       reads=[("ps", sbk), "cf"], writes=["stage"])
                PT = PTs[idx]
                pn = PTn[idx]
                op("act", lambda e, SK=SK, PT=PT: e.activation(PT[0:SK, :, 0:C], sct[0:SK, :, 0:C], AF.Exp), reads=["stage"], writes=[pn])
                if masked:
                    op("dve", lambda e, SK=SK, PT=PT: e.tensor_scalar(PT[0:SK, :, 0:C], PT[0:SK, :, 0:C], mcore[0:SK, 0:1], None, ALU.mult),
                       reads=[pn, "mcore"], writes=[pn])
            for half in ((0, 1) if SWL >= 3 else []):
                bk = 4
                for hh in range(half * 4, half * 4 + 4):
                    kv = hh // 4
                    hp = (hh % 2) * 4 + hh // 2
                    for idx, (dl, kfn, kres, vap, vres, SK, masked) in enumerate(keys):
                        op("pe", lambda e, hh=hh, kv=kv, bk=bk, vap=vap, SK=SK, idx=idx, hp=hp: e.matmul(
                            pf[bk][0:C, (hh % 4) * 66:(hh % 4) * 66 + 66], PTs[idx][0:SK, hp, 0:C], vap[0:SK, kv * 66:kv * 66 + 66],
                            start=(idx == 0), stop=(idx == 2)),
                           reads=[PTn[idx]] + vres, writes=[("ps", bk)])
                o3 = pf[bk][0:C, 0:264].rearrange("p (h d) -> p h d", h=4)
                op("dve", lambda e, o3=o3, half=half: e.tensor_tensor(den[0:C, half * 4:half * 4 + 4], o3[:, :, 64],
                                                                      prm2[0:C, 5 + half * 4:9 + half * 4], ALU.add),
                   reads=[("ps", bk), "prm2"], writes=["den"], small=True)
                op("dve", lambda e, half=half: e.reciprocal(den[0:C, half * 4:half * 4 + 4], den[0:C, half * 4:half * 4 + 4]),
                   reads=["den"], writes=["den"], small=True)
                op("dve", lambda e, o3=o3, half=half: e.tensor_tensor(osw[0:C, half * 4:half * 4 + 4, :], o3[:, :, 0:64],
                                                                      den[0:C, half * 4:half * 4 + 4].unsqueeze(2).broadcast_to([C, 4, 64]), ALU.mult),
                   reads=[("ps", bk), "den"], writes=["osw"])
            if SWL < 5:
                return lst
            for j in range(4):
                op("pe", lambda e, j=j: e.transpose(pb[:, 1, 256 + j * C:256 + (j + 1) * C], osw[0:C, 2 * j:2 * j + 2, :].rearrange("p a d -> p (a d)"), cfb[0:C, 0:C]),
                   reads=["osw", "cfb"], writes=[("ps", 7)])
            op("act", lambda e: e.copy(U[:, 16:20, cs], pb[:, 1, 256:256 + 4 * C].rearrange("p (j i) -> p j i", j=4)),
               reads=[("ps", 7)], writes=rq)
            return lst

        def merge(s, tiles, g):
            import os
            MRG = os.environ.get("MRG", "both")
            for passi, (wo, ucb, gc0) in enumerate([(wodn, 12, C_GA), (woswa, 16, C_GB)]):
                if (MRG == "dn" and passi == 1):
                    continue
                for cb in range(4):
                    (vo, vgp), ro = load_panels([(wo[s, :, cb * 256:(cb + 1) * 256], 512, 256),
                                                 (win[s, :, gc0 + cb * 256:gc0 + (cb + 1) * 256], D, 256)])
                    rgp = ro
                    for jj in range(2):
                        c = cb * 2 + jj
                        for (c0, n) in T_(tiles):
                            lc0 = c0 - g * 1024
                            by = next_bank([0, 1])
                            bg = by + 2
                            for k in range(4):
                                op("pe", lambda e, k=k, jj=jj, lc0=lc0, n=n, by=by, vo=vo, ucb=ucb: e.matmul(
                                    pf[by][:, 0:n], vo[:, k, jj * 128:(jj + 1) * 128], U[:, ucb + k, lc0:lc0 + n],
                                    start=(k == 0), stop=(k == 3)),
                                   reads=RU(ucb + k, lc0, n) + [ro], writes=[("ps", by)])
                            for k in range(8):
                                op("pe", lambda e, k=k, jj=jj, lc0=lc0, n=n, bg=bg, vgp=vgp: e.matmul(
                                    pf[bg][:, 0:n], vgp[:, k, jj * 128:(jj + 1) * 128], hT[:, k, lc0:lc0 + n],
                                    start=(k == 0), stop=(k == 7)),
                                   reads=RH(lc0, n) + [rgp], writes=[("ps", bg)])
                            sl = by
                            op("act", lambda e, n=n, bg=bg, sl=sl: e.activation(sigb[sl][:, 0:n], pf[bg][:, 0:n], AF.Sigmoid),
                               reads=[("ps", bg)], writes=[SG[sl]])
                            if passi == 0:
                                op("dve", lambda e, n=n, by=by, sl=sl, c=c, lc0=lc0: e.tensor_tensor(
                                    U[:, c, lc0:lc0 + n], sigb[sl][:, 0:n], pf[by][:, 0:n], ALU.mult),
                                   reads=[SG[sl], ("ps", by)], writes=RU(c, lc0, n))
                            else:
                                op("dve", lambda e, n=n, by=by, sl=sl: e.tensor_tensor(
                                    sigb[sl][:, 0:n], sigb[sl][:, 0:n], pf[by][:, 0:n], ALU.mult),
                                   reads=[SG[sl], ("ps", by)], writes=[SG[sl]])
                                op("dve", lambda e, n=n, sl=sl, c=c, lc0=lc0: e.tensor_tensor(
                                    U[:, c, lc0:lc0 + n], sigb[sl][:, 0:n], U[:, c, lc0:lc0 + n], ALU.add),
                                   reads=[SG[sl]] + RU(c, lc0, n), writes=RU(c, lc0, n))
            for cb in range(2):
                vw, rw = load_panel(wout[s, :, cb * 512:(cb + 1) * 512], D, 512)
                for jj in range(4):
                    c = cb * 4 + jj
                    for (c0, n) in T_(tiles):
                        lc0 = c0 - g * 1024
                        b = next_bank([0, 1, 2, 3])
                        for k in range(8):
                            op("pe", lambda e, k=k, jj=jj, lc0=lc0, n=n, b=b, vw=vw: e.matmul(
                                pf[b][:, 0:n], vw[:, k, jj * 128:(jj + 1) * 128], U[:, k, lc0:lc0 + n],
                                start=(k == 0), stop=(k == 7)),
                               reads=RU(k, lc0, n) + [rw], writes=[("ps", b)])
                        op("dve", lambda e, c=c, c0=c0, n=n, b=b: e.tensor_tensor(
                            xT[:, c, c0:c0 + n], pf[b][:, 0:n], xT[:, c, c0:c0 + n], ALU.add),
                           reads=[("ps", b)] + RX(c0, n), writes=RX(c0, n))

        halo_all = lambda stq: [("halo", stq, j) for j in range(12)]
        for s in range(nslot):
            op("sp", lambda e, s=s: e.dma_start(out=prm[:], in_=prm_d[s]), writes=["prm"], dma="ld_p")
            op("act", lambda e: e.mul(prm2[:, 0:1], prm[:, P_QNW:P_QNW + 1], 0.125), reads=["prm"], writes=["prm2"], small=True)
            op("act", lambda e: e.activation(prm2[:, 1:5], prm[:, P_ALOG:P_ALOG + 4], AF.Exp), reads=["prm"], writes=["prm2"], small=True)
            op("act", lambda e: e.activation(prm2[:, 5:13], prm[:, P_SINK:P_SINK + 8], AF.Exp), reads=["prm"], writes=["prm2"], small=True)
            op("sp", lambda e, s=s: e.dma_start(out=S_f[:, 1].rearrange("p h d -> p (h d)"), in_=sdn_d[s]), writes=["S_f1"], dma="ld_s1")
            op("sp", lambda e, s=s: e.dma_start(out=halo[:, 1].rearrange("p j r -> p (j r)"), in_=sconv_d[s]), writes=halo_all(1), dma="ld_s2")
            op("sp", lambda e, s=s: e.dma_start(out=kscf[:], in_=skc_d[s]), writes=["cact"], dma="ld_s3")
            op("sp", lambda e, s=s: e.dma_start(out=vscf[:], in_=svc_d[s]), writes=["cact"], dma="ld_s4")
            op("dve", lambda e: e.tensor_copy(ksc[:].rearrange("p a b -> p (a b)"), kscf[:]), reads=["cact"], writes=["ksc"])
            op("dve", lambda e: e.tensor_copy(vsb[:, 0:2, :].rearrange("p c (k d) -> p c k d", k=2)[:, :, :, 0:64], vscf[:]),
               reads=["cact", "vsb_all"], writes=[("vsb", 0), ("vsb", 1)])
            op("act", lambda e: e.copy(S_b[:, 1], S_f[:, 1]), reads=["S_f1"], writes=["S_b1"])
            if s >= 1 and "handoff" in PH:
                op("sp", lambda e: e.dma_start(out=rtmp[:], in_=recv_f[0:128, :]), reads=["recv_f"], writes=["cacc"], dma="ld_r1")
                op("sp", lambda e: e.dma_start(out=rtmpb[:], in_=recv_b[0:128, :]), reads=["recv_b"], writes=["sqb"], dma="ld_r2")
                op("dve", lambda e: e.tensor_scalar(S_f[:, 0].rearrange("p h d -> p (h d)"), rtmp[:, 0:512], mcore[:, 0:1], None, ALU.mult),
                   reads=["cacc", "mcore"], writes=["S_f0"])
                op("dve", lambda e: e.tensor_scalar(halo[:, 0].rearrange("p j r -> p (j r)"), rtmp[:, 512:548], mcore[:, 0:1], None, ALU.mult),
                   reads=["cacc", "mcore"], writes=halo_all(0))
                op("dve", lambda e: e.tensor_scalar(kwin[:].rearrange("p a b -> p (a b)"), rtmpb[:, 0:256], mcore[:, 0:1], None, ALU.mult),
                   reads=["sqb", "mcore"], writes=["kwin"])
                op("dve", lambda e: e.tensor_scalar(vbuf[:, 0:2, :].rearrange("p a b -> p (a b)"), rtmpb[0:64, 256:520], mcore[0:64, 0:1], None, ALU.mult),
                   reads=["sqb", "mcore"], writes=[("vbuf", 0), ("vbuf", 1)])
            op("act", lambda e: e.copy(S_b[:, 0], S_f[:, 0]), reads=["S_f0"], writes=["S_b0"])

            for g in range(ngroup):
                tiles = [(g * 1024, 512), (g * 1024 + 512, 512)]
                if g == ngroup - 1:
                    tiles.append((NP_, NS_))
                if "ffn1" in PH:
                    rms_norm_to_hT(tiles, g, 0)
                    ffn(s, tiles, g, 0)
                rms_norm_to_hT(tiles, g, 1)
                if "proj" in PH:
                    proj_in(s, tiles, g)
                chunks = chunk_list(g)
                S.force_small = False
                if "dn" in PH:
                    dn_params(g)
                streams = []
                dnl = [dn_chunk(ci, stq, lc0, C) for ci, (stq, lc0, C) in enumerate(chunks)] if "dn" in PH else None
                swl = [swa_chunk(ci, stq, lc0, C, g) for ci, (stq, lc0, C) in enumerate(chunks)] if "swa" in PH else None

                def emit_merged(lists):
                    items = []
                    for li, L in enumerate(lists):
                        for k, it in enumerate(L):
                            items.append(((k + 0.5) / len(L), li, k, it))
                    items.sort(key=lambda t: (t[0], t[1], t[2]))
                    for _, _, _, (a, k) in items:
                        S.op(*a, **k)
                import os
                if os.environ.get("SEQ", "") == "2":
                    emit_merged([dnl[0][0]])
                    for ci in range(len(chunks)):
                        if ci + 1 < len(chunks):
                            emit_merged([dnl[ci + 1][0]])
                        emit_merged([dnl[ci][1]])
                    dnl = None
                if os.environ.get("SEQ", "0") == "3":
                    for ci in range(len(chunks)):
                        lists = []
                        if dnl is not None:
                            lists.append(dnl[ci][0] + dnl[ci][1])
                        if swl is not None:
                            lists.append(swl[ci])
                        emit_merged([L for L in lists if L])
                    dnl = None
                    swl = None
                if os.environ.get("SEQ", "") == "1":
                    for ci in range(len(chunks)):
                        if dnl is not None:
                            emit_merged([dnl[ci][0]])
                            emit_merged([dnl[ci][1]])
                        if swl is not None:
                            emit_merged([swl[ci]])
                    dnl = None
                    swl = None
                if dnl is not None:
                    emit_merged([dnl[0][0]])
                for ci in range(len(chunks)):
                    lists = []
                    if dnl is not None:
                        lists.append(dnl[ci][1])
                        if ci + 1 < len(chunks):
                            lists.append(dnl[ci + 1][0])
                    if swl is not None:
                        lists.append(swl[ci])
                    emit_merged([L for L in lists if L])
                op("dve", lambda e: e.tensor_copy(kwin[:], U[:, 20:22, 896:1024]),
                   reads=RU(20, 896, 128) + RU(21, 896, 128), writes=["kwin"])
                op("dve", lambda e: e.tensor_copy(vbuf[:, 0:2, :], vbuf[:, 16:18, :]),
                   reads=[("vbuf", 16), ("vbuf", 17)], writes=[("vbuf", 0), ("vbuf", 1)])
                if DUMPU and s == 0:
                    op("sp", lambda e, g=g: e.dma_start(out=dU[g], in_=U[:, 12:20, :]),
                       reads=[("U", j, b) for j in range(12, 20) for b in range(17)], dma="dbgU")
                if "merge" in PH:
                    merge(s, tiles, g)
                if "ffn2" in PH:
                    rms_norm_to_hT(tiles, g, 2)
                    ffn(s, tiles, g, 1)

            op("sp", lambda e, s=s: e.dma_start(out=o_dn[s, 0], in_=S_f[:, 0].rearrange("p h d -> p (h d)")), reads=["S_f0"], dma="o_S0")
            op("sp", lambda e, s=s: e.dma_start(out=o_dn[s, 1], in_=S_f[:, 1].rearrange("p h d -> p (h d)")), reads=["S_f1"], dma="o_S1")
            op("sp", lambda e, s=s: e.dma_start(out=o_conv[s, 0], in_=halo[:, 0].rearrange("p j r -> p (j r)")), reads=halo_all(0), dma="o_h0")
            op("sp", lambda e, s=s: e.dma_start(out=o_conv[s, 1], in_=halo[:, 1].rearrange("p j r -> p (j r)")), reads=halo_all(1), dma="o_h1")
            op("sp", lambda e, s=s: e.dma_start(out=o_kp[s].rearrange("p (a b) -> p a b", a=2), in_=kout[:, :, 0:128]), reads=["kout"], dma="o_k")
            op("sp", lambda e, s=s: e.dma_start(out=o_ks[s].rearrange("p (a b) -> p a b", a=2), in_=kout[:, :, 128:160]), reads=["kout"], dma="o_k")
            op("sp", lambda e, s=s: e.dma_start(out=o_vp[s].rearrange("p (a b) -> p a b", a=2), in_=vout[:, 0:2, :]), reads=["vout"], dma="o_v")
            op("sp", lambda e, s=s: e.dma_start(out=o_vs[s], in_=vout[0:32, 2, :]), reads=["vout"], dma="o_v")
            if s < nslot - 1 and "handoff" in PH:
                op("sp", lambda e: e.dma_start(out=send_f[:, 0:512], in_=S_f[:, 0].rearrange("p h d -> p (h d)")), reads=["S_f0", "recv_f"], writes=["send_f"], dma="snd_f")
                op("sp", lambda e: e.dma_start(out=send_f[:, 512:548], in_=halo[:, 0].rearrange("p j r -> p (j r)")), reads=halo_all(0), writes=["send_f"], dma="snd_f")
                op("sp", lambda e: e.dma_start(out=send_b[:, 0:256], in_=kwin[:].rearrange("p a b -> p (a b)")), reads=["kwin", "recv_b"], writes=["send_b"], dma="snd_b")
                op("sp", lambda e: e.dma_start(out=send_b[0:64, 256:520], in_=vbuf[:, 0:2, :].rearrange("p a b -> p (a b)")), reads=[("vbuf", 0), ("vbuf", 1)], writes=["send_b"], dma="snd_b")
                groups = [[0, 1], [2, 3], [4, 5], [6, 7]]
                op("pool", lambda e: e.collective_compute("AllGather", ALU.bypass, replica_groups=groups, ins=[send_f], outs=[recv_f]),
                   reads=["send_f"], writes=["recv_f"], dma="cc_f", amt=1)
                op("pool", lambda e: e.collective_compute("AllGather", ALU.bypass, replica_groups=groups, ins=[send_b], outs=[recv_b]),
                   reads=["send_b"], writes=["recv_b"], dma="cc_b", amt=1)
        op("sp", lambda e: e.dma_start(out=yT.rearrange("j p t -> p j t"), in_=xT[:]), reads=[("xT", b) for b in range(5)], dma="out")
        nops = S.emit(final_waits=(["dbgU"] if DUMPU else []) + ["out", "o_S0", "o_S1", "o_h0", "o_h1", "o_k", "o_v"])
    return nc, nops


def _consts():
    cf = np.zeros((128, 1920), np.float32)
    cf[:, 0:128] = np.eye(128, dtype=np.float32)
    p = np.arange(64)[:, None]
    j = np.arange(64)[None, :]
    cf[0:64, 128:192] = (p > j)
    cf[0:64, 192:256] = (p <= j)
    cf[0:64, 256:384] = 1.0
    slopes = 2.0 ** (-8.0 * np.arange(1, 9, dtype=np.float32) / 8)
    s_ = np.arange(64)[:, None].astype(np.float32)
    i_ = np.arange(64)[None, :].astype(np.float32)
    BT = np.zeros((64, 3, 8, 64), np.float32)
    for dl in range(3):
        dist = np.abs(i_ + 64 * dl - s_)
        for h in range(8):
            BT[:, dl, (h % 2) * 4 + h // 2, :] = -slopes[h] * dist
    cf[0:64, 384:1920] = BT.reshape(64, -1)
    return cf


def _slot_stack(arr, odd):
    z = np.zeros((1,) + arr.shape[1:], arr.dtype)
    return np.ascontiguousarray(np.concatenate([z, arr] if odd else [arr, z], axis=0))


def kernel(**inp):
    f = lambda k: np.asarray(inp[k], dtype=np.float32)
    x_prompt, x_sample = f("x_prompt"), f("x_sample")
    state_dn, state_conv = f("state_dn"), f("state_conv")
    cache_k, cache_v = f("cache_swa_k"), f("cache_swa_v")
    w_in = f("w_in")
    idx = np.concatenate([np.arange(0, 1536), np.arange(1536, 2048), np.arange(2056, 2568),
                          np.arange(2568, 2632), np.arange(2568, 2632), np.arange(2632, 2696), np.arange(2632, 2696),
                          np.arange(2696, 2824), np.arange(2048, 2056), np.arange(2824, 3848), np.arange(3848, 4872)])
    assert idx.size == WIN_COLS
    win_r = w_in[:, :, idx]
    prm = np.zeros((DEPTH, 128, NPRM), np.float32)
    for i, k in enumerate(["ffn1_norm", "mix_norm", "ffn2_norm"]):
        prm[:, :, P_NW + i * 8:P_NW + i * 8 + 8] = f(k).reshape(DEPTH, 8, 128).transpose(0, 2, 1)
    prm[:, :, P_CW:P_CW + 48] = f("conv_w").reshape(DEPTH, 4, 12, 128).transpose(0, 3, 2, 1).reshape(DEPTH, 128, 48)
    prm[:, :, P_DNW:P_DNW + 128] = f("dn_norm")[:, None, :]
    prm[:, :, P_QNW] = np.tile(f("q_norm"), (1, 2))
    prm[:, :, P_KNW] = np.tile(f("k_norm"), (1, 2))
    prm[:, :, P_ALOG:P_ALOG + 4] = f("a_log")[:, None, :]
    prm[:, :, P_DTB:P_DTB + 4] = f("dt_bias")[:, None, :]
    prm[:, :, P_SINK:P_SINK + 8] = f("sinks")[:, None, :]
    big = {"wg1": f("ffn1_wg"), "wu1": f("ffn1_wu"), "wd1": f("ffn1_wd"), "wg2": f("ffn2_wg"), "wu2": f("ffn2_wu"),
           "wd2": f("ffn2_wd"), "win": win_r, "wodn": f("w_o_dn"), "woswa": f("w_o_swa"), "wout": f("w_out"), "prm": prm}
    stacks = [{k: _slot_stack(v, odd) for k, v in big.items()} for odd in (False, True)]
    cf = _consts()
    in_maps = []
    for c in range(8):
        b, odd = c // 2, c % 2
        m = dict(stacks[odd])
        xt = np.concatenate([x_prompt[b, odd * NP_:(odd + 1) * NP_], x_sample[c]], axis=0)
        m["xT0"] = np.ascontiguousarray(xt.T.reshape(8, 128, NT))
        sdn = state_dn[:, c].transpose(0, 2, 1, 3).reshape(DEPTH, 128, 512)
        sconv = state_conv[:, c].reshape(DEPTH, 3, 12, 128).transpose(0, 3, 2, 1).reshape(DEPTH, 128, 36)
        ck = cache_k[:, c].transpose(0, 2, 3, 1)
        skc = np.concatenate([ck, ck], axis=2).transpose(0, 2, 1, 3).reshape(DEPTH, 128, 256)
        svc = cache_v[:, c].reshape(DEPTH, 2, 64, 2, 64).transpose(0, 2, 1, 3, 4)
        m["sdn"] = _slot_stack(sdn, odd)
        m["sconv"] = _slot_stack(sconv, odd)
        m["skc"] = _slot_stack(skc, odd)
        m["svc"] = _slot_stack(svc, odd)
        m["mcore"] = np.full((128, 1), float(odd), np.float32)
        m["cf"] = cf
        in_maps.append(m)
    nc, _ = build_nc()
    res = run_bass_kernel_spmd(nc, in_maps, core_ids=list(range(8)))
    R = res.results
    y_prompt = np.zeros((4, 4096, D), np.float32)
    y_sample = np.zeros((8, NS_, D), np.float32)
    dn_prompt = np.zeros((DEPTH, 4, 4, 128, 128), np.float32)
    dn_sample = np.zeros((DEPTH, 8, 4, 128, 128), np.float32)
    conv_prompt = np.zeros((DEPTH, 4, 3, 1536), np.float32)
    conv_sample = np.zeros((DEPTH, 8, 3, 1536), np.float32)
    kp = np.zeros((DEPTH, 4, 128, 2, 64), np.float32)
    vp = np.zeros((DEPTH, 4, 128, 2, 64), np.float32)
    ks = np.zeros((DEPTH, 8, NS_, 2, 64), np.float32)
    vs = np.zeros((DEPTH, 8, NS_, 2, 64), np.float32)
    for c in range(8):
        b, odd = c // 2, c % 2
        r = R[c]
        yt = r["yT"].reshape(D, NT).T
        y_prompt[b, odd * NP_:(odd + 1) * NP_] = yt[0:NP_]
        y_sample[c] = yt[NP_:]
        for l in range(DEPTH):
            s = l + odd
            dn_sample[l, c] = r["o_dn"][s, 1].reshape(128, 4, 128).transpose(1, 0, 2)
            conv_sample[l, c] = r["o_conv"][s, 1].reshape(128, 12, 3).transpose(2, 1, 0).reshape(3, 1536)
            ks[l, c] = r["o_ks"][s].reshape(128, 2, 32)[0:64].transpose(2, 1, 0)
            vs[l, c] = r["o_vs"][s].reshape(32, 2, 64)
            if odd:
                dn_prompt[l, b] = r["o_dn"][s, 0].reshape(128, 4, 128).transpose(1, 0, 2)
                conv_prompt[l, b] = r["o_conv"][s, 0].reshape(128, 12, 3).transpose(2, 1, 0).reshape(3, 1536)
                kp[l, b] = r["o_kp"][s].reshape(128, 2, 128)[0:64].transpose(2, 1, 0)
                vp[l, b] = r["o_vp"][s].reshape(64, 2, 2, 64).transpose(1, 0, 2, 3).reshape(128, 2, 64)
    return (y_prompt, y_sample, dn_prompt, dn_sample, conv_prompt, conv_sample, kp, vp, ks, vs)
```

```python
import contextlib
import numpy as np
import concourse.bass as bass
import concourse.mybir as mybir
from concourse.bass_utils import run_bass_kernel_spmd

F32 = mybir.dt.float32
BF16 = mybir.dt.bfloat16
ALU = mybir.AluOpType
AF = mybir.ActivationFunctionType
AX = mybir.AxisListType

D = 1024
DEPTH = 4
NSLOT = DEPTH + 1
NP_ = 2048
NS_ = 32
NT = NP_ + NS_
GT = 1056
DFF = 2816
WIN_COLS = 5000
C_QKV, C_GATE, C_QS, C_KD, C_VBA, C_GA, C_GB = 0, 1536, 2048, 2560, 2816, 2952, 3976
NPRM = 218
P_NW, P_CW, P_DNW, P_QNW, P_KNW, P_ALOG, P_DTB, P_SINK = 0, 24, 72, 200, 201, 202, 206, 210
RMS_EPS = 1e-6
L2_EPS = 1e-6
WSLOT = 5632


class Sched:
    ENGS = ("pe", "act", "dve", "pool", "sp")

    def __init__(self, nc):
        self.nc = nc
        self.ops = []
        self.lastw = {}
        self.readers = {}
        self.chan_tot = {}
        self.bank_rd = {}
        self.force_small = False

    def op(self, eng, fn, reads=(), writes=(), dma=None, amt=16, small=False):
        i = len(self.ops)
        deps = set()
        raw = set()
        for r in reads:
            w = self.lastw.get(r)
            if w is not None:
                deps.add(w)
                raw.add(w)
        for r in writes:
            w = self.lastw.get(r)
            if w is not None:
                deps.add(w)
            q = self.readers.get(r)
            if q:
                deps.update(q)
        for r in reads:
            self.readers.setdefault(r, []).append(i)
            if isinstance(r, tuple) and r[0] == "ps":
                br = self.bank_rd.setdefault(r[1], {})
                for e2, j in br.items():
                    if e2 != eng:
                        deps.add(j)
                br[eng] = i
        for r in writes:
            self.lastw[r] = i
            self.readers[r] = []
        o = dict(eng=eng, fn=fn, deps=deps, dma=dma, inc=False, cnt=None, amt=amt, small=(small or self.force_small), raw=raw)
        if dma is not None:
            self.chan_tot[dma] = self.chan_tot.get(dma, 0) + amt
            o["cnt"] = self.chan_tot[dma]
            o["inc"] = True
        self.ops.append(o)
        return i

    def emit(self, final_waits=()):
        nc = self.nc
        ops = self.ops
        waited = {e: {} for e in self.ENGS}
        for i, o in enumerate(ops):
            e = o["eng"]
            ws = []
            for d in sorted(o["deps"]):
                p = ops[d]
                if p["dma"] is not None:
                    key = ("ch", p["dma"])
                    if waited[e].get(key, -1) >= p["cnt"]:
                        continue
                    waited[e][key] = p["cnt"]
                    ws.append(d)
                else:
                    if p["eng"] == e and o["dma"] is None and not (p["small"] and d in o["raw"]):
                        continue
                    key = ("en", p["eng"])
                    if waited[e].get(key, -1) >= d:
                        continue
                    waited[e][key] = d
                    p["inc"] = True
                    ws.append(d)
            o["waits"] = ws
        cnt = {e: 0 for e in self.ENGS}
        for o in ops:
            if o["dma"] is None and o["inc"]:
                cnt[o["eng"]] += 1
                o["cnt"] = cnt[o["eng"]]
        chans = sorted(self.chan_tot)
        with contextlib.ExitStack() as st:
            esem = {e: st.enter_context(nc.semaphore("se_" + e)) for e in self.ENGS}
            csem = {c: st.enter_context(nc.semaphore("sc_" + c)) for c in chans}
            block = st.enter_context(nc.Block())
            handles = {"pe": "tensor", "act": "scalar", "dve": "vector", "pool": "gpsimd", "sp": "sync"}

            def make(e):
                def body(eng):
                    for o in ops:
                        if o["eng"] != e:
                            continue
                        best = {}
                        for d in o["waits"]:
                            p = ops[d]
                            key = ("ch", p["dma"]) if p["dma"] is not None else ("en", p["eng"])
                            if best.get(key, -1) < p["cnt"]:
                                best[key] = p["cnt"]
                        for key, v in best.items():
                            sem = csem[key[1]] if key[0] == "ch" else esem[key[1]]
                            eng.wait_ge(sem, v)
                        ins = o["fn"](eng)
                        if o["dma"] is not None:
                            ins.then_inc(csem[o["dma"]], o["amt"])
                        elif o["inc"]:
                            ins.then_inc(esem[e], 1)
                    if e == "sp":
                        for c in final_waits:
                            eng.wait_ge(csem[c], self.chan_tot[c])
                return body

            for e in self.ENGS:
                getattr(block, handles[e])(make(e))
        return len(ops)


def build_nc(nslot=NSLOT, ngroup=2, phases=None):
    PH = phases or {"ffn1", "proj", "dn", "swa", "merge", "ffn2", "handoff"}
    if "proj" in PH and not (PH & {"p_qkv", "p_gate", "p_qk", "p_vba"}):
        PH = PH | {"p_qkv", "p_gate", "p_qk", "p_vba"}
    NSLOT = nslot
    nc = bass.Bass("TRN2", target_bir_lowering=False)

    def din(name, shape, dt=F32):
        return nc.dram_tensor(name, list(shape), dt, kind="ExternalInput").ap()

    def dout(name, shape, dt=F32):
        return nc.dram_tensor(name, list(shape), dt, kind="ExternalOutput").ap()

    xT0 = din("xT0", [8, 128, NT])
    wg = [din("wg1", [NSLOT, D, DFF]), din("wg2", [NSLOT, D, DFF])]
    wu = [din("wu1", [NSLOT, D, DFF]), din("wu2", [NSLOT, D, DFF])]
    wd = [din("wd1", [NSLOT, DFF, D]), din("wd2", [NSLOT, DFF, D])]
    win = din("win", [NSLOT, D, WIN_COLS])
    wodn = din("wodn", [NSLOT, 512, D])
    woswa = din("woswa", [NSLOT, 512, D])
    wout = din("wout", [NSLOT, D, D])
    prm_d = din("prm", [NSLOT, 128, NPRM])
    sdn_d = din("sdn", [NSLOT, 128, 512])
    sconv_d = din("sconv", [NSLOT, 128, 36])
    skc_d = din("skc", [NSLOT, 128, 256])
    svc_d = din("svc", [NSLOT, 64, 2, 2, 64])
    mcore_d = din("mcore", [128, 1])
    cf_d = din("cf", [128, 128 + 64 + 64 + 128 + 1536])
    yT = dout("yT", [8, 128, NT])
    o_dn = dout("o_dn", [NSLOT, 2, 128, 512])
    o_conv = dout("o_conv", [NSLOT, 2, 128, 36])
    o_kp = dout("o_kp", [NSLOT, 128, 256])
    o_ks = dout("o_ks", [NSLOT, 128, 64])
    o_vp = dout("o_vp", [NSLOT, 64, 256])
    o_vs = dout("o_vs", [NSLOT, 32, 128])
    import os
    DUMPU = os.environ.get("DUMPU", "") == "1"
    if DUMPU:
        dU = dout("dU", [2, 128, 8, GT], BF16)
    send_f = nc.dram_tensor("send_f", [128, 548], F32).ap()
    recv_f = nc.dram_tensor("recv_f", [256, 548], F32).ap()
    send_b = nc.dram_tensor("send_b", [128, 520], BF16).ap()
    recv_b = nc.dram_tensor("recv_b", [256, 520], BF16).ap()

    S = Sched(nc)
    with contextlib.ExitStack() as st:
        def sb(name, shape, dt=F32):
            return st.enter_context(nc.sbuf_tensor("s_" + name, list(shape), dt))

        def psum(name, shape, dt=F32):
            return st.enter_context(nc.psum_tensor(name, list(shape), dt))

        xT = sb("xT", [128, 8, NT])
        hT = sb("hT", [128, 8, GT], BF16)
        U = sb("U", [128, 22, GT], BF16)
        wring = sb("wring", [128, 2, WSLOT], BF16)
        kwin = sb("kwin", [128, 2, 128], BF16)
        ksc = sb("ksc", [128, 2, 128], BF16)
        vbuf = sb("vbuf", [64, 18, 132], BF16)
        vsb = sb("vsb", [64, 3, 132], BF16)
        S_f = sb("S_f", [128, 2, 4, 128])
        S_b = sb("S_b", [128, 2, 4, 128], BF16)
        halo = sb("halo", [128, 2, 12, 3])
        stage = sb("stage", [128, 520])
        cacc = sb("cacc", [128, 548])
        cact = sb("cact", [128, 512])
        rstd = sb("rstd", [128, 512])
        sqb = sb("sqb", [128, 520], BF16)
        sigb = [cacc, cact]
        SG = ["cacc", "cact"]
        kscf = cact[:, 0:256]
        vscf = cact[0:64, 256:512].rearrange("p (a b c) -> p a b c", a=2, b=2)
        cf = sb("cf", [128, 1920])
        cfb = sb("cfb", [128, 384], BF16)
        prm = sb("prm", [128, NPRM])
        prm2 = sb("prm2", [128, 16])
        mcore = sb("mcore", [128, 1])
        kout = sb("kout", [128, 2, 160])
        vout = sb("vout", [64, 3, 128])
        rtmp = cacc
        rtmpb = sqb
        zba = sb("zba", [64, 17, 8])
        tp_beta = sb("tp_beta", [64, 17, 4])
        tp_negb = sb("tp_negb", [64, 17, 4])
        tp_g = sb("tp_g", [64, 17, 4])
        tp_gc = sb("tp_gc", [64, 17, 4])
        tp_egc = sb("tp_egc", [64, 17, 4])
        tp_ekr = sb("tp_ekr", [64, 17, 4])
        tp_egl = sb("tp_egl", [128, 17, 4])
        tp_t = sb("tp_t", [64, 17, 4])
        Gm = sb("Gm", [64, 4, 64])
        dnA = sb("dnA", [64, 4, 128])
        DTs = dnA[:, :, 0:64]
        DTu = dnA[:, :, 64:128]
        tA = Gm
        Pk0 = sb("Pk0", [64, 4, 64])
        PTk0 = sb("PTk0", [64, 4, 64])
        Pk = [Pk0, Pk0]
        PTk = [PTk0, PTk0]
        Rk0 = sb("Rk0", [64, 4, 64])
        Rk = [Rk0, Rk0]
        Rbf = sb("Rbf", [64, 4, 64], BF16)
        inT2 = [sb("inT0", [64, 4, 64], BF16), sb("inT1", [64, 4, 64], BF16)]
        kg = sb("kg", [64, 4, 128], BF16)
        kr2 = [sb("kr0", [64, 4, 128], BF16), sb("kr1", [64, 4, 128], BF16)]
        vtm = sb("vtm", [64, 4, 128], BF16)
        u_sb2 = [sb("u_sb", [64, 4, 128]), cact[0:64, 0:512].rearrange("p (h d) -> p h d", h=4)]
        w0T2 = [sb("w0T0", [128, 4, 64], BF16), sb("w0T1", [128, 4, 64], BF16)]
        vnew = sb("vnew", [64, 4, 128], BF16)
        o_sb = rstd[0:64, 0:512].rearrange("p (h d) -> p h d", h=4)
        on2 = sqb[0:64, 0:512].rearrange("p (h d) -> p h d", h=4)
        ss4 = sb("ss4", [64, 8])
        tS = sb("tS", [128, 4, 128])
        sct = stage[0:64, 0:512].rearrange("p (h d) -> p h d", h=8)
        t1 = cacc[0:64, 0:512].rearrange("p (h d) -> p h d", h=4)
        PTs = [sb("PT0", [64, 8, 64], BF16), sb("PT1", [64, 8, 64], BF16), sb("PT2", [64, 8, 64], BF16)]
        PTn = ["PT0", "PT1", "PT2"]
        den = sb("den", [64, 8])
        osw = sb("osw", [64, 8, 64], BF16)

        pf = [psum("pf%d" % i, [128, 512]) for i in range(7)]
        pb = psum("pb", [128, 2, 512], BF16)

        ident_f = cf[:, 0:128]
        SL = cf[0:64, 128:192]
        UT = cf[0:64, 192:256]
        ones_f = cf[0:64, 256:384]
        BT = cf[0:64, 384:1920].rearrange("p (a h i) -> p a h i", a=3, h=8)
        ident_b = cfb[:, 0:128]
        ones_b = cfb[:, 128:256]
        blk_b = cfb[:, 256:384]

        op = S.op

        def T_(tiles):
            for (c0, n) in tiles:
                S.force_small = (n < 128)
                yield (c0, n)
            S.force_small = False

        def RU(j, lc0, n):
            return [("U", j, b) for b in range(lc0 // 64, (lc0 + n - 1) // 64 + 1)]

        def RH(lc0, n):
            return [("hT", b) for b in range(lc0 // 512, (lc0 + n - 1) // 512 + 1)]

        def RX(c0, n):
            return [("xT", b) for b in range(c0 // 512, (c0 + n - 1) // 512 + 1)]

        bank_rr = {"i": 0}

        op("sp", lambda e: e.dma_start(out=xT[:], in_=xT0.rearrange("j p t -> p j t")),
           writes=[("xT", b) for b in range(5)], dma="ld_x")
        op("sp", lambda e: e.dma_start(out=cf[:], in_=cf_d), writes=["cf"], dma="ld_cf")
        op("sp", lambda e: e.dma_start(out=mcore[:], in_=mcore_d), writes=["mcore"], dma="ld_mc")
        op("dve", lambda e: e.tensor_copy(cfb[:, 0:128], cf[:, 0:128]), reads=["cf"], writes=["cfb"])
        op("dve", lambda e: e.memset(cfb[:, 128:256], 1.0), writes=["cfb"])
        op("dve", lambda e: e.memset(cfb[:, 256:384], 0.0), writes=["cfb"])
        op("dve", lambda e: e.memset(cfb[0:64, 256:320], 1.0), writes=["cfb"])
        op("dve", lambda e: e.memset(cfb[64:128, 320:384], 1.0), writes=["cfb"])
        op("dve", lambda e: e.memset(vbuf[:], 1.0), writes=["vbuf_all"])
        op("dve", lambda e: e.memset(vsb[:], 1.0), writes=["vsb_all"])
        op("dve", lambda e: e.memset(zba[:], 0.0), writes=["zba"])
        op("dve", lambda e: e.memset(S_f[:, 0], 0.0), writes=["S_f0"])
        op("dve", lambda e: e.memset(halo[:, 0], 0.0), writes=[("halo", 0, j) for j in range(12)])
        op("dve", lambda e: e.memset(kwin[:], 0.0), writes=["kwin"])
        op("dve", lambda e: e.memset(vbuf[:, 0:2, 0:64], 0.0), reads=["vbuf_all"], writes=[("vbuf", 0), ("vbuf", 1)])
        op("dve", lambda e: e.memset(vbuf[:, 0:2, 66:130], 0.0), writes=[("vbuf", 0), ("vbuf", 1)])

        wstate = {"n": 0}

        def load_panel(src2d, K, ncols):
            KC = K // 128
            s = wstate["n"] % 2
            wstate["n"] += 1
            view = wring[:, s, 0:KC * ncols].rearrange("p (k n) -> p k n", k=KC)
            op("pool", lambda e: e.dma_start(out=view, in_=src2d.rearrange("(k p) n -> p k n", p=128)),
               writes=[("wr", s)], dma="w%d" % s)
            return view, ("wr", s)

        def load_panels(specs):
            s_ = wstate["n"] % 2
            wstate["n"] += 1
            off = 0
            views = []
            for (src2d, K, ncols) in specs:
                KC = K // 128
                view = wring[:, s_, off:off + KC * ncols].rearrange("p (k n) -> p k n", k=KC)
                off += KC * ncols
                assert off <= WSLOT
                op("pool", lambda e, view=view, src2d=src2d: e.dma_start(out=view, in_=src2d.rearrange("(k p) n -> p k n", p=128)),
                   writes=[("wr", s_)], dma="w%d" % s_)
                views.append(view)
            return views, ("wr", s_)

        def next_bank(cands):
            b = cands[bank_rr["i"] % len(cands)]
            bank_rr["i"] += 1
            return b

        def rms_norm_to_hT(tiles, g, nwi):
            for (c0, n) in T_(tiles):
                lc0 = c0 - g * 1024
                hview = hT[:, :, lc0:lc0 + n]
                op("act", lambda e, hview=hview, c0=c0, n=n: e.activation(hview, xT[:, :, c0:c0 + n], AF.Square),
                   reads=RX(c0, n), writes=RH(lc0, n))
                ps = pf[4]
                for k in range(8):
                    op("pe", lambda e, k=k, lc0=lc0, n=n: e.matmul(ps[:, 0:n], ones_b, hT[:, k, lc0:lc0 + n],
                                                                    start=(k == 0), stop=(k == 7)),
                       reads=RH(lc0, n) + ["cfb"], writes=[("ps", 4)])
                op("act", lambda e, n=n: e.activation(rstd[:, 0:n], ps[:, 0:n], AF.Ln, bias=RMS_EPS, scale=1.0 / D),
                   reads=[("ps", 4)], writes=["rstd"])
                op("act", lambda e, n=n: e.activation(rstd[:, 0:n], rstd[:, 0:n], AF.Exp, scale=-0.5),
                   reads=["rstd"], writes=["rstd"])
                for k in range(8):
                    op("dve", lambda e, k=k, c0=c0, lc0=lc0, n=n: e.scalar_tensor_tensor(
                        hT[:, k, lc0:lc0 + n], xT[:, k, c0:c0 + n], prm[:, P_NW + nwi * 8 + k:P_NW + nwi * 8 + k + 1],
                        rstd[:, 0:n], ALU.mult, ALU.mult),
                       reads=RX(c0, n) + ["rstd", "prm"], writes=RH(lc0, n))

        def ffn(s, tiles, g, fi):
            for f0 in range(0, DFF, 256):
                ncols = 256
                (vg, vu), rg = load_panels([(wg[fi][s, :, f0:f0 + ncols], D, ncols), (wu[fi][s, :, f0:f0 + ncols], D, ncols)])
                ru = rg
                for jj in range(ncols // 128):
                    f = f0 // 128 + jj
                    for (c0, n) in T_(tiles):
                        lc0 = c0 - g * 1024
                        bg = next_bank([0, 1])
                        bu = bg + 2
                        for k in range(8):
                            op("pe", lambda e, k=k, jj=jj, lc0=lc0, n=n, bg=bg, vg=vg: e.matmul(
                                pf[bg][:, 0:n], vg[:, k, jj * 128:(jj + 1) * 128], hT[:, k, lc0:lc0 + n],
                                start=(k == 0), stop=(k == 7)),
                               reads=RH(lc0, n) + [rg], writes=[("ps", bg)])
                        for k in range(8):
                            op("pe", lambda e, k=k, jj=jj, lc0=lc0, n=n, bu=bu, vu=vu: e.matmul(
                                pf[bu][:, 0:n], vu[:, k, jj * 128:(jj + 1) * 128], hT[:, k, lc0:lc0 + n],
                                start=(k == 0), stop=(k == 7)),
                               reads=RH(lc0, n) + [ru], writes=[("ps", bu)])
                        sl = bg
                        op("act", lambda e, n=n, bg=bg, sl=sl: e.activation(sigb[sl][:, 0:n], pf[bg][:, 0:n], AF.Silu),
                           reads=[("ps", bg)], writes=[SG[sl]])
                        op("dve", lambda e, n=n, bu=bu, sl=sl, f=f, lc0=lc0: e.tensor_tensor(
                            U[:, f, lc0:lc0 + n], sigb[sl][:, 0:n], pf[bu][:, 0:n], ALU.mult),
                           reads=[SG[sl], ("ps", bu)], writes=RU(f, lc0, n))
            for ob in range(4):
                vd, rd = load_panel(wd[fi][s, :, ob * 256:(ob + 1) * 256], DFF, 256)
                for jj in range(2):
                    o = ob * 2 + jj
                    for (c0, n) in T_(tiles):
                        lc0 = c0 - g * 1024
                        b = next_bank([0, 1, 2, 3])
                        for k in range(22):
                            op("pe", lambda e, k=k, jj=jj, lc0=lc0, n=n, b=b, vd=vd: e.matmul(
                                pf[b][:, 0:n], vd[:, k, jj * 128:(jj + 1) * 128], U[:, k, lc0:lc0 + n],
                                start=(k == 0), stop=(k == 21)),
                               reads=RU(k, lc0, n) + [rd], writes=[("ps", b)])
                        op("dve", lambda e, o=o, c0=c0, n=n, b=b: e.scalar_tensor_tensor(
                            xT[:, o, c0:c0 + n], pf[b][:, 0:n], 0.5, xT[:, o, c0:c0 + n], ALU.mult, ALU.add),
                           reads=[("ps", b)] + RX(c0, n), writes=RX(c0, n))

        def rsqrt_chain(ps_ap, n, scale, eps):
            op("act", lambda e: e.activation(rstd[:, 0:n], ps_ap, AF.Ln, bias=eps, scale=scale),
               reads=[("ps", 4)], writes=["rstd"])
            op("act", lambda e: e.activation(rstd[:, 0:n], rstd[:, 0:n], AF.Exp, scale=-0.5),
               reads=["rstd"], writes=["rstd"])

        def proj_in(s, tiles, g):
            W = win
            tSf = tS[:].rearrange("p h d -> p (h d)")
            gop = S.op
            stA, stB = [], []
            itn = [0]
            for pb_ in (range(3) if "p_qkv" in PH else []):
                first_in_panel = [True]
                sl_ = wstate["n"] % 2
                wstate["n"] += 1
                v = wring[:, sl_, 0:8 * 512].rearrange("p (k n) -> p k n", k=8)
                r = ("wr", sl_)
                src2d = W[s, :, C_QKV + pb_ * 512:C_QKV + (pb_ + 1) * 512]
                ld = (("pool", lambda e, v=v, src2d=src2d: e.dma_start(out=v, in_=src2d.rearrange("(k p) n -> p k n", p=128))),
                      dict(writes=[r], dma="w%d" % sl_))
                for jj in range(4):
                    j = pb_ * 4 + jj
                    for (c0, n) in tiles:
                        A, B = [], []
                        fs = (n < 128)

                        def opA(*a, **k):
                            k["small"] = k.get("small", False) or fs
                            A.append((a, k))

                        def opB(*a, **k):
                            k["small"] = k.get("small", False) or fs
                            B.append((a, k))
                        if first_in_panel[0]:
                            A.append(ld)
                            first_in_panel[0] = False
                        par = itn[0] % 2
                        itn[0] += 1
                        cb_ = cacc if par == 0 else tSf
                        cn_ = "cacc" if par == 0 else "tS"
                        lc0 = c0 - g * 1024
                        stq = 1 if c0 >= NP_ else 0
                        b = next_bank([0, 1, 2, 3])
                        for k in range(8):
                            opA("pe", lambda e, k=k, jj=jj, lc0=lc0, n=n, b=b, v=v: e.matmul(
                                pf[b][:, 0:n], v[:, k, jj * 128:(jj + 1) * 128], hT[:, k, lc0:lc0 + n],
                                start=(k == 0), stop=(k == 7)),
                                reads=RH(lc0, n) + [r], writes=[("ps", b)])
                        opA("dve", lambda e, stq=stq, j=j: e.tensor_copy(stage[:, 0:3], halo[:, stq, j, :]),
                            reads=[("halo", stq, j)], writes=["stage"], small=True)
                        opA("act", lambda e, n=n, b=b: e.copy(stage[:, 3:3 + n], pf[b][:, 0:n]),
                            reads=[("ps", b)], writes=["stage"])
                        opA("dve", lambda e, stq=stq, j=j, n=n: e.tensor_copy(halo[:, stq, j, :], stage[:, n:n + 3]),
                            reads=["stage"], writes=[("halo", stq, j)], small=True)
                        cw0 = P_CW + j * 4
                        opA("dve", lambda e, n=n, cw0=cw0, cb_=cb_: e.tensor_scalar(
                            cb_[:, 0:n], stage[:, 0:n], prm[:, cw0:cw0 + 1], None, ALU.mult),
                            reads=["stage", "prm"], writes=[cn_])
                        for t in range(1, 4):
                            opA("dve", lambda e, n=n, cw0=cw0, t=t, cb_=cb_: e.scalar_tensor_tensor(
                                cb_[:, 0:n], stage[:, t:t + n], prm[:, cw0 + t:cw0 + t + 1], cb_[:, 0:n],
                                ALU.mult, ALU.add),
                                reads=["stage", "prm", cn_], writes=[cn_])
                        if j >= 8:
                            opB("act", lambda e, n=n, j=j, lc0=lc0, cb_=cb_: e.activation(U[:, j, lc0:lc0 + n], cb_[:, 0:n], AF.Silu),
                                reads=[cn_], writes=RU(j, lc0, n))
                        else:
                            opB("act", lambda e, n=n, cb_=cb_: e.activation(cact[:, 0:n], cb_[:, 0:n], AF.Silu),
                                reads=[cn_], writes=["cact"])
                            opB("act", lambda e, n=n: e.activation(sqb[:, 0:n], cact[:, 0:n], AF.Square),
                                reads=["cact"], writes=["sqb"])
                            opB("pe", lambda e, n=n: e.matmul(pf[4][:, 0:n], ones_b, sqb[:, 0:n], start=True, stop=True),
                                reads=["sqb", "cfb"], writes=[("ps", 4)])
                            opB("act", lambda e, n=n: e.activation(rstd[:, 0:n], pf[4][:, 0:n], AF.Ln, bias=L2_EPS, scale=1.0),
                                reads=[("ps", 4)], writes=["rstd"])
                            opB("act", lambda e, n=n: e.activation(rstd[:, 0:n], rstd[:, 0:n], AF.Exp, scale=-0.5),
                                reads=["rstd"], writes=["rstd"])
                            qs = (128.0 ** -0.5) if j < 4 else 1.0
                            opB("dve", lambda e, n=n, j=j, lc0=lc0, qs=qs: e.scalar_tensor_tensor(
                                U[:, j, lc0:lc0 + n], cact[:, 0:n], qs, rstd[:, 0:n], ALU.mult, ALU.mult),
                                reads=["cact", "rstd"], writes=RU(j, lc0, n))
                        stA.append(A)
                        stB.append(B)
            for i in range(len(stA)):
                if i == 0:
                    for (a, k) in stA[0]:
                        gop(*a, **k)
                if i + 1 < len(stA):
                    for (a, k) in stA[i + 1]:
                        gop(*a, **k)
                for (a, k) in stB[i]:
                    gop(*a, **k)
            v, r = load_panel(W[s, :, C_GATE:C_GATE + 512], D, 512)
            for jj in (range(4) if "p_gate" in PH else []):
                for (c0, n) in T_(tiles):
                    lc0 = c0 - g * 1024
                    b = next_bank([0, 1, 2, 3])
                    for k in range(8):
                        op("pe", lambda e, k=k, jj=jj, lc0=lc0, n=n, b=b, v=v: e.matmul(
                            pf[b][:, 0:n], v[:, k, jj * 128:(jj + 1) * 128], hT[:, k, lc0:lc0 + n],
                            start=(k == 0), stop=(k == 7)),
                           reads=RH(lc0, n) + [r], writes=[("ps", b)])
                    op("act", lambda e, n=n, b=b, jj=jj, lc0=lc0: e.activation(U[:, 12 + jj, lc0:lc0 + n], pf[b][:, 0:n], AF.Silu),
                       reads=[("ps", b)], writes=RU(12 + jj, lc0, n))
            for (cbase, nch, ubase, pcol) in ([(C_QS, 4, 16, "q"), (C_KD, 2, 20, "k")] if "p_qk" in PH else []):
                v, r = load_panel(W[s, :, cbase:cbase + nch * 128], D, nch * 128)
                for jj in range(nch):
                    for (c0, n) in T_(tiles):
                        lc0 = c0 - g * 1024
                        b = next_bank([0, 1, 2, 3])
                        for k in range(8):
                            op("pe", lambda e, k=k, jj=jj, lc0=lc0, n=n, b=b, v=v: e.matmul(
                                pf[b][:, 0:n], v[:, k, jj * 128:(jj + 1) * 128], hT[:, k, lc0:lc0 + n],
                                start=(k == 0), stop=(k == 7)),
                               reads=RH(lc0, n) + [r], writes=[("ps", b)])
                        op("act", lambda e, n=n, b=b: e.activation(sqb[:, 0:n], pf[b][:, 0:n], AF.Square),
                           reads=[("ps", b)], writes=["sqb"])
                        op("pe", lambda e, n=n: e.matmul(pf[4][:, 0:n], blk_b, sqb[:, 0:n], start=True, stop=True),
                           reads=["sqb", "cfb"], writes=[("ps", 4)])
                        rsqrt_chain(pf[4][:, 0:n], n, 1.0 / 64, RMS_EPS)
                        sc = prm2[:, 0:1] if pcol == "q" else prm[:, P_KNW:P_KNW + 1]
                        op("dve", lambda e, n=n, b=b, jj=jj, lc0=lc0, sc=sc, ubase=ubase: e.scalar_tensor_tensor(
                            U[:, ubase + jj, lc0:lc0 + n], pf[b][:, 0:n], sc, rstd[:, 0:n], ALU.mult, ALU.mult),
                           reads=[("ps", b), "rstd", "prm", "prm2"], writes=RU(ubase + jj, lc0, n))
                        if pcol == "k":
                            if c0 >= NP_:
                                op("dve", lambda e, n=n, b=b, jj=jj, sc=sc: e.scalar_tensor_tensor(
                                    kout[:, jj, 128:160], pf[b][:, 0:n], sc, rstd[:, 0:n], ALU.mult, ALU.mult),
                                   reads=[("ps", b), "rstd", "prm"], writes=["kout"])
                            elif c0 + n == NP_:
                                op("dve", lambda e, n=n, b=b, jj=jj, sc=sc: e.scalar_tensor_tensor(
                                    kout[:, jj, 0:128], pf[b][:, n - 128:n], sc, rstd[:, n - 128:n], ALU.mult, ALU.mult),
                                   reads=[("ps", b), "rstd", "prm"], writes=["kout"])
            v, r = load_panel(W[s, :, C_VBA:C_VBA + 136], D, 136)
            chunks = chunk_list(g) if "p_vba" in PH else []
            for ci, (stq, lc0, C) in enumerate(chunks):
                b = next_bank([0, 1, 2, 3])
                for k in range(8):
                    op("pe", lambda e, k=k, lc0=lc0, C=C, b=b, v=v: e.matmul(
                        pf[b][0:C, 0:136], hT[:, k, lc0:lc0 + C], v[:, k, :], start=(k == 0), stop=(k == 7)),
                       reads=RH(lc0, C) + [r], writes=[("ps", b)])
                if stq == 0:
                    vdst = vbuf[0:C, 2 + ci, :]
                    vres = ("vbuf", 2 + ci)
                else:
                    vdst = vsb[0:C, 2, :]
                    vres = ("vsb", 2)
                import os
                DV = os.environ.get("DBGV", "")
                if "noact" not in DV: op("dve", lambda e, C=C, b=b, vdst=vdst: e.tensor_copy(
                    vdst.rearrange("p (k d) -> p k d", k=2)[:, :, 0:64],
                    pf[b][0:C, 0:128].rearrange("p (k d) -> p k d", k=2)),
                   reads=[("ps", b), "vbuf_all", "vsb_all"], writes=[vres])
                if "nozba" not in DV: op("dve", lambda e, C=C, b=b, ci=ci: e.tensor_copy(zba[0:C, ci, :], pf[b][0:C, 128:136]),
                   reads=[("ps", b)], writes=["zba"], small=True)
                if "novout" in DV:
                    pass
                elif stq == 1:
                    op("dve", lambda e, C=C, b=b: e.tensor_copy(vout[0:C, 2, :], pf[b][0:C, 0:128]),
                       reads=[("ps", b)], writes=["vout"])
                elif g == ngroup - 1 and ci >= 14:
                    op("dve", lambda e, C=C, b=b, ci=ci: e.tensor_copy(vout[0:C, ci - 14, :], pf[b][0:C, 0:128]),
                       reads=[("ps", b)], writes=["vout"])

        def chunk_list(g):
            ch = [(0, 64 * i, 64) for i in range(16)]
            if g == ngroup - 1:
                ch.append((1, 1024, 32))
            return ch

        def dn_params(g):
            chunks = chunk_list(g)
            nch = len(chunks)
            npc = 16
            has_s = nch > 16
            A3 = lambda t, a=0, b=4: t[:, 0:nch, a:b]
            op("act", lambda e: e.activation(tp_beta[:, 0:nch, :], zba[:, 0:nch, 0:4], AF.Sigmoid),
               reads=["zba"], writes=["tp_beta"], small=True)
            op("dve", lambda e: e.tensor_scalar(tp_negb[:, 0:nch, :], tp_beta[:, 0:nch, :], -1.0, None, ALU.mult),
               reads=["tp_beta"], writes=["tp_negb"], small=True)
            op("dve", lambda e: e.tensor_tensor(tp_t[:, 0:nch, :], zba[:, 0:nch, 4:8],
                                                prm[0:64, P_DTB:P_DTB + 4].unsqueeze(1).broadcast_to([64, nch, 4]), ALU.add),
               reads=["zba", "prm"], writes=["tp_t"], small=True)
            op("act", lambda e: e.activation(tp_t[:, 0:nch, :], tp_t[:, 0:nch, :], AF.Exp), reads=["tp_t"], writes=["tp_t"], small=True)
            op("act", lambda e: e.activation(tp_t[:, 0:nch, :], tp_t[:, 0:nch, :], AF.Ln, bias=1.0), reads=["tp_t"], writes=["tp_t"], small=True)
            op("dve", lambda e: e.scalar_tensor_tensor(tp_g[:, 0:nch, :], tp_t[:, 0:nch, :], -1.0,
                                                       prm2[0:64, 1:5].unsqueeze(1).broadcast_to([64, nch, 4]), ALU.mult, ALU.mult),
               reads=["tp_t", "prm2"], writes=["tp_g"], small=True)
            g2 = tp_g[:].rearrange("p c h -> p (c h)")
            op("pe", lambda e: e.matmul(pf[5][0:64, 0:npc * 4], UT, g2[:, 0:npc * 4], start=True, stop=True),
               reads=["tp_g", "cf"], writes=[("ps", 5)])
            op("pe", lambda e: e.matmul(pf[6][:, 0:npc * 4], ones_f, g2[:, 0:npc * 4], start=True, stop=True),
               reads=["tp_g", "cf"], writes=[("ps", 6)])
            if has_s:
                op("pe", lambda e: e.matmul(pf[5][0:32, 64:68], cf[0:32, 192:224], g2[0:32, 64:68], start=True, stop=True),
                   reads=["tp_g", "cf"], writes=[("ps", 5)])
                op("pe", lambda e: e.matmul(pf[6][:, 64:68], cf[0:32, 256:384], g2[0:32, 64:68], start=True, stop=True),
                   reads=["tp_g", "cf"], writes=[("ps", 6)])
            n4 = nch * 4
            f2 = lambda t: t[:].rearrange("p c h -> p (c h)")[:, 0:n4]
            op("act", lambda e: e.copy(f2(tp_gc), pf[5][0:64, 0:n4]), reads=[("ps", 5)], writes=["tp_gc"], small=True)
            op("act", lambda e: e.activation(f2(tp_egc), pf[5][0:64, 0:n4], AF.Exp), reads=[("ps", 5)], writes=["tp_egc"], small=True)
            op("dve", lambda e: e.tensor_tensor(f2(tp_ekr), pf[6][0:64, 0:n4], f2(tp_gc), ALU.subtract),
               reads=[("ps", 6), "tp_gc"], writes=["tp_ekr"], small=True)
            op("act", lambda e: e.activation(f2(tp_ekr), f2(tp_ekr), AF.Exp), reads=["tp_ekr"], writes=["tp_ekr"], small=True)
            op("act", lambda e: e.activation(tp_egl[:].rearrange("p c h -> p (c h)")[:, 0:n4], pf[6][:, 0:n4], AF.Exp),
               reads=[("ps", 6)], writes=["tp_egl"], small=True)

        def bc_h(ap2, C, n):
            return ap2.unsqueeze(2).broadcast_to([C, 4, n])

        def bc_m(ap2, C):
            return ap2.unsqueeze(1).broadcast_to([C, 4, C])

        def dn_chunk(ci, stq, lc0, C):
            fs = (C < 64)
            pre, stp = [], []
            curl = [pre]

            def op(*a, **k):
                k["small"] = k.get("small", False) or fs
                curl[0].append((a, k))
            par = ci % 2
            inT, kr, u_sb, w0T = inT2[par], kr2[par], u_sb2[par], w0T2[par]
            n_inT, n_kr, n_u, n_w0 = "inT%d" % par, "kr%d" % par, ("u_sb" if par == 0 else "cact"), "w0T%d" % par
            cs = slice(lc0, lc0 + C)
            qT = lambda h: U[:, h, cs]
            kT = lambda h: U[:, 4 + h, cs]
            vT = lambda h: U[:, 8 + h, cs]
            rq = [r for h in range(4) for r in RU(h, lc0, C)]
            rk = [r for h in range(4) for r in RU(4 + h, lc0, C)]
            rv = [r for h in range(4) for r in RU(8 + h, lc0, C)]
            v3 = lambda t, n=None: (t[0:C, :, 0:(n or C)])
            op("dve", lambda e: e.tensor_tensor(v3(Gm), bc_h(tp_g[0:C, ci, :], C, C), bc_m(cf[0:C, 192:192 + C], C), ALU.mult),
               reads=["tp_g", "cf"], writes=["Gm"])
            op("pe", lambda e: e.matmul(pf[5][0:C, 0:4 * C].rearrange("p (h i) -> p h i", h=4), cf[0:C, 128:128 + C], v3(Gm),
                                        start=True, stop=True), reads=["Gm", "cf"], writes=[("ps", 5)])
            op("act", lambda e: e.activation(v3(DTu), pf[5][0:C, 0:4 * C].rearrange("p (h i) -> p h i", h=4), AF.Exp),
               reads=[("ps", 5)], writes=["DTu"])
            op("dve", lambda e: e.tensor_tensor(v3(DTs), v3(DTu), bc_m(cf[0:C, 192:192 + C], C), ALU.mult),
               reads=["DTu", "cf"], writes=["DTs"])
            for h in range(4):
                op("pe", lambda e, h=h: e.matmul(pf[6][0:C, h * C:(h + 1) * C], kT(h), kT(h), start=True, stop=True),
                   reads=RU(4 + h, lc0, C), writes=[("ps", 6)])
            for h in range(4):
                op("pe", lambda e, h=h: e.matmul(pf[5][0:C, h * C:(h + 1) * C], kT(h), qT(h), start=True, stop=True),
                   reads=RU(4 + h, lc0, C) + RU(h, lc0, C) + ["DTu"], writes=[("ps", 5)])
            for h in range(4):
                op("pe", lambda e, h=h: e.transpose(pb[0:C, 0, h * 128:(h + 1) * 128], kT(h), ident_b),
                   reads=RU(4 + h, lc0, C) + ["cfb"], writes=[("ps", 7)])
            pb3 = lambda i: pb[0:C, i, :].rearrange("p (h d) -> p h d", h=4)
            op("dve", lambda e: e.tensor_tensor(kg[0:C], pb3(0), bc_h(tp_egc[0:C, ci, :], C, 128), ALU.mult),
               reads=[("ps", 7), "tp_egc"], writes=["kg"])
            op("dve", lambda e: e.tensor_tensor(kr[0:C], pb3(0), bc_h(tp_ekr[0:C, ci, :], C, 128), ALU.mult),
               reads=[("ps", 7), "tp_ekr"], writes=[n_kr])
            for h in range(4):
                op("pe", lambda e, h=h: e.transpose(pb[0:C, 0, h * 128:(h + 1) * 128], vT(h), ident_b),
                   reads=RU(8 + h, lc0, C) + ["cfb"], writes=[("ps", 7)])
            op("act", lambda e: e.copy(vtm[0:C], pb3(0)), reads=[("ps", 7)], writes=["vtm"])
            ps3 = lambda b: pf[b][0:C, 0:4 * C].rearrange("p (h i) -> p h i", h=4)
            op("dve", lambda e: e.tensor_tensor(v3(inT), ps3(5), v3(DTs), ALU.mult), reads=[("ps", 5), "DTs"], writes=[n_inT])
            op("dve", lambda e: e.tensor_tensor(v3(DTu), v3(DTs), bc_m(cf[0:C, 0:C], C), ALU.subtract),
               reads=["DTs", "cf"], writes=["DTu"])
            op("dve", lambda e: e.tensor_tensor(v3(tA), ps3(6), v3(DTu), ALU.mult), reads=[("ps", 6), "DTu"], writes=["Gm"])
            op("dve", lambda e: e.tensor_tensor(v3(Pk[0]), v3(tA), bc_h(tp_negb[0:C, ci, :], C, C), ALU.mult),
               reads=["Gm", "tp_negb"], writes=["Pk"])
            for h in range(4):
                op("pe", lambda e, h=h: e.transpose(pf[6][0:C, h * C:(h + 1) * C], Pk[0][0:C, h, 0:C], cf[0:C, 0:C]),
                   reads=["Pk", "cf"], writes=[("ps", 6)])
            op("act", lambda e: e.copy(v3(PTk[0]), ps3(6)), reads=[("ps", 6)], writes=["PTk"])
            op("dve", lambda e: e.tensor_tensor(v3(Rk[0]), v3(Pk[0]), bc_m(cf[0:C, 0:C], C), ALU.add),
               reads=["Pk", "cf"], writes=["Rk"])
            nlev = 5 if C == 64 else 4
            cur = 0
            for lv in range(nlev):
                nx = 1 - cur
                last = (lv == nlev - 1)
                if not last:
                    for h in range(4):
                        op("pe", lambda e, h=h, cur=cur: e.matmul(pf[5][0:C, h * C:(h + 1) * C], PTk[cur][0:C, h, 0:C], Pk[cur][0:C, h, 0:C],
                                                                  start=True, stop=True),
                           reads=["Pk", "PTk"], writes=[("ps", 5)])
                for h in range(4):
                    op("pe", lambda e, h=h, cur=cur: e.matmul(pf[6][0:C, h * C:(h + 1) * C], Pk[cur][0:C, h, 0:C], PTk[cur][0:C, h, 0:C],
                                                              start=True, stop=True),
                       reads=["Pk", "PTk"], writes=[("ps", 6)])
                if not last:
                    op("act", lambda e, nx=nx: e.copy(v3(Pk[nx]), ps3(5)), reads=[("ps", 5)], writes=["Pk"])
                op("dve", lambda e, nx=nx: e.tensor_copy(v3(PTk[nx]), ps3(6)), reads=[("ps", 6)], writes=["PTk"])
                for h in range(4):
                    op("pe", lambda e, h=h, cur=cur, nx=nx: e.matmul(pf[5][0:C, h * C:(h + 1) * C], PTk[nx][0:C, h, 0:C], Rk[cur][0:C, h, 0:C],
                                                                     start=True, stop=True),
                       reads=["PTk", "Rk"], writes=[("ps", 5)])
                if not last:
                    op("dve", lambda e, cur=cur, nx=nx: e.tensor_tensor(v3(Rk[nx]), ps3(5), v3(Rk[cur]), ALU.add),
                       reads=[("ps", 5), "Rk"], writes=["Rk"])
                else:
                    op("dve", lambda e, cur=cur: e.tensor_tensor(v3(Rbf), ps3(5), v3(Rk[cur]), ALU.add),
                       reads=[("ps", 5), "Rk"], writes=["Rbf"])
                cur = nx
            for h in range(4):
                op("pe", lambda e, h=h: e.matmul(pf[6][0:C, h * 128:(h + 1) * 128], Rbf[0:C, h, 0:C], vtm[0:C, h, :], start=True, stop=True),
                   reads=["Rbf", "vtm"], writes=[("ps", 6)])
            for h in range(4):
                op("pe", lambda e, h=h: e.matmul(pf[5][:, h * C:(h + 1) * C], kg[0:C, h, :], Rbf[0:C, h, 0:C], start=True, stop=True),
                   reads=["Rbf", "kg"], writes=[("ps", 5)])
            op("dve", lambda e: e.tensor_tensor(u_sb[0:C], pf[6][0:C, :].rearrange("p (h d) -> p h d", h=4),
                                                bc_h(tp_beta[0:C, ci, :], C, 128), ALU.mult),
               reads=[("ps", 6), "tp_beta"], writes=[n_u])
            op("act", lambda e: e.copy(w0T[:, :, 0:C], pf[5][:, 0:4 * C].rearrange("p (h i) -> p h i", h=4)),
               reads=[("ps", 5)], writes=[n_w0])
            curl[0] = stp
            Sres = "S_b%d" % stq
            Sfres = "S_f%d" % stq
            for h in range(4):
                op("pe", lambda e, h=h: e.matmul(pf[1][0:C, h * 128:(h + 1) * 128], w0T[:, h, 0:C], S_b[:, stq, h, :], start=True, stop=True),
                   reads=[n_w0, Sres], writes=[("ps", 1)])
            p64 = lambda b: pf[b][0:C, :].rearrange("p (h d) -> p h d", h=4)
            op("dve", lambda e: e.tensor_tensor(t1[0:C], p64(1), bc_h(tp_negb[0:C, ci, :], C, 128), ALU.mult),
               reads=[("ps", 1), "tp_negb"], writes=["cacc"])
            op("dve", lambda e: e.tensor_tensor(vnew[0:C], t1[0:C], u_sb[0:C], ALU.add), reads=["cacc", n_u], writes=["vnew"])
            for h in range(4):
                op("pe", lambda e, h=h: e.matmul(pf[0][0:C, h * 128:(h + 1) * 128], qT(h), S_b[:, stq, h, :], start=True, stop=True),
                   reads=RU(h, lc0, C) + [Sres], writes=[("ps", 0)])
            for h in range(4):
                op("pe", lambda e, h=h: e.matmul(pf[1][0:C, h * 128:(h + 1) * 128], inT[0:C, h, 0:C], vnew[0:C, h, :], start=True, stop=True),
                   reads=[n_inT, "vnew"], writes=[("ps", 1)])
            op("dve", lambda e: e.tensor_tensor(t1[0:C], p64(0), bc_h(tp_egc[0:C, ci, :], C, 128), ALU.mult),
               reads=[("ps", 0), "tp_egc"], writes=["cacc"])
            op("dve", lambda e: e.tensor_tensor(o_sb[0:C], t1[0:C], p64(1), ALU.add), reads=["cacc", ("ps", 1)], writes=["rstd"])
            for h in range(4):
                op("pe", lambda e, h=h: e.matmul(pf[0][:, h * 128:(h + 1) * 128], kr[0:C, h, :], vnew[0:C, h, :], start=True, stop=True),
                   reads=[n_kr, "vnew"], writes=[("ps", 0)])
            op("dve", lambda e: e.tensor_tensor(tS[:], S_f[:, stq], tp_egl[:, ci, :].unsqueeze(2).broadcast_to([128, 4, 128]), ALU.mult),
               reads=[Sfres, "tp_egl"], writes=["tS"])
            op("dve", lambda e: e.tensor_tensor(S_f[:, stq], tS[:], pf[0][:, :].rearrange("p (h d) -> p h d", h=4), ALU.add),
               reads=["tS", ("ps", 0)], writes=[Sfres])
            op("act", lambda e: e.copy(S_b[:, stq], S_f[:, stq]), reads=[Sfres], writes=[Sres])
            op("dve", lambda e: e.tensor_tensor(t1[0:C], o_sb[0:C], o_sb[0:C], ALU.mult), reads=["rstd"], writes=["cacc"])
            op("dve", lambda e: e.tensor_reduce(ss4[0:C, 0:4], t1[0:C], AX.X, ALU.add), reads=["cacc"], writes=["ss4"], small=True)
            op("act", lambda e: e.activation(ss4[0:C, 0:4], ss4[0:C, 0:4], AF.Ln, bias=RMS_EPS, scale=1.0 / 128), reads=["ss4"], writes=["ss4"], small=True)
            op("act", lambda e: e.activation(ss4[0:C, 0:4], ss4[0:C, 0:4], AF.Exp, scale=-0.5), reads=["ss4"], writes=["ss4"], small=True)
            op("dve", lambda e: e.tensor_tensor(t1[0:C], o_sb[0:C], bc_h(ss4[0:C, 0:4], C, 128), ALU.mult),
               reads=["rstd", "ss4"], writes=["cacc"])
            op("dve", lambda e: e.tensor_tensor(on2[0:C], t1[0:C], prm[0:C, P_DNW:P_DNW + 128].unsqueeze(1).broadcast_to([C, 4, 128]), ALU.mult),
               reads=["cacc", "prm"], writes=["sqb"])
            for h in range(4):
                op("pe", lambda e, h=h: e.transpose(pb[:, 1, h * C:(h + 1) * C], on2[0:C, h, :], cfb[0:C, 0:C]),
                   reads=["sqb", "cfb"], writes=[("ps", 7)])
            gview = U[:, 12:16, cs]
            rg = [r for h in range(4) for r in RU(12 + h, lc0, C)]
            op("dve", lambda e: e.tensor_tensor(gview, pb[:, 1, 0:4 * C].rearrange("p (h i) -> p h i", h=4), gview, ALU.mult),
               reads=[("ps", 7)] + rg, writes=rg)

            return pre, stp

        def swa_chunk(ci, stq, lc0, C, g):
            fs = (C < 64)
            lst = []

            def op(*a, **k):
                k["small"] = k.get("small", False) or fs
                lst.append((a, k))
            cs = slice(lc0, lc0 + C)
            if stq == 0:
                keys = []
                for dl in (2, 1, 0):
                    kc = ci - dl
                    if kc < 0:
                        w0 = (kc + 2) * 64
                        keys.append((dl, lambda kv, hf, w0=w0: kwin[hf * 64:(hf + 1) * 64, kv, w0:w0 + 64], ["kwin"],
                                     vbuf[:, kc + 2, :], [("vbuf", kc + 2)], 64, g == 0))
                    else:
                        keys.append((dl, lambda kv, hf, kc=kc: U[hf * 64:(hf + 1) * 64, 20 + kv, kc * 64:kc * 64 + 64],
                                     RU(20, kc * 64, 64) + RU(21, kc * 64, 64),
                                     vbuf[:, kc + 2, :], [("vbuf", kc + 2)], 64, False))
            else:
                keys = [(2, lambda kv, hf: ksc[hf * 64:(hf + 1) * 64, kv, 0:64], ["ksc"], vsb[:, 0, :], [("vsb", 0)], 64, False),
                        (1, lambda kv, hf: ksc[hf * 64:(hf + 1) * 64, kv, 64:128], ["ksc"], vsb[:, 1, :], [("vsb", 1)], 64, False),
                        (0, lambda kv, hf: U[hf * 64:(hf + 1) * 64, 20 + kv, cs], RU(20, lc0, C) + RU(21, lc0, C),
                         vsb[:, 2, :], [("vsb", 2)], 32, False)]
            rq = [r for j in range(4) for r in RU(16 + j, lc0, C)]
            for idx, (dl, kfn, kres, vap, vres, SK, masked) in enumerate(keys):
                for hh in range(8):
                    kv, j, hf = hh // 4, hh // 2, hh % 2
                    sbk = 2 if hf == 0 else 3
                    op("pe", lambda e, hh=hh, kv=kv, j=j, hf=hf, kfn=kfn, SK=SK, sbk=sbk: e.matmul(
                        pf[sbk][0:SK, j * 64:j * 64 + C], kfn(kv, hf), U[hf * 64:(hf + 1) * 64, 16 + j, cs], start=True, stop=True),
                       reads=kres + rq, writes=[("ps", sbk)])
                import os
                SWL = int(os.environ.get("SWL", "9"))
                if SWL < 2:
                    continue
                for hf, sbk in ((0, 2), (1, 3)):
                    sc3 = pf[sbk][0:SK, 0:256].rearrange("p (h i) -> p h i", h=4)[:, :, 0:C]
                    op("dve", lambda e, sc3=sc3, dl=dl, SK=SK, hf=hf: e.tensor_tensor(
                        sct[0:SK, hf * 4:hf * 4 + 4, 0:C], sc3, BT[0:SK, dl, hf * 4:hf * 4 + 4, 0:C], ALU.add),
                       reads=[("ps", sbk), "cf"], writes=["stage"])
                PT = PTs[idx]
                pn = PTn[idx]
                op("act", lambda e, SK=SK, PT=PT: e.activation(PT[0:SK, :, 0:C], sct[0:SK, :, 0:C], AF.Exp), reads=["stage"], writes=[pn])
                if masked:
                    op("dve", lambda e, SK=SK, PT=PT: e.tensor_scalar(PT[0:SK, :, 0:C], PT[0:SK, :, 0:C], mcore[0:SK, 0:1], None, ALU.mult),
                       reads=[pn, "mcore"], writes=[pn])
            for half in ((0, 1) if SWL >= 3 else []):
                bk = 4
                for hh in range(half * 4, half * 4 + 4):
                    kv = hh // 4
                    hp = (hh % 2) * 4 + hh // 2
                    for idx, (dl, kfn, kres, vap, vres, SK, masked) in enumerate(keys):
                        op("pe", lambda e, hh=hh, kv=kv, bk=bk, vap=vap, SK=SK, idx=idx, hp=hp: e.matmul(
                            pf[bk][0:C, (hh % 4) * 66:(hh % 4) * 66 + 66], PTs[idx][0:SK, hp, 0:C], vap[0:SK, kv * 66:kv * 66 + 66],
                            start=(idx == 0), stop=(idx == 2)),
                           reads=[PTn[idx]] + vres, writes=[("ps", bk)])
                o3 = pf[bk][0:C, 0:264].rearrange("p (h d) -> p h d", h=4)
                op("dve", lambda e, o3=o3, half=half: e.tensor_tensor(den[0:C, half * 4:half * 4 + 4], o3[:, :, 64],
                                                                      prm2[0:C, 5 + half * 4:9 + half * 4], ALU.add),
                   reads=[("ps", bk), "prm2"], writes=["den"], small=True)
                op("dve", lambda e, half=half: e.reciprocal(den[0:C, half * 4:half * 4 + 4], den[0:C, half * 4:half * 4 + 4]),
                   reads=["den"], writes=["den"], small=True)
                op("dve", lambda e, o3=o3, half=half: e.tensor_tensor(osw[0:C, half * 4:half * 4 + 4, :], o3[:, :, 0:64],
                                                                      den[0:C, half * 4:half * 4 + 4].unsqueeze(2).broadcast_to([C, 4, 64]), ALU.mult),
                   reads=[("ps", bk), "den"], writes=["osw"])
            if SWL < 5:
                return lst
            for j in range(4):
                op("pe", lambda e, j=j: e.transpose(pb[:, 1, 256 + j * C:256 + (j + 1) * C], osw[0:C, 2 * j:2 * j + 2, :].rearrange("p a d -> p (a d)"), cfb[0:C, 0:C]),
                   reads=["osw", "cfb"], writes=[("ps", 7)])
            op("act", lambda e: e.copy(U[:, 16:20, cs], pb[:, 1, 256:256 + 4 * C].rearrange("p (j i) -> p j i", j=4)),
               reads=[("ps", 7)], writes=rq)
            return lst

        def merge(s, tiles, g):
            import os
            MRG = os.environ.get("MRG", "both")
            for passi, (wo, ucb, gc0) in enumerate([(wodn, 12, C_GA), (woswa, 16, C_GB)]):
                if (MRG == "dn" and passi == 1):
                    continue
                for cb in range(4):
                    (vo, vgp), ro = load_panels([(wo[s, :, cb * 256:(cb + 1) * 256], 512, 256),
                                                 (win[s, :, gc0 + cb * 256:gc0 + (cb + 1) * 256], D, 256)])
                    rgp = ro
                    for jj in range(2):
                        c = cb * 2 + jj
                        for (c0, n) in T_(tiles):
                            lc0 = c0 - g * 1024
                            by = next_bank([0, 1])
                            bg = by + 2
                            for k in range(4):
                                op("pe", lambda e, k=k, jj=jj, lc0=lc0, n=n, by=by, vo=vo, ucb=ucb: e.matmul(
                                    pf[by][:, 0:n], vo[:, k, jj * 128:(jj + 1) * 128], U[:, ucb + k, lc0:lc0 + n],
                                    start=(k == 0), stop=(k == 3)),
                                   reads=RU(ucb + k, lc0, n) + [ro], writes=[("ps", by)])
                            for k in range(8):
                                op("pe", lambda e, k=k, jj=jj, lc0=lc0, n=n, bg=bg, vgp=vgp: e.matmul(
                                    pf[bg][:, 0:n], vgp[:, k, jj * 128:(jj + 1) * 128], hT[:, k, lc0:lc0 + n],
                                    start=(k == 0), stop=(k == 7)),
                                   reads=RH(lc0, n) + [rgp], writes=[("ps", bg)])
                            sl = by
                            op("act", lambda e, n=n, bg=bg, sl=sl: e.activation(sigb[sl][:, 0:n], pf[bg][:, 0:n], AF.Sigmoid),
                               reads=[("ps", bg)], writes=[SG[sl]])
                            if passi == 0:
                                op("dve", lambda e, n=n, by=by, sl=sl, c=c, lc0=lc0: e.tensor_tensor(
                                    U[:, c, lc0:lc0 + n], sigb[sl][:, 0:n], pf[by][:, 0:n], ALU.mult),
                                   reads=[SG[sl], ("ps", by)], writes=RU(c, lc0, n))
                            else:
                                op("dve", lambda e, n=n, by=by, sl=sl: e.tensor_tensor(
                                    sigb[sl][:, 0:n], sigb[sl][:, 0:n], pf[by][:, 0:n], ALU.mult),
                                   reads=[SG[sl], ("ps", by)], writes=[SG[sl]])
                                op("dve", lambda e, n=n, sl=sl, c=c, lc0=lc0: e.tensor_tensor(
                                    U[:, c, lc0:lc0 + n], sigb[sl][:, 0:n], U[:, c, lc0:lc0 + n], ALU.add),
                                   reads=[SG[sl]] + RU(c, lc0, n), writes=RU(c, lc0, n))
            for cb in range(2):
                vw, rw = load_panel(wout[s, :, cb * 512:(cb + 1) * 512], D, 512)
                for jj in range(4):
                    c = cb * 4 + jj
                    for (c0, n) in T_(tiles):
                        lc0 = c0 - g * 1024
                        b = next_bank([0, 1, 2, 3])
                        for k in range(8):
                            op("pe", lambda e, k=k, jj=jj, lc0=lc0, n=n, b=b, vw=vw: e.matmul(
                                pf[b][:, 0:n], vw[:, k, jj * 128:(jj + 1) * 128], U[:, k, lc0:lc0 + n],
                                start=(k == 0), stop=(k == 7)),
                               reads=RU(k, lc0, n) + [rw], writes=[("ps", b)])
                        op("dve", lambda e, c=c, c0=c0, n=n, b=b: e.tensor_tensor(
                            xT[:, c, c0:c0 + n], pf[b][:, 0:n], xT[:, c, c0:c0 + n], ALU.add),
                           reads=[("ps", b)] + RX(c0, n), writes=RX(c0, n))

        halo_all = lambda stq: [("halo", stq, j) for j in range(12)]
        for s in range(nslot):
            op("sp", lambda e, s=s: e.dma_start(out=prm[:], in_=prm_d[s]), writes=["prm"], dma="ld_p")
            op("act", lambda e: e.mul(prm2[:, 0:1], prm[:, P_QNW:P_QNW + 1], 0.125), reads=["prm"], writes=["prm2"], small=True)
            op("act", lambda e: e.activation(prm2[:, 1:5], prm[:, P_ALOG:P_ALOG + 4], AF.Exp), reads=["prm"], writes=["prm2"], small=True)
            op("act", lambda e: e.activation(prm2[:, 5:13], prm[:, P_SINK:P_SINK + 8], AF.Exp), reads=["prm"], writes=["prm2"], small=True)
            op("sp", lambda e, s=s: e.dma_start(out=S_f[:, 1].rearrange("p h d -> p (h d)"), in_=sdn_d[s]), writes=["S_f1"], dma="ld_s1")
            op("sp", lambda e, s=s: e.dma_start(out=halo[:, 1].rearrange("p j r -> p (j r)"), in_=sconv_d[s]), writes=halo_all(1), dma="ld_s2")
            op("sp", lambda e, s=s: e.dma_start(out=kscf[:], in_=skc_d[s]), writes=["cact"], dma="ld_s3")
            op("sp", lambda e, s=s: e.dma_start(out=vscf[:], in_=svc_d[s]), writes=["cact"], dma="ld_s4")
            op("dve", lambda e: e.tensor_copy(ksc[:].rearrange("p a b -> p (a b)"), kscf[:]), reads=["cact"], writes=["ksc"])
            op("dve", lambda e: e.tensor_copy(vsb[:, 0:2, :].rearrange("p c (k d) -> p c k d", k=2)[:, :, :, 0:64], vscf[:]),
               reads=["cact", "vsb_all"], writes=[("vsb", 0), ("vsb", 1)])
            op("act", lambda e: e.copy(S_b[:, 1], S_f[:, 1]), reads=["S_f1"], writes=["S_b1"])
            if s >= 1 and "handoff" in PH:
                op("sp", lambda e: e.dma_start(out=rtmp[:], in_=recv_f[0:128, :]), reads=["recv_f"], writes=["cacc"], dma="ld_r1")
                op("sp", lambda e: e.dma_start(out=rtmpb[:], in_=recv_b[0:128, :]), reads=["recv_b"], writes=["sqb"], dma="ld_r2")
                op("dve", lambda e: e.tensor_scalar(S_f[:, 0].rearrange("p h d -> p (h d)"), rtmp[:, 0:512], mcore[:, 0:1], None, ALU.mult),
                   reads=["cacc", "mcore"], writes=["S_f0"])
                op("dve", lambda e: e.tensor_scalar(halo[:, 0].rearrange("p j r -> p (j r)"), rtmp[:, 512:548], mcore[:, 0:1], None, ALU.mult),
                   reads=["cacc", "mcore"], writes=halo_all(0))
                op("dve", lambda e: e.tensor_scalar(kwin[:].rearrange("p a b -> p (a b)"), rtmpb[:, 0:256], mcore[:, 0:1], None, ALU.mult),
                   reads=["sqb", "mcore"], writes=["kwin"])
                op("dve", lambda e: e.tensor_scalar(vbuf[:, 0:2, :].rearrange("p a b -> p (a b)"), rtmpb[0:64, 256:520], mcore[0:64, 0:1], None, ALU.mult),
                   reads=["sqb", "mcore"], writes=[("vbuf", 0), ("vbuf", 1)])
            op("act", lambda e: e.copy(S_b[:, 0], S_f[:, 0]), reads=["S_f0"], writes=["S_b0"])

            for g in range(ngroup):
                tiles = [(g * 1024, 512), (g * 1024 + 512, 512)]
                if g == ngroup - 1:
                    tiles.append((NP_, NS_))
                if "ffn1" in PH:
                    rms_norm_to_hT(tiles, g, 0)
                    ffn(s, tiles, g, 0)
                rms_norm_to_hT(tiles, g, 1)
                if "proj" in PH:
                    proj_in(s, tiles, g)
                chunks = chunk_list(g)
                S.force_small = False
                if "dn" in PH:
                    dn_params(g)
                streams = []
                dnl = [dn_chunk(ci, stq, lc0, C) for ci, (stq, lc0, C) in enumerate(chunks)] if "dn" in PH else None
                swl = [swa_chunk(ci, stq, lc0, C, g) for ci, (stq, lc0, C) in enumerate(chunks)] if "swa" in PH else None

                def emit_merged(lists):
                    items = []
                    for li, L in enumerate(lists):
                        for k, it in enumerate(L):
                            items.append(((k + 0.5) / len(L), li, k, it))
                    items.sort(key=lambda t: (t[0], t[1], t[2]))
                    for _, _, _, (a, k) in items:
                        S.op(*a, **k)
                import os
                if os.environ.get("SEQ", "") == "2":
                    emit_merged([dnl[0][0]])
                    for ci in range(len(chunks)):
                        if ci + 1 < len(chunks):
                            emit_merged([dnl[ci + 1][0]])
                        emit_merged([dnl[ci][1]])
                    dnl = None
                if os.environ.get("SEQ", "0") == "3":
                    for ci in range(len(chunks)):
                        lists = []
                        if dnl is not None:
                            lists.append(dnl[ci][0] + dnl[ci][1])
                        if swl is not None:
                            lists.append(swl[ci])
                        emit_merged([L for L in lists if L])
                    dnl = None
                    swl = None
                if os.environ.get("SEQ", "") == "1":
                    for ci in range(len(chunks)):
                        if dnl is not None:
                            emit_merged([dnl[ci][0]])
                            emit_merged([dnl[ci][1]])
                        if swl is not None:
                            emit_merged([swl[ci]])
                    dnl = None
                    swl = None
                if dnl is not None:
                    emit_merged([dnl[0][0]])
                for ci in range(len(chunks)):
                    lists = []
                    if dnl is not None:
                        lists.append(dnl[ci][1])
                        if ci + 1 < len(chunks):
                            lists.append(dnl[ci + 1][0])
                    if swl is not None:
                        lists.append(swl[ci])
                    emit_merged([L for L in lists if L])
                op("dve", lambda e: e.tensor_copy(kwin[:], U[:, 20:22, 896:1024]),
                   reads=RU(20, 896, 128) + RU(21, 896, 128), writes=["kwin"])
                op("dve", lambda e: e.tensor_copy(vbuf[:, 0:2, :], vbuf[:, 16:18, :]),
                   reads=[("vbuf", 16), ("vbuf", 17)], writes=[("vbuf", 0), ("vbuf", 1)])
                if DUMPU and s == 0:
                    op("sp", lambda e, g=g: e.dma_start(out=dU[g], in_=U[:, 12:20, :]),
                       reads=[("U", j, b) for j in range(12, 20) for b in range(17)], dma="dbgU")
                if "merge" in PH:
                    merge(s, tiles, g)
                if "ffn2" in PH:
                    rms_norm_to_hT(tiles, g, 2)
                    ffn(s, tiles, g, 1)

            op("sp", lambda e, s=s: e.dma_start(out=o_dn[s, 0], in_=S_f[:, 0].rearrange("p h d -> p (h d)")), reads=["S_f0"], dma="o_S0")
            op("sp", lambda e, s=s: e.dma_start(out=o_dn[s, 1], in_=S_f[:, 1].rearrange("p h d -> p (h d)")), reads=["S_f1"], dma="o_S1")
            op("sp", lambda e, s=s: e.dma_start(out=o_conv[s, 0], in_=halo[:, 0].rearrange("p j r -> p (j r)")), reads=halo_all(0), dma="o_h0")
            op("sp", lambda e, s=s: e.dma_start(out=o_conv[s, 1], in_=halo[:, 1].rearrange("p j r -> p (j r)")), reads=halo_all(1), dma="o_h1")
            op("sp", lambda e, s=s: e.dma_start(out=o_kp[s].rearrange("p (a b) -> p a b", a=2), in_=kout[:, :, 0:128]), reads=["kout"], dma="o_k")
            op("sp", lambda e, s=s: e.dma_start(out=o_ks[s].rearrange("p (a b) -> p a b", a=2), in_=kout[:, :, 128:160]), reads=["kout"], dma="o_k")
            op("sp", lambda e, s=s: e.dma_start(out=o_vp[s].rearrange("p (a b) -> p a b", a=2), in_=vout[:, 0:2, :]), reads=["vout"], dma="o_v")
            op("sp", lambda e, s=s: e.dma_start(out=o_vs[s], in_=vout[0:32, 2, :]), reads=["vout"], dma="o_v")
            if s < nslot - 1 and "handoff" in PH:
                op("sp", lambda e: e.dma_start(out=send_f[:, 0:512], in_=S_f[:, 0].rearrange("p h d -> p (h d)")), reads=["S_f0", "recv_f"], writes=["send_f"], dma="snd_f")
                op("sp", lambda e: e.dma_start(out=send_f[:, 512:548], in_=halo[:, 0].rearrange("p j r -> p (j r)")), reads=halo_all(0), writes=["send_f"], dma="snd_f")
                op("sp", lambda e: e.dma_start(out=send_b[:, 0:256], in_=kwin[:].rearrange("p a b -> p (a b)")), reads=["kwin", "recv_b"], writes=["send_b"], dma="snd_b")
                op("sp", lambda e: e.dma_start(out=send_b[0:64, 256:520], in_=vbuf[:, 0:2, :].rearrange("p a b -> p (a b)")), reads=[("vbuf", 0), ("vbuf", 1)], writes=["send_b"], dma="snd_b")
                groups = [[0, 1], [2, 3], [4, 5], [6, 7]]
                op("pool", lambda e: e.collective_compute("AllGather", ALU.bypass, replica_groups=groups, ins=[send_f], outs=[recv_f]),
                   reads=["send_f"], writes=["recv_f"], dma="cc_f", amt=1)
                op("pool", lambda e: e.collective_compute("AllGather", ALU.bypass, replica_groups=groups, ins=[send_b], outs=[recv_b]),
                   reads=["send_b"], writes=["recv_b"], dma="cc_b", amt=1)
        op("sp", lambda e: e.dma_start(out=yT.rearrange("j p t -> p j t"), in_=xT[:]), reads=[("xT", b) for b in range(5)], dma="out")
        nops = S.emit(final_waits=(["dbgU"] if DUMPU else []) + ["out", "o_S0", "o_S1", "o_h0", "o_h1", "o_k", "o_v"])
    return nc, nops


def _consts():
    cf = np.zeros((128, 1920), np.float32)
    cf[:, 0:128] = np.eye(128, dtype=np.float32)
    p = np.arange(64)[:, None]
    j = np.arange(64)[None, :]
    cf[0:64, 128:192] = (p > j)
    cf[0:64, 192:256] = (p <= j)
    cf[0:64, 256:384] = 1.0
    slopes = 2.0 ** (-8.0 * np.arange(1, 9, dtype=np.float32) / 8)
    s_ = np.arange(64)[:, None].astype(np.float32)
    i_ = np.arange(64)[None, :].astype(np.float32)
    BT = np.zeros((64, 3, 8, 64), np.float32)
    for dl in range(3):
        dist = np.abs(i_ + 64 * dl - s_)
        for h in range(8):
            BT[:, dl, (h % 2) * 4 + h // 2, :] = -slopes[h] * dist
    cf[0:64, 384:1920] = BT.reshape(64, -1)
    return cf


def _slot_stack(arr, odd):
    z = np.zeros((1,) + arr.shape[1:], arr.dtype)
    return np.ascontiguousarray(np.concatenate([z, arr] if odd else [arr, z], axis=0))


def kernel(**inp):
    f = lambda k: np.asarray(inp[k], dtype=np.float32)
    x_prompt, x_sample = f("x_prompt"), f("x_sample")
    state_dn, state_conv = f("state_dn"), f("state_conv")
    cache_k, cache_v = f("cache_swa_k"), f("cache_swa_v")
    w_in = f("w_in")
    idx = np.concatenate([np.arange(0, 1536), np.arange(1536, 2048), np.arange(2056, 2568),
                          np.arange(2568, 2632), np.arange(2568, 2632), np.arange(2632, 2696), np.arange(2632, 2696),
                          np.arange(2696, 2824), np.arange(2048, 2056), np.arange(2824, 3848), np.arange(3848, 4872)])
    assert idx.size == WIN_COLS
    win_r = w_in[:, :, idx]
    prm = np.zeros((DEPTH, 128, NPRM), np.float32)
    for i, k in enumerate(["ffn1_norm", "mix_norm", "ffn2_norm"]):
        prm[:, :, P_NW + i * 8:P_NW + i * 8 + 8] = f(k).reshape(DEPTH, 8, 128).transpose(0, 2, 1)
    prm[:, :, P_CW:P_CW + 48] = f("conv_w").reshape(DEPTH, 4, 12, 128).transpose(0, 3, 2, 1).reshape(DEPTH, 128, 48)
    prm[:, :, P_DNW:P_DNW + 128] = f("dn_norm")[:, None, :]
    prm[:, :, P_QNW] = np.tile(f("q_norm"), (1, 2))
    prm[:, :, P_KNW] = np.tile(f("k_norm"), (1, 2))
    prm[:, :, P_ALOG:P_ALOG + 4] = f("a_log")[:, None, :]
    prm[:, :, P_DTB:P_DTB + 4] = f("dt_bias")[:, None, :]
    prm[:, :, P_SINK:P_SINK + 8] = f("sinks")[:, None, :]
    big = {"wg1": f("ffn1_wg"), "wu1": f("ffn1_wu"), "wd1": f("ffn1_wd"), "wg2": f("ffn2_wg"), "wu2": f("ffn2_wu"),
           "wd2": f("ffn2_wd"), "win": win_r, "wodn": f("w_o_dn"), "woswa": f("w_o_swa"), "wout": f("w_out"), "prm": prm}
    stacks = [{k: _slot_stack(v, odd) for k, v in big.items()} for odd in (False, True)]
    cf = _consts()
    in_maps = []
    for c in range(8):
        b, odd = c // 2, c % 2
        m = dict(stacks[odd])
        xt = np.concatenate([x_prompt[b, odd * NP_:(odd + 1) * NP_], x_sample[c]], axis=0)
        m["xT0"] = np.ascontiguousarray(xt.T.reshape(8, 128, NT))
        sdn = state_dn[:, c].transpose(0, 2, 1, 3).reshape(DEPTH, 128, 512)
        sconv = state_conv[:, c].reshape(DEPTH, 3, 12, 128).transpose(0, 3, 2, 1).reshape(DEPTH, 128, 36)
        ck = cache_k[:, c].transpose(0, 2, 3, 1)
        skc = np.concatenate([ck, ck], axis=2).transpose(0, 2, 1, 3).reshape(DEPTH, 128, 256)
        svc = cache_v[:, c].reshape(DEPTH, 2, 64, 2, 64).transpose(0, 2, 1, 3, 4)
        m["sdn"] = _slot_stack(sdn, odd)
        m["sconv"] = _slot_stack(sconv, odd)
        m["skc"] = _slot_stack(skc, odd)
        m["svc"] = _slot_stack(svc, odd)
        m["mcore"] = np.full((128, 1), float(odd), np.float32)
        m["cf"] = cf
        in_maps.append(m)
    nc, _ = build_nc()
    res = run_bass_kernel_spmd(nc, in_maps, core_ids=list(range(8)))
    R = res.results
    y_prompt = np.zeros((4, 4096, D), np.float32)
    y_sample = np.zeros((8, NS_, D), np.float32)
    dn_prompt = np.zeros((DEPTH, 4, 4, 128, 128), np.float32)
    dn_sample = np.zeros((DEPTH, 8, 4, 128, 128), np.float32)
    conv_prompt = np.zeros((DEPTH, 4, 3, 1536), np.float32)
    conv_sample = np.zeros((DEPTH, 8, 3, 1536), np.float32)
    kp = np.zeros((DEPTH, 4, 128, 2, 64), np.float32)
    vp = np.zeros((DEPTH, 4, 128, 2, 64), np.float32)
    ks = np.zeros((DEPTH, 8, NS_, 2, 64), np.float32)
    vs = np.zeros((DEPTH, 8, NS_, 2, 64), np.float32)
    for c in range(8):
        b, odd = c // 2, c % 2
        r = R[c]
        yt = r["yT"].reshape(D, NT).T
        y_prompt[b, odd * NP_:(odd + 1) * NP_] = yt[0:NP_]
        y_sample[c] = yt[NP_:]
        for l in range(DEPTH):
            s = l + odd
            dn_sample[l, c] = r["o_dn"][s, 1].reshape(128, 4, 128).transpose(1, 0, 2)
            conv_sample[l, c] = r["o_conv"][s, 1].reshape(128, 12, 3).transpose(2, 1, 0).reshape(3, 1536)
            ks[l, c] = r["o_ks"][s].reshape(128, 2, 32)[0:64].transpose(2, 1, 0)
            vs[l, c] = r["o_vs"][s].reshape(32, 2, 64)
            if odd:
                dn_prompt[l, b] = r["o_dn"][s, 0].reshape(128, 4, 128).transpose(1, 0, 2)
                conv_prompt[l, b] = r["o_conv"][s, 0].reshape(128, 12, 3).transpose(2, 1, 0).reshape(3, 1536)
                kp[l, b] = r["o_kp"][s].reshape(128, 2, 128)[0:64].transpose(2, 1, 0)
                vp[l, b] = r["o_vp"][s].reshape(64, 2, 2, 64).transpose(1, 0, 2, 3).reshape(128, 2, 64)
    return (y_prompt, y_sample, dn_prompt, dn_sample, conv_prompt, conv_sample, kp, vp, ks, vs)
```

```python
import contextlib
import numpy as np
import concourse.bass as bass
import concourse.mybir as mybir
from concourse.bass_utils import run_bass_kernel_spmd

F32 = mybir.dt.float32
BF16 = mybir.dt.bfloat16
ALU = mybir.AluOpType
AF = mybir.ActivationFunctionType
AX = mybir.AxisListType

D = 1024
DEPTH = 4
NSLOT = DEPTH + 1
NP_ = 2048
NS_ = 32
NT = NP_ + NS_
GT = 1056
DFF = 2816
WIN_COLS = 5000
C_QKV, C_GATE, C_QS, C_KD, C_VBA, C_GA, C_GB = 0, 1536, 2048, 2560, 2816, 2952, 3976
NPRM = 218
P_NW, P_CW, P_DNW, P_QNW, P_KNW, P_ALOG, P_DTB, P_SINK = 0, 24, 72, 200, 201, 202, 206, 210
RMS_EPS = 1e-6
L2_EPS = 1e-6
WSLOT = 5632


class Sched:
    ENGS = ("pe", "act", "dve", "pool", "sp")

    def __init__(self, nc):
        self.nc = nc
        self.ops = []
        self.lastw = {}
        self.readers = {}
        self.chan_tot = {}
        self.bank_rd = {}
        self.force_small = False

    def op(self, eng, fn, reads=(), writes=(), dma=None, amt=16, small=False):
        i = len(self.ops)
        deps = set()
        raw = set()
        for r in reads:
            w = self.lastw.get(r)
            if w is not None:
                deps.add(w)
                raw.add(w)
        for r in writes:
            w = self.lastw.get(r)
            if w is not None:
                deps.add(w)
            q = self.readers.get(r)
            if q:
                deps.update(q)
        for r in reads:
            self.readers.setdefault(r, []).append(i)
            if isinstance(r, tuple) and r[0] == "ps":
                br = self.bank_rd.setdefault(r[1], {})
                for e2, j in br.items():
                    if e2 != eng:
                        deps.add(j)
                br[eng] = i
        for r in writes:
            self.lastw[r] = i
            self.readers[r] = []
        o = dict(eng=eng, fn=fn, deps=deps, dma=dma, inc=False, cnt=None, amt=amt, small=(small or self.force_small), raw=raw)
        if dma is not None:
            self.chan_tot[dma] = self.chan_tot.get(dma, 0) + amt
            o["cnt"] = self.chan_tot[dma]
            o["inc"] = True
        self.ops.append(o)
        return i

    def emit(self, final_waits=()):
        nc = self.nc
        ops = self.ops
        waited = {e: {} for e in self.ENGS}
        for i, o in enumerate(ops):
            e = o["eng"]
            ws = []
            for d in sorted(o["deps"]):
                p = ops[d]
                if p["dma"] is not None:
                    key = ("ch", p["dma"])
                    if waited[e].get(key, -1) >= p["cnt"]:
                        continue
                    waited[e][key] = p["cnt"]
                    ws.append(d)
                else:
                    if p["eng"] == e and o["dma"] is None and not (p["small"] and d in o["raw"]):
                        continue
                    key = ("en", p["eng"])
                    if waited[e].get(key, -1) >= d:
                        continue
                    waited[e][key] = d
                    p["inc"] = True
                    ws.append(d)
            o["waits"] = ws
        cnt = {e: 0 for e in self.ENGS}
        for o in ops:
            if o["dma"] is None and o["inc"]:
                cnt[o["eng"]] += 1
                o["cnt"] = cnt[o["eng"]]
        chans = sorted(self.chan_tot)
        with contextlib.ExitStack() as st:
            esem = {e: st.enter_context(nc.semaphore("se_" + e)) for e in self.ENGS}
            csem = {c: st.enter_context(nc.semaphore("sc_" + c)) for c in chans}
            block = st.enter_context(nc.Block())
            handles = {"pe": "tensor", "act": "scalar", "dve": "vector", "pool": "gpsimd", "sp": "sync"}

            def make(e):
                def body(eng):
                    for o in ops:
                        if o["eng"] != e:
                            continue
                        best = {}
                        for d in o["waits"]:
                            p = ops[d]
                            key = ("ch", p["dma"]) if p["dma"] is not None else ("en", p["eng"])
                            if best.get(key, -1) < p["cnt"]:
                                best[key] = p["cnt"]
                        for key, v in best.items():
                            sem = csem[key[1]] if key[0] == "ch" else esem[key[1]]
                            eng.wait_ge(sem, v)
                        ins = o["fn"](eng)
                        if o["dma"] is not None:
                            ins.then_inc(csem[o["dma"]], o["amt"])
                        elif o["inc"]:
                            ins.then_inc(esem[e], 1)
                    if e == "sp":
                        for c in final_waits:
                            eng.wait_ge(csem[c], self.chan_tot[c])
                return body

            for e in self.ENGS:
                getattr(block, handles[e])(make(e))
        return len(ops)


def build_nc(nslot=NSLOT, ngroup=2, phases=None):
    PH = phases or {"ffn1", "proj", "dn", "swa", "merge", "ffn2", "handoff"}
    if "proj" in PH and not (PH & {"p_qkv", "p_gate", "p_qk", "p_vba"}):
        PH = PH | {"p_qkv", "p_gate", "p_qk", "p_vba"}
    NSLOT = nslot
    nc = bass.Bass("TRN2", target_bir_lowering=False)

    def din(name, shape, dt=F32):
        return nc.dram_tensor(name, list(shape), dt, kind="ExternalInput").ap()

    def dout(name, shape, dt=F32):
        return nc.dram_tensor(name, list(shape), dt, kind="ExternalOutput").ap()

    xT0 = din("xT0", [8, 128, NT])
    wg = [din("wg1", [NSLOT, D, DFF]), din("wg2", [NSLOT, D, DFF])]
    wu = [din("wu1", [NSLOT, D, DFF]), din("wu2", [NSLOT, D, DFF])]
    wd = [din("wd1", [NSLOT, DFF, D]), din("wd2", [NSLOT, DFF, D])]
    win = din("win", [NSLOT, D, WIN_COLS])
    wodn = din("wodn", [NSLOT, 512, D])
    woswa = din("woswa", [NSLOT, 512, D])
    wout = din("wout", [NSLOT, D, D])
    prm_d = din("prm", [NSLOT, 128, NPRM])
    sdn_d = din("sdn", [NSLOT, 128, 512])
    sconv_d = din("sconv", [NSLOT, 128, 36])
    skc_d = din("skc", [NSLOT, 128, 256])
    svc_d = din("svc", [NSLOT, 64, 2, 2, 64])
    mcore_d = din("mcore", [128, 1])
    cf_d = din("cf", [128, 128 + 64 + 64 + 128 + 1536])
    yT = dout("yT", [8, 128, NT])
    o_dn = dout("o_dn", [NSLOT, 2, 128, 512])
    o_conv = dout("o_conv", [NSLOT, 2, 128, 36])
    o_kp = dout("o_kp", [NSLOT, 128, 256])
    o_ks = dout("o_ks", [NSLOT, 128, 64])
    o_vp = dout("o_vp", [NSLOT, 64, 256])
    o_vs = dout("o_vs", [NSLOT, 32, 128])
    import os
    DUMPU = os.environ.get("DUMPU", "") == "1"
    if DUMPU:
        dU = dout("dU", [2, 128, 8, GT], BF16)
    send_f = nc.dram_tensor("send_f", [128, 548], F32).ap()
    recv_f = nc.dram_tensor("recv_f", [256, 548], F32).ap()
    send_b = nc.dram_tensor("send_b", [128, 520], BF16).ap()
    recv_b = nc.dram_tensor("recv_b", [256, 520], BF16).ap()

    S = Sched(nc)
    with contextlib.ExitStack() as st:
        def sb(name, shape, dt=F32):
            return st.enter_context(nc.sbuf_tensor("s_" + name, list(shape), dt))

        def psum(name, shape, dt=F32):
            return st.enter_context(nc.psum_tensor(name, list(shape), dt))

        xT = sb("xT", [128, 8, NT])
        hT = sb("hT", [128, 8, GT], BF16)
        U = sb("U", [128, 22, GT], BF16)
        wring = sb("wring", [128, 2, WSLOT], BF16)
        kwin = sb("kwin", [128, 2, 128], BF16)
        ksc = sb("ksc", [128, 2, 128], BF16)
        vbuf = sb("vbuf", [64, 18, 132], BF16)
        vsb = sb("vsb", [64, 3, 132], BF16)
        S_f = sb("S_f", [128, 2, 4, 128])
        S_b = sb("S_b", [128, 2, 4, 128], BF16)
        halo = sb("halo", [128, 2, 12, 3])
        stage = sb("stage", [128, 520])
        cacc = sb("cacc", [128, 548])
        cact = sb("cact", [128, 512])
        rstd = sb("rstd", [128, 512])
        sqb = sb("sqb", [128, 520], BF16)
        sigb = [cacc, cact]
        SG = ["cacc", "cact"]
        kscf = cact[:, 0:256]
        vscf = cact[0:64, 256:512].rearrange("p (a b c) -> p a b c", a=2, b=2)
        cf = sb("cf", [128, 1920])
        cfb = sb("cfb", [128, 384], BF16)
        prm = sb("prm", [128, NPRM])
        prm2 = sb("prm2", [128, 16])
        mcore = sb("mcore", [128, 1])
        kout = sb("kout", [128, 2, 160])
        vout = sb("vout", [64, 3, 128])
        rtmp = cacc
        rtmpb = sqb
        zba = sb("zba", [64, 17, 8])
        tp_beta = sb("tp_beta", [64, 17, 4])
        tp_negb = sb("tp_negb", [64, 17, 4])
        tp_g = sb("tp_g", [64, 17, 4])
        tp_gc = sb("tp_gc", [64, 17, 4])
        tp_egc = sb("tp_egc", [64, 17, 4])
        tp_ekr = sb("tp_ekr", [64, 17, 4])
        tp_egl = sb("tp_egl", [128, 17, 4])
        tp_t = sb("tp_t", [64, 17, 4])
        Gm = sb("Gm", [64, 4, 64])
        dnA = sb("dnA", [64, 4, 128])
        DTs = dnA[:, :, 0:64]
        DTu = dnA[:, :, 64:128]
        tA = Gm
        Pk0 = sb("Pk0", [64, 4, 64], BF16)
        PTk0 = sb("PTk0", [64, 4, 64], BF16)
        Rkb = sb("Rkb", [64, 4, 64], BF16)
        Pk = [Pk0, Pk0]
        PTk = [PTk0, PTk0]
        Rk0 = sb("Rk0", [64, 4, 64])
        Rk = [Rk0, Rk0]
        Rbf = sb("Rbf", [64, 4, 64], BF16)
        inT2 = [sb("inT0", [64, 4, 64], BF16), sb("inT1", [64, 4, 64], BF16)]
        kg = sb("kg", [64, 4, 128], BF16)
        kr2 = [sb("kr0", [64, 4, 128], BF16), sb("kr1", [64, 4, 128], BF16)]
        vtm = sb("vtm", [64, 4, 128], BF16)
        u_sb2 = [sb("u_sb", [64, 4, 128]), cact[0:64, 0:512].rearrange("p (h d) -> p h d", h=4)]
        w0T2 = [sb("w0T0", [128, 4, 64], BF16), sb("w0T1", [128, 4, 64], BF16)]
        vnew = sb("vnew", [64, 4, 128], BF16)
        o_sb = rstd[0:64, 0:512].rearrange("p (h d) -> p h d", h=4)
        on2 = sqb[0:64, 0:512].rearrange("p (h d) -> p h d", h=4)
        ss4 = sb("ss4", [64, 8])
        tS = sb("tS", [128, 4, 128])
        sct = stage[0:64, 0:512].rearrange("p (h d) -> p h d", h=8)
        t1 = cacc[0:64, 0:512].rearrange("p (h d) -> p h d", h=4)
        PTs = [sb("PT0", [64, 8, 64], BF16), sb("PT1", [64, 8, 64], BF16), sb("PT2", [64, 8, 64], BF16)]
        PTn = ["PT0", "PT1", "PT2"]
        den = sb("den", [64, 8])
        osw = sb("osw", [64, 8, 64], BF16)

        pf = [psum("pf%d" % i, [128, 512]) for i in range(7)]
        pb = psum("pb", [128, 2, 512], BF16)

        ident_f = cf[:, 0:128]
        SL = cf[0:64, 128:192]
        UT = cf[0:64, 192:256]
        ones_f = cf[0:64, 256:384]
        BT = cf[0:64, 384:1920].rearrange("p (a h i) -> p a h i", a=3, h=8)
        ident_b = cfb[:, 0:128]
        ones_b = cfb[:, 128:256]
        blk_b = cfb[:, 256:384]

        op = S.op

        def T_(tiles):
            for (c0, n) in tiles:
                S.force_small = (n < 128)
                yield (c0, n)
            S.force_small = False

        def RU(j, lc0, n):
            return [("U", j, b) for b in range(lc0 // 64, (lc0 + n - 1) // 64 + 1)]

        def RH(lc0, n):
            return [("hT", b) for b in range(lc0 // 512, (lc0 + n - 1) // 512 + 1)]

        def RX(c0, n):
            return [("xT", b) for b in range(c0 // 512, (c0 + n - 1) // 512 + 1)]

        bank_rr = {"i": 0}

        op("sp", lambda e: e.dma_start(out=xT[:], in_=xT0.rearrange("j p t -> p j t")),
           writes=[("xT", b) for b in range(5)], dma="ld_x")
        op("sp", lambda e: e.dma_start(out=cf[:], in_=cf_d), writes=["cf"], dma="ld_cf")
        op("sp", lambda e: e.dma_start(out=mcore[:], in_=mcore_d), writes=["mcore"], dma="ld_mc")
        op("dve", lambda e: e.tensor_copy(cfb[:, 0:128], cf[:, 0:128]), reads=["cf"], writes=["cfb"])
        op("dve", lambda e: e.memset(cfb[:, 128:256], 1.0), writes=["cfb"])
        op("dve", lambda e: e.memset(cfb[:, 256:384], 0.0), writes=["cfb"])
        op("dve", lambda e: e.memset(cfb[0:64, 256:320], 1.0), writes=["cfb"])
        op("dve", lambda e: e.memset(cfb[64:128, 320:384], 1.0), writes=["cfb"])
        op("dve", lambda e: e.memset(vbuf[:], 1.0), writes=["vbuf_all"])
        op("dve", lambda e: e.memset(vsb[:], 1.0), writes=["vsb_all"])
        op("dve", lambda e: e.memset(zba[:], 0.0), writes=["zba"])
        op("dve", lambda e: e.memset(S_f[:, 0], 0.0), writes=["S_f0"])
        op("dve", lambda e: e.memset(halo[:, 0], 0.0), writes=[("halo", 0, j) for j in range(12)])
        op("dve", lambda e: e.memset(kwin[:], 0.0), writes=["kwin"])
        op("dve", lambda e: e.memset(vbuf[:, 0:2, 0:64], 0.0), reads=["vbuf_all"], writes=[("vbuf", 0), ("vbuf", 1)])
        op("dve", lambda e: e.memset(vbuf[:, 0:2, 66:130], 0.0), writes=[("vbuf", 0), ("vbuf", 1)])

        wstate = {"n": 0}

        def load_panel(src2d, K, ncols):
            KC = K // 128
            s = wstate["n"] % 2
            wstate["n"] += 1
            view = wring[:, s, 0:KC * ncols].rearrange("p (k n) -> p k n", k=KC)
            op("pool", lambda e: e.dma_start(out=view, in_=src2d.rearrange("(k p) n -> p k n", p=128)),
               writes=[("wr", s)], dma="w%d" % s)
            return view, ("wr", s)

        def load_panels(specs):
            s_ = wstate["n"] % 2
            wstate["n"] += 1
            off = 0
            views = []
            for (src2d, K, ncols) in specs:
                KC = K // 128
                view = wring[:, s_, off:off + KC * ncols].rearrange("p (k n) -> p k n", k=KC)
                off += KC * ncols
                assert off <= WSLOT
                op("pool", lambda e, view=view, src2d=src2d: e.dma_start(out=view, in_=src2d.rearrange("(k p) n -> p k n", p=128)),
                   writes=[("wr", s_)], dma="w%d" % s_)
                views.append(view)
            return views, ("wr", s_)

        def next_bank(cands):
            b = cands[bank_rr["i"] % len(cands)]
            bank_rr["i"] += 1
            return b

        def rms_norm_to_hT(tiles, g, nwi):
            stA, stB = [], []
            for ti, (c0, n) in enumerate(tiles):
                A, B = [], []
                fs = (n < 128)
                lc0 = c0 - g * 1024
                hview = hT[:, :, lc0:lc0 + n]
                rb = rstd if ti % 2 == 0 else cacc
                rn = "rstd" if ti % 2 == 0 else "cacc"
                bk = 4 if ti % 2 == 0 else 5
                ps = pf[bk]
                A.append((("act", lambda e, hview=hview, c0=c0, n=n: e.activation(hview, xT[:, :, c0:c0 + n], AF.Square)),
                          dict(reads=RX(c0, n), writes=RH(lc0, n), small=fs)))
                for k in range(8):
                    A.append((("pe", lambda e, k=k, lc0=lc0, n=n, ps=ps: e.matmul(ps[:, 0:n], ones_b, hT[:, k, lc0:lc0 + n],
                                                                                  start=(k == 0), stop=(k == 7))),
                              dict(reads=RH(lc0, n) + ["cfb"], writes=[("ps", bk)], small=fs)))
                B.append((("act", lambda e, n=n, ps=ps, rb=rb: e.activation(rb[:, 0:n], ps[:, 0:n], AF.Ln, bias=RMS_EPS, scale=1.0 / D)),
                          dict(reads=[("ps", bk)], writes=[rn], small=fs)))
                B.append((("act", lambda e, n=n, rb=rb: e.activation(rb[:, 0:n], rb[:, 0:n], AF.Exp, scale=-0.5)),
                          dict(reads=[rn], writes=[rn], small=fs)))
                for k in range(8):
                    B.append((("dve", lambda e, k=k, c0=c0, lc0=lc0, n=n, rb=rb: e.scalar_tensor_tensor(
                        hT[:, k, lc0:lc0 + n], xT[:, k, c0:c0 + n], prm[:, P_NW + nwi * 8 + k:P_NW + nwi * 8 + k + 1],
                        rb[:, 0:n], ALU.mult, ALU.mult)),
                        dict(reads=RX(c0, n) + [rn, "prm"], writes=RH(lc0, n), small=fs)))
                stA.append(A)
                stB.append(B)
            for i in range(len(stA)):
                if i == 0:
                    for (a, k) in stA[0]:
                        S.op(*a, **k)
                if i + 1 < len(stA):
                    for (a, k) in stA[i + 1]:
                        S.op(*a, **k)
                for (a, k) in stB[i]:
                    S.op(*a, **k)

        def ffn(s, tiles, g, fi):
            for f0 in range(0, DFF, 256):
                ncols = 256
                (vg, vu), rg = load_panels([(wg[fi][s, :, f0:f0 + ncols], D, ncols), (wu[fi][s, :, f0:f0 + ncols], D, ncols)])
                ru = rg
                for jj in range(ncols // 128):
                    f = f0 // 128 + jj
                    for (c0, n) in T_(tiles):
                        lc0 = c0 - g * 1024
                        bg = next_bank([0, 1])
                        bu = bg + 2
                        for k in range(8):
                            op("pe", lambda e, k=k, jj=jj, lc0=lc0, n=n, bg=bg, vg=vg: e.matmul(
                                pf[bg][:, 0:n], vg[:, k, jj * 128:(jj + 1) * 128], hT[:, k, lc0:lc0 + n],
                                start=(k == 0), stop=(k == 7)),
                               reads=RH(lc0, n) + [rg], writes=[("ps", bg)])
                        for k in range(8):
                            op("pe", lambda e, k=k, jj=jj, lc0=lc0, n=n, bu=bu, vu=vu: e.matmul(
                                pf[bu][:, 0:n], vu[:, k, jj * 128:(jj + 1) * 128], hT[:, k, lc0:lc0 + n],
                                start=(k == 0), stop=(k == 7)),
                               reads=RH(lc0, n) + [ru], writes=[("ps", bu)])
                        sl = bg
                        op("act", lambda e, n=n, bg=bg, sl=sl: e.activation(sigb[sl][:, 0:n], pf[bg][:, 0:n], AF.Silu),
                           reads=[("ps", bg)], writes=[SG[sl]])
                        op("dve", lambda e, n=n, bu=bu, sl=sl, f=f, lc0=lc0: e.tensor_tensor(
                            U[:, f, lc0:lc0 + n], sigb[sl][:, 0:n], pf[bu][:, 0:n], ALU.mult),
                           reads=[SG[sl], ("ps", bu)], writes=RU(f, lc0, n))
            for ob in range(4):
                vd, rd = load_panel(wd[fi][s, :, ob * 256:(ob + 1) * 256], DFF, 256)
                for jj in range(2):
                    o = ob * 2 + jj
                    for (c0, n) in T_(tiles):
                        lc0 = c0 - g * 1024
                        b = next_bank([0, 1, 2, 3])
                        for k in range(22):
                            op("pe", lambda e, k=k, jj=jj, lc0=lc0, n=n, b=b, vd=vd: e.matmul(
                                pf[b][:, 0:n], vd[:, k, jj * 128:(jj + 1) * 128], U[:, k, lc0:lc0 + n],
                                start=(k == 0), stop=(k == 21)),
                               reads=RU(k, lc0, n) + [rd], writes=[("ps", b)])
                        op("dve", lambda e, o=o, c0=c0, n=n, b=b: e.scalar_tensor_tensor(
                            xT[:, o, c0:c0 + n], pf[b][:, 0:n], 0.5, xT[:, o, c0:c0 + n], ALU.mult, ALU.add),
                           reads=[("ps", b)] + RX(c0, n), writes=RX(c0, n))

        def rsqrt_chain(ps_ap, n, scale, eps):
            op("act", lambda e: e.activation(rstd[:, 0:n], ps_ap, AF.Ln, bias=eps, scale=scale),
               reads=[("ps", 4)], writes=["rstd"])
            op("act", lambda e: e.activation(rstd[:, 0:n], rstd[:, 0:n], AF.Exp, scale=-0.5),
               reads=["rstd"], writes=["rstd"])

        def proj_in(s, tiles, g):
            W = win
            tSf = tS[:].rearrange("p h d -> p (h d)")
            gop = S.op
            stA, stB = [], []
            itn = [0]
            for pb_ in (range(3) if "p_qkv" in PH else []):
                first_in_panel = [True]
                sl_ = wstate["n"] % 2
                wstate["n"] += 1
                v = wring[:, sl_, 0:8 * 512].rearrange("p (k n) -> p k n", k=8)
                r = ("wr", sl_)
                src2d = W[s, :, C_QKV + pb_ * 512:C_QKV + (pb_ + 1) * 512]
                ld = (("pool", lambda e, v=v, src2d=src2d: e.dma_start(out=v, in_=src2d.rearrange("(k p) n -> p k n", p=128))),
                      dict(writes=[r], dma="w%d" % sl_))
                for jj in range(4):
                    j = pb_ * 4 + jj
                    for (c0, n) in tiles:
                        A, B = [], []
                        fs = (n < 128)

                        def opA(*a, **k):
                            k["small"] = k.get("small", False) or fs
                            A.append((a, k))

                        def opB(*a, **k):
                            k["small"] = k.get("small", False) or fs
                            B.append((a, k))
                        if first_in_panel[0]:
                            A.append(ld)
                            first_in_panel[0] = False
                        par = itn[0] % 2
                        itn[0] += 1
                        cb_ = cacc if par == 0 else tSf
                        cn_ = "cacc" if par == 0 else "tS"
                        lc0 = c0 - g * 1024
                        stq = 1 if c0 >= NP_ else 0
                        b = next_bank([0, 1, 2, 3])
                        for k in range(8):
                            opA("pe", lambda e, k=k, jj=jj, lc0=lc0, n=n, b=b, v=v: e.matmul(
                                pf[b][:, 0:n], v[:, k, jj * 128:(jj + 1) * 128], hT[:, k, lc0:lc0 + n],
                                start=(k == 0), stop=(k == 7)),
                                reads=RH(lc0, n) + [r], writes=[("ps", b)])
                        opA("dve", lambda e, stq=stq, j=j: e.tensor_copy(stage[:, 0:3], halo[:, stq, j, :]),
                            reads=[("halo", stq, j)], writes=["stage"], small=True)
                        opA("act", lambda e, n=n, b=b: e.copy(stage[:, 3:3 + n], pf[b][:, 0:n]),
                            reads=[("ps", b)], writes=["stage"])
                        opA("dve", lambda e, stq=stq, j=j, n=n: e.tensor_copy(halo[:, stq, j, :], stage[:, n:n + 3]),
                            reads=["stage"], writes=[("halo", stq, j)], small=True)
                        cw0 = P_CW + j * 4
                        opA("dve", lambda e, n=n, cw0=cw0, cb_=cb_: e.tensor_scalar(
                            cb_[:, 0:n], stage[:, 0:n], prm[:, cw0:cw0 + 1], None, ALU.mult),
                            reads=["stage", "prm"], writes=[cn_])
                        for t in range(1, 4):
                            opA("dve", lambda e, n=n, cw0=cw0, t=t, cb_=cb_: e.scalar_tensor_tensor(
                                cb_[:, 0:n], stage[:, t:t + n], prm[:, cw0 + t:cw0 + t + 1], cb_[:, 0:n],
                                ALU.mult, ALU.add),
                                reads=["stage", "prm", cn_], writes=[cn_])
                        if j >= 8:
                            opB("act", lambda e, n=n, j=j, lc0=lc0, cb_=cb_: e.activation(U[:, j, lc0:lc0 + n], cb_[:, 0:n], AF.Silu),
                                reads=[cn_], writes=RU(j, lc0, n))
                        else:
                            opB("act", lambda e, n=n, cb_=cb_: e.activation(cact[:, 0:n], cb_[:, 0:n], AF.Silu),
                                reads=[cn_], writes=["cact"])
                            opB("act", lambda e, n=n: e.activation(sqb[:, 0:n], cact[:, 0:n], AF.Square),
                                reads=["cact"], writes=["sqb"])
                            opB("pe", lambda e, n=n: e.matmul(pf[4][:, 0:n], ones_b, sqb[:, 0:n], start=True, stop=True),
                                reads=["sqb", "cfb"], writes=[("ps", 4)])
                            opB("act", lambda e, n=n: e.activation(rstd[:, 0:n], pf[4][:, 0:n], AF.Ln, bias=L2_EPS, scale=1.0),
                                reads=[("ps", 4)], writes=["rstd"])
                            opB("act", lambda e, n=n: e.activation(rstd[:, 0:n], rstd[:, 0:n], AF.Exp, scale=-0.5),
                                reads=["rstd"], writes=["rstd"])
                            qs = (128.0 ** -0.5) if j < 4 else 1.0
                            opB("dve", lambda e, n=n, j=j, lc0=lc0, qs=qs: e.scalar_tensor_tensor(
                                U[:, j, lc0:lc0 + n], cact[:, 0:n], qs, rstd[:, 0:n], ALU.mult, ALU.mult),
                                reads=["cact", "rstd"], writes=RU(j, lc0, n))
                        stA.append(A)
                        stB.append(B)
            for i in range(len(stA)):
                if i == 0:
                    for (a, k) in stA[0]:
                        gop(*a, **k)
                if i + 1 < len(stA):
                    for (a, k) in stA[i + 1]:
                        gop(*a, **k)
                for (a, k) in stB[i]:
                    gop(*a, **k)
            v, r = load_panel(W[s, :, C_GATE:C_GATE + 512], D, 512)
            for jj in (range(4) if "p_gate" in PH else []):
                for (c0, n) in T_(tiles):
                    lc0 = c0 - g * 1024
                    b = next_bank([0, 1, 2, 3])
                    for k in range(8):
                        op("pe", lambda e, k=k, jj=jj, lc0=lc0, n=n, b=b, v=v: e.matmul(
                            pf[b][:, 0:n], v[:, k, jj * 128:(jj + 1) * 128], hT[:, k, lc0:lc0 + n],
                            start=(k == 0), stop=(k == 7)),
                           reads=RH(lc0, n) + [r], writes=[("ps", b)])
                    op("act", lambda e, n=n, b=b, jj=jj, lc0=lc0: e.activation(U[:, 12 + jj, lc0:lc0 + n], pf[b][:, 0:n], AF.Silu),
                       reads=[("ps", b)], writes=RU(12 + jj, lc0, n))
            for (cbase, nch, ubase, pcol) in ([(C_QS, 4, 16, "q"), (C_KD, 2, 20, "k")] if "p_qk" in PH else []):
                v, r = load_panel(W[s, :, cbase:cbase + nch * 128], D, nch * 128)
                for jj in range(nch):
                    for (c0, n) in T_(tiles):
                        lc0 = c0 - g * 1024
                        b = next_bank([0, 1, 2, 3])
                        for k in range(8):
                            op("pe", lambda e, k=k, jj=jj, lc0=lc0, n=n, b=b, v=v: e.matmul(
                                pf[b][:, 0:n], v[:, k, jj * 128:(jj + 1) * 128], hT[:, k, lc0:lc0 + n],
                                start=(k == 0), stop=(k == 7)),
                               reads=RH(lc0, n) + [r], writes=[("ps", b)])
                        op("act", lambda e, n=n, b=b: e.activation(sqb[:, 0:n], pf[b][:, 0:n], AF.Square),
                           reads=[("ps", b)], writes=["sqb"])
                        op("pe", lambda e, n=n: e.matmul(pf[4][:, 0:n], blk_b, sqb[:, 0:n], start=True, stop=True),
                           reads=["sqb", "cfb"], writes=[("ps", 4)])
                        rsqrt_chain(pf[4][:, 0:n], n, 1.0 / 64, RMS_EPS)
                        sc = prm2[:, 0:1] if pcol == "q" else prm[:, P_KNW:P_KNW + 1]
                        op("dve", lambda e, n=n, b=b, jj=jj, lc0=lc0, sc=sc, ubase=ubase: e.scalar_tensor_tensor(
                            U[:, ubase + jj, lc0:lc0 + n], pf[b][:, 0:n], sc, rstd[:, 0:n], ALU.mult, ALU.mult),
                           reads=[("ps", b), "rstd", "prm", "prm2"], writes=RU(ubase + jj, lc0, n))
                        if pcol == "k":
                            if c0 >= NP_:
                                op("dve", lambda e, n=n, b=b, jj=jj, sc=sc: e.scalar_tensor_tensor(
                                    kout[:, jj, 128:160], pf[b][:, 0:n], sc, rstd[:, 0:n], ALU.mult, ALU.mult),
                                   reads=[("ps", b), "rstd", "prm"], writes=["kout"])
                            elif c0 + n == NP_:
                                op("dve", lambda e, n=n, b=b, jj=jj, sc=sc: e.scalar_tensor_tensor(
                                    kout[:, jj, 0:128], pf[b][:, n - 128:n], sc, rstd[:, n - 128:n], ALU.mult, ALU.mult),
                                   reads=[("ps", b), "rstd", "prm"], writes=["kout"])
            v, r = load_panel(W[s, :, C_VBA:C_VBA + 136], D, 136)
            chunks = chunk_list(g) if "p_vba" in PH else []
            for ci, (stq, lc0, C) in enumerate(chunks):
                b = next_bank([0, 1, 2, 3])
                for k in range(8):
                    op("pe", lambda e, k=k, lc0=lc0, C=C, b=b, v=v: e.matmul(
                        pf[b][0:C, 0:136], hT[:, k, lc0:lc0 + C], v[:, k, :], start=(k == 0), stop=(k == 7)),
                       reads=RH(lc0, C) + [r], writes=[("ps", b)])
                if stq == 0:
                    vdst = vbuf[0:C, 2 + ci, :]
                    vres = ("vbuf", 2 + ci)
                else:
                    vdst = vsb[0:C, 2, :]
                    vres = ("vsb", 2)
                import os
                DV = os.environ.get("DBGV", "")
                if "noact" not in DV: op("dve", lambda e, C=C, b=b, vdst=vdst: e.tensor_copy(
                    vdst.rearrange("p (k d) -> p k d", k=2)[:, :, 0:64],
                    pf[b][0:C, 0:128].rearrange("p (k d) -> p k d", k=2)),
                   reads=[("ps", b), "vbuf_all", "vsb_all"], writes=[vres])
                if "nozba" not in DV: op("dve", lambda e, C=C, b=b, ci=ci: e.tensor_copy(zba[0:C, ci, :], pf[b][0:C, 128:136]),
                   reads=[("ps", b)], writes=["zba"], small=True)
                if "novout" in DV:
                    pass
                elif stq == 1:
                    op("dve", lambda e, C=C, b=b: e.tensor_copy(vout[0:C, 2, :], pf[b][0:C, 0:128]),
                       reads=[("ps", b)], writes=["vout"])
                elif g == ngroup - 1 and ci >= 14:
                    op("dve", lambda e, C=C, b=b, ci=ci: e.tensor_copy(vout[0:C, ci - 14, :], pf[b][0:C, 0:128]),
                       reads=[("ps", b)], writes=["vout"])

        def chunk_list(g):
            ch = [(0, 64 * i, 64) for i in range(16)]
            if g == ngroup - 1:
                ch.append((1, 1024, 32))
            return ch

        def dn_params(g):
            chunks = chunk_list(g)
            nch = len(chunks)
            npc = 16
            has_s = nch > 16
            A3 = lambda t, a=0, b=4: t[:, 0:nch, a:b]
            op("act", lambda e: e.activation(tp_beta[:, 0:nch, :], zba[:, 0:nch, 0:4], AF.Sigmoid),
               reads=["zba"], writes=["tp_beta"], small=True)
            op("dve", lambda e: e.tensor_scalar(tp_negb[:, 0:nch, :], tp_beta[:, 0:nch, :], -1.0, None, ALU.mult),
               reads=["tp_beta"], writes=["tp_negb"], small=True)
            op("dve", lambda e: e.tensor_tensor(tp_t[:, 0:nch, :], zba[:, 0:nch, 4:8],
                                                prm[0:64, P_DTB:P_DTB + 4].unsqueeze(1).broadcast_to([64, nch, 4]), ALU.add),
               reads=["zba", "prm"], writes=["tp_t"], small=True)
            op("act", lambda e: e.activation(tp_t[:, 0:nch, :], tp_t[:, 0:nch, :], AF.Exp), reads=["tp_t"], writes=["tp_t"], small=True)
            op("act", lambda e: e.activation(tp_t[:, 0:nch, :], tp_t[:, 0:nch, :], AF.Ln, bias=1.0), reads=["tp_t"], writes=["tp_t"], small=True)
            op("dve", lambda e: e.scalar_tensor_tensor(tp_g[:, 0:nch, :], tp_t[:, 0:nch, :], -1.0,
                                                       prm2[0:64, 1:5].unsqueeze(1).broadcast_to([64, nch, 4]), ALU.mult, ALU.mult),
               reads=["tp_t", "prm2"], writes=["tp_g"], small=True)
            g2 = tp_g[:].rearrange("p c h -> p (c h)")
            op("pe", lambda e: e.matmul(pf[5][0:64, 0:npc * 4], UT, g2[:, 0:npc * 4], start=True, stop=True),
               reads=["tp_g", "cf"], writes=[("ps", 5)])
            op("pe", lambda e: e.matmul(pf[6][:, 0:npc * 4], ones_f, g2[:, 0:npc * 4], start=True, stop=True),
               reads=["tp_g", "cf"], writes=[("ps", 6)])
            if has_s:
                op("pe", lambda e: e.matmul(pf[5][0:32, 64:68], cf[0:32, 192:224], g2[0:32, 64:68], start=True, stop=True),
                   reads=["tp_g", "cf"], writes=[("ps", 5)])
                op("pe", lambda e: e.matmul(pf[6][:, 64:68], cf[0:32, 256:384], g2[0:32, 64:68], start=True, stop=True),
                   reads=["tp_g", "cf"], writes=[("ps", 6)])
            n4 = nch * 4
            f2 = lambda t: t[:].rearrange("p c h -> p (c h)")[:, 0:n4]
            op("act", lambda e: e.copy(f2(tp_gc), pf[5][0:64, 0:n4]), reads=[("ps", 5)], writes=["tp_gc"], small=True)
            op("act", lambda e: e.activation(f2(tp_egc), pf[5][0:64, 0:n4], AF.Exp), reads=[("ps", 5)], writes=["tp_egc"], small=True)
            op("dve", lambda e: e.tensor_tensor(f2(tp_ekr), pf[6][0:64, 0:n4], f2(tp_gc), ALU.subtract),
               reads=[("ps", 6), "tp_gc"], writes=["tp_ekr"], small=True)
            op("act", lambda e: e.activation(f2(tp_ekr), f2(tp_ekr), AF.Exp), reads=["tp_ekr"], writes=["tp_ekr"], small=True)
            op("act", lambda e: e.activation(tp_egl[:].rearrange("p c h -> p (c h)")[:, 0:n4], pf[6][:, 0:n4], AF.Exp),
               reads=[("ps", 6)], writes=["tp_egl"], small=True)

        def bc_h(ap2, C, n):
            return ap2.unsqueeze(2).broadcast_to([C, 4, n])

        def bc_m(ap2, C):
            return ap2.unsqueeze(1).broadcast_to([C, 4, C])

        def dn_chunk(ci, stq, lc0, C):
            fs = (C < 64)
            pre, stp = [], []
            curl = [pre]

            def op(*a, **k):
                k["small"] = k.get("small", False) or fs
                curl[0].append((a, k))
            par = ci % 2
            inT, kr, u_sb, w0T = inT2[par], kr2[par], u_sb2[par], w0T2[par]
            n_inT, n_kr, n_u, n_w0 = "inT%d" % par, "kr%d" % par, ("u_sb" if par == 0 else "cact"), "w0T%d" % par
            cs = slice(lc0, lc0 + C)
            qT = lambda h: U[:, h, cs]
            kT = lambda h: U[:, 4 + h, cs]
            vT = lambda h: U[:, 8 + h, cs]
            rq = [r for h in range(4) for r in RU(h, lc0, C)]
            rk = [r for h in range(4) for r in RU(4 + h, lc0, C)]
            rv = [r for h in range(4) for r in RU(8 + h, lc0, C)]
            v3 = lambda t, n=None: (t[0:C, :, 0:(n or C)])
            op("dve", lambda e: e.tensor_tensor(v3(Gm), bc_h(tp_g[0:C, ci, :], C, C), bc_m(cf[0:C, 192:192 + C], C), ALU.mult),
               reads=["tp_g", "cf"], writes=["Gm"])
            op("pe", lambda e: e.matmul(pf[5][0:C, 0:4 * C].rearrange("p (h i) -> p h i", h=4), cf[0:C, 128:128 + C], v3(Gm),
                                        start=True, stop=True), reads=["Gm", "cf"], writes=[("ps", 5)])
            op("act", lambda e: e.activation(v3(DTu), pf[5][0:C, 0:4 * C].rearrange("p (h i) -> p h i", h=4), AF.Exp),
               reads=[("ps", 5)], writes=["DTu"])
            op("dve", lambda e: e.tensor_tensor(v3(DTs), v3(DTu), bc_m(cf[0:C, 192:192 + C], C), ALU.mult),
               reads=["DTu", "cf"], writes=["DTs"])
            for h in range(4):
                op("pe", lambda e, h=h: e.matmul(pf[6][0:C, h * C:(h + 1) * C], kT(h), kT(h), start=True, stop=True),
                   reads=RU(4 + h, lc0, C), writes=[("ps", 6)])
            for h in range(4):
                op("pe", lambda e, h=h: e.matmul(pf[5][0:C, h * C:(h + 1) * C], kT(h), qT(h), start=True, stop=True),
                   reads=RU(4 + h, lc0, C) + RU(h, lc0, C) + ["DTu"], writes=[("ps", 5)])
            for h in range(4):
                op("pe", lambda e, h=h: e.transpose(pb[0:C, 0, h * 128:(h + 1) * 128], kT(h), ident_b),
                   reads=RU(4 + h, lc0, C) + ["cfb"], writes=[("ps", 7)])
            pb3 = lambda i: pb[0:C, i, :].rearrange("p (h d) -> p h d", h=4)
            op("dve", lambda e: e.tensor_tensor(kg[0:C], pb3(0), bc_h(tp_egc[0:C, ci, :], C, 128), ALU.mult),
               reads=[("ps", 7), "tp_egc"], writes=["kg"])
            op("dve", lambda e: e.tensor_tensor(kr[0:C], pb3(0), bc_h(tp_ekr[0:C, ci, :], C, 128), ALU.mult),
               reads=[("ps", 7), "tp_ekr"], writes=[n_kr])
            for h in range(4):
                op("pe", lambda e, h=h: e.transpose(pb[0:C, 0, h * 128:(h + 1) * 128], vT(h), ident_b),
                   reads=RU(8 + h, lc0, C) + ["cfb"], writes=[("ps", 7)])
            op("act", lambda e: e.copy(vtm[0:C], pb3(0)), reads=[("ps", 7)], writes=["vtm"])
            ps3 = lambda b: pf[b][0:C, 0:4 * C].rearrange("p (h i) -> p h i", h=4)
            op("dve", lambda e: e.tensor_tensor(v3(inT), ps3(5), v3(DTs), ALU.mult), reads=[("ps", 5), "DTs"], writes=[n_inT])
            op("dve", lambda e: e.tensor_tensor(v3(DTu), v3(DTs), bc_m(cf[0:C, 0:C], C), ALU.subtract),
               reads=["DTs", "cf"], writes=["DTu"])
            op("dve", lambda e: e.tensor_tensor(v3(tA), ps3(6), v3(DTu), ALU.mult), reads=[("ps", 6), "DTu"], writes=["Gm"])
            op("dve", lambda e: e.tensor_tensor(v3(Pk[0]), v3(tA), bc_h(tp_negb[0:C, ci, :], C, C), ALU.mult),
               reads=["Gm", "tp_negb"], writes=["Pk"])
            op("dve", lambda e: e.tensor_tensor(v3(Rk[0]), v3(tA), bc_h(tp_negb[0:C, ci, :], C, C), ALU.mult),
               reads=["Gm", "tp_negb"], writes=["Rk"])
            for h in range(4):
                op("pe", lambda e, h=h: e.transpose(pb[0:C, 0, h * C:(h + 1) * C], Pk[0][0:C, h, 0:C], cfb[0:C, 0:C]),
                   reads=["Pk", "cfb"], writes=[("ps", 7)])
            op("act", lambda e: e.copy(v3(PTk[0]), pb[0:C, 0, 0:4 * C].rearrange("p (h i) -> p h i", h=4)), reads=[("ps", 7)], writes=["PTk"])
            op("dve", lambda e: e.tensor_tensor(v3(Rk[0]), v3(Rk[0]), bc_m(cf[0:C, 0:C], C), ALU.add),
               reads=["Rk", "cf"], writes=["Rk"])
            op("dve", lambda e: e.tensor_copy(v3(Rkb), v3(Rk[0])), reads=["Rk"], writes=["Rkb"])
            nlev = 5 if C == 64 else 4
            cur = 0
            for lv in range(nlev):
                nx = 1 - cur
                last = (lv == nlev - 1)
                if not last:
                    for h in range(4):
                        op("pe", lambda e, h=h, cur=cur: e.matmul(pf[5][0:C, h * C:(h + 1) * C], PTk[cur][0:C, h, 0:C], Pk[cur][0:C, h, 0:C],
                                                                  start=True, stop=True),
                           reads=["Pk", "PTk"], writes=[("ps", 5)])
                for h in range(4):
                    op("pe", lambda e, h=h, cur=cur: e.matmul(pf[6][0:C, h * C:(h + 1) * C], Pk[cur][0:C, h, 0:C], PTk[cur][0:C, h, 0:C],
                                                              start=True, stop=True),
                       reads=["Pk", "PTk"], writes=[("ps", 6)])
                if not last:
                    op("act", lambda e, nx=nx: e.copy(v3(Pk[nx]), ps3(5)), reads=[("ps", 5)], writes=["Pk"])
                op("dve", lambda e, nx=nx: e.tensor_copy(v3(PTk[nx]), ps3(6)), reads=[("ps", 6)], writes=["PTk"])
                for h in range(4):
                    op("pe", lambda e, h=h, cur=cur, nx=nx: e.matmul(pf[5][0:C, h * C:(h + 1) * C], PTk[nx][0:C, h, 0:C], Rkb[0:C, h, 0:C],
                                                                     start=True, stop=True),
                       reads=["PTk", "Rkb"], writes=[("ps", 5)])
                if not last:
                    op("dve", lambda e, cur=cur: e.tensor_tensor(v3(Rkb), ps3(5), v3(Rk[cur]), ALU.add),
                       reads=[("ps", 5), "Rk"], writes=["Rkb"])
                    op("dve", lambda e, cur=cur, nx=nx: e.tensor_tensor(v3(Rk[nx]), ps3(5), v3(Rk[cur]), ALU.add),
                       reads=[("ps", 5), "Rk"], writes=["Rk"])
                else:
                    op("dve", lambda e, cur=cur: e.tensor_tensor(v3(Rbf), ps3(5), v3(Rk[cur]), ALU.add),
                       reads=[("ps", 5), "Rk"], writes=["Rbf"])
                cur = nx
            for h in range(4):
                op("pe", lambda e, h=h: e.matmul(pf[6][0:C, h * 128:(h + 1) * 128], Rbf[0:C, h, 0:C], vtm[0:C, h, :], start=True, stop=True),
                   reads=["Rbf", "vtm"], writes=[("ps", 6)])
            for h in range(4):
                op("pe", lambda e, h=h: e.matmul(pf[5][:, h * C:(h + 1) * C], kg[0:C, h, :], Rbf[0:C, h, 0:C], start=True, stop=True),
                   reads=["Rbf", "kg"], writes=[("ps", 5)])
            op("dve", lambda e: e.tensor_tensor(u_sb[0:C], pf[6][0:C, :].rearrange("p (h d) -> p h d", h=4),
                                                bc_h(tp_beta[0:C, ci, :], C, 128), ALU.mult),
               reads=[("ps", 6), "tp_beta"], writes=[n_u])
            op("act", lambda e: e.copy(w0T[:, :, 0:C], pf[5][:, 0:4 * C].rearrange("p (h i) -> p h i", h=4)),
               reads=[("ps", 5)], writes=[n_w0])
            curl[0] = stp
            Sres = "S_b%d" % stq
            Sfres = "S_f%d" % stq
            for h in range(4):
                op("pe", lambda e, h=h: e.matmul(pf[1][0:C, h * 128:(h + 1) * 128], w0T[:, h, 0:C], S_b[:, stq, h, :], start=True, stop=True),
                   reads=[n_w0, Sres], writes=[("ps", 1)])
            p64 = lambda b: pf[b][0:C, :].rearrange("p (h d) -> p h d", h=4)
            op("dve", lambda e: e.tensor_tensor(t1[0:C], p64(1), bc_h(tp_negb[0:C, ci, :], C, 128), ALU.mult),
               reads=[("ps", 1), "tp_negb"], writes=["cacc"])
            op("dve", lambda e: e.tensor_tensor(vnew[0:C], t1[0:C], u_sb[0:C], ALU.add), reads=["cacc", n_u], writes=["vnew"])
            for h in range(4):
                op("pe", lambda e, h=h: e.matmul(pf[0][0:C, h * 128:(h + 1) * 128], qT(h), S_b[:, stq, h, :], start=True, stop=True),
                   reads=RU(h, lc0, C) + [Sres], writes=[("ps", 0)])
            for h in range(4):
                op("pe", lambda e, h=h: e.matmul(pf[1][0:C, h * 128:(h + 1) * 128], inT[0:C, h, 0:C], vnew[0:C, h, :], start=True, stop=True),
                   reads=[n_inT, "vnew"], writes=[("ps", 1)])
            op("dve", lambda e: e.tensor_tensor(t1[0:C], p64(0), bc_h(tp_egc[0:C, ci, :], C, 128), ALU.mult),
               reads=[("ps", 0), "tp_egc"], writes=["cacc"])
            op("dve", lambda e: e.tensor_tensor(o_sb[0:C], t1[0:C], p64(1), ALU.add), reads=["cacc", ("ps", 1)], writes=["rstd"])
            for h in range(4):
                op("pe", lambda e, h=h: e.matmul(pf[0][:, h * 128:(h + 1) * 128], kr[0:C, h, :], vnew[0:C, h, :], start=True, stop=True),
                   reads=[n_kr, "vnew"], writes=[("ps", 0)])
            op("dve", lambda e: e.tensor_tensor(tS[:], S_f[:, stq], tp_egl[:, ci, :].unsqueeze(2).broadcast_to([128, 4, 128]), ALU.mult),
               reads=[Sfres, "tp_egl"], writes=["tS"])
            op("dve", lambda e: e.tensor_tensor(S_f[:, stq], tS[:], pf[0][:, :].rearrange("p (h d) -> p h d", h=4), ALU.add),
               reads=["tS", ("ps", 0)], writes=[Sfres])
            op("act", lambda e: e.copy(S_b[:, stq], S_f[:, stq]), reads=[Sfres], writes=[Sres])
            op("dve", lambda e: e.tensor_tensor(t1[0:C], o_sb[0:C], o_sb[0:C], ALU.mult), reads=["rstd"], writes=["cacc"])
            op("dve", lambda e: e.tensor_reduce(ss4[0:C, 0:4], t1[0:C], AX.X, ALU.add), reads=["cacc"], writes=["ss4"], small=True)
            op("act", lambda e: e.activation(ss4[0:C, 0:4], ss4[0:C, 0:4], AF.Ln, bias=RMS_EPS, scale=1.0 / 128), reads=["ss4"], writes=["ss4"], small=True)
            op("act", lambda e: e.activation(ss4[0:C, 0:4], ss4[0:C, 0:4], AF.Exp, scale=-0.5), reads=["ss4"], writes=["ss4"], small=True)
            op("dve", lambda e: e.tensor_tensor(t1[0:C], o_sb[0:C], bc_h(ss4[0:C, 0:4], C, 128), ALU.mult),
               reads=["rstd", "ss4"], writes=["cacc"])
            op("dve", lambda e: e.tensor_tensor(on2[0:C], t1[0:C], prm[0:C, P_DNW:P_DNW + 128].unsqueeze(1).broadcast_to([C, 4, 128]), ALU.mult),
               reads=["cacc", "prm"], writes=["sqb"])
            for h in range(4):
                op("pe", lambda e, h=h: e.transpose(pb[:, 1, h * C:(h + 1) * C], on2[0:C, h, :], cfb[0:C, 0:C]),
                   reads=["sqb", "cfb"], writes=[("ps", 7)])
            gview = U[:, 12:16, cs]
            rg = [r for h in range(4) for r in RU(12 + h, lc0, C)]
            op("dve", lambda e: e.tensor_tensor(gview, pb[:, 1, 0:4 * C].rearrange("p (h i) -> p h i", h=4), gview, ALU.mult),
               reads=[("ps", 7)] + rg, writes=rg)

            return pre, stp

        def swa_chunk(ci, stq, lc0, C, g):
            fs = (C < 64)
            lst = []

            def op(*a, **k):
                k["small"] = k.get("small", False) or fs
                lst.append((a, k))
            cs = slice(lc0, lc0 + C)
            if stq == 0:
                keys = []
                for dl in (2, 1, 0):
                    kc = ci - dl
                    if kc < 0:
                        w0 = (kc + 2) * 64
                        keys.append((dl, lambda kv, hf, w0=w0: kwin[hf * 64:(hf + 1) * 64, kv, w0:w0 + 64], ["kwin"],
                                     vbuf[:, kc + 2, :], [("vbuf", kc + 2)], 64, g == 0))
                    else:
                        keys.append((dl, lambda kv, hf, kc=kc: U[hf * 64:(hf + 1) * 64, 20 + kv, kc * 64:kc * 64 + 64],
                                     RU(20, kc * 64, 64) + RU(21, kc * 64, 64),
                                     vbuf[:, kc + 2, :], [("vbuf", kc + 2)], 64, False))
            else:
                keys = [(2, lambda kv, hf: ksc[hf * 64:(hf + 1) * 64, kv, 0:64], ["ksc"], vsb[:, 0, :], [("vsb", 0)], 64, False),
                        (1, lambda kv, hf: ksc[hf * 64:(hf + 1) * 64, kv, 64:128], ["ksc"], vsb[:, 1, :], [("vsb", 1)], 64, False),
                        (0, lambda kv, hf: U[hf * 64:(hf + 1) * 64, 20 + kv, cs], RU(20, lc0, C) + RU(21, lc0, C),
                         vsb[:, 2, :], [("vsb", 2)], 32, False)]
            rq = [r for j in range(4) for r in RU(16 + j, lc0, C)]
            for idx, (dl, kfn, kres, vap, vres, SK, masked) in enumerate(keys):
                for hh in range(8):
                    kv, j, hf = hh // 4, hh // 2, hh % 2
                    sbk = 2 if hf == 0 else 3
                    op("pe", lambda e, hh=hh, kv=kv, j=j, hf=hf, kfn=kfn, SK=SK, sbk=sbk: e.matmul(
                        pf[sbk][0:SK, j * 64:j * 64 + C], kfn(kv, hf), U[hf * 64:(hf + 1) * 64, 16 + j, cs], start=True, stop=True),
                       reads=kres + rq, writes=[("ps", sbk)])
                import os
                SWL = int(os.environ.get("SWL", "9"))
                if SWL < 2:
                    continue
                for hf, sbk in ((0, 2), (1, 3)):
                    sc3 = pf[sbk][0:SK, 0:256].rearrange("p (h i) -> p h i", h=4)[:, :, 0:C]
                    op("dve", lambda e, sc3=sc3, dl=dl, SK=SK, hf=hf: e.tensor_tensor(
                        sct[0:SK, hf * 4:hf * 4 + 4, 0:C], sc3, BT[0:SK, dl, hf * 4:hf * 4 + 4, 0:C], ALU.add),
                       reads=[("ps", sbk), "cf"], writes=["stage"])
                PT = PTs[idx]
                pn = PTn[idx]
                op("act", lambda e, SK=SK, PT=PT: e.activation(PT[0:SK, :, 0:C], sct[0:SK, :, 0:C], AF.Exp), reads=["stage"], writes=[pn])
                if masked:
                    op("dve", lambda e, SK=SK, PT=PT: e.tensor_scalar(PT[0:SK, :, 0:C], PT[0:SK, :, 0:C], mcore[0:SK, 0:1], None, ALU.mult),
                       reads=[pn, "mcore"], writes=[pn])
            for half in ((0, 1) if SWL >= 3 else []):
                bk = 4
                for hh in range(half * 4, half * 4 + 4):
                    kv = hh // 4
                    hp = (hh % 2) * 4 + hh // 2
                    for idx, (dl, kfn, kres, vap, vres, SK, masked) in enumerate(keys):
                        op("pe", lambda e, hh=hh, kv=kv, bk=bk, vap=vap, SK=SK, idx=idx, hp=hp: e.matmul(
                            pf[bk][0:C, (hh % 4) * 66:(hh % 4) * 66 + 66], PTs[idx][0:SK, hp, 0:C], vap[0:SK, kv * 66:kv * 66 + 66],
                            start=(idx == 0), stop=(idx == 2)),
                           reads=[PTn[idx]] + vres, writes=[("ps", bk)])
                o3 = pf[bk][0:C, 0:264].rearrange("p (h d) -> p h d", h=4)
                op("dve", lambda e, o3=o3, half=half: e.tensor_tensor(den[0:C, half * 4:half * 4 + 4], o3[:, :, 64],
                                                                      prm2[0:C, 5 + half * 4:9 + half * 4], ALU.add),
                   reads=[("ps", bk), "prm2"], writes=["den"], small=True)
                op("dve", lambda e, half=half: e.reciprocal(den[0:C, half * 4:half * 4 + 4], den[0:C, half * 4:half * 4 + 4]),
                   reads=["den"], writes=["den"], small=True)
                op("dve", lambda e, o3=o3, half=half: e.tensor_tensor(osw[0:C, half * 4:half * 4 + 4, :], o3[:, :, 0:64],
                                                                      den[0:C, half * 4:half * 4 + 4].unsqueeze(2).broadcast_to([C, 4, 64]), ALU.mult),
                   reads=[("ps", bk), "den"], writes=["osw"])
            if SWL < 5:
                return lst
            for j in range(4):
                op("pe", lambda e, j=j: e.transpose(pb[:, 1, 256 + j * C:256 + (j + 1) * C], osw[0:C, 2 * j:2 * j + 2, :].rearrange("p a d -> p (a d)"), cfb[0:C, 0:C]),
                   reads=["osw", "cfb"], writes=[("ps", 7)])
            op("act", lambda e: e.copy(U[:, 16:20, cs], pb[:, 1, 256:256 + 4 * C].rearrange("p (j i) -> p j i", j=4)),
               reads=[("ps", 7)], writes=rq)
            return lst

        def merge(s, tiles, g):
            import os
            MRG = os.environ.get("MRG", "both")
            for passi, (wo, ucb, gc0) in enumerate([(wodn, 12, C_GA), (woswa, 16, C_GB)]):
                if (MRG == "dn" and passi == 1):
                    continue
                for cb in range(4):
                    (vo, vgp), ro = load_panels([(wo[s, :, cb * 256:(cb + 1) * 256], 512, 256),
                                                 (win[s, :, gc0 + cb * 256:gc0 + (cb + 1) * 256], D, 256)])
                    rgp = ro
                    for jj in range(2):
                        c = cb * 2 + jj
                        for (c0, n) in T_(tiles):
                            lc0 = c0 - g * 1024
                            by = next_bank([0, 1])
                            bg = by + 2
                            for k in range(4):
                                op("pe", lambda e, k=k, jj=jj, lc0=lc0, n=n, by=by, vo=vo, ucb=ucb: e.matmul(
                                    pf[by][:, 0:n], vo[:, k, jj * 128:(jj + 1) * 128], U[:, ucb + k, lc0:lc0 + n],
                                    start=(k == 0), stop=(k == 3)),
                                   reads=RU(ucb + k, lc0, n) + [ro], writes=[("ps", by)])
                            for k in range(8):
                                op("pe", lambda e, k=k, jj=jj, lc0=lc0, n=n, bg=bg, vgp=vgp: e.matmul(
                                    pf[bg][:, 0:n], vgp[:, k, jj * 128:(jj + 1) * 128], hT[:, k, lc0:lc0 + n],
                                    start=(k == 0), stop=(k == 7)),
                                   reads=RH(lc0, n) + [rgp], writes=[("ps", bg)])
                            sl = by
                            op("act", lambda e, n=n, bg=bg, sl=sl: e.activation(sigb[sl][:, 0:n], pf[bg][:, 0:n], AF.Sigmoid),
                               reads=[("ps", bg)], writes=[SG[sl]])
                            if passi == 0:
                                op("dve", lambda e, n=n, by=by, sl=sl, c=c, lc0=lc0: e.tensor_tensor(
                                    U[:, c, lc0:lc0 + n], sigb[sl][:, 0:n], pf[by][:, 0:n], ALU.mult),
                                   reads=[SG[sl], ("ps", by)], writes=RU(c, lc0, n))
                            else:
                                op("dve", lambda e, n=n, by=by, sl=sl: e.tensor_tensor(
                                    sigb[sl][:, 0:n], sigb[sl][:, 0:n], pf[by][:, 0:n], ALU.mult),
                                   reads=[SG[sl], ("ps", by)], writes=[SG[sl]])
                                op("dve", lambda e, n=n, sl=sl, c=c, lc0=lc0: e.tensor_tensor(
                                    U[:, c, lc0:lc0 + n], sigb[sl][:, 0:n], U[:, c, lc0:lc0 + n], ALU.add),
                                   reads=[SG[sl]] + RU(c, lc0, n), writes=RU(c, lc0, n))
            for cb in range(2):
                vw, rw = load_panel(wout[s, :, cb * 512:(cb + 1) * 512], D, 512)
                for jj in range(4):
                    c = cb * 4 + jj
                    for (c0, n) in T_(tiles):
                        lc0 = c0 - g * 1024
                        b = next_bank([0, 1, 2, 3])
                        for k in range(8):
                            op("pe", lambda e, k=k, jj=jj, lc0=lc0, n=n, b=b, vw=vw: e.matmul(
                                pf[b][:, 0:n], vw[:, k, jj * 128:(jj + 1) * 128], U[:, k, lc0:lc0 + n],
                                start=(k == 0), stop=(k == 7)),
                               reads=RU(k, lc0, n) + [rw], writes=[("ps", b)])
                        op("dve", lambda e, c=c, c0=c0, n=n, b=b: e.tensor_tensor(
                            xT[:, c, c0:c0 + n], pf[b][:, 0:n], xT[:, c, c0:c0 + n], ALU.add),
                           reads=[("ps", b)] + RX(c0, n), writes=RX(c0, n))

        halo_all = lambda stq: [("halo", stq, j) for j in range(12)]
        for s in range(nslot):
            op("sp", lambda e, s=s: e.dma_start(out=prm[:], in_=prm_d[s]), writes=["prm"], dma="ld_p")
            op("act", lambda e: e.mul(prm2[:, 0:1], prm[:, P_QNW:P_QNW + 1], 0.125), reads=["prm"], writes=["prm2"], small=True)
            op("act", lambda e: e.activation(prm2[:, 1:5], prm[:, P_ALOG:P_ALOG + 4], AF.Exp), reads=["prm"], writes=["prm2"], small=True)
            op("act", lambda e: e.activation(prm2[:, 5:13], prm[:, P_SINK:P_SINK + 8], AF.Exp), reads=["prm"], writes=["prm2"], small=True)
            op("sp", lambda e, s=s: e.dma_start(out=S_f[:, 1].rearrange("p h d -> p (h d)"), in_=sdn_d[s]), writes=["S_f1"], dma="ld_s1")
            op("sp", lambda e, s=s: e.dma_start(out=halo[:, 1].rearrange("p j r -> p (j r)"), in_=sconv_d[s]), writes=halo_all(1), dma="ld_s2")
            op("sp", lambda e, s=s: e.dma_start(out=kscf[:], in_=skc_d[s]), writes=["cact"], dma="ld_s3")
            op("sp", lambda e, s=s: e.dma_start(out=vscf[:], in_=svc_d[s]), writes=["cact"], dma="ld_s4")
            op("dve", lambda e: e.tensor_copy(ksc[:].rearrange("p a b -> p (a b)"), kscf[:]), reads=["cact"], writes=["ksc"])
            op("dve", lambda e: e.tensor_copy(vsb[:, 0:2, :].rearrange("p c (k d) -> p c k d", k=2)[:, :, :, 0:64], vscf[:]),
               reads=["cact", "vsb_all"], writes=[("vsb", 0), ("vsb", 1)])
            op("act", lambda e: e.copy(S_b[:, 1], S_f[:, 1]), reads=["S_f1"], writes=["S_b1"])
            if s >= 1 and "handoff" in PH:
                op("sp", lambda e: e.dma_start(out=rtmp[:], in_=recv_f[0:128, :]), reads=["recv_f"], writes=["cacc"], dma="ld_r1")
                op("sp", lambda e: e.dma_start(out=rtmpb[:], in_=recv_b[0:128, :]), reads=["recv_b"], writes=["sqb"], dma="ld_r2")
                op("dve", lambda e: e.tensor_scalar(S_f[:, 0].rearrange("p h d -> p (h d)"), rtmp[:, 0:512], mcore[:, 0:1], None, ALU.mult),
                   reads=["cacc", "mcore"], writes=["S_f0"])
                op("dve", lambda e: e.tensor_scalar(halo[:, 0].rearrange("p j r -> p (j r)"), rtmp[:, 512:548], mcore[:, 0:1], None, ALU.mult),
                   reads=["cacc", "mcore"], writes=halo_all(0))
                op("dve", lambda e: e.tensor_scalar(kwin[:].rearrange("p a b -> p (a b)"), rtmpb[:, 0:256], mcore[:, 0:1], None, ALU.mult),
                   reads=["sqb", "mcore"], writes=["kwin"])
                op("dve", lambda e: e.tensor_scalar(vbuf[:, 0:2, :].rearrange("p a b -> p (a b)"), rtmpb[0:64, 256:520], mcore[0:64, 0:1], None, ALU.mult),
                   reads=["sqb", "mcore"], writes=[("vbuf", 0), ("vbuf", 1)])
            op("act", lambda e: e.copy(S_b[:, 0], S_f[:, 0]), reads=["S_f0"], writes=["S_b0"])

            for g in range(ngroup):
                tiles = [(g * 1024, 512), (g * 1024 + 512, 512)]
                if g == ngroup - 1:
                    tiles.append((NP_, NS_))
                if "ffn1" in PH:
                    rms_norm_to_hT(tiles, g, 0)
                    ffn(s, tiles, g, 0)
                rms_norm_to_hT(tiles, g, 1)
                if "proj" in PH:
                    proj_in(s, tiles, g)
                chunks = chunk_list(g)
                S.force_small = False
                if "dn" in PH:
                    dn_params(g)
                streams = []
                dnl = [dn_chunk(ci, stq, lc0, C) for ci, (stq, lc0, C) in enumerate(chunks)] if "dn" in PH else None
                swl = [swa_chunk(ci, stq, lc0, C, g) for ci, (stq, lc0, C) in enumerate(chunks)] if "swa" in PH else None

                def emit_merged(lists):
                    items = []
                    for li, L in enumerate(lists):
                        for k, it in enumerate(L):
                            items.append(((k + 0.5) / len(L), li, k, it))
                    items.sort(key=lambda t: (t[0], t[1], t[2]))
                    for _, _, _, (a, k) in items:
                        S.op(*a, **k)
                import os
                if os.environ.get("SEQ", "") == "2":
                    emit_merged([dnl[0][0]])
                    for ci in range(len(chunks)):
                        if ci + 1 < len(chunks):
                            emit_merged([dnl[ci + 1][0]])
                        emit_merged([dnl[ci][1]])
                    dnl = None
                if os.environ.get("SEQ", "0") == "3":
                    for ci in range(len(chunks)):
                        lists = []
                        if dnl is not None:
                            lists.append(dnl[ci][0] + dnl[ci][1])
                        if swl is not None:
                            lists.append(swl[ci])
                        emit_merged([L for L in lists if L])
                    dnl = None
                    swl = None
                if os.environ.get("SEQ", "") == "1":
                    for ci in range(len(chunks)):
                        if dnl is not None:
                            emit_merged([dnl[ci][0]])
                            emit_merged([dnl[ci][1]])
                        if swl is not None:
                            emit_merged([swl[ci]])
                    dnl = None
                    swl = None
                if dnl is not None:
                    emit_merged([dnl[0][0]])
                for ci in range(len(chunks)):
                    lists = []
                    if dnl is not None:
                        lists.append(dnl[ci][1])
                        if ci + 1 < len(chunks):
                            lists.append(dnl[ci + 1][0])
                    if swl is not None:
                        lists.append(swl[ci])
                    emit_merged([L for L in lists if L])
                op("dve", lambda e: e.tensor_copy(kwin[:], U[:, 20:22, 896:1024]),
                   reads=RU(20, 896, 128) + RU(21, 896, 128), writes=["kwin"])
                op("dve", lambda e: e.tensor_copy(vbuf[:, 0:2, :], vbuf[:, 16:18, :]),
                   reads=[("vbuf", 16), ("vbuf", 17)], writes=[("vbuf", 0), ("vbuf", 1)])
                if DUMPU and s == 0:
                    op("sp", lambda e, g=g: e.dma_start(out=dU[g], in_=U[:, 12:20, :]),
                       reads=[("U", j, b) for j in range(12, 20) for b in range(17)], dma="dbgU")
                if "merge" in PH:
                    merge(s, tiles, g)
                if "ffn2" in PH:
                    rms_norm_to_hT(tiles, g, 2)
                    ffn(s, tiles, g, 1)

            op("sp", lambda e, s=s: e.dma_start(out=o_dn[s, 0], in_=S_f[:, 0].rearrange("p h d -> p (h d)")), reads=["S_f0"], dma="o_S0")
            op("sp", lambda e, s=s: e.dma_start(out=o_dn[s, 1], in_=S_f[:, 1].rearrange("p h d -> p (h d)")), reads=["S_f1"], dma="o_S1")
            op("sp", lambda e, s=s: e.dma_start(out=o_conv[s, 0], in_=halo[:, 0].rearrange("p j r -> p (j r)")), reads=halo_all(0), dma="o_h0")
            op("sp", lambda e, s=s: e.dma_start(out=o_conv[s, 1], in_=halo[:, 1].rearrange("p j r -> p (j r)")), reads=halo_all(1), dma="o_h1")
            op("sp", lambda e, s=s: e.dma_start(out=o_kp[s].rearrange("p (a b) -> p a b", a=2), in_=kout[:, :, 0:128]), reads=["kout"], dma="o_k")
            op("sp", lambda e, s=s: e.dma_start(out=o_ks[s].rearrange("p (a b) -> p a b", a=2), in_=kout[:, :, 128:160]), reads=["kout"], dma="o_k")
            op("sp", lambda e, s=s: e.dma_start(out=o_vp[s].rearrange("p (a b) -> p a b", a=2), in_=vout[:, 0:2, :]), reads=["vout"], dma="o_v")
            op("sp", lambda e, s=s: e.dma_start(out=o_vs[s], in_=vout[0:32, 2, :]), reads=["vout"], dma="o_v")
            if s < nslot - 1 and "handoff" in PH:
                op("sp", lambda e: e.dma_start(out=send_f[:, 0:512], in_=S_f[:, 0].rearrange("p h d -> p (h d)")), reads=["S_f0", "recv_f"], writes=["send_f"], dma="snd_f")
                op("sp", lambda e: e.dma_start(out=send_f[:, 512:548], in_=halo[:, 0].rearrange("p j r -> p (j r)")), reads=halo_all(0), writes=["send_f"], dma="snd_f")
                op("sp", lambda e: e.dma_start(out=send_b[:, 0:256], in_=kwin[:].rearrange("p a b -> p (a b)")), reads=["kwin", "recv_b"], writes=["send_b"], dma="snd_b")
                op("sp", lambda e: e.dma_start(out=send_b[0:64, 256:520], in_=vbuf[:, 0:2, :].rearrange("p a b -> p (a b)")), reads=[("vbuf", 0), ("vbuf", 1)], writes=["send_b"], dma="snd_b")
                groups = [[0, 1], [2, 3], [4, 5], [6, 7]]
                op("pool", lambda e: e.collective_compute("AllGather", ALU.bypass, replica_groups=groups, ins=[send_f], outs=[recv_f]),
                   reads=["send_f"], writes=["recv_f"], dma="cc_f", amt=1)
                op("pool", lambda e: e.collective_compute("AllGather", ALU.bypass, replica_groups=groups, ins=[send_b], outs=[recv_b]),
                   reads=["send_b"], writes=["recv_b"], dma="cc_b", amt=1)
        op("sp", lambda e: e.dma_start(out=yT.rearrange("j p t -> p j t"), in_=xT[:]), reads=[("xT", b) for b in range(5)], dma="out")
        nops = S.emit(final_waits=(["dbgU"] if DUMPU else []) + ["out", "o_S0", "o_S1", "o_h0", "o_h1", "o_k", "o_v"])
    return nc, nops


def _consts():
    cf = np.zeros((128, 1920), np.float32)
    cf[:, 0:128] = np.eye(128, dtype=np.float32)
    p = np.arange(64)[:, None]
    j = np.arange(64)[None, :]
    cf[0:64, 128:192] = (p > j)
    cf[0:64, 192:256] = (p <= j)
    cf[0:64, 256:384] = 1.0
    slopes = 2.0 ** (-8.0 * np.arange(1, 9, dtype=np.float32) / 8)
    s_ = np.arange(64)[:, None].astype(np.float32)
    i_ = np.arange(64)[None, :].astype(np.float32)
    BT = np.zeros((64, 3, 8, 64), np.float32)
    for dl in range(3):
        dist = np.abs(i_ + 64 * dl - s_)
        for h in range(8):
            BT[:, dl, (h % 2) * 4 + h // 2, :] = -slopes[h] * dist
    cf[0:64, 384:1920] = BT.reshape(64, -1)
    return cf


def _slot_stack(arr, odd):
    z = np.zeros((1,) + arr.shape[1:], arr.dtype)
    return np.ascontiguousarray(np.concatenate([z, arr] if odd else [arr, z], axis=0))


def kernel(**inp):
    f = lambda k: np.asarray(inp[k], dtype=np.float32)
    x_prompt, x_sample = f("x_prompt"), f("x_sample")
    state_dn, state_conv = f("state_dn"), f("state_conv")
    cache_k, cache_v = f("cache_swa_k"), f("cache_swa_v")
    w_in = f("w_in")
    idx = np.concatenate([np.arange(0, 1536), np.arange(1536, 2048), np.arange(2056, 2568),
                          np.arange(2568, 2632), np.arange(2568, 2632), np.arange(2632, 2696), np.arange(2632, 2696),
                          np.arange(2696, 2824), np.arange(2048, 2056), np.arange(2824, 3848), np.arange(3848, 4872)])
    assert idx.size == WIN_COLS
    win_r = w_in[:, :, idx]
    prm = np.zeros((DEPTH, 128, NPRM), np.float32)
    for i, k in enumerate(["ffn1_norm", "mix_norm", "ffn2_norm"]):
        prm[:, :, P_NW + i * 8:P_NW + i * 8 + 8] = f(k).reshape(DEPTH, 8, 128).transpose(0, 2, 1)
    prm[:, :, P_CW:P_CW + 48] = f("conv_w").reshape(DEPTH, 4, 12, 128).transpose(0, 3, 2, 1).reshape(DEPTH, 128, 48)
    prm[:, :, P_DNW:P_DNW + 128] = f("dn_norm")[:, None, :]
    prm[:, :, P_QNW] = np.tile(f("q_norm"), (1, 2))
    prm[:, :, P_KNW] = np.tile(f("k_norm"), (1, 2))
    prm[:, :, P_ALOG:P_ALOG + 4] = f("a_log")[:, None, :]
    prm[:, :, P_DTB:P_DTB + 4] = f("dt_bias")[:, None, :]
    prm[:, :, P_SINK:P_SINK + 8] = f("sinks")[:, None, :]
    big = {"wg1": f("ffn1_wg"), "wu1": f("ffn1_wu"), "wd1": f("ffn1_wd"), "wg2": f("ffn2_wg"), "wu2": f("ffn2_wu"),
           "wd2": f("ffn2_wd"), "win": win_r, "wodn": f("w_o_dn"), "woswa": f("w_o_swa"), "wout": f("w_out"), "prm": prm}
    stacks = [{k: _slot_stack(v, odd) for k, v in big.items()} for odd in (False, True)]
    cf = _consts()
    in_maps = []
    for c in range(8):
        b, odd = c // 2, c % 2
        m = dict(stacks[odd])
        xt = np.concatenate([x_prompt[b, odd * NP_:(odd + 1) * NP_], x_sample[c]], axis=0)
        m["xT0"] = np.ascontiguousarray(xt.T.reshape(8, 128, NT))
        sdn = state_dn[:, c].transpose(0, 2, 1, 3).reshape(DEPTH, 128, 512)
        sconv = state_conv[:, c].reshape(DEPTH, 3, 12, 128).transpose(0, 3, 2, 1).reshape(DEPTH, 128, 36)
        ck = cache_k[:, c].transpose(0, 2, 3, 1)
        skc = np.concatenate([ck, ck], axis=2).transpose(0, 2, 1, 3).reshape(DEPTH, 128, 256)
        svc = cache_v[:, c].reshape(DEPTH, 2, 64, 2, 64).transpose(0, 2, 1, 3, 4)
        m["sdn"] = _slot_stack(sdn, odd)
        m["sconv"] = _slot_stack(sconv, odd)
        m["skc"] = _slot_stack(skc, odd)
        m["svc"] = _slot_stack(svc, odd)
        m["mcore"] = np.full((128, 1), float(odd), np.float32)
        m["cf"] = cf
        in_maps.append(m)
    nc, _ = build_nc()
    res = run_bass_kernel_spmd(nc, in_maps, core_ids=list(range(8)))
    R = res.results
    y_prompt = np.zeros((4, 4096, D), np.float32)
    y_sample = np.zeros((8, NS_, D), np.float32)
    dn_prompt = np.zeros((DEPTH, 4, 4, 128, 128), np.float32)
    dn_sample = np.zeros((DEPTH, 8, 4, 128, 128), np.float32)
    conv_prompt = np.zeros((DEPTH, 4, 3, 1536), np.float32)
    conv_sample = np.zeros((DEPTH, 8, 3, 1536), np.float32)
    kp = np.zeros((DEPTH, 4, 128, 2, 64), np.float32)
    vp = np.zeros((DEPTH, 4, 128, 2, 64), np.float32)
    ks = np.zeros((DEPTH, 8, NS_, 2, 64), np.float32)
    vs = np.zeros((DEPTH, 8, NS_, 2, 64), np.float32)
    for c in range(8):
        b, odd = c // 2, c % 2
        r = R[c]
        yt = r["yT"].reshape(D, NT).T
        y_prompt[b, odd * NP_:(odd + 1) * NP_] = yt[0:NP_]
        y_sample[c] = yt[NP_:]
        for l in range(DEPTH):
            s = l + odd
            dn_sample[l, c] = r["o_dn"][s, 1].reshape(128, 4, 128).transpose(1, 0, 2)
            conv_sample[l, c] = r["o_conv"][s, 1].reshape(128, 12, 3).transpose(2, 1, 0).reshape(3, 1536)
            ks[l, c] = r["o_ks"][s].reshape(128, 2, 32)[0:64].transpose(2, 1, 0)
            vs[l, c] = r["o_vs"][s].reshape(32, 2, 64)
            if odd:
                dn_prompt[l, b] = r["o_dn"][s, 0].reshape(128, 4, 128).transpose(1, 0, 2)
                conv_prompt[l, b] = r["o_conv"][s, 0].reshape(128, 12, 3).transpose(2, 1, 0).reshape(3, 1536)
                kp[l, b] = r["o_kp"][s].reshape(128, 2, 128)[0:64].transpose(2, 1, 0)
                vp[l, b] = r["o_vp"][s].reshape(64, 2, 2, 64).transpose(1, 0, 2, 3).reshape(128, 2, 64)
    return (y_prompt, y_sample, dn_prompt, dn_sample, conv_prompt, conv_sample, kp, vp, ks, vs)
```

```python
import contextlib
import numpy as np
import concourse.bass as bass
import concourse.mybir as mybir
from concourse.bass_utils import run_bass_kernel_spmd

F32 = mybir.dt.float32
BF16 = mybir.dt.bfloat16
ALU = mybir.AluOpType
AF = mybir.ActivationFunctionType
AX = mybir.AxisListType

D = 1024
DEPTH = 4
NSLOT = DEPTH + 1
NP_ = 2048
NS_ = 32
NT = NP_ + NS_
GT = 1056
DFF = 2816
WIN_COLS = 5000
C_QKV, C_GATE, C_QS, C_KD, C_VBA, C_GA, C_GB = 0, 1536, 2048, 2560, 2816, 2952, 3976
NPRM = 218
P_NW, P_CW, P_DNW, P_QNW, P_KNW, P_ALOG, P_DTB, P_SINK = 0, 24, 72, 200, 201, 202, 206, 210
RMS_EPS = 1e-6
L2_EPS = 1e-6
WSLOT = 5632


class Sched:
    ENGS = ("pe", "act", "dve", "pool", "sp")

    def __init__(self, nc):
        self.nc = nc
        self.ops = []
        self.lastw = {}
        self.readers = {}
        self.chan_tot = {}
        self.bank_rd = {}
        self.force_small = False

    def op(self, eng, fn, reads=(), writes=(), dma=None, amt=16, small=False):
        i = len(self.ops)
        deps = set()
        raw = set()
        for r in reads:
            w = self.lastw.get(r)
            if w is not None:
                deps.add(w)
                raw.add(w)
        for r in writes:
            w = self.lastw.get(r)
            if w is not None:
                deps.add(w)
            q = self.readers.get(r)
            if q:
                deps.update(q)
        for r in reads:
            self.readers.setdefault(r, []).append(i)
            if isinstance(r, tuple) and r[0] == "ps":
                br = self.bank_rd.setdefault(r[1], {})
                for e2, j in br.items():
                    if e2 != eng:
                        deps.add(j)
                br[eng] = i
        for r in writes:
            self.lastw[r] = i
            self.readers[r] = []
        o = dict(eng=eng, fn=fn, deps=deps, dma=dma, inc=False, cnt=None, amt=amt, small=(small or self.force_small), raw=raw)
        if dma is not None:
            self.chan_tot[dma] = self.chan_tot.get(dma, 0) + amt
            o["cnt"] = self.chan_tot[dma]
            o["inc"] = True
        self.ops.append(o)
        return i

    def emit(self, final_waits=()):
        nc = self.nc
        ops = self.ops
        waited = {e: {} for e in self.ENGS}
        for i, o in enumerate(ops):
            e = o["eng"]
            ws = []
            for d in sorted(o["deps"]):
                p = ops[d]
                if p["dma"] is not None:
                    key = ("ch", p["dma"])
                    if waited[e].get(key, -1) >= p["cnt"]:
                        continue
                    waited[e][key] = p["cnt"]
                    ws.append(d)
                else:
                    if p["eng"] == e and o["dma"] is None and not (p["small"] and d in o["raw"]):
                        continue
                    key = ("en", p["eng"])
                    if waited[e].get(key, -1) >= d:
                        continue
                    waited[e][key] = d
                    p["inc"] = True
                    ws.append(d)
            o["waits"] = ws
        cnt = {e: 0 for e in self.ENGS}
        for o in ops:
            if o["dma"] is None and o["inc"]:
                cnt[o["eng"]] += 1
                o["cnt"] = cnt[o["eng"]]
        chans = sorted(self.chan_tot)
        with contextlib.ExitStack() as st:
            esem = {e: st.enter_context(nc.semaphore("se_" + e)) for e in self.ENGS}
            csem = {c: st.enter_context(nc.semaphore("sc_" + c)) for c in chans}
            block = st.enter_context(nc.Block())
            handles = {"pe": "tensor", "act": "scalar", "dve": "vector", "pool": "gpsimd", "sp": "sync"}

            def make(e):
                def body(eng):
                    for o in ops:
                        if o["eng"] != e:
                            continue
                        best = {}
                        for d in o["waits"]:
                            p = ops[d]
                            key = ("ch", p["dma"]) if p["dma"] is not None else ("en", p["eng"])
                            if best.get(key, -1) < p["cnt"]:
                                best[key] = p["cnt"]
                        for key, v in best.items():
                            sem = csem[key[1]] if key[0] == "ch" else esem[key[1]]
                            eng.wait_ge(sem, v)
                        ins = o["fn"](eng)
                        if o["dma"] is not None:
                            ins.then_inc(csem[o["dma"]], o["amt"])
                        elif o["inc"]:
                            ins.then_inc(esem[e], 1)
                    if e == "sp":
                        for c in final_waits:
                            eng.wait_ge(csem[c], self.chan_tot[c])
                return body

            for e in self.ENGS:
                getattr(block, handles[e])(make(e))
        return len(ops)


def build_nc(nslot=NSLOT, ngroup=2, phases=None):
    PH = phases or {"ffn1", "proj", "dn", "swa", "merge", "ffn2", "handoff"}
    if "proj" in PH and not (PH & {"p_qkv", "p_gate", "p_qk", "p_vba"}):
        PH = PH | {"p_qkv", "p_gate", "p_qk", "p_vba"}
    NSLOT = nslot
    nc = bass.Bass("TRN2", target_bir_lowering=False)

    def din(name, shape, dt=F32):
        return nc.dram_tensor(name, list(shape), dt, kind="ExternalInput").ap()

    def dout(name, shape, dt=F32):
        return nc.dram_tensor(name, list(shape), dt, kind="ExternalOutput").ap()

    xT0 = din("xT0", [8, 128, NT])
    wg = [din("wg1", [NSLOT, D, DFF]), din("wg2", [NSLOT, D, DFF])]
    wu = [din("wu1", [NSLOT, D, DFF]), din("wu2", [NSLOT, D, DFF])]
    wd = [din("wd1", [NSLOT, DFF, D]), din("wd2", [NSLOT, DFF, D])]
    win = din("win", [NSLOT, D, WIN_COLS])
    wodn = din("wodn", [NSLOT, 512, D])
    woswa = din("woswa", [NSLOT, 512, D])
    wout = din("wout", [NSLOT, D, D])
    prm_d = din("prm", [NSLOT, 128, NPRM])
    sdn_d = din("sdn", [NSLOT, 128, 512])
    sconv_d = din("sconv", [NSLOT, 128, 36])
    skc_d = din("skc", [NSLOT, 128, 256])
    svc_d = din("svc", [NSLOT, 64, 2, 2, 64])
    mcore_d = din("mcore", [128, 1])
    cf_d = din("cf", [128, 128 + 64 + 64 + 128 + 1536])
    yT = dout("yT", [8, 128, NT])
    o_dn = dout("o_dn", [NSLOT, 2, 128, 512])
    o_conv = dout("o_conv", [NSLOT, 2, 128, 36])
    o_kp = dout("o_kp", [NSLOT, 128, 256])
    o_ks = dout("o_ks", [NSLOT, 128, 64])
    o_vp = dout("o_vp", [NSLOT, 64, 256])
    o_vs = dout("o_vs", [NSLOT, 32, 128])
    import os
    DUMPU = os.environ.get("DUMPU", "") == "1"
    if DUMPU:
        dU = dout("dU", [2, 128, 8, GT], BF16)
    send_f = nc.dram_tensor("send_f", [128, 548], F32).ap()
    recv_f = nc.dram_tensor("recv_f", [256, 548], F32).ap()
    send_b = nc.dram_tensor("send_b", [128, 520], BF16).ap()
    recv_b = nc.dram_tensor("recv_b", [256, 520], BF16).ap()

    S = Sched(nc)
    with contextlib.ExitStack() as st:
        def sb(name, shape, dt=F32):
            return st.enter_context(nc.sbuf_tensor("s_" + name, list(shape), dt))

        def psum(name, shape, dt=F32):
            return st.enter_context(nc.psum_tensor(name, list(shape), dt))

        xT = sb("xT", [128, 8, NT])
        hT = sb("hT", [128, 8, GT], BF16)
        U = sb("U", [128, 22, GT], BF16)
        wring = sb("wring", [128, 2, WSLOT], BF16)
        kwin = sb("kwin", [128, 2, 128], BF16)
        ksc = sb("ksc", [128, 2, 128], BF16)
        vbuf = sb("vbuf", [64, 18, 132], BF16)
        vsb = sb("vsb", [64, 3, 132], BF16)
        S_f = sb("S_f", [128, 2, 4, 128])
        S_b = sb("S_b", [128, 2, 4, 128], BF16)
        halo = sb("halo", [128, 2, 12, 3])
        stage = sb("stage", [128, 520])
        cacc = sb("cacc", [128, 548])
        cact = sb("cact", [128, 512])
        rstd = sb("rstd", [128, 512])
        sqb = sb("sqb", [128, 520], BF16)
        sigb = [cacc, cact]
        SG = ["cacc", "cact"]
        kscf = cact[:, 0:256]
        vscf = cact[0:64, 256:512].rearrange("p (a b c) -> p a b c", a=2, b=2)
        cf = sb("cf", [128, 1920])
        cfb = sb("cfb", [128, 384], BF16)
        prm = sb("prm", [128, NPRM])
        prm2 = sb("prm2", [128, 16])
        mcore = sb("mcore", [128, 1])
        kout = sb("kout", [128, 2, 160])
        vout = sb("vout", [64, 3, 128])
        rtmp = cacc
        rtmpb = sqb
        zba = sb("zba", [64, 17, 8])
        tp_beta = sb("tp_beta", [64, 17, 4])
        tp_negb = sb("tp_negb", [64, 17, 4])
        tp_g = sb("tp_g", [64, 17, 4])
        tp_gc = sb("tp_gc", [64, 17, 4])
        tp_egc = sb("tp_egc", [64, 17, 4])
        tp_ekr = sb("tp_ekr", [64, 17, 4])
        tp_egl = sb("tp_egl", [128, 17, 4])
        tp_t = sb("tp_t", [64, 17, 4])
        Gm = sb("Gm", [64, 4, 64])
        dnA = sb("dnA", [64, 4, 128])
        DTs = dnA[:, :, 0:64]
        DTu = dnA[:, :, 64:128]
        tA = Gm
        Pk0 = sb("Pk0", [64, 4, 64], BF16)
        PTk0 = sb("PTk0", [64, 4, 64], BF16)
        Rkb = sb("Rkb", [64, 4, 64], BF16)
        Pk = [Pk0, Pk0]
        PTk = [PTk0, PTk0]
        Rk0 = sb("Rk0", [64, 4, 64])
        Rk = [Rk0, Rk0]
        Rbf = sb("Rbf", [64, 4, 64], BF16)
        inT2 = [sb("inT0", [64, 4, 64], BF16), sb("inT1", [64, 4, 64], BF16)]
        kg = sb("kg", [64, 4, 128], BF16)
        kr2 = [sb("kr0", [64, 4, 128], BF16), sb("kr1", [64, 4, 128], BF16)]
        vtm = sb("vtm", [64, 4, 128], BF16)
        u_sb2 = [sb("u_sb", [64, 4, 128]), cact[0:64, 0:512].rearrange("p (h d) -> p h d", h=4)]
        w0T2 = [sb("w0T0", [128, 4, 64], BF16), sb("w0T1", [128, 4, 64], BF16)]
        vnew = sb("vnew", [64, 4, 128], BF16)
        o_sb = rstd[0:64, 0:512].rearrange("p (h d) -> p h d", h=4)
        on2 = sqb[0:64, 0:512].rearrange("p (h d) -> p h d", h=4)
        ss4 = sb("ss4", [64, 8])
        tS = sb("tS", [128, 4, 128])
        sct = stage[0:64, 0:512].rearrange("p (h d) -> p h d", h=8)
        t1 = cacc[0:64, 0:512].rearrange("p (h d) -> p h d", h=4)
        PTs = [sb("PT0", [64, 8, 64], BF16), sb("PT1", [64, 8, 64], BF16), sb("PT2", [64, 8, 64], BF16)]
        PTn = ["PT0", "PT1", "PT2"]
        den = sb("den", [64, 8])
        osw = sb("osw", [64, 8, 64], BF16)

        pf = [psum("pf%d" % i, [128, 512]) for i in range(7)]
        pb = psum("pb", [128, 2, 512], BF16)

        ident_f = cf[:, 0:128]
        SL = cf[0:64, 128:192]
        UT = cf[0:64, 192:256]
        ones_f = cf[0:64, 256:384]
        BT = cf[0:64, 384:1920].rearrange("p (a h i) -> p a h i", a=3, h=8)
        ident_b = cfb[:, 0:128]
        ones_b = cfb[:, 128:256]
        blk_b = cfb[:, 256:384]

        op = S.op

        def T_(tiles):
            for (c0, n) in tiles:
                S.force_small = (n < 128)
                yield (c0, n)
            S.force_small = False

        def RU(j, lc0, n):
            return [("U", j, b) for b in range(lc0 // 64, (lc0 + n - 1) // 64 + 1)]

        def RH(lc0, n):
            return [("hT", b) for b in range(lc0 // 512, (lc0 + n - 1) // 512 + 1)]

        def RX(c0, n):
            return [("xT", b) for b in range(c0 // 512, (c0 + n - 1) // 512 + 1)]

        bank_rr = {"i": 0}

        op("sp", lambda e: e.dma_start(out=xT[:], in_=xT0.rearrange("j p t -> p j t")),
           writes=[("xT", b) for b in range(5)], dma="ld_x")
        op("sp", lambda e: e.dma_start(out=cf[:], in_=cf_d), writes=["cf"], dma="ld_cf")
        op("sp", lambda e: e.dma_start(out=mcore[:], in_=mcore_d), writes=["mcore"], dma="ld_mc")
        op("dve", lambda e: e.tensor_copy(cfb[:, 0:128], cf[:, 0:128]), reads=["cf"], writes=["cfb"])
        op("dve", lambda e: e.memset(cfb[:, 128:256], 1.0), writes=["cfb"])
        op("dve", lambda e: e.memset(cfb[:, 256:384], 0.0), writes=["cfb"])
        op("dve", lambda e: e.memset(cfb[0:64, 256:320], 1.0), writes=["cfb"])
        op("dve", lambda e: e.memset(cfb[64:128, 320:384], 1.0), writes=["cfb"])
        op("dve", lambda e: e.memset(vbuf[:], 1.0), writes=["vbuf_all"])
        op("dve", lambda e: e.memset(vsb[:], 1.0), writes=["vsb_all"])
        op("dve", lambda e: e.memset(zba[:], 0.0), writes=["zba"])
        op("dve", lambda e: e.memset(S_f[:, 0], 0.0), writes=["S_f0"])
        op("dve", lambda e: e.memset(halo[:, 0], 0.0), writes=[("halo", 0, j) for j in range(12)])
        op("dve", lambda e: e.memset(kwin[:], 0.0), writes=["kwin"])
        op("dve", lambda e: e.memset(vbuf[:, 0:2, 0:64], 0.0), reads=["vbuf_all"], writes=[("vbuf", 0), ("vbuf", 1)])
        op("dve", lambda e: e.memset(vbuf[:, 0:2, 66:130], 0.0), writes=[("vbuf", 0), ("vbuf", 1)])

        wstate = {"n": 0}

        def load_panel(src2d, K, ncols):
            KC = K // 128
            s = wstate["n"] % 2
            wstate["n"] += 1
            view = wring[:, s, 0:KC * ncols].rearrange("p (k n) -> p k n", k=KC)
            op("pool", lambda e: e.dma_start(out=view, in_=src2d.rearrange("(k p) n -> p k n", p=128)),
               writes=[("wr", s)], dma="w%d" % s)
            return view, ("wr", s)

        def load_panels(specs):
            s_ = wstate["n"] % 2
            wstate["n"] += 1
            off = 0
            views = []
            for (src2d, K, ncols) in specs:
                KC = K // 128
                view = wring[:, s_, off:off + KC * ncols].rearrange("p (k n) -> p k n", k=KC)
                off += KC * ncols
                assert off <= WSLOT
                op("pool", lambda e, view=view, src2d=src2d: e.dma_start(out=view, in_=src2d.rearrange("(k p) n -> p k n", p=128)),
                   writes=[("wr", s_)], dma="w%d" % s_)
                views.append(view)
            return views, ("wr", s_)

        def next_bank(cands):
            b = cands[bank_rr["i"] % len(cands)]
            bank_rr["i"] += 1
            return b

        def rms_norm_to_hT(tiles, g, nwi):
            stA, stB = [], []
            for ti, (c0, n) in enumerate(tiles):
                A, B = [], []
                fs = (n < 128)
                lc0 = c0 - g * 1024
                hview = hT[:, :, lc0:lc0 + n]
                rb = rstd if ti % 2 == 0 else cacc
                rn = "rstd" if ti % 2 == 0 else "cacc"
                bk = 4 if ti % 2 == 0 else 5
                ps = pf[bk]
                A.append((("act", lambda e, hview=hview, c0=c0, n=n: e.activation(hview, xT[:, :, c0:c0 + n], AF.Square)),
                          dict(reads=RX(c0, n), writes=RH(lc0, n), small=fs)))
                for k in range(8):
                    A.append((("pe", lambda e, k=k, lc0=lc0, n=n, ps=ps: e.matmul(ps[:, 0:n], ones_b, hT[:, k, lc0:lc0 + n],
                                                                                  start=(k == 0), stop=(k == 7))),
                              dict(reads=RH(lc0, n) + ["cfb"], writes=[("ps", bk)], small=fs)))
                B.append((("act", lambda e, n=n, ps=ps, rb=rb: e.activation(rb[:, 0:n], ps[:, 0:n], AF.Ln, bias=RMS_EPS, scale=1.0 / D)),
                          dict(reads=[("ps", bk)], writes=[rn], small=fs)))
                B.append((("act", lambda e, n=n, rb=rb: e.activation(rb[:, 0:n], rb[:, 0:n], AF.Exp, scale=-0.5)),
                          dict(reads=[rn], writes=[rn], small=fs)))
                for k in range(8):
                    B.append((("dve", lambda e, k=k, c0=c0, lc0=lc0, n=n, rb=rb: e.scalar_tensor_tensor(
                        hT[:, k, lc0:lc0 + n], xT[:, k, c0:c0 + n], prm[:, P_NW + nwi * 8 + k:P_NW + nwi * 8 + k + 1],
                        rb[:, 0:n], ALU.mult, ALU.mult)),
                        dict(reads=RX(c0, n) + [rn, "prm"], writes=RH(lc0, n), small=fs)))
                stA.append(A)
                stB.append(B)
            for i in range(len(stA)):
                if i == 0:
                    for (a, k) in stA[0]:
                        S.op(*a, **k)
                if i + 1 < len(stA):
                    for (a, k) in stA[i + 1]:
                        S.op(*a, **k)
                for (a, k) in stB[i]:
                    S.op(*a, **k)

        def ffn(s, tiles, g, fi):
            for f0 in range(0, DFF, 256):
                ncols = 256
                (vg, vu), rg = load_panels([(wg[fi][s, :, f0:f0 + ncols], D, ncols), (wu[fi][s, :, f0:f0 + ncols], D, ncols)])
                ru = rg
                for jj in range(ncols // 128):
                    f = f0 // 128 + jj
                    for (c0, n) in T_(tiles):
                        lc0 = c0 - g * 1024
                        bg = next_bank([0, 1])
                        bu = bg + 2
                        for k in range(8):
                            op("pe", lambda e, k=k, jj=jj, lc0=lc0, n=n, bg=bg, vg=vg: e.matmul(
                                pf[bg][:, 0:n], vg[:, k, jj * 128:(jj + 1) * 128], hT[:, k, lc0:lc0 + n],
                                start=(k == 0), stop=(k == 7)),
                               reads=RH(lc0, n) + [rg], writes=[("ps", bg)])
                        for k in range(8):
                            op("pe", lambda e, k=k, jj=jj, lc0=lc0, n=n, bu=bu, vu=vu: e.matmul(
                                pf[bu][:, 0:n], vu[:, k, jj * 128:(jj + 1) * 128], hT[:, k, lc0:lc0 + n],
                                start=(k == 0), stop=(k == 7)),
                               reads=RH(lc0, n) + [ru], writes=[("ps", bu)])
                        sl = bg
                        op("act", lambda e, n=n, bg=bg, sl=sl: e.activation(sigb[sl][:, 0:n], pf[bg][:, 0:n], AF.Silu),
                           reads=[("ps", bg)], writes=[SG[sl]])
                        op("dve", lambda e, n=n, bu=bu, sl=sl, f=f, lc0=lc0: e.tensor_tensor(
                            U[:, f, lc0:lc0 + n], sigb[sl][:, 0:n], pf[bu][:, 0:n], ALU.mult),
                           reads=[SG[sl], ("ps", bu)], writes=RU(f, lc0, n))
            for ob in range(4):
                vd, rd = load_panel(wd[fi][s, :, ob * 256:(ob + 1) * 256], DFF, 256)
                for jj in range(2):
                    o = ob * 2 + jj
                    for (c0, n) in T_(tiles):
                        lc0 = c0 - g * 1024
                        b = next_bank([0, 1, 2, 3])
                        for k in range(22):
                            op("pe", lambda e, k=k, jj=jj, lc0=lc0, n=n, b=b, vd=vd: e.matmul(
                                pf[b][:, 0:n], vd[:, k, jj * 128:(jj + 1) * 128], U[:, k, lc0:lc0 + n],
                                start=(k == 0), stop=(k == 21)),
                               reads=RU(k, lc0, n) + [rd], writes=[("ps", b)])
                        op("dve", lambda e, o=o, c0=c0, n=n, b=b: e.scalar_tensor_tensor(
                            xT[:, o, c0:c0 + n], pf[b][:, 0:n], 0.5, xT[:, o, c0:c0 + n], ALU.mult, ALU.add),
                           reads=[("ps", b)] + RX(c0, n), writes=RX(c0, n))

        def rsqrt_chain(ps_ap, n, scale, eps):
            op("act", lambda e: e.activation(rstd[:, 0:n], ps_ap, AF.Ln, bias=eps, scale=scale),
               reads=[("ps", 4)], writes=["rstd"])
            op("act", lambda e: e.activation(rstd[:, 0:n], rstd[:, 0:n], AF.Exp, scale=-0.5),
               reads=["rstd"], writes=["rstd"])

        def proj_in(s, tiles, g):
            W = win
            tSf = tS[:].rearrange("p h d -> p (h d)")
            gop = S.op
            stA, stB = [], []
            itn = [0]
            for pb_ in (range(3) if "p_qkv" in PH else []):
                first_in_panel = [True]
                sl_ = wstate["n"] % 2
                wstate["n"] += 1
                v = wring[:, sl_, 0:8 * 512].rearrange("p (k n) -> p k n", k=8)
                r = ("wr", sl_)
                src2d = W[s, :, C_QKV + pb_ * 512:C_QKV + (pb_ + 1) * 512]
                ld = (("pool", lambda e, v=v, src2d=src2d: e.dma_start(out=v, in_=src2d.rearrange("(k p) n -> p k n", p=128))),
                      dict(writes=[r], dma="w%d" % sl_))
                for jj in range(4):
                    j = pb_ * 4 + jj
                    for (c0, n) in tiles:
                        A, B = [], []
                        fs = (n < 128)

                        def opA(*a, **k):
                            k["small"] = k.get("small", False) or fs
                            A.append((a, k))

                        def opB(*a, **k):
                            k["small"] = k.get("small", False) or fs
                            B.append((a, k))
                        if first_in_panel[0]:
                            A.append(ld)
                            first_in_panel[0] = False
                        par = itn[0] % 2
                        itn[0] += 1
                        cb_ = cacc if par == 0 else tSf
                        cn_ = "cacc" if par == 0 else "tS"
                        lc0 = c0 - g * 1024
                        stq = 1 if c0 >= NP_ else 0
                        b = next_bank([0, 1, 2, 3])
                        for k in range(8):
                            opA("pe", lambda e, k=k, jj=jj, lc0=lc0, n=n, b=b, v=v: e.matmul(
                                pf[b][:, 0:n], v[:, k, jj * 128:(jj + 1) * 128], hT[:, k, lc0:lc0 + n],
                                start=(k == 0), stop=(k == 7)),
                                reads=RH(lc0, n) + [r], writes=[("ps", b)])
                        opA("dve", lambda e, stq=stq, j=j: e.tensor_copy(stage[:, 0:3], halo[:, stq, j, :]),
                            reads=[("halo", stq, j)], writes=["stage"], small=True)
                        opA("act", lambda e, n=n, b=b: e.copy(stage[:, 3:3 + n], pf[b][:, 0:n]),
                            reads=[("ps", b)], writes=["stage"])
                        opA("dve", lambda e, stq=stq, j=j, n=n: e.tensor_copy(halo[:, stq, j, :], stage[:, n:n + 3]),
                            reads=["stage"], writes=[("halo", stq, j)], small=True)
                        cw0 = P_CW + j * 4
                        opA("dve", lambda e, n=n, cw0=cw0, cb_=cb_: e.tensor_scalar(
                            cb_[:, 0:n], stage[:, 0:n], prm[:, cw0:cw0 + 1], None, ALU.mult),
                            reads=["stage", "prm"], writes=[cn_])
                        for t in range(1, 4):
                            opA("dve", lambda e, n=n, cw0=cw0, t=t, cb_=cb_: e.scalar_tensor_tensor(
                                cb_[:, 0:n], stage[:, t:t + n], prm[:, cw0 + t:cw0 + t + 1], cb_[:, 0:n],
                                ALU.mult, ALU.add),
                                reads=["stage", "prm", cn_], writes=[cn_])
                        if j >= 8:
                            opB("act", lambda e, n=n, j=j, lc0=lc0, cb_=cb_: e.activation(U[:, j, lc0:lc0 + n], cb_[:, 0:n], AF.Silu),
                                reads=[cn_], writes=RU(j, lc0, n))
                        else:
                            opB("act", lambda e, n=n, cb_=cb_: e.activation(cact[:, 0:n], cb_[:, 0:n], AF.Silu),
                                reads=[cn_], writes=["cact"])
                            opB("act", lambda e, n=n: e.activation(sqb[:, 0:n], cact[:, 0:n], AF.Square),
                                reads=["cact"], writes=["sqb"])
                            opB("pe", lambda e, n=n: e.matmul(pf[4][:, 0:n], ones_b, sqb[:, 0:n], start=True, stop=True),
                                reads=["sqb", "cfb"], writes=[("ps", 4)])
                            opB("act", lambda e, n=n: e.activation(rstd[:, 0:n], pf[4][:, 0:n], AF.Ln, bias=L2_EPS, scale=1.0),
                                reads=[("ps", 4)], writes=["rstd"])
                            opB("act", lambda e, n=n: e.activation(rstd[:, 0:n], rstd[:, 0:n], AF.Exp, scale=-0.5),
                                reads=["rstd"], writes=["rstd"])
                            qs = (128.0 ** -0.5) if j < 4 else 1.0
                            opB("dve", lambda e, n=n, j=j, lc0=lc0, qs=qs: e.scalar_tensor_tensor(
                                U[:, j, lc0:lc0 + n], cact[:, 0:n], qs, rstd[:, 0:n], ALU.mult, ALU.mult),
                                reads=["cact", "rstd"], writes=RU(j, lc0, n))
                        stA.append(A)
                        stB.append(B)
            for i in range(len(stA)):
                if i == 0:
                    for (a, k) in stA[0]:
                        gop(*a, **k)
                if i + 1 < len(stA):
                    for (a, k) in stA[i + 1]:
                        gop(*a, **k)
                for (a, k) in stB[i]:
                    gop(*a, **k)
            v, r = load_panel(W[s, :, C_GATE:C_GATE + 512], D, 512)
            for jj in (range(4) if "p_gate" in PH else []):
                for (c0, n) in T_(tiles):
                    lc0 = c0 - g * 1024
                    b = next_bank([0, 1, 2, 3])
                    for k in range(8):
                        op("pe", lambda e, k=k, jj=jj, lc0=lc0, n=n, b=b, v=v: e.matmul(
                            pf[b][:, 0:n], v[:, k, jj * 128:(jj + 1) * 128], hT[:, k, lc0:lc0 + n],
                            start=(k == 0), stop=(k == 7)),
                           reads=RH(lc0, n) + [r], writes=[("ps", b)])
                    op("act", lambda e, n=n, b=b, jj=jj, lc0=lc0: e.activation(U[:, 12 + jj, lc0:lc0 + n], pf[b][:, 0:n], AF.Silu),
                       reads=[("ps", b)], writes=RU(12 + jj, lc0, n))
            for (cbase, nch, ubase, pcol) in ([(C_QS, 4, 16, "q"), (C_KD, 2, 20, "k")] if "p_qk" in PH else []):
                v, r = load_panel(W[s, :, cbase:cbase + nch * 128], D, nch * 128)
                for jj in range(nch):
                    for (c0, n) in T_(tiles):
                        lc0 = c0 - g * 1024
                        b = next_bank([0, 1, 2, 3])
                        for k in range(8):
                            op("pe", lambda e, k=k, jj=jj, lc0=lc0, n=n, b=b, v=v: e.matmul(
                                pf[b][:, 0:n], v[:, k, jj * 128:(jj + 1) * 128], hT[:, k, lc0:lc0 + n],
                                start=(k == 0), stop=(k == 7)),
                               reads=RH(lc0, n) + [r], writes=[("ps", b)])
                        op("act", lambda e, n=n, b=b: e.activation(sqb[:, 0:n], pf[b][:, 0:n], AF.Square),
                           reads=[("ps", b)], writes=["sqb"])
                        op("pe", lambda e, n=n: e.matmul(pf[4][:, 0:n], blk_b, sqb[:, 0:n], start=True, stop=True),
                           reads=["sqb", "cfb"], writes=[("ps", 4)])
                        rsqrt_chain(pf[4][:, 0:n], n, 1.0 / 64, RMS_EPS)
                        sc = prm2[:, 0:1] if pcol == "q" else prm[:, P_KNW:P_KNW + 1]
                        op("dve", lambda e, n=n, b=b, jj=jj, lc0=lc0, sc=sc, ubase=ubase: e.scalar_tensor_tensor(
                            U[:, ubase + jj, lc0:lc0 + n], pf[b][:, 0:n], sc, rstd[:, 0:n], ALU.mult, ALU.mult),
                           reads=[("ps", b), "rstd", "prm", "prm2"], writes=RU(ubase + jj, lc0, n))
                        if pcol == "k":
                            if c0 >= NP_:
                                op("dve", lambda e, n=n, b=b, jj=jj, sc=sc: e.scalar_tensor_tensor(
                                    kout[:, jj, 128:160], pf[b][:, 0:n], sc, rstd[:, 0:n], ALU.mult, ALU.mult),
                                   reads=[("ps", b), "rstd", "prm"], writes=["kout"])
                            elif c0 + n == NP_:
                                op("dve", lambda e, n=n, b=b, jj=jj, sc=sc: e.scalar_tensor_tensor(
                                    kout[:, jj, 0:128], pf[b][:, n - 128:n], sc, rstd[:, n - 128:n], ALU.mult, ALU.mult),
                                   reads=[("ps", b), "rstd", "prm"], writes=["kout"])
            v, r = load_panel(W[s, :, C_VBA:C_VBA + 136], D, 136)
            chunks = chunk_list(g) if "p_vba" in PH else []
            for ci, (stq, lc0, C) in enumerate(chunks):
                b = next_bank([0, 1, 2, 3])
                for k in range(8):
                    op("pe", lambda e, k=k, lc0=lc0, C=C, b=b, v=v: e.matmul(
                        pf[b][0:C, 0:136], hT[:, k, lc0:lc0 + C], v[:, k, :], start=(k == 0), stop=(k == 7)),
                       reads=RH(lc0, C) + [r], writes=[("ps", b)])
                if stq == 0:
                    vdst = vbuf[0:C, 2 + ci, :]
                    vres = ("vbuf", 2 + ci)
                else:
                    vdst = vsb[0:C, 2, :]
                    vres = ("vsb", 2)
                import os
                DV = os.environ.get("DBGV", "")
                if "noact" not in DV: op("dve", lambda e, C=C, b=b, vdst=vdst: e.tensor_copy(
                    vdst.rearrange("p (k d) -> p k d", k=2)[:, :, 0:64],
                    pf[b][0:C, 0:128].rearrange("p (k d) -> p k d", k=2)),
                   reads=[("ps", b), "vbuf_all", "vsb_all"], writes=[vres])
                if "nozba" not in DV: op("dve", lambda e, C=C, b=b, ci=ci: e.tensor_copy(zba[0:C, ci, :], pf[b][0:C, 128:136]),
                   reads=[("ps", b)], writes=["zba"], small=True)
                if "novout" in DV:
                    pass
                elif stq == 1:
                    op("dve", lambda e, C=C, b=b: e.tensor_copy(vout[0:C, 2, :], pf[b][0:C, 0:128]),
                       reads=[("ps", b)], writes=["vout"])
                elif g == ngroup - 1 and ci >= 14:
                    op("dve", lambda e, C=C, b=b, ci=ci: e.tensor_copy(vout[0:C, ci - 14, :], pf[b][0:C, 0:128]),
                       reads=[("ps", b)], writes=["vout"])

        def chunk_list(g):
            ch = [(0, 64 * i, 64) for i in range(16)]
            if g == ngroup - 1:
                ch.append((1, 1024, 32))
            return ch

        def dn_params(g):
            chunks = chunk_list(g)
            nch = len(chunks)
            npc = 16
            has_s = nch > 16
            A3 = lambda t, a=0, b=4: t[:, 0:nch, a:b]
            op("act", lambda e: e.activation(tp_beta[:, 0:nch, :], zba[:, 0:nch, 0:4], AF.Sigmoid),
               reads=["zba"], writes=["tp_beta"], small=True)
            op("dve", lambda e: e.tensor_scalar(tp_negb[:, 0:nch, :], tp_beta[:, 0:nch, :], -1.0, None, ALU.mult),
               reads=["tp_beta"], writes=["tp_negb"], small=True)
            op("dve", lambda e: e.tensor_tensor(tp_t[:, 0:nch, :], zba[:, 0:nch, 4:8],
                                                prm[0:64, P_DTB:P_DTB + 4].unsqueeze(1).broadcast_to([64, nch, 4]), ALU.add),
               reads=["zba", "prm"], writes=["tp_t"], small=True)
            op("act", lambda e: e.activation(tp_t[:, 0:nch, :], tp_t[:, 0:nch, :], AF.Exp), reads=["tp_t"], writes=["tp_t"], small=True)
            op("act", lambda e: e.activation(tp_t[:, 0:nch, :], tp_t[:, 0:nch, :], AF.Ln, bias=1.0), reads=["tp_t"], writes=["tp_t"], small=True)
            op("dve", lambda e: e.scalar_tensor_tensor(tp_g[:, 0:nch, :], tp_t[:, 0:nch, :], -1.0,
                                                       prm2[0:64, 1:5].unsqueeze(1).broadcast_to([64, nch, 4]), ALU.mult, ALU.mult),
               reads=["tp_t", "prm2"], writes=["tp_g"], small=True)
            g2 = tp_g[:].rearrange("p c h -> p (c h)")
            op("pe", lambda e: e.matmul(pf[5][0:64, 0:npc * 4], UT, g2[:, 0:npc * 4], start=True, stop=True),
               reads=["tp_g", "cf"], writes=[("ps", 5)])
            op("pe", lambda e: e.matmul(pf[6][:, 0:npc * 4], ones_f, g2[:, 0:npc * 4], start=True, stop=True),
               reads=["tp_g", "cf"], writes=[("ps", 6)])
            if has_s:
                op("pe", lambda e: e.matmul(pf[5][0:32, 64:68], cf[0:32, 192:224], g2[0:32, 64:68], start=True, stop=True),
                   reads=["tp_g", "cf"], writes=[("ps", 5)])
                op("pe", lambda e: e.matmul(pf[6][:, 64:68], cf[0:32, 256:384], g2[0:32, 64:68], start=True, stop=True),
                   reads=["tp_g", "cf"], writes=[("ps", 6)])
            n4 = nch * 4
            f2 = lambda t: t[:].rearrange("p c h -> p (c h)")[:, 0:n4]
            op("act", lambda e: e.copy(f2(tp_gc), pf[5][0:64, 0:n4]), reads=[("ps", 5)], writes=["tp_gc"], small=True)
            op("act", lambda e: e.activation(f2(tp_egc), pf[5][0:64, 0:n4], AF.Exp), reads=[("ps", 5)], writes=["tp_egc"], small=True)
            op("dve", lambda e: e.tensor_tensor(f2(tp_ekr), pf[6][0:64, 0:n4], f2(tp_gc), ALU.subtract),
               reads=[("ps", 6), "tp_gc"], writes=["tp_ekr"], small=True)
            op("act", lambda e: e.activation(f2(tp_ekr), f2(tp_ekr), AF.Exp), reads=["tp_ekr"], writes=["tp_ekr"], small=True)
            op("act", lambda e: e.activation(tp_egl[:].rearrange("p c h -> p (c h)")[:, 0:n4], pf[6][:, 0:n4], AF.Exp),
               reads=[("ps", 6)], writes=["tp_egl"], small=True)

        def bc_h(ap2, C, n):
            return ap2.unsqueeze(2).broadcast_to([C, 4, n])

        def bc_m(ap2, C):
            return ap2.unsqueeze(1).broadcast_to([C, 4, C])

        def dn_chunk(ci, stq, lc0, C):
            fs = (C < 64)
            pre, stp = [], []
            curl = [pre]

            def op(*a, **k):
                k["small"] = k.get("small", False) or fs
                curl[0].append((a, k))
            par = ci % 2
            inT, kr, u_sb, w0T = inT2[par], kr2[par], u_sb2[par], w0T2[par]
            n_inT, n_kr, n_u, n_w0 = "inT%d" % par, "kr%d" % par, ("u_sb" if par == 0 else "cact"), "w0T%d" % par
            cs = slice(lc0, lc0 + C)
            qT = lambda h: U[:, h, cs]
            kT = lambda h: U[:, 4 + h, cs]
            vT = lambda h: U[:, 8 + h, cs]
            rq = [r for h in range(4) for r in RU(h, lc0, C)]
            rk = [r for h in range(4) for r in RU(4 + h, lc0, C)]
            rv = [r for h in range(4) for r in RU(8 + h, lc0, C)]
            v3 = lambda t, n=None: (t[0:C, :, 0:(n or C)])
            op("dve", lambda e: e.tensor_tensor(v3(Gm), bc_h(tp_g[0:C, ci, :], C, C), bc_m(cf[0:C, 192:192 + C], C), ALU.mult),
               reads=["tp_g", "cf"], writes=["Gm"])
            op("pe", lambda e: e.matmul(pf[5][0:C, 0:4 * C].rearrange("p (h i) -> p h i", h=4), cf[0:C, 128:128 + C], v3(Gm),
                                        start=True, stop=True), reads=["Gm", "cf"], writes=[("ps", 5)])
            op("act", lambda e: e.activation(v3(DTu), pf[5][0:C, 0:4 * C].rearrange("p (h i) -> p h i", h=4), AF.Exp),
               reads=[("ps", 5)], writes=["DTu"])
            op("dve", lambda e: e.tensor_tensor(v3(DTs), v3(DTu), bc_m(cf[0:C, 192:192 + C], C), ALU.mult),
               reads=["DTu", "cf"], writes=["DTs"])
            for h in range(4):
                op("pe", lambda e, h=h: e.matmul(pf[6][0:C, h * C:(h + 1) * C], kT(h), kT(h), start=True, stop=True),
                   reads=RU(4 + h, lc0, C), writes=[("ps", 6)])
            for h in range(4):
                op("pe", lambda e, h=h: e.matmul(pf[5][0:C, h * C:(h + 1) * C], kT(h), qT(h), start=True, stop=True),
                   reads=RU(4 + h, lc0, C) + RU(h, lc0, C) + ["DTu"], writes=[("ps", 5)])
            for h in range(4):
                op("pe", lambda e, h=h: e.transpose(pb[0:C, 0, h * 128:(h + 1) * 128], kT(h), ident_b),
                   reads=RU(4 + h, lc0, C) + ["cfb"], writes=[("ps", 7)])
            pb3 = lambda i: pb[0:C, i, :].rearrange("p (h d) -> p h d", h=4)
            op("dve", lambda e: e.tensor_tensor(kg[0:C], pb3(0), bc_h(tp_egc[0:C, ci, :], C, 128), ALU.mult),
               reads=[("ps", 7), "tp_egc"], writes=["kg"])
            op("dve", lambda e: e.tensor_tensor(kr[0:C], pb3(0), bc_h(tp_ekr[0:C, ci, :], C, 128), ALU.mult),
               reads=[("ps", 7), "tp_ekr"], writes=[n_kr])
            for h in range(4):
                op("pe", lambda e, h=h: e.transpose(pb[0:C, 0, h * 128:(h + 1) * 128], vT(h), ident_b),
                   reads=RU(8 + h, lc0, C) + ["cfb"], writes=[("ps", 7)])
            op("act", lambda e: e.copy(vtm[0:C], pb3(0)), reads=[("ps", 7)], writes=["vtm"])
            ps3 = lambda b: pf[b][0:C, 0:4 * C].rearrange("p (h i) -> p h i", h=4)
            op("dve", lambda e: e.tensor_tensor(v3(inT), ps3(5), v3(DTs), ALU.mult), reads=[("ps", 5), "DTs"], writes=[n_inT])
            op("dve", lambda e: e.tensor_tensor(v3(DTu), v3(DTs), bc_m(cf[0:C, 0:C], C), ALU.subtract),
               reads=["DTs", "cf"], writes=["DTu"])
            op("dve", lambda e: e.tensor_tensor(v3(tA), ps3(6), v3(DTu), ALU.mult), reads=[("ps", 6), "DTu"], writes=["Gm"])
            op("dve", lambda e: e.tensor_tensor(v3(Pk[0]), v3(tA), bc_h(tp_negb[0:C, ci, :], C, C), ALU.mult),
               reads=["Gm", "tp_negb"], writes=["Pk"])
            op("dve", lambda e: e.tensor_tensor(v3(Rk[0]), v3(tA), bc_h(tp_negb[0:C, ci, :], C, C), ALU.mult),
               reads=["Gm", "tp_negb"], writes=["Rk"])
            for h in range(4):
                op("pe", lambda e, h=h: e.transpose(pb[0:C, 0, h * C:(h + 1) * C], Pk[0][0:C, h, 0:C], cfb[0:C, 0:C]),
                   reads=["Pk", "cfb"], writes=[("ps", 7)])
            op("act", lambda e: e.copy(v3(PTk[0]), pb[0:C, 0, 0:4 * C].rearrange("p (h i) -> p h i", h=4)), reads=[("ps", 7)], writes=["PTk"])
            op("dve", lambda e: e.tensor_tensor(v3(Rk[0]), v3(Rk[0]), bc_m(cf[0:C, 0:C], C), ALU.add),
               reads=["Rk", "cf"], writes=["Rk"])
            op("dve", lambda e: e.tensor_copy(v3(Rkb), v3(Rk[0])), reads=["Rk"], writes=["Rkb"])
            nlev = 5 if C == 64 else 4
            cur = 0
            for lv in range(nlev):
                nx = 1 - cur
                last = (lv == nlev - 1)
                if not last:
                    for h in range(4):
                        op("pe", lambda e, h=h, cur=cur: e.matmul(pf[5][0:C, h * C:(h + 1) * C], PTk[cur][0:C, h, 0:C], Pk[cur][0:C, h, 0:C],
                                                                  start=True, stop=True),
                           reads=["Pk", "PTk"], writes=[("ps", 5)])
                for h in range(4):
                    op("pe", lambda e, h=h, cur=cur: e.matmul(pf[6][0:C, h * C:(h + 1) * C], Pk[cur][0:C, h, 0:C], PTk[cur][0:C, h, 0:C],
                                                              start=True, stop=True),
                       reads=["Pk", "PTk"], writes=[("ps", 6)])
                if not last:
                    op("act", lambda e, nx=nx: e.copy(v3(Pk[nx]), ps3(5)), reads=[("ps", 5)], writes=["Pk"])
                op("dve", lambda e, nx=nx: e.tensor_copy(v3(PTk[nx]), ps3(6)), reads=[("ps", 6)], writes=["PTk"])
                for h in range(4):
                    op("pe", lambda e, h=h, cur=cur, nx=nx: e.matmul(pf[5][0:C, h * C:(h + 1) * C], PTk[nx][0:C, h, 0:C], Rkb[0:C, h, 0:C],
                                                                     start=True, stop=True),
                       reads=["PTk", "Rkb"], writes=[("ps", 5)])
                if not last:
                    op("dve", lambda e, cur=cur: e.tensor_tensor(v3(Rkb), ps3(5), v3(Rk[cur]), ALU.add),
                       reads=[("ps", 5), "Rk"], writes=["Rkb"])
                    op("dve", lambda e, cur=cur, nx=nx: e.tensor_tensor(v3(Rk[nx]), ps3(5), v3(Rk[cur]), ALU.add),
                       reads=[("ps", 5), "Rk"], writes=["Rk"])
                else:
                    op("dve", lambda e, cur=cur: e.tensor_tensor(v3(Rbf), ps3(5), v3(Rk[cur]), ALU.add),
                       reads=[("ps", 5), "Rk"], writes=["Rbf"])
                cur = nx
            for h in range(4):
                op("pe", lambda e, h=h: e.matmul(pf[6][0:C, h * 128:(h + 1) * 128], Rbf[0:C, h, 0:C], vtm[0:C, h, :], start=True, stop=True),
                   reads=["Rbf", "vtm"], writes=[("ps", 6)])
            for h in range(4):
                op("pe", lambda e, h=h: e.matmul(pf[5][:, h * C:(h + 1) * C], kg[0:C, h, :], Rbf[0:C, h, 0:C], start=True, stop=True),
                   reads=["Rbf", "kg"], writes=[("ps", 5)])
            op("dve", lambda e: e.tensor_tensor(u_sb[0:C], pf[6][0:C, :].rearrange("p (h d) -> p h d", h=4),
                                                bc_h(tp_beta[0:C, ci, :], C, 128), ALU.mult),
               reads=[("ps", 6), "tp_beta"], writes=[n_u])
            op("act", lambda e: e.copy(w0T[:, :, 0:C], pf[5][:, 0:4 * C].rearrange("p (h i) -> p h i", h=4)),
               reads=[("ps", 5)], writes=[n_w0])
            curl[0] = stp
            Sres = "S_b%d" % stq
            Sfres = "S_f%d" % stq
            for h in range(4):
                op("pe", lambda e, h=h: e.matmul(pf[1][0:C, h * 128:(h + 1) * 128], w0T[:, h, 0:C], S_b[:, stq, h, :], start=True, stop=True),
                   reads=[n_w0, Sres], writes=[("ps", 1)])
            p64 = lambda b: pf[b][0:C, :].rearrange("p (h d) -> p h d", h=4)
            op("dve", lambda e: e.tensor_tensor(t1[0:C], p64(1), bc_h(tp_negb[0:C, ci, :], C, 128), ALU.mult),
               reads=[("ps", 1), "tp_negb"], writes=["cacc"])
            op("dve", lambda e: e.tensor_tensor(vnew[0:C], t1[0:C], u_sb[0:C], ALU.add), reads=["cacc", n_u], writes=["vnew"])
            for h in range(4):
                op("pe", lambda e, h=h: e.matmul(pf[0][0:C, h * 128:(h + 1) * 128], qT(h), S_b[:, stq, h, :], start=True, stop=True),
                   reads=RU(h, lc0, C) + [Sres], writes=[("ps", 0)])
            for h in range(4):
                op("pe", lambda e, h=h: e.matmul(pf[1][0:C, h * 128:(h + 1) * 128], inT[0:C, h, 0:C], vnew[0:C, h, :], start=True, stop=True),
                   reads=[n_inT, "vnew"], writes=[("ps", 1)])
            op("dve", lambda e: e.tensor_tensor(t1[0:C], p64(0), bc_h(tp_egc[0:C, ci, :], C, 128), ALU.mult),
               reads=[("ps", 0), "tp_egc"], writes=["cacc"])
            op("dve", lambda e: e.tensor_tensor(o_sb[0:C], t1[0:C], p64(1), ALU.add), reads=["cacc", ("ps", 1)], writes=["rstd"])
            for h in range(4):
                op("pe", lambda e, h=h: e.matmul(pf[0][:, h * 128:(h + 1) * 128], kr[0:C, h, :], vnew[0:C, h, :], start=True, stop=True),
                   reads=[n_kr, "vnew"], writes=[("ps", 0)])
            op("dve", lambda e: e.tensor_tensor(tS[:], S_f[:, stq], tp_egl[:, ci, :].unsqueeze(2).broadcast_to([128, 4, 128]), ALU.mult),
               reads=[Sfres, "tp_egl"], writes=["tS"])
            op("dve", lambda e: e.tensor_tensor(S_f[:, stq], tS[:], pf[0][:, :].rearrange("p (h d) -> p h d", h=4), ALU.add),
               reads=["tS", ("ps", 0)], writes=[Sfres])
            op("act", lambda e: e.copy(S_b[:, stq], S_f[:, stq]), reads=[Sfres], writes=[Sres])
            op("dve", lambda e: e.tensor_tensor(t1[0:C], o_sb[0:C], o_sb[0:C], ALU.mult), reads=["rstd"], writes=["cacc"])
            op("dve", lambda e: e.tensor_reduce(ss4[0:C, 0:4], t1[0:C], AX.X, ALU.add), reads=["cacc"], writes=["ss4"], small=True)
            op("act", lambda e: e.activation(ss4[0:C, 0:4], ss4[0:C, 0:4], AF.Ln, bias=RMS_EPS, scale=1.0 / 128), reads=["ss4"], writes=["ss4"], small=True)
            op("act", lambda e: e.activation(ss4[0:C, 0:4], ss4[0:C, 0:4], AF.Exp, scale=-0.5), reads=["ss4"], writes=["ss4"], small=True)
            op("dve", lambda e: e.tensor_tensor(t1[0:C], o_sb[0:C], bc_h(ss4[0:C, 0:4], C, 128), ALU.mult),
               reads=["rstd", "ss4"], writes=["cacc"])
            op("dve", lambda e: e.tensor_tensor(on2[0:C], t1[0:C], prm[0:C, P_DNW:P_DNW + 128].unsqueeze(1).broadcast_to([C, 4, 128]), ALU.mult),
               reads=["cacc", "prm"], writes=["sqb"])
            for h in range(4):
                op("pe", lambda e, h=h: e.transpose(pb[:, 1, h * C:(h + 1) * C], on2[0:C, h, :], cfb[0:C, 0:C]),
                   reads=["sqb", "cfb"], writes=[("ps", 7)])
            gview = U[:, 12:16, cs]
            rg = [r for h in range(4) for r in RU(12 + h, lc0, C)]
            op("dve", lambda e: e.tensor_tensor(gview, pb[:, 1, 0:4 * C].rearrange("p (h i) -> p h i", h=4), gview, ALU.mult),
               reads=[("ps", 7)] + rg, writes=rg)

            return pre, stp

        def swa_chunk(ci, stq, lc0, C, g):
            fs = (C < 64)
            lst = []

            def op(*a, **k):
                k["small"] = k.get("small", False) or fs
                lst.append((a, k))
            cs = slice(lc0, lc0 + C)
            if stq == 0:
                keys = []
                for dl in (2, 1, 0):
                    kc = ci - dl
                    if kc < 0:
                        w0 = (kc + 2) * 64
                        keys.append((dl, lambda kv, hf, w0=w0: kwin[hf * 64:(hf + 1) * 64, kv, w0:w0 + 64], ["kwin"],
                                     vbuf[:, kc + 2, :], [("vbuf", kc + 2)], 64, g == 0))
                    else:
                        keys.append((dl, lambda kv, hf, kc=kc: U[hf * 64:(hf + 1) * 64, 20 + kv, kc * 64:kc * 64 + 64],
                                     RU(20, kc * 64, 64) + RU(21, kc * 64, 64),
                                     vbuf[:, kc + 2, :], [("vbuf", kc + 2)], 64, False))
            else:
                keys = [(2, lambda kv, hf: ksc[hf * 64:(hf + 1) * 64, kv, 0:64], ["ksc"], vsb[:, 0, :], [("vsb", 0)], 64, False),
                        (1, lambda kv, hf: ksc[hf * 64:(hf + 1) * 64, kv, 64:128], ["ksc"], vsb[:, 1, :], [("vsb", 1)], 64, False),
                        (0, lambda kv, hf: U[hf * 64:(hf + 1) * 64, 20 + kv, cs], RU(20, lc0, C) + RU(21, lc0, C),
                         vsb[:, 2, :], [("vsb", 2)], 32, False)]
            rq = [r for j in range(4) for r in RU(16 + j, lc0, C)]
            for idx, (dl, kfn, kres, vap, vres, SK, masked) in enumerate(keys):
                for hh in range(8):
                    kv, j, hf = hh // 4, hh // 2, hh % 2
                    sbk = 2 if hf == 0 else 3
                    op("pe", lambda e, hh=hh, kv=kv, j=j, hf=hf, kfn=kfn, SK=SK, sbk=sbk: e.matmul(
                        pf[sbk][0:SK, j * 64:j * 64 + C], kfn(kv, hf), U[hf * 64:(hf + 1) * 64, 16 + j, cs], start=True, stop=True),
                       reads=kres + rq, writes=[("ps", sbk)])
                import os
                SWL = int(os.environ.get("SWL", "9"))
                if SWL < 2:
                    continue
                for hf, sbk in ((0, 2), (1, 3)):
                    sc3 = pf[sbk][0:SK, 0:256].rearrange("p (h i) -> p h i", h=4)[:, :, 0:C]
                    op("dve", lambda e, sc3=sc3, dl=dl, SK=SK, hf=hf: e.tensor_tensor(
                        sct[0:SK, hf * 4:hf * 4 + 4, 0:C], sc3, BT[0:SK, dl, hf * 4:hf * 4 + 4, 0:C], ALU.add),
                       reads=[("ps", sbk), "cf"], writes=["stage"])
                PT = PTs[idx]
                pn = PTn[idx]
                op("act", lambda e, SK=SK, PT=PT: e.activation(PT[0:SK, :, 0:C], sct[0:SK, :, 0:C], AF.Exp), reads=["stage"], writes=[pn])
                if masked:
                    op("dve", lambda e, SK=SK, PT=PT: e.tensor_scalar(PT[0:SK, :, 0:C], PT[0:SK, :, 0:C], mcore[0:SK, 0:1], None, ALU.mult),
                       reads=[pn, "mcore"], writes=[pn])
            for half in ((0, 1) if SWL >= 3 else []):
                bk = 4
                for hh in range(half * 4, half * 4 + 4):
                    kv = hh // 4
                    hp = (hh % 2) * 4 + hh // 2
                    for idx, (dl, kfn, kres, vap, vres, SK, masked) in enumerate(keys):
                        op("pe", lambda e, hh=hh, kv=kv, bk=bk, vap=vap, SK=SK, idx=idx, hp=hp: e.matmul(
                            pf[bk][0:C, (hh % 4) * 66:(hh % 4) * 66 + 66], PTs[idx][0:SK, hp, 0:C], vap[0:SK, kv * 66:kv * 66 + 66],
                            start=(idx == 0), stop=(idx == 2)),
                           reads=[PTn[idx]] + vres, writes=[("ps", bk)])
                o3 = pf[bk][0:C, 0:264].rearrange("p (h d) -> p h d", h=4)
                op("dve", lambda e, o3=o3, half=half: e.tensor_tensor(den[0:C, half * 4:half * 4 + 4], o3[:, :, 64],
                                                                      prm2[0:C, 5 + half * 4:9 + half * 4], ALU.add),
                   reads=[("ps", bk), "prm2"], writes=["den"], small=True)
                op("dve", lambda e, half=half: e.reciprocal(den[0:C, half * 4:half * 4 + 4], den[0:C, half * 4:half * 4 + 4]),
                   reads=["den"], writes=["den"], small=True)
                op("dve", lambda e, o3=o3, half=half: e.tensor_tensor(osw[0:C, half * 4:half * 4 + 4, :], o3[:, :, 0:64],
                                                                      den[0:C, half * 4:half * 4 + 4].unsqueeze(2).broadcast_to([C, 4, 64]), ALU.mult),
                   reads=[("ps", bk), "den"], writes=["osw"])
            if SWL < 5:
                return lst
            for j in range(4):
                op("pe", lambda e, j=j: e.transpose(pb[:, 1, 256 + j * C:256 + (j + 1) * C], osw[0:C, 2 * j:2 * j + 2, :].rearrange("p a d -> p (a d)"), cfb[0:C, 0:C]),
                   reads=["osw", "cfb"], writes=[("ps", 7)])
            op("act", lambda e: e.copy(U[:, 16:20, cs], pb[:, 1, 256:256 + 4 * C].rearrange("p (j i) -> p j i", j=4)),
               reads=[("ps", 7)], writes=rq)
            return lst

        def merge(s, tiles, g):
            import os
            MRG = os.environ.get("MRG", "both")
            for passi, (wo, ucb, gc0) in enumerate([(wodn, 12, C_GA), (woswa, 16, C_GB)]):
                if (MRG == "dn" and passi == 1):
                    continue
                for cb in range(4):
                    (vo, vgp), ro = load_panels([(wo[s, :, cb * 256:(cb + 1) * 256], 512, 256),
                                                 (win[s, :, gc0 + cb * 256:gc0 + (cb + 1) * 256], D, 256)])
                    rgp = ro
                    for jj in range(2):
                        c = cb * 2 + jj
                        for (c0, n) in T_(tiles):
                            lc0 = c0 - g * 1024
                            by = next_bank([0, 1])
                            bg = by + 2
                            for k in range(4):
                                op("pe", lambda e, k=k, jj=jj, lc0=lc0, n=n, by=by, vo=vo, ucb=ucb: e.matmul(
                                    pf[by][:, 0:n], vo[:, k, jj * 128:(jj + 1) * 128], U[:, ucb + k, lc0:lc0 + n],
                                    start=(k == 0), stop=(k == 3)),
                                   reads=RU(ucb + k, lc0, n) + [ro], writes=[("ps", by)])
                            for k in range(8):
                                op("pe", lambda e, k=k, jj=jj, lc0=lc0, n=n, bg=bg, vgp=vgp: e.matmul(
                                    pf[bg][:, 0:n], vgp[:, k, jj * 128:(jj + 1) * 128], hT[:, k, lc0:lc0 + n],
                                    start=(k == 0), stop=(k == 7)),
                                   reads=RH(lc0, n) + [rgp], writes=[("ps", bg)])
                            sl = by
                            op("act", lambda e, n=n, bg=bg, sl=sl: e.activation(sigb[sl][:, 0:n], pf[bg][:, 0:n], AF.Sigmoid),
                               reads=[("ps", bg)], writes=[SG[sl]])
                            if passi == 0:
                                op("dve", lambda e, n=n, by=by, sl=sl, c=c, lc0=lc0: e.tensor_tensor(
                                    U[:, c, lc0:lc0 + n], sigb[sl][:, 0:n], pf[by][:, 0:n], ALU.mult),
                                   reads=[SG[sl], ("ps", by)], writes=RU(c, lc0, n))
                            else:
                                op("dve", lambda e, n=n, by=by, sl=sl: e.tensor_tensor(
                                    sigb[sl][:, 0:n], sigb[sl][:, 0:n], pf[by][:, 0:n], ALU.mult),
                                   reads=[SG[sl], ("ps", by)], writes=[SG[sl]])
                                op("dve", lambda e, n=n, sl=sl, c=c, lc0=lc0: e.tensor_tensor(
                                    U[:, c, lc0:lc0 + n], sigb[sl][:, 0:n], U[:, c, lc0:lc0 + n], ALU.add),
                                   reads=[SG[sl]] + RU(c, lc0, n), writes=RU(c, lc0, n))
            for cb in range(2):
                vw, rw = load_panel(wout[s, :, cb * 512:(cb + 1) * 512], D, 512)
                for jj in range(4):
                    c = cb * 4 + jj
                    for (c0, n) in T_(tiles):
                        lc0 = c0 - g * 1024
                        b = next_bank([0, 1, 2, 3])
                        for k in range(8):
                            op("pe", lambda e, k=k, jj=jj, lc0=lc0, n=n, b=b, vw=vw: e.matmul(
                                pf[b][:, 0:n], vw[:, k, jj * 128:(jj + 1) * 128], U[:, k, lc0:lc0 + n],
                                start=(k == 0), stop=(k == 7)),
                               reads=RU(k, lc0, n) + [rw], writes=[("ps", b)])
                        op("dve", lambda e, c=c, c0=c0, n=n, b=b: e.tensor_tensor(
                            xT[:, c, c0:c0 + n], pf[b][:, 0:n], xT[:, c, c0:c0 + n], ALU.add),
                           reads=[("ps", b)] + RX(c0, n), writes=RX(c0, n))

        halo_all = lambda stq: [("halo", stq, j) for j in range(12)]
        for s in range(nslot):
            op("sp", lambda e, s=s: e.dma_start(out=prm[:], in_=prm_d[s]), writes=["prm"], dma="ld_p")
            op("act", lambda e: e.mul(prm2[:, 0:1], prm[:, P_QNW:P_QNW + 1], 0.125), reads=["prm"], writes=["prm2"], small=True)
            op("act", lambda e: e.activation(prm2[:, 1:5], prm[:, P_ALOG:P_ALOG + 4], AF.Exp), reads=["prm"], writes=["prm2"], small=True)
            op("act", lambda e: e.activation(prm2[:, 5:13], prm[:, P_SINK:P_SINK + 8], AF.Exp), reads=["prm"], writes=["prm2"], small=True)
            op("sp", lambda e, s=s: e.dma_start(out=S_f[:, 1].rearrange("p h d -> p (h d)"), in_=sdn_d[s]), writes=["S_f1"], dma="ld_s1")
            op("sp", lambda e, s=s: e.dma_start(out=halo[:, 1].rearrange("p j r -> p (j r)"), in_=sconv_d[s]), writes=halo_all(1), dma="ld_s2")
            op("sp", lambda e, s=s: e.dma_start(out=kscf[:], in_=skc_d[s]), writes=["cact"], dma="ld_s3")
            op("sp", lambda e, s=s: e.dma_start(out=vscf[:], in_=svc_d[s]), writes=["cact"], dma="ld_s4")
            op("dve", lambda e: e.tensor_copy(ksc[:].rearrange("p a b -> p (a b)"), kscf[:]), reads=["cact"], writes=["ksc"])
            op("dve", lambda e: e.tensor_copy(vsb[:, 0:2, :].rearrange("p c (k d) -> p c k d", k=2)[:, :, :, 0:64], vscf[:]),
               reads=["cact", "vsb_all"], writes=[("vsb", 0), ("vsb", 1)])
            op("act", lambda e: e.copy(S_b[:, 1], S_f[:, 1]), reads=["S_f1"], writes=["S_b1"])
            if s >= 1 and "handoff" in PH:
                op("sp", lambda e: e.dma_start(out=rtmp[:], in_=recv_f[0:128, :]), reads=["recv_f"], writes=["cacc"], dma="ld_r1")
                op("sp", lambda e: e.dma_start(out=rtmpb[:], in_=recv_b[0:128, :]), reads=["recv_b"], writes=["sqb"], dma="ld_r2")
                op("dve", lambda e: e.tensor_scalar(S_f[:, 0].rearrange("p h d -> p (h d)"), rtmp[:, 0:512], mcore[:, 0:1], None, ALU.mult),
                   reads=["cacc", "mcore"], writes=["S_f0"])
                op("dve", lambda e: e.tensor_scalar(halo[:, 0].rearrange("p j r -> p (j r)"), rtmp[:, 512:548], mcore[:, 0:1], None, ALU.mult),
                   reads=["cacc", "mcore"], writes=halo_all(0))
                op("dve", lambda e: e.tensor_scalar(kwin[:].rearrange("p a b -> p (a b)"), rtmpb[:, 0:256], mcore[:, 0:1], None, ALU.mult),
                   reads=["sqb", "mcore"], writes=["kwin"])
                op("dve", lambda e: e.tensor_scalar(vbuf[:, 0:2, :].rearrange("p a b -> p (a b)"), rtmpb[0:64, 256:520], mcore[0:64, 0:1], None, ALU.mult),
                   reads=["sqb", "mcore"], writes=[("vbuf", 0), ("vbuf", 1)])
            op("act", lambda e: e.copy(S_b[:, 0], S_f[:, 0]), reads=["S_f0"], writes=["S_b0"])

            for g in range(ngroup):
                tiles = [(g * 1024, 512), (g * 1024 + 512, 512)]
                if g == ngroup - 1:
                    tiles.append((NP_, NS_))
                dtiles = tiles if g != ngroup - 1 else [(g * 1024, 352), (g * 1024 + 352, 352), (g * 1024 + 704, 352)]
                if "ffn1" in PH:
                    rms_norm_to_hT(dtiles, g, 0)
                    ffn(s, dtiles, g, 0)
                rms_norm_to_hT(dtiles, g, 1)
                if "proj" in PH:
                    proj_in(s, tiles, g)
                chunks = chunk_list(g)
                S.force_small = False
                if "dn" in PH:
                    dn_params(g)
                streams = []
                dnl = [dn_chunk(ci, stq, lc0, C) for ci, (stq, lc0, C) in enumerate(chunks)] if "dn" in PH else None
                swl = [swa_chunk(ci, stq, lc0, C, g) for ci, (stq, lc0, C) in enumerate(chunks)] if "swa" in PH else None

                def emit_merged(lists):
                    items = []
                    for li, L in enumerate(lists):
                        for k, it in enumerate(L):
                            items.append(((k + 0.5) / len(L), li, k, it))
                    items.sort(key=lambda t: (t[0], t[1], t[2]))
                    for _, _, _, (a, k) in items:
                        S.op(*a, **k)
                import os
                if os.environ.get("SEQ", "") == "2":
                    emit_merged([dnl[0][0]])
                    for ci in range(len(chunks)):
                        if ci + 1 < len(chunks):
                            emit_merged([dnl[ci + 1][0]])
                        emit_merged([dnl[ci][1]])
                    dnl = None
                if os.environ.get("SEQ", "0") == "3":
                    for ci in range(len(chunks)):
                        lists = []
                        if dnl is not None:
                            lists.append(dnl[ci][0] + dnl[ci][1])
                        if swl is not None:
                            lists.append(swl[ci])
                        emit_merged([L for L in lists if L])
                    dnl = None
                    swl = None
                if os.environ.get("SEQ", "") == "1":
                    for ci in range(len(chunks)):
                        if dnl is not None:
                            emit_merged([dnl[ci][0]])
                            emit_merged([dnl[ci][1]])
                        if swl is not None:
                            emit_merged([swl[ci]])
                    dnl = None
                    swl = None
                if dnl is not None:
                    emit_merged([dnl[0][0]])
                for ci in range(len(chunks)):
                    lists = []
                    if dnl is not None:
                        lists.append(dnl[ci][1])
                        if ci + 1 < len(chunks):
                            lists.append(dnl[ci + 1][0])
                    if swl is not None:
                        lists.append(swl[ci])
                    emit_merged([L for L in lists if L])
                op("dve", lambda e: e.tensor_copy(kwin[:], U[:, 20:22, 896:1024]),
                   reads=RU(20, 896, 128) + RU(21, 896, 128), writes=["kwin"])
                op("dve", lambda e: e.tensor_copy(vbuf[:, 0:2, :], vbuf[:, 16:18, :]),
                   reads=[("vbuf", 16), ("vbuf", 17)], writes=[("vbuf", 0), ("vbuf", 1)])
                if DUMPU and s == 0:
                    op("sp", lambda e, g=g: e.dma_start(out=dU[g], in_=U[:, 12:20, :]),
                       reads=[("U", j, b) for j in range(12, 20) for b in range(17)], dma="dbgU")
                if "merge" in PH:
                    merge(s, dtiles, g)
                if "ffn2" in PH:
                    rms_norm_to_hT(dtiles, g, 2)
                    ffn(s, dtiles, g, 1)

            op("sp", lambda e, s=s: e.dma_start(out=o_dn[s, 0], in_=S_f[:, 0].rearrange("p h d -> p (h d)")), reads=["S_f0"], dma="o_S0")
            op("sp", lambda e, s=s: e.dma_start(out=o_dn[s, 1], in_=S_f[:, 1].rearrange("p h d -> p (h d)")), reads=["S_f1"], dma="o_S1")
            op("sp", lambda e, s=s: e.dma_start(out=o_conv[s, 0], in_=halo[:, 0].rearrange("p j r -> p (j r)")), reads=halo_all(0), dma="o_h0")
            op("sp", lambda e, s=s: e.dma_start(out=o_conv[s, 1], in_=halo[:, 1].rearrange("p j r -> p (j r)")), reads=halo_all(1), dma="o_h1")
            op("sp", lambda e, s=s: e.dma_start(out=o_kp[s].rearrange("p (a b) -> p a b", a=2), in_=kout[:, :, 0:128]), reads=["kout"], dma="o_k")
            op("sp", lambda e, s=s: e.dma_start(out=o_ks[s].rearrange("p (a b) -> p a b", a=2), in_=kout[:, :, 128:160]), reads=["kout"], dma="o_k")
            op("sp", lambda e, s=s: e.dma_start(out=o_vp[s].rearrange("p (a b) -> p a b", a=2), in_=vout[:, 0:2, :]), reads=["vout"], dma="o_v")
            op("sp", lambda e, s=s: e.dma_start(out=o_vs[s], in_=vout[0:32, 2, :]), reads=["vout"], dma="o_v")
            if s < nslot - 1 and "handoff" in PH:
                op("sp", lambda e: e.dma_start(out=send_f[:, 0:512], in_=S_f[:, 0].rearrange("p h d -> p (h d)")), reads=["S_f0", "recv_f"], writes=["send_f"], dma="snd_f")
                op("sp", lambda e: e.dma_start(out=send_f[:, 512:548], in_=halo[:, 0].rearrange("p j r -> p (j r)")), reads=halo_all(0), writes=["send_f"], dma="snd_f")
                op("sp", lambda e: e.dma_start(out=send_b[:, 0:256], in_=kwin[:].rearrange("p a b -> p (a b)")), reads=["kwin", "recv_b"], writes=["send_b"], dma="snd_b")
                op("sp", lambda e: e.dma_start(out=send_b[0:64, 256:520], in_=vbuf[:, 0:2, :].rearrange("p a b -> p (a b)")), reads=[("vbuf", 0), ("vbuf", 1)], writes=["send_b"], dma="snd_b")
                groups = [[0, 1], [2, 3], [4, 5], [6, 7]]
                op("pool", lambda e: e.collective_compute("AllGather", ALU.bypass, replica_groups=groups, ins=[send_f], outs=[recv_f]),
                   reads=["send_f"], writes=["recv_f"], dma="cc_f", amt=1)
                op("pool", lambda e: e.collective_compute("AllGather", ALU.bypass, replica_groups=groups, ins=[send_b], outs=[recv_b]),
                   reads=["send_b"], writes=["recv_b"], dma="cc_b", amt=1)
        op("sp", lambda e: e.dma_start(out=yT.rearrange("j p t -> p j t"), in_=xT[:]), reads=[("xT", b) for b in range(5)], dma="out")
        nops = S.emit(final_waits=(["dbgU"] if DUMPU else []) + ["out", "o_S0", "o_S1", "o_h0", "o_h1", "o_k", "o_v"])
    return nc, nops


def _consts():
    cf = np.zeros((128, 1920), np.float32)
    cf[:, 0:128] = np.eye(128, dtype=np.float32)
    p = np.arange(64)[:, None]
    j = np.arange(64)[None, :]
    cf[0:64, 128:192] = (p > j)
    cf[0:64, 192:256] = (p <= j)
    cf[0:64, 256:384] = 1.0
    slopes = 2.0 ** (-8.0 * np.arange(1, 9, dtype=np.float32) / 8)
    s_ = np.arange(64)[:, None].astype(np.float32)
    i_ = np.arange(64)[None, :].astype(np.float32)
    BT = np.zeros((64, 3, 8, 64), np.float32)
    for dl in range(3):
        dist = np.abs(i_ + 64 * dl - s_)
        for h in range(8):
            BT[:, dl, (h % 2) * 4 + h // 2, :] = -slopes[h] * dist
    cf[0:64, 384:1920] = BT.reshape(64, -1)
    return cf


def _slot_stack(arr, odd):
    z = np.zeros((1,) + arr.shape[1:], arr.dtype)
    return np.ascontiguousarray(np.concatenate([z, arr] if odd else [arr, z], axis=0))


def kernel(**inp):
    f = lambda k: np.asarray(inp[k], dtype=np.float32)
    x_prompt, x_sample = f("x_prompt"), f("x_sample")
    state_dn, state_conv = f("state_dn"), f("state_conv")
    cache_k, cache_v = f("cache_swa_k"), f("cache_swa_v")
    w_in = f("w_in")
    idx = np.concatenate([np.arange(0, 1536), np.arange(1536, 2048), np.arange(2056, 2568),
                          np.arange(2568, 2632), np.arange(2568, 2632), np.arange(2632, 2696), np.arange(2632, 2696),
                          np.arange(2696, 2824), np.arange(2048, 2056), np.arange(2824, 3848), np.arange(3848, 4872)])
    assert idx.size == WIN_COLS
    win_r = w_in[:, :, idx]
    prm = np.zeros((DEPTH, 128, NPRM), np.float32)
    for i, k in enumerate(["ffn1_norm", "mix_norm", "ffn2_norm"]):
        prm[:, :, P_NW + i * 8:P_NW + i * 8 + 8] = f(k).reshape(DEPTH, 8, 128).transpose(0, 2, 1)
    prm[:, :, P_CW:P_CW + 48] = f("conv_w").reshape(DEPTH, 4, 12, 128).transpose(0, 3, 2, 1).reshape(DEPTH, 128, 48)
    prm[:, :, P_DNW:P_DNW + 128] = f("dn_norm")[:, None, :]
    prm[:, :, P_QNW] = np.tile(f("q_norm"), (1, 2))
    prm[:, :, P_KNW] = np.tile(f("k_norm"), (1, 2))
    prm[:, :, P_ALOG:P_ALOG + 4] = f("a_log")[:, None, :]
    prm[:, :, P_DTB:P_DTB + 4] = f("dt_bias")[:, None, :]
    prm[:, :, P_SINK:P_SINK + 8] = f("sinks")[:, None, :]
    big = {"wg1": f("ffn1_wg"), "wu1": f("ffn1_wu"), "wd1": f("ffn1_wd"), "wg2": f("ffn2_wg"), "wu2": f("ffn2_wu"),
           "wd2": f("ffn2_wd"), "win": win_r, "wodn": f("w_o_dn"), "woswa": f("w_o_swa"), "wout": f("w_out"), "prm": prm}
    stacks = [{k: _slot_stack(v, odd) for k, v in big.items()} for odd in (False, True)]
    cf = _consts()
    in_maps = []
    for c in range(8):
        b, odd = c // 2, c % 2
        m = dict(stacks[odd])
        xt = np.concatenate([x_prompt[b, odd * NP_:(odd + 1) * NP_], x_sample[c]], axis=0)
        m["xT0"] = np.ascontiguousarray(xt.T.reshape(8, 128, NT))
        sdn = state_dn[:, c].transpose(0, 2, 1, 3).reshape(DEPTH, 128, 512)
        sconv = state_conv[:, c].reshape(DEPTH, 3, 12, 128).transpose(0, 3, 2, 1).reshape(DEPTH, 128, 36)
        ck = cache_k[:, c].transpose(0, 2, 3, 1)
        skc = np.concatenate([ck, ck], axis=2).transpose(0, 2, 1, 3).reshape(DEPTH, 128, 256)
        svc = cache_v[:, c].reshape(DEPTH, 2, 64, 2, 64).transpose(0, 2, 1, 3, 4)
        m["sdn"] = _slot_stack(sdn, odd)
        m["sconv"] = _slot_stack(sconv, odd)
        m["skc"] = _slot_stack(skc, odd)
        m["svc"] = _slot_stack(svc, odd)
        m["mcore"] = np.full((128, 1), float(odd), np.float32)
        m["cf"] = cf
        in_maps.append(m)
    nc, _ = build_nc()
    res = run_bass_kernel_spmd(nc, in_maps, core_ids=list(range(8)))
    R = res.results
    y_prompt = np.zeros((4, 4096, D), np.float32)
    y_sample = np.zeros((8, NS_, D), np.float32)
    dn_prompt = np.zeros((DEPTH, 4, 4, 128, 128), np.float32)
    dn_sample = np.zeros((DEPTH, 8, 4, 128, 128), np.float32)
    conv_prompt = np.zeros((DEPTH, 4, 3, 1536), np.float32)
    conv_sample = np.zeros((DEPTH, 8, 3, 1536), np.float32)
    kp = np.zeros((DEPTH, 4, 128, 2, 64), np.float32)
    vp = np.zeros((DEPTH, 4, 128, 2, 64), np.float32)
    ks = np.zeros((DEPTH, 8, NS_, 2, 64), np.float32)
    vs = np.zeros((DEPTH, 8, NS_, 2, 64), np.float32)
    for c in range(8):
        b, odd = c // 2, c % 2
        r = R[c]
        yt = r["yT"].reshape(D, NT).T
        y_prompt[b, odd * NP_:(odd + 1) * NP_] = yt[0:NP_]
        y_sample[c] = yt[NP_:]
        for l in range(DEPTH):
            s = l + odd
            dn_sample[l, c] = r["o_dn"][s, 1].reshape(128, 4, 128).transpose(1, 0, 2)
            conv_sample[l, c] = r["o_conv"][s, 1].reshape(128, 12, 3).transpose(2, 1, 0).reshape(3, 1536)
            ks[l, c] = r["o_ks"][s].reshape(128, 2, 32)[0:64].transpose(2, 1, 0)
            vs[l, c] = r["o_vs"][s].reshape(32, 2, 64)
            if odd:
                dn_prompt[l, b] = r["o_dn"][s, 0].reshape(128, 4, 128).transpose(1, 0, 2)
                conv_prompt[l, b] = r["o_conv"][s, 0].reshape(128, 12, 3).transpose(2, 1, 0).reshape(3, 1536)
                kp[l, b] = r["o_kp"][s].reshape(128, 2, 128)[0:64].transpose(2, 1, 0)
                vp[l, b] = r["o_vp"][s].reshape(64, 2, 2, 64).transpose(1, 0, 2, 3).reshape(128, 2, 64)
    return (y_prompt, y_sample, dn_prompt, dn_sample, conv_prompt, conv_sample, kp, vp, ks, vs)
```
